# Optimizing a Trainium2 kernel written in Bass

```python
import functools
import jax, jax.numpy as jnp
from jax import lax
import numpy as np

D_MODEL = 1024
BATCH = 32
SEQ = 256
DEPTH = 4
DEC_BATCH = 4
DEC_SEQ = 4096
PAST_LEN = 256

GRID_W = 64
D_MIX = D_MODEL
A_W = D_MIX // 4
B_W = D_MIX // 4
B_GROUPS = 4
B_HD = B_W // B_GROUPS
CHUNK = 128
C_W = D_MIX // 2
C_HEADS = 8
C_HD = C_W // C_HEADS
WIN_ROWS = 8
WIN_COLS = 16
Q_BLOCK = WIN_COLS
K_BLOCK = 2 * WIN_COLS
N_COL_BLOCKS = GRID_W // Q_BLOCK
CONV_K = 3
ALPHA = (2 * DEPTH) ** 0.25
BETA = (8 * DEPTH) ** -0.25
ATTN_SCALE = C_HD ** -0.5
NEG_INF = -1e30
_BRANCH_WIDTHS = (A_W, A_W, A_W, A_W, B_W, B_W, B_W, C_W, C_W, C_W, C_W)
IN_W = sum(_BRANCH_WIDTHS)
SPLIT_POINTS = tuple(sum(_BRANCH_WIDTHS[:i + 1]) for i in range(len(_BRANCH_WIDTHS) - 1))

kernel_name = "hybrid_dit_conv_gmlp_natten_step"


def _layernorm(x, eps=1e-5):
    xf = x.astype(jnp.float32)
    mu = jnp.mean(xf, axis=-1, keepdims=True)
    var = jnp.mean(jnp.square(xf - mu), axis=-1, keepdims=True)
    return ((xf - mu) * lax.rsqrt(var + eps)).astype(x.dtype)


def _short_conv(xa, ba, ca, conv_w):
    z = ca * xa
    zp = jnp.pad(z, ((0, 0), (1, 1), (0, 0)))
    conv = conv_w[0] * zp[:, :-2] + conv_w[1] * zp[:, 1:-1] + conv_w[2] * zp[:, 2:]
    return ba * conv


def _chunk_gmlp(u, v, ln_g, ln_b, w_s, b_s):
    bn, n, _ = v.shape
    v = _layernorm(v) * ln_g + ln_b
    v = v.reshape(bn, n // CHUNK, CHUNK, B_GROUPS, B_HD)
    sv = jnp.einsum('gpq,bnqgc->bnpgc', w_s, v) + b_s.T[:, :, None]
    return u * sv.reshape(bn, n, B_W)


def _ctx_attention(q, k, v):
    s = jnp.einsum('bqhd,bkhd->bhqk', q, k, preferred_element_type=jnp.float32) * ATTN_SCALE
    p = jax.nn.softmax(s, axis=-1).astype(v.dtype)
    return jnp.einsum('bhqk,bkhd->bqhd', p, v)


def _neighbourhood_attention(q, k, v, ctx_k, ctx_v, rpb):
    bn, n = q.shape[:2]
    rows = n // GRID_W
    wr = min(WIN_ROWS, rows)
    r = jnp.arange(rows)
    row_start = jnp.clip(r - wr // 2, 0, rows - wr)
    row_idx = row_start[:, None] + jnp.arange(wr)
    qcol = jnp.arange(N_COL_BLOCKS)[:, None] * Q_BLOCK + jnp.arange(Q_BLOCK)
    blk_start = jnp.clip(jnp.arange(N_COL_BLOCKS) * Q_BLOCK - WIN_COLS // 2, 0, GRID_W - K_BLOCK)
    col_idx = blk_start[:, None] + jnp.arange(K_BLOCK)

    ri = row_idx[:, None, :, None]
    ci = col_idx[None, :, None, :]
    kg = k.reshape(bn, rows, GRID_W, C_HEADS, C_HD)
    vg = v.reshape(bn, rows, GRID_W, C_HEADS, C_HD)
    kb = kg[:, ri, ci].reshape(bn, rows, N_COL_BLOCKS, wr * K_BLOCK, C_HEADS, C_HD)
    vb = vg[:, ri, ci].reshape(bn, rows, N_COL_BLOCKS, wr * K_BLOCK, C_HEADS, C_HD)
    qb = q.reshape(bn, rows, N_COL_BLOCKS, Q_BLOCK, C_HEADS, C_HD)

    dr_idx = row_idx - r[:, None] + (WIN_ROWS - 1)
    dc = col_idx[:, None, :] - qcol[:, :, None]
    dc_idx = jnp.clip(dc, -(WIN_COLS - 1), WIN_COLS - 1) + (WIN_COLS - 1)
    bias = rpb[:, dr_idx[:, None, None, :, None], dc_idx[None, :, :, None, :]]
    bias = jnp.moveaxis(bias, 0, 2).reshape(rows, N_COL_BLOCKS, C_HEADS, Q_BLOCK, wr * K_BLOCK)
    col_start = jnp.clip(qcol - WIN_COLS // 2, 0, GRID_W - WIN_COLS)
    valid = (col_idx[:, None, :] >= col_start[..., None]) & (col_idx[:, None, :] < col_start[..., None] + WIN_COLS)
    valid = jnp.broadcast_to(valid[:, :, None, :], (N_COL_BLOCKS, Q_BLOCK, wr, K_BLOCK))
    valid = valid.reshape(N_COL_BLOCKS, 1, Q_BLOCK, wr * K_BLOCK)

    s_loc = jnp.einsum('brjqhd,brjkhd->brjhqk', qb, kb, preferred_element_type=jnp.float32) * ATTN_SCALE
    s_loc = jnp.where(valid, s_loc + bias, NEG_INF)
    s_ctx = jnp.einsum('brjqhd,blhd->brjhql', qb, ctx_k, preferred_element_type=jnp.float32) * ATTN_SCALE
    p = jax.nn.softmax(jnp.concatenate([s_loc, s_ctx], axis=-1), axis=-1).astype(v.dtype)
    p_loc, p_ctx = p[..., :wr * K_BLOCK], p[..., wr * K_BLOCK:]
    out = (jnp.einsum('brjhqk,brjkhd->brjqhd', p_loc, vb)
           + jnp.einsum('brjhql,blhd->brjqhd', p_ctx, ctx_v))
    return out.reshape(bn, n, C_HEADS, C_HD)


def _layer(x, mod, w_in_l, conv_w_l, gln_g, gln_b, ws_l, bs_l, w_out_l, ln_g_l, ln_b_l, attn_fn):
    shift, scale, gate = jnp.split(mod, 3, axis=-1)
    h = _layernorm(x) * (1 + scale[:, None]) + shift[:, None]
    proj = h @ w_in_l
    xa, ba, ca, ga, u, v, gb, q, k, vv, gc = jnp.split(proj, SPLIT_POINTS, axis=-1)
    bn, n, _ = x.shape
    ya = _short_conv(xa, ba, ca, conv_w_l) * jax.nn.silu(ga)
    yb = _chunk_gmlp(u, v, gln_g, gln_b, ws_l, bs_l) * jax.nn.silu(gb)
    q = q.reshape(bn, n, C_HEADS, C_HD)
    k = k.reshape(bn, n, C_HEADS, C_HD)
    vv = vv.reshape(bn, n, C_HEADS, C_HD)
    yc = attn_fn(q, k, vv).reshape(bn, n, C_W) * jax.nn.silu(gc)
    out = jnp.concatenate([ya, yb, yc], axis=-1) @ w_out_l
    x_new = _layernorm(ALPHA * x + gate[:, None] * out) * ln_g_l + ln_b_l
    return x_new, k, vv


def setup_inputs(seed: int = 0) -> dict:
    key = jax.random.key(seed)
    ks = jax.random.split(key, 20)
    nrm = jax.random.normal
    d = D_MODEL
    return {
        "x_prompt": nrm(ks[0], (BATCH, SEQ, d), jnp.float32),
        "x_sample": nrm(ks[1], (DEC_BATCH, DEC_SEQ, d), jnp.float32),
        "cache_k": nrm(ks[2], (DEC_BATCH, DEPTH, PAST_LEN, C_HEADS, C_HD), jnp.float32),
        "cache_v": nrm(ks[3], (DEC_BATCH, DEPTH, PAST_LEN, C_HEADS, C_HD), jnp.float32),
        "c": nrm(ks[4], (DEC_BATCH, d), jnp.float32),
        "c_ctx": nrm(ks[5], (d,), jnp.float32),
        "w_ada": nrm(ks[6], (DEPTH, d, 3 * d), jnp.float32) * (0.5 * d ** -0.5),
        "b_ada": 0.01 * nrm(ks[7], (DEPTH, 3 * d), jnp.float32),
        "w_in": nrm(ks[8], (DEPTH, d, IN_W), jnp.float32) * d ** -0.5,
        "conv_w": nrm(ks[9], (DEPTH, CONV_K, A_W), jnp.float32) * CONV_K ** -0.5,
        "gmlp_ln_g": 1.0 + 0.05 * nrm(ks[10], (DEPTH, B_W), jnp.float32),
        "gmlp_ln_b": 0.02 * nrm(ks[11], (DEPTH, B_W), jnp.float32),
        "w_spatial": nrm(ks[12], (DEPTH, B_GROUPS, CHUNK, CHUNK), jnp.float32) * CHUNK ** -0.5,
        "b_spatial": 1.0 + 0.02 * nrm(ks[13], (DEPTH, B_GROUPS, CHUNK), jnp.float32),
        "rpb": 0.1 * nrm(ks[14], (DEPTH, C_HEADS, 2 * WIN_ROWS - 1, 2 * WIN_COLS - 1), jnp.float32),
        "w_out": nrm(ks[15], (DEPTH, D_MIX, d), jnp.float32) * (D_MIX ** -0.5 * BETA),
        "ln_g": 1.0 + 0.05 * nrm(ks[16], (DEPTH, d), jnp.float32),
        "ln_b": 0.02 * nrm(ks[17], (DEPTH, d), jnp.float32),
    }


def reference(x_prompt, x_sample, cache_k, cache_v, c, c_ctx, w_ada, b_ada, w_in, conv_w,
              gmlp_ln_g, gmlp_ln_b, w_spatial, b_spatial, rpb, w_out, ln_g, ln_b):
    xp = x_prompt
    xs = x_sample
    silu_ctx = jax.nn.silu(c_ctx)[None]
    silu_c = jax.nn.silu(c)
    new_k = []
    new_v = []
    for l in range(DEPTH):
        weights_l = (w_in[l], conv_w[l], gmlp_ln_g[l], gmlp_ln_b[l], w_spatial[l], b_spatial[l],
                     w_out[l], ln_g[l], ln_b[l])
        mod_ctx = silu_ctx @ w_ada[l] + b_ada[l]
        xp, k_ctx, v_ctx = _layer(xp, mod_ctx, *weights_l, _ctx_attention)
        new_k.append(k_ctx)
        new_v.append(v_ctx)
        mod_lat = silu_c @ w_ada[l] + b_ada[l]
        attn_lat = functools.partial(_neighbourhood_attention, ctx_k=cache_k[:, l],
                                     ctx_v=cache_v[:, l], rpb=rpb[l])
        xs, _, _ = _layer(xs, mod_lat, *weights_l, attn_lat)
    new_k_arr = jnp.stack(new_k, axis=1)
    new_v_arr = jnp.stack(new_v, axis=1)
    return (xp, xs, new_k_arr, new_v_arr)
```

```python
import contextlib
import numpy as np
import concourse.bass as bass
import concourse.mybir as mybir
from concourse.bass_utils import run_bass_kernel_spmd

F32 = mybir.dt.float32
BF16 = mybir.dt.bfloat16
ALU = mybir.AluOpType
AF = mybir.ActivationFunctionType

D = 1024
DEPTH = 4
ALPHA = (2 * DEPTH) ** 0.25
SCALE = 64 ** -0.5
NEG = -30000.0
ENGS = ("pe", "act", "dve", "pool", "sp")
MAXOPS = 10 ** 9
SCHEDULE = True
SKIPP = False
SKIPS = False


class Buf:
    __slots__ = ("name", "excl", "last_w", "readers")

    def __init__(self, name, excl=False):
        self.name = name
        self.excl = excl
        self.last_w = None
        self.readers = []


class Op:
    __slots__ = ("eng", "fn", "is_dma", "sem_key", "deps", "signal", "count", "dma_count", "idx", "reads", "writes",
                 "preds", "succs", "npred", "cost", "fin", "pos")

    def __init__(self, eng, fn, is_dma, sem_key):
        self.eng = eng
        self.fn = fn
        self.is_dma = is_dma
        self.sem_key = sem_key
        self.deps = []
        self.signal = False
        self.count = 0
        self.dma_count = 0


class _Rec:
    def __init__(self):
        self.name = None
        self.kw = {}
        self.args = ()

    def __getattr__(self, name):
        def f(*a, **k):
            self.name, self.args, self.kw = name, a, k
            return self
        return f


def _nfree(ap):
    n = 1
    for d in tuple(ap.shape)[1:]:
        n *= int(d)
    return n


def _is_psum_or_f32(ap):
    try:
        return ("psum" in str(ap.space).lower()) or (ap.dtype == F32)
    except Exception:
        return True


def _cost_ns(op):
    r = _Rec()
    try:
        op.fn(r)
    except Exception:
        return 300.0
    kw, a, nm_ = r.kw, r.args, r.name
    out = kw.get("out", a[0] if a else None)
    try:
        if op.is_dma:
            byts = _nfree(out) * int(tuple(out.shape)[0]) * 4
            return 2000.0 + byts / 200.0
        if op.eng == "pe":
            if nm_ == "transpose":
                return 75.0
            rhs = kw.get("rhs", a[2] if len(a) > 2 else None)
            return max(45.0, 30.0 + _nfree(rhs) * 0.45)
        n = _nfree(out)
        if op.eng == "act":
            return 200.0 + n / 1.15
        if op.eng == "pool":
            if nm_ == "memset":
                return 150.0 + n * 0.9
            return 250.0 + n * 2.2
        if nm_ == "reciprocal":
            return 1000.0
        if nm_ == "bn_aggr":
            return 200.0
        srcs = [kw.get(k) for k in ("in0", "in1", "in_") if kw.get(k) is not None]
        slow = any(_is_psum_or_f32(x) for x in srcs) or nm_ in ("bn_stats",)
        return 160.0 + n * (1.04 if slow else 0.55)
    except Exception:
        return 300.0


class Sched:
    def __init__(self, nc):
        self.nc = nc
        self.all = []
        self.ops = {e: [] for e in ENGS}
        self.dma_counts = {}
        self.dma_keys = []
        self.final_waits = []
        self.dve_token = Buf("dve_token")

    def _add(self, eng, fn, reads, writes, is_dma=False, sem_key=None):
        op = Op(eng, fn, is_dma, sem_key)
        if eng == "dve" and not is_dma:
            reads = reads + [self.dve_token]
        op.reads, op.writes = reads, writes
        op.idx = len(self.all)
        self.all.append(op)
        if is_dma:
            if sem_key not in self.dma_counts:
                self.dma_counts[sem_key] = 0
                self.dma_keys.append(sem_key)
            self.dma_counts[sem_key] += 16
            op.dma_count = self.dma_counts[sem_key]
        return op

    def op(self, eng, fn, reads=(), writes=(), excl_dve=False):
        w = list(writes)
        if excl_dve:
            w.append(self.dve_token)
        return self._add(eng, fn, list(reads), w)

    def dma(self, eng, fn, reads=(), writes=(), sem_key=None, final=False):
        o = self._add(eng, fn, list(reads), list(writes), is_dma=True, sem_key=sem_key)
        if final:
            self.final_waits.append(o)
        return o

    def _edges(self):
        last_key = {}
        for op in self.all:
            preds = {}
            for b in op.reads:
                if b.last_w is not None:
                    preds[id(b.last_w)] = b.last_w
                if b.excl:
                    for r in b.readers:
                        if r.eng != op.eng:
                            preds[id(r)] = r
            for b in op.writes:
                if b.last_w is not None:
                    preds[id(b.last_w)] = b.last_w
                for r in b.readers:
                    preds[id(r)] = r
            if op.is_dma:
                p = last_key.get(op.sem_key)
                if p is not None:
                    preds[id(p)] = p
                last_key[op.sem_key] = op
            preds.pop(id(op), None)
            for b in op.reads:
                b.readers.append(op)
            for b in op.writes:
                b.last_w = op
                b.readers = []
            op.preds = list(preds.values())
            op.succs = []
        for op in self.all:
            for p in op.preds:
                p.succs.append(op)

    def _schedule(self):
        SEM = 120.0
        for op in self.all:
            op.cost = _cost_ns(op)
            op.npred = len(op.preds)
            op.fin = None
        cand = {e: [] for e in ENGS}
        ready = {}
        for op in self.all:
            if op.npred == 0:
                cand[op.eng].append(op)
                ready[id(op)] = 0.0
        free = {e: 0.0 for e in ENGS}
        order = {e: [] for e in ENGS}
        left = len(self.all)
        while left:
            best = None
            for e in ENGS:
                cl = cand[e]
                if not cl:
                    continue
                t = free[e]
                pick = None
                for o in cl:
                    if ready[id(o)] <= t:
                        if pick is None or o.idx < pick.idx:
                            pick = o
                if pick is not None:
                    st_ = t
                else:
                    for o in cl:
                        rt = ready[id(o)]
                        if pick is None or rt < ready[id(pick)] or (rt == ready[id(pick)] and o.idx < pick.idx):
                            pick = o
                    st_ = ready[id(pick)]
                if best is None or st_ < best[0] or (st_ == best[0] and pick.idx < best[1].idx):
                    best = (st_, pick)
            st_, o = best
            e = o.eng
            cand[e].remove(o)
            if o.is_dma:
                free[e] = st_ + 60.0
            else:
                free[e] = st_ + o.cost
            o.fin = st_ + o.cost
            o.pos = len(order[e])
            order[e].append(o)
            left -= 1
            for s_ in o.succs:
                s_.npred -= 1
                if s_.npred == 0:
                    ready[id(s_)] = max(p.fin for p in s_.preds) + SEM
                    cand[s_.eng].append(s_)
        self.model_ns = max(o.fin for o in self.all)
        return order

    def emit(self):
        nc = self.nc
        self._edges()
        if SCHEDULE:
            self.ops = self._schedule()
        else:
            self.ops = {e: [o for o in self.all if o.eng == e] for e in ENGS}
            for e in ENGS:
                for i, o in enumerate(self.ops[e]):
                    o.pos = i
        for e in ENGS:
            for o in self.ops[e]:
                lastp = {}
                dl = []
                for p in o.preds:
                    if p.is_dma:
                        dl.append(p)
                        continue
                    if p.eng == "pe" and o.eng == "pe" and not o.is_dma:
                        continue
                    q = lastp.get(p.eng)
                    if q is None or p.pos > q.pos:
                        lastp[p.eng] = p
                o.deps = dl + list(lastp.values())
                for p in lastp.values():
                    p.signal = True
        for e in ENGS:
            c = 0
            for o in self.ops[e]:
                if (not o.is_dma) and o.signal:
                    c += 1
                    o.count = c
        with contextlib.ExitStack() as st:
            esem = {e: st.enter_context(nc.semaphore("s_" + e)) for e in ENGS}
            dsem = {k: st.enter_context(nc.semaphore("d_" + str(k))) for k in self.dma_keys}
            block = st.enter_context(nc.Block())
            engobj = {"pe": "tensor", "act": "scalar", "dve": "vector", "pool": "gpsimd", "sp": "sync"}

            def make(e):
                def body(eng):
                    waited = {}
                    for o in self.ops[e]:
                        for d in o.deps:
                            if d.is_dma:
                                s, v, key = dsem[d.sem_key], d.dma_count, ("d", d.sem_key)
                            else:
                                s, v, key = esem[d.eng], d.count, ("e", d.eng)
                            if waited.get(key, 0) >= v:
                                continue
                            waited[key] = v
                            eng.wait_ge(s, v)
                        ins = o.fn(eng)
                        if o.is_dma:
                            ins.then_inc(dsem[o.sem_key], 16)
                        elif o.signal:
                            ins.then_inc(esem[e], 1)
                    if e == "sp":
                        fin = {}
                        for o in self.final_waits:
                            fin[o.sem_key] = max(fin.get(o.sem_key, 0), o.dma_count)
                        for k, v in fin.items():
                            if waited.get(("d", k), 0) >= v:
                                continue
                            eng.wait_ge(dsem[k], v)

                return body

            for e in ENGS:
                getattr(block, engobj[e])(make(e))


def attn_keys(t):
    if t >= 2:
        return [(t + d, d + 2) for d in (-2, -1, 0, 1, 2)]
    if t == 0:
        return [(0, 2), (1, 3), (2, 5), (3, 6)]
    return [(0, 1), (1, 2), (2, 3), (3, 5)]


def build(depth=DEPTH, n_pseq=4, ns_units=12):
    nc = bass.Bass("TRN2", target_bir_lowering=False)

    def din(name, shape):
        return nc.dram_tensor(name, shape, F32, kind="ExternalInput").ap()

    def dout(name, shape):
        return nc.dram_tensor(name, shape, F32, kind="ExternalOutput").ap()

    xp_d = din("xp", [n_pseq * 256, D])
    xs_d = din("xs", [3072, D])
    c2_d = din("c2", [2, D])
    ckT_d = din("ckT", [DEPTH, 512, 256])
    cv_d = din("cv", [DEPTH, 256, 512])
    wada_d = din("w_ada", [DEPTH, D, 3 * D])
    bada_d = din("b_ada", [DEPTH, 3 * D])
    win_d = din("w_in", [DEPTH, D, 3840])
    wout_d = din("w_out", [DEPTH, D, D])
    convw_d = din("conv_w", [DEPTH, 3, 256])
    glg_d = din("gmlp_ln_g", [DEPTH, 256])
    glb_d = din("gmlp_ln_b", [DEPTH, 256])
    wsT_d = din("wsT", [DEPTH, 128, 4, 128])
    bsT_d = din("bsT", [DEPTH, 128, 4])
    ebt_d = din("ebt", [DEPTH, 7, 128, 1024])
    lng_d = din("ln_g", [DEPTH, D])
    lnb_d = din("ln_b", [DEPTH, D])
    id_d = din("ident", [128, 128])
    yp_d = dout("yp", [n_pseq * 256, D])
    ys_d = dout("ys", [2048, D])
    nk_d = dout("nk", [n_pseq, DEPTH, 256, 512])
    nv_d = dout("nv", [n_pseq, DEPTH, 256, 512])
    sxp_d = nc.dram_tensor("sxp", [n_pseq * 256, D], F32).ap()
    sxs_d = nc.dram_tensor("sxs", [3072, D], F32).ap()

    S = Sched(nc)
    st = contextlib.ExitStack()
    with st:
        def sb(name, shape, dt=F32):
            return st.enter_context(nc.sbuf_tensor(name, shape, dt))

        def ps(name, shape, dt=F32):
            return st.enter_context(nc.psum_tensor(name, shape, dt))

        wF = sb("wF", [128, 8, 1536], BF16); b_wF = Buf("wF"); b_wFq = Buf("wFq")
        wT = sb("wT", [128, 8, 2304], BF16); b_wTk = Buf("wTk"); b_wTv = Buf("wTv"); b_wTvu = Buf("wTvu"); b_wTgb = Buf("wTgb"); b_wTgc = Buf("wTgc")
        wo = sb("wo", [128, 8, 1024], BF16); b_wo = Buf("wo")
        NH = 4
        hT = [sb("hT%d" % i, [128, 8, 256], BF16) for i in range(NH)]; b_hT = [Buf("hT%d" % i) for i in range(NH)]
        xA = [sb("xA%d" % i, [128, D]) for i in range(2)]; b_xA = [Buf("xA%d" % i) for i in range(2)]
        xn = [sb("xn%d" % i, [128, D], BF16) for i in range(2)]; b_xn = [Buf("xn%d" % i) for i in range(2)]
        QT = [sb("QT%d" % i, [128, 4, 256], BF16) for i in range(2)]; b_QT = [Buf("QT%d" % i) for i in range(2)]
        NK = 3
        KT = [sb("KT%d" % i, [128, 4, 256], BF16) for i in range(NK)]; b_KT = [Buf("KT%d" % i) for i in range(NK)]
        Vr = sb("Vr", [128, 2 * NK, 8, 65], BF16); b_V = [Buf("V%d" % i) for i in range(2 * NK)]
        yT = [sb("yT%d" % i, [128, 8, 256], BF16) for i in range(2)]; b_yT = [Buf("yT%d" % i) for i in range(2)]
        sgc = [sb("sgc%d" % i, [128, 2, 512], BF16) for i in range(2)]; b_sgc = [Buf("sgc%d" % i) for i in range(2)]
        ktok = sb("ktok", [128, 512], BF16); b_ktok = Buf("ktok")
        ckT = sb("ckT_sb", [128, 4, 256], BF16); b_ckT = Buf("ckT")
        cV = sb("cV", [128, 2, 8, 65], BF16); b_cV = Buf("cV")
        EB = sb("EB", [128, 7, 1024], BF16); b_EB = Buf("EB")
        tU = [sb("tU%d" % i, [128, D]) for i in range(2)]; b_tU = [Buf("tU%d" % i) for i in range(2)]
        Eb = sb("Eb", [128, 14, 512], BF16); b_E = [Buf("E%d" % i) for i in range(14)]
        c2f = sb("c2f", [128, 2, 8]); b_c2f = Buf("c2f")
        c2t = sb("c2t", [128, 2, 8]); b_c2t = Buf("c2t")
        scT = sb("scT", [128, 8, 2], BF16); b_scT = Buf("scT")
        sc1p = sb("sc1p", [128, 2, 8]); b_sc1p = Buf("sc1p")
        shp = sb("shp", [128, 2, 8]); b_shp = Buf("shp")
        badc = sb("badc", [128, 16]); b_badc = Buf("badc")
        gate = sb("gate", [128, 2, D]); b_gate = Buf("gate")
        lng = sb("lng", [128, D]); b_lng = Buf("lng")
        lnb = sb("lnb", [128, D]); b_lnb = Buf("lnb")
        glg = sb("glg", [128, 256]); b_glg = Buf("glg")
        glb = sb("glb", [128, 256]); b_glb = Buf("glb")
        bsT = sb("bsT_sb", [128, 4]); b_bsT = Buf("bsT")
        bsb = sb("bsb", [128, 256]); b_bsb = Buf("bsb")
        rsw = sb("rsw", [128, 4]); b_rsw = Buf("rsw")
        ones1 = sb("ones1", [128, 2], BF16); b_ones1 = Buf("ones1")
        wsT = sb("wsT_sb", [128, 4, 128], BF16); b_wsT = Buf("wsT")
        cw = sb("cw", [128, 2, 3]); b_cw = Buf("cw")
        ident = sb("ident_sb", [128, 128], BF16); b_id = Buf("ident")
        nh = sb("nh", [128, 2]); b_nh = Buf("nh")
        stt = [sb("stt%d" % i, [128, 2, 6]) for i in range(4)]
        mv = [sb("mv%d" % i, [128, 2]) for i in range(4)]
        ve = [sb("ve%d" % i, [128, 1]) for i in range(4)]
        rs = [sb("rs%d" % i, [128, 1]) for i in range(4)]
        nm = [sb("nm%d" % i, [128, 1]) for i in range(4)]
        b_stt = [Buf("stt%d" % i) for i in range(4)]
        b_mv = [Buf("mv%d" % i) for i in range(4)]
        b_ve = [Buf("ve%d" % i) for i in range(4)]
        b_rs = [Buf("rs%d" % i) for i in range(4)]
        b_nm = [Buf("nm%d" % i) for i in range(4)]
        wkall = sb("wkall", [128, 4, 256])
        wk = {}
        for i_, nme in enumerate(("xa", "acc", "tg", "s2")):
            wk[nme] = (wkall[:, i_, :], Buf("wk_" + nme))
        wk["m"] = wk["xa"]
        tgc = wkall[:, 2:4, :].rearrange("p a t -> p (a t)")
        b_tgc_l = [wk["tg"][1], wk["s2"][1]]
        zb = sb("zb", [128, 258]); b_zb = Buf("zb")
        xh = sb("xh", [128, 4]); b_xh = Buf("xh")
        zh = sb("zh", [128, 4]); b_zh = Buf("zh")
        vn3 = sb("vn3", [128, 256], BF16); b_vn3 = Buf("vn3")
        ybt = sb("ybt", [128, 256], BF16); b_ybt = Buf("ybt")
        rinv = sb("rinv", [128, 8]); b_rinv = Buf("rinv")
        ob = sb("ob", [128, 512]); b_ob = Buf("ob")
        ob2 = sb("ob2", [128, 512]); b_ob2 = Buf("ob2")
        kst = ob2; b_kst = b_ob2
        vst = ob; b_vst = b_ob
        yct = sb("yct", [128, 512], BF16); b_yct = Buf("yct")

        NMM = 6
        mmb = [ps("mm%d" % i, [128, 512]) for i in range(NMM)]; b_mm = [Buf("mm%d" % i, True) for i in range(NMM)]
        pv = [ps("pv%d" % i, [128, 512]) for i in range(2)]; b_pv = [Buf("pv0", True), Buf("pv1", True)]
        mm_i = [0]

        pool_banks = {None: [0, 1, 2, 3, 4, 5], "S": [2, 3, 4, 5], "G": [0, 1]}
        pool_i = {None: 0, "S": 0, "G": 0}

        def mm(pool=None):
            bl = pool_banks[pool]
            i = bl[pool_i[pool] % len(bl)]
            pool_i[pool] += 1
            return mmb[i], b_mm[i]

        def mmt(pool=None):
            t, b = mm(pool)
            return t[:].bitcast(BF16), b

        def ln_small(site, eng_after="dve"):
            S.op("dve", lambda e: e.tensor_scalar(out=ve[site][:], in0=mv[site][:, 1:2], scalar1=1e-5, scalar2=None, op0=ALU.add),
                 reads=[b_mv[site]], writes=[b_ve[site]])
            S.op("pool", lambda e: e.tensor_tensor(out=rs[site][:], in0=ve[site][:], in1=nh[:, 0:1], op=ALU.pow),
                 reads=[b_ve[site], b_nh], writes=[b_rs[site]], excl_dve=True)
            S.op("dve", lambda e: e.scalar_tensor_tensor(out=nm[site][:], in0=mv[site][:, 0:1], scalar=-1.0, in1=rs[site][:], op0=ALU.mult, op1=ALU.mult),
                 reads=[b_mv[site], b_rs[site]], writes=[b_nm[site]])

        S.dma("pool", lambda e: e.dma_start(out=ident[:], in_=id_d), writes=[b_id], sem_key="c_id")
        S.op("pool", lambda e: e.memset(nh[:], -0.5), writes=[b_nh])
        S.op("pool", lambda e: e.memset(ones1[:], 1.0), writes=[b_ones1])
        S.op("pool", lambda e: e.memset(Vr[:].rearrange("p a h d -> p (a h d)"), 1.0), writes=b_V)
        S.op("pool", lambda e: e.memset(cV[:].rearrange("p a h d -> p (a h d)"), 1.0), writes=[b_cV])
        for v in range(2):
            S.dma("sp", lambda e, v=v: e.dma_start(out=c2f[:, v, :], in_=c2_d[v, :].rearrange("(k p) -> p k", p=128), allow_slow_non_contiguous=True), writes=[b_c2f], sem_key="c_c2")
        S.op("act", lambda e: e.activation(out=c2t[:], in_=c2f[:], func=AF.Tanh, scale=0.5), reads=[b_c2f], writes=[b_c2t])
        S.op("dve", lambda e: e.scalar_tensor_tensor(out=c2t[:], in0=c2t[:], scalar=1.0, in1=c2f[:], op0=ALU.add, op1=ALU.mult), reads=[b_c2f, b_c2t], writes=[b_c2t])
        S.op("dve", lambda e: e.tensor_scalar(out=scT[:].rearrange("p k v -> p v k"), in0=c2t[:], scalar1=0.5, scalar2=None, op0=ALU.mult), reads=[b_c2t], writes=[b_scT])
        b_sxs = [Buf("sxs%d" % u) for u in range(12)]
        b_sxp = [Buf("sxp%d" % u) for u in range(n_pseq)]

        def layer(l):
            last = (l == depth - 1)
            def wdma(dst, src, bufd, key):
                S.dma("pool", lambda e: e.dma_start(out=dst, in_=src.rearrange("(k p) n -> p k n", p=128)), writes=[bufd], sem_key=key)
            wdma(wT[:, :, 1792:2304], win_d[l][:, 2304:2816], b_wTk, "w_Tk")
            wdma(wT[:, :, 768:1280], win_d[l][:, 2816:3328], b_wTv, "w_Tv")
            for c_ in range(2):
                S.dma("sp", lambda e, c_=c_: e.dma_start(out=cw[:, c_, :], in_=convw_d[l][:, c_ * 128:(c_ + 1) * 128].rearrange("j p -> p j"), allow_slow_non_contiguous=True), writes=[b_cw], sem_key="p_cw")
            S.dma("sp", lambda e: e.dma_start(out=glg[:], in_=glg_d[l, :].partition_broadcast(128)), writes=[b_glg], sem_key="p_glg")
            S.dma("sp", lambda e: e.dma_start(out=glb[:], in_=glb_d[l, :].partition_broadcast(128)), writes=[b_glb], sem_key="p_glb")
            S.dma("sp", lambda e: e.dma_start(out=lng[:], in_=lng_d[l, :].partition_broadcast(128)), writes=[b_lng], sem_key="p_lng")
            S.dma("sp", lambda e: e.dma_start(out=lnb[:], in_=lnb_d[l, :].partition_broadcast(128)), writes=[b_lnb], sem_key="p_lnb")
            S.dma("sp", lambda e: e.dma_start(out=bsT[:], in_=bsT_d[l]), writes=[b_bsT], sem_key="p_bsT")
            S.op("dve", lambda e: e.tensor_copy(out=bsb[:].rearrange("p (g c) -> p g c", g=4), in_=bsT[:, :].unsqueeze(2).to_broadcast([128, 4, 64])), reads=[b_bsT], writes=[b_bsb])
            S.dma("pool", lambda e: e.dma_start(out=wsT[:], in_=wsT_d[l]), writes=[b_wsT], sem_key="p_wsT")
            prs, bprs = mm()
            for g in range(4):
                S.op("pe", lambda e, g=g, prs=prs: e.matmul(prs[:, g:g + 1], lhsT=wsT[:, g, :], rhs=ones1[:, 0:1], start=True, stop=True), reads=[b_wsT, b_ones1], writes=[bprs])
            S.op("dve", lambda e, prs=prs: e.tensor_copy(out=rsw[:], in_=prs[:, 0:4]), reads=[bprs], writes=[b_rsw])
            S.op("dve", lambda e: e.tensor_tensor(out=glb[:].rearrange("p (g c) -> p g c", g=4), in0=glb[:].rearrange("p (g c) -> p g c", g=4), in1=rsw[:, :].unsqueeze(2).to_broadcast([128, 4, 64]), op=ALU.mult),
                 reads=[b_glb, b_rsw], writes=[b_glb])
            S.op("dve", lambda e: e.tensor_tensor(out=bsb[:], in0=bsb[:], in1=glb[:], op=ALU.add), reads=[b_bsb, b_glb], writes=[b_bsb])
            S.dma("sp", lambda e: e.dma_start(out=badc[:], in_=bada_d[l, 0:2048].rearrange("(j p) -> p j", p=128), allow_slow_non_contiguous=True), writes=[b_badc], sem_key="p_badc")
            wad = Eb[:, 0:8, :]
            Ssil = Eb[:, 8:12, :].rearrange("p (v a) (b t) -> p v (a b) t", v=2, b=4)
            b_Ssil_l = b_E[8:12]
            for v in range(2):
                S.op("dve", lambda e, v=v: e.tensor_copy(out=Ssil[:, v], in_=scT[:, :, v:v + 1].to_broadcast([128, 8, 128])), reads=[b_scT], writes=b_Ssil_l)
            for ch in range(6):
                S.dma("pool", lambda e, ch=ch: e.dma_start(out=wad, in_=wada_d[l][:, ch * 512:(ch + 1) * 512].rearrange("(k p) n -> p k n", p=128)),
                      writes=b_E[0:8], sem_key="w_ada")
                if ch < 4:
                    pt, bpt = mm()
                    for jb in range(4):
                        for kc in range(8):
                            S.op("pe", lambda e, jb=jb, kc=kc, pt=pt: e.matmul(pt[:, jb * 2:jb * 2 + 2], lhsT=wad[:, kc, jb * 128:(jb + 1) * 128], rhs=scT[:, kc, :], start=(kc == 0), stop=(kc == 7)),
                                 reads=b_E[0:8] + [b_scT], writes=[bpt])
                    for v in range(2):
                        dst = (shp if ch < 2 else sc1p)
                        bd = (b_shp if ch < 2 else b_sc1p)
                        j0 = (ch % 2) * 4
                        S.op("dve", lambda e, v=v, dst=dst, j0=j0, pt=pt, ch=ch: e.scalar_tensor_tensor(
                            out=dst[:, v, j0:j0 + 4], in0=pt[:, v:8:2], scalar=(0.0 if ch < 2 else 1.0), in1=badc[:, ch * 4:ch * 4 + 4], op0=ALU.add, op1=ALU.add),
                            reads=[bpt, b_badc], writes=[bd])
                else:
                    half = ch - 4
                    S.dma("sp", lambda e, half=half: e.dma_start(out=tU[0][:, 0:512], in_=bada_d[l, 2048 + half * 512:2048 + (half + 1) * 512].partition_broadcast(128)), writes=[b_tU[0]], sem_key="p_bg")
                    for v in range(2):
                        pt, bpt = mm()
                        for kc in range(8):
                            S.op("pe", lambda e, v=v, kc=kc, pt=pt: e.matmul(pt[:, :], lhsT=Ssil[:, v, kc, :], rhs=wad[:, kc, :], start=(kc == 0), stop=(kc == 7)),
                                 reads=b_E[0:8] + b_Ssil_l, writes=[bpt])
                        S.op("dve", lambda e, v=v, pt=pt, half=half: e.tensor_tensor(out=gate[:, v, half * 512:(half + 1) * 512], in0=pt[:, :], in1=tU[0][:, 0:512], op=ALU.add),
                             reads=[bpt, b_tU[0]], writes=[b_gate])
            wdma(wF[:, :, 1024:1536], win_d[l][:, 1792:2304], b_wFq, "w_Fq")
            wdma(wF[:, :, 0:1024], win_d[l][:, 0:1024], b_wF, "w_F")
            wdma(wT[:, :, 0:256], win_d[l][:, 1280:1536], b_wTvu, "w_Tvu")
            wdma(wT[:, :, 256:512], win_d[l][:, 1024:1280], b_wTvu, "w_Tvu2")
            wdma(wT[:, :, 1280:1792], win_d[l][:, 3328:3840], b_wTgc, "w_Tgc")
            wdma(wT[:, :, 512:768], win_d[l][:, 1536:1792], b_wTgb, "w_Tgb")
            wdma(wo[:], wout_d[l], b_wo, "w_o")
            S.dma("pool", lambda e: e.dma_start(out=ckT[:], in_=ckT_d[l].rearrange("(a p) k -> p a k", p=128)), writes=[b_ckT], sem_key="c_k")
            for a in range(2):
                S.dma("pool", lambda e, a=a: e.dma_start(out=cV[:, a, :, 0:64], in_=cv_d[l][a * 128:(a + 1) * 128, :].rearrange("p (h d) -> p h d", d=64)), writes=[b_cV], sem_key="c_v")
            for tid in range(7):
                S.dma("sp", lambda e, tid=tid: e.dma_start(out=tU[tid % 2][:], in_=ebt_d[l, tid]), writes=[b_tU[tid % 2]], sem_key="p_eb%d" % (tid % 2))
                S.op("act", lambda e, tid=tid: e.activation(out=EB[:, tid, :], in_=tU[tid % 2][:], func=AF.Exp), reads=[b_tU[tid % 2]], writes=[b_EB])

            def src_x(grp, tile):
                if l == 0:
                    return (xs_d if grp == "S" else xp_d)[tile * 128:(tile + 1) * 128, :]
                return (sxs_d if grp == "S" else sxp_d)[tile * 128:(tile + 1) * 128, :]

            def dst_x(grp, tile):
                if last:
                    return (ys_d if grp == "S" else yp_d)[tile * 128:(tile + 1) * 128, :]
                return (sxs_d if grp == "S" else sxp_d)[tile * 128:(tile + 1) * 128, :]

            def xbuf(grp, u):
                return (b_sxs if grp == "S" else b_sxp)[u]

            cnt = {"xA": 0, "tU": 0}

            def A_dma(grp, u):
                xis = []
                for j in range(2):
                    tile = 2 * u + j
                    xi = cnt["xA"] % 2
                    cnt["xA"] += 1
                    rd = [xbuf(grp, u)] if l > 0 else []
                    S.dma("sp", lambda e, xi=xi, tile=tile: e.dma_start(out=xA[xi][:], in_=src_x(grp, tile)), reads=rd, writes=[b_xA[xi]], sem_key="xA%d" % xi)
                    xis.append(xi)
                return xis

            def A_ln(grp, u, xis):
                for j in range(2):
                    xi = xis[j]
                    for i in range(2):
                        S.op("dve", lambda e, i=i, xi=xi, j=j: e.bn_stats(out=stt[j][:, i, :], in_=xA[xi][:, i * 512:(i + 1) * 512]), reads=[b_xA[xi]], writes=[b_stt[j]])
                    S.op("dve", lambda e, j=j: e.bn_aggr(out=mv[j][:], in_=stt[j][:].rearrange("p a b -> p (a b)")), reads=[b_stt[j]], writes=[b_mv[j]])
                    ln_small(j)
                    S.op("act", lambda e, xi=xi, j=j: e.activation(out=xn[j][:], in_=xA[xi][:], func=AF.Identity, bias=nm[j][:], scale=rs[j][:]),
                         reads=[b_xA[xi], b_nm[j], b_rs[j]], writes=[b_xn[j]])

            def A_pre(grp, u):
                A_ln(grp, u, A_dma(grp, u))

            def A_pe(grp, u, hs):
                v = 0 if grp == "S" else 1
                for j in range(2):
                    tr, b_tr = mmt()
                    for kc in range(8):
                        S.op("pe", lambda e, kc=kc, j=j, tr=tr: e.transpose(out=tr[:, kc * 128:(kc + 1) * 128], in_=xn[j][:, kc * 128:(kc + 1) * 128], identity=ident[:]),
                             reads=[b_xn[j], b_id], writes=[b_tr])
                    for kc in range(8):
                        if kc % 2 == 0:
                            S.op("dve", lambda e, kc=kc, j=j, tr=tr: e.tensor_scalar(out=hT[hs][:, kc, j * 128:(j + 1) * 128], in0=tr[:, kc * 128:(kc + 1) * 128],
                                                                                      scalar1=sc1p[:, v, kc:kc + 1], scalar2=shp[:, v, kc:kc + 1], op0=ALU.mult, op1=ALU.add),
                                 reads=[b_tr, b_sc1p, b_shp], writes=[b_hT[hs]])
                    for kc in range(8):
                        if kc % 2 == 1:
                            S.op("act", lambda e, kc=kc, j=j, tr=tr: e.activation(out=hT[hs][:, kc, j * 128:(j + 1) * 128], in_=tr[:, kc * 128:(kc + 1) * 128], func=AF.Identity,
                                                                                    scale=sc1p[:, v, kc:kc + 1], bias=shp[:, v, kc:kc + 1]),
                                 reads=[b_tr, b_sc1p, b_shp], writes=[b_hT[hs]])

            def proj_T(hs, j, c0, c1):
                pt, bpt = mm()
                n = c1 - c0
                for kc in range(8):
                    S.op("pe", lambda e, kc=kc, pt=pt: e.matmul(pt[:, 0:n], lhsT=hT[hs][:, kc, j * 128:(j + 1) * 128], rhs=wT[:, kc, c0:c1], start=(kc == 0), stop=(kc == 7)),
                         reads=[b_hT[hs], {1792: b_wTk, 768: b_wTv, 0: b_wTvu, 512: b_wTgb, 1280: b_wTgc}[c0]], writes=[bpt])
                return pt, bpt

            def proj_F(hs, pt, bpt, off, cb):
                for kc in range(8):
                    S.op("pe", lambda e, kc=kc: e.matmul(pt[:, off:off + 256], lhsT=wF[:, kc, cb * 128:(cb + 1) * 128], rhs=hT[hs][:, kc, :], start=(kc == 0), stop=(kc == 7)),
                         reads=[b_hT[hs], (b_wFq if cb >= 8 else b_wF)], writes=[bpt])

            def B_kv(grp, u, hs, ks):
                with_out = (grp == "P")
                for j in range(2):
                    vslot = 2 * ks + j
                    pt, bpt = proj_T(hs, j, 1792, 2304)
                    S.op("act", lambda e, pt=pt: e.activation(out=ktok[:], in_=pt[:, :], func=AF.Copy), reads=[bpt], writes=[b_ktok])
                    if with_out:
                        S.op("dve", lambda e, pt=pt: e.tensor_copy(out=kst[:], in_=pt[:, :]), reads=[bpt], writes=[b_kst])
                        S.dma("sp", lambda e, j=j: e.dma_start(out=nk_d[u, l, j * 128:(j + 1) * 128, :], in_=kst[:]), reads=[b_kst], sem_key="o_k", final=True)
                    tr, b_tr = mmt()
                    for a in range(4):
                        S.op("pe", lambda e, a=a, tr=tr: e.transpose(out=tr[:, a * 128:(a + 1) * 128], in_=ktok[:, a * 128:(a + 1) * 128], identity=ident[:]),
                             reads=[b_ktok, b_id], writes=[b_tr])
                    S.op("dve", lambda e, j=j, tr=tr: e.tensor_copy(out=KT[ks][:, :, j * 128:(j + 1) * 128], in_=tr[:, 0:512].rearrange("p (a t) -> p a t", a=4)),
                         reads=[b_tr], writes=[b_KT[ks]])
                    pt, bpt = proj_T(hs, j, 768, 1280)
                    S.op("act", lambda e, pt=pt, vslot=vslot: e.activation(out=Vr[:, vslot, :, 0:64], in_=pt[:, :].rearrange("p (h d) -> p h d", d=64), func=AF.Copy),
                         reads=[bpt], writes=[b_V[vslot]])
                    if with_out:
                        S.op("dve", lambda e, pt=pt: e.tensor_copy(out=vst[:], in_=pt[:, :]), reads=[bpt], writes=[b_vst])
                        S.dma("sp", lambda e, j=j: e.dma_start(out=nv_d[u, l, j * 128:(j + 1) * 128, :], in_=vst[:]), reads=[b_vst], sem_key="o_v", final=True)

            def proj_F_g(hs, pt, bpt, off, cb):
                proj_F(hs, pt, bpt, off, cb)
                yield

            def B_q_gen(grp, u, hs, qs, pool=None):
                for half in range(2):
                    pq, bpq = mm(pool)
                    yield from proj_F_g(hs, pq, bpq, 0, 8 + 2 * half)
                    yield from proj_F_g(hs, pq, bpq, 256, 9 + 2 * half)
                    S.op("act", lambda e, pq=pq, half=half: e.activation(out=QT[qs][:, 2 * half:2 * half + 2, :], in_=pq[:, :].rearrange("p (a t) -> p a t", a=2), func=AF.Copy),
                         reads=[bpq], writes=[b_QT[qs]])

            def run(gen):
                for _ in gen:
                    pass

            def zipper(ga, gb, na=1, nb=2):
                da = db = False
                while not (da and db):
                    for _ in range(na):
                        if not da:
                            try:
                                next(ga)
                            except StopIteration:
                                da = True
                    for _ in range(nb):
                        if not db:
                            try:
                                next(gb)
                            except StopIteration:
                                db = True

            def B_q(grp, u, hs, qs):
                run(B_q_gen(grp, u, hs, qs))

            def B_conv_gen(grp, u, hs, qs, hs_prev, hs_next, pool=None):
                yt, byt = yT[qs], b_yT[qs]
                have_h = [hs_prev is not None, hs_next is not None]
                hal = pv[1][:, 384:392]
                b_halo = b_pv[1]
                for cbi, cb in enumerate((0, 1, 4, 5)):
                    for side in range(2):
                        if not have_h[side]:
                            continue
                        hsrc = hT[hs_prev][:, :, 255:256] if side == 0 else hT[hs_next][:, :, 0:1]
                        bsrc = b_hT[hs_prev] if side == 0 else b_hT[hs_next]
                        for kc in range(8):
                            S.op("pe", lambda e, kc=kc, cb=cb, hsrc=hsrc, col=cbi * 2 + side: e.matmul(hal[:, col:col + 1], lhsT=wF[:, kc, cb * 128:(cb + 1) * 128], rhs=hsrc[:, kc, :], start=(kc == 0), stop=(kc == 7)),
                                 reads=[bsrc, b_wF], writes=[b_halo])
                    yield
                S.op("pool", lambda e: e.memset(zh[:], 0.0), writes=[b_zh])
                for side in range(2):
                    if have_h[side]:
                        S.op("dve", lambda e, side=side: e.tensor_copy(out=xh[:, side:4:2], in_=hal[:, side:4:2]), reads=[b_halo], writes=[b_xh])
                        S.op("dve", lambda e, side=side: e.tensor_tensor(out=zh[:, side:4:2], in0=hal[:, 4 + side:8:2], in1=xh[:, side:4:2], op=ALU.mult), reads=[b_halo, b_xh], writes=[b_zh])
                for c in range(2):
                    p1, bp1 = mm(pool)
                    yield from proj_F_g(hs, p1, bp1, 0, 0 + c)
                    yield from proj_F_g(hs, p1, bp1, 256, 4 + c)
                    p2, bp2 = mm(pool)
                    yield from proj_F_g(hs, p2, bp2, 0, 2 + c)
                    yield from proj_F_g(hs, p2, bp2, 256, 6 + c)
                    xa_t, bxa = wk["xa"]; acc, bacc = wk["acc"]; tg, btg = wk["tg"]; s2, bs2 = wk["s2"]; m_, bm = wk["m"]
                    S.op("act", lambda e, p1=p1: e.activation(out=xa_t[:], in_=p1[:, 0:256], func=AF.Copy), reads=[bp1], writes=[bxa])
                    S.op("dve", lambda e, p1=p1: e.tensor_tensor(out=zb[:, 1:257], in0=p1[:, 256:512], in1=xa_t[:], op=ALU.mult), reads=[bp1, bxa], writes=[b_zb])
                    S.op("dve", lambda e, c=c: e.tensor_copy(out=zb[:, 0:258:257], in_=zh[:, 2 * c:2 * c + 2]), reads=[b_zh], writes=[b_zb])
                    S.op("act", lambda e, c=c: e.activation(out=acc[:], in_=zb[:, 1:257], func=AF.Identity, scale=cw[:, c, 1:2]), reads=[b_zb, b_cw], writes=[bacc])
                    S.op("dve", lambda e, c=c: e.scalar_tensor_tensor(out=acc[:], in0=zb[:, 0:256], scalar=cw[:, c, 0:1], in1=acc[:], op0=ALU.mult, op1=ALU.add), reads=[b_zb, b_cw, bacc], writes=[bacc])
                    S.op("dve", lambda e, c=c: e.scalar_tensor_tensor(out=acc[:], in0=zb[:, 2:258], scalar=cw[:, c, 2:3], in1=acc[:], op0=ALU.mult, op1=ALU.add), reads=[b_zb, b_cw, bacc], writes=[bacc])
                    S.op("act", lambda e, p2=p2: e.activation(out=tg[:], in_=p2[:, 256:512], func=AF.Tanh, scale=0.5), reads=[bp2], writes=[btg])
                    S.op("dve", lambda e, p2=p2: e.scalar_tensor_tensor(out=s2[:], in0=tg[:], scalar=1.0, in1=p2[:, 256:512], op0=ALU.add, op1=ALU.mult), reads=[btg, bp2], writes=[bs2])
                    S.op("dve", lambda e, p2=p2: e.tensor_tensor(out=m_[:], in0=p2[:, 0:256], in1=acc[:], op=ALU.mult), reads=[bp2, bacc], writes=[bm])
                    S.op("dve", lambda e, c=c: e.scalar_tensor_tensor(out=yt[:, c, :], in0=m_[:], scalar=0.5, in1=s2[:], op0=ALU.mult, op1=ALU.mult), reads=[bm, bs2], writes=[byt])
                    yield

            def B_conv(grp, u, hs, qs, hs_prev, hs_next):
                run(B_conv_gen(grp, u, hs, qs, hs_prev, hs_next))

            def B_gm(grp, u, hs, qs):
                yt, byt = yT[qs], b_yT[qs]
                for j in range(2):
                    xa_t, bxa = wk["xa"]; acc, bacc = wk["acc"]; tg, btg = wk["tg"]; s2, bs2 = wk["s2"]; m_, bm = wk["m"]
                    pvu, bpvu = proj_T(hs, j, 0, 512)
                    S.op("dve", lambda e, pvu=pvu: e.bn_stats(out=stt[2][:, 0, :], in_=pvu[:, 0:256]), reads=[bpvu], writes=[b_stt[2]])
                    S.op("dve", lambda e: e.bn_aggr(out=mv[2][:], in_=stt[2][:, 0, :]), reads=[b_stt[2]], writes=[b_mv[2]])
                    ln_small(2)
                    pgc, bpgc = proj_T(hs, j, 1280, 1792)
                    S.op("act", lambda e, pgc=pgc: e.activation(out=tgc[:], in_=pgc[:, :], func=AF.Tanh, scale=0.5), reads=[bpgc], writes=b_tgc_l)
                    S.op("dve", lambda e, pgc=pgc, j=j: e.scalar_tensor_tensor(out=sgc[qs][:, j, :], in0=tgc[:], scalar=1.0, in1=pgc[:, :], op0=ALU.add, op1=ALU.mult), reads=b_tgc_l + [bpgc], writes=[b_sgc[qs]])
                    pgb, bpgb = proj_T(hs, j, 512, 768)
                    S.op("act", lambda e, pvu=pvu: e.activation(out=vn3[:], in_=pvu[:, 0:256], func=AF.Identity, bias=nm[2][:], scale=rs[2][:]), reads=[bpvu, b_nm[2], b_rs[2]], writes=[b_vn3])
                    psv, bpsv = mm()
                    for g in range(4):
                        S.op("pe", lambda e, g=g, psv=psv: e.matmul(psv[:, g * 64:(g + 1) * 64], lhsT=wsT[:, g, :], rhs=vn3[:, g * 64:(g + 1) * 64], start=True, stop=True),
                             reads=[b_wsT, b_vn3], writes=[bpsv])
                    S.op("dve", lambda e, psv=psv: e.tensor_tensor(out=acc[:], in0=psv[:, 0:256], in1=glg[:], op=ALU.mult), reads=[bpsv, b_glg], writes=[bacc])
                    S.op("dve", lambda e: e.tensor_tensor(out=acc[:], in0=acc[:], in1=bsb[:], op=ALU.add), reads=[bacc, b_bsb], writes=[bacc])
                    S.op("dve", lambda e, pvu=pvu: e.tensor_tensor(out=m_[:], in0=pvu[:, 256:512], in1=acc[:], op=ALU.mult), reads=[bpvu, bacc], writes=[bm])
                    S.op("act", lambda e, pgb=pgb: e.activation(out=tg[:], in_=pgb[:, 0:256], func=AF.Tanh, scale=0.5), reads=[bpgb], writes=[btg])
                    S.op("dve", lambda e, pgb=pgb: e.scalar_tensor_tensor(out=s2[:], in0=tg[:], scalar=1.0, in1=pgb[:, 0:256], op0=ALU.add, op1=ALU.mult), reads=[btg, bpgb], writes=[bs2])
                    S.op("dve", lambda e: e.scalar_tensor_tensor(out=ybt[:], in0=m_[:], scalar=0.5, in1=s2[:], op0=ALU.mult, op1=ALU.mult), reads=[bm, bs2], writes=[b_ybt])
                    tr, b_tr = mmt()
                    for a in range(2):
                        S.op("pe", lambda e, a=a, tr=tr: e.transpose(out=tr[:, a * 128:(a + 1) * 128], in_=ybt[:, a * 128:(a + 1) * 128], identity=ident[:]), reads=[b_ybt, b_id], writes=[b_tr])
                    S.op("act", lambda e, j=j, tr=tr: e.activation(out=yt[:, 2:4, j * 128:(j + 1) * 128], in_=tr[:, 0:256].rearrange("p (a t) -> p a t", a=2), func=AF.Copy), reads=[b_tr], writes=[byt])

            def C_s_gen(grp, u, qs, chunks, j, pool=None):
                for ci, (kf, kb, vf, vb, tid) in enumerate(chunks):
                    SE, b_SE = mm(pool)
                    SO, b_SO = mm(pool)
                    for h in range(8):
                        pair, half = h // 2, h % 2
                        bank, bbank = (SE, b_SE) if half == 0 else (SO, b_SO)
                        S.op("pe", lambda e, kf=kf, pair=pair, half=half, bank=bank: e.matmul(
                            bank[:, pair * 128:(pair + 1) * 128], lhsT=kf(pair, half), rhs=QT[qs][64 * half:64 * half + 64, pair, j * 128:(j + 1) * 128], start=True, stop=True),
                            reads=[kb, b_QT[qs]], writes=[bbank])
                    for half in range(2):
                        bank, bbank = (SE, b_SE) if half == 0 else (SO, b_SO)
                        ei = 2 * ci + half
                        S.op("act", lambda e, bank=bank, ei=ei: e.activation(out=Eb[:, ei, :], in_=bank[:, :], func=AF.Exp, scale=SCALE), reads=[bbank], writes=[b_E[ei]])
                        if tid is not None:
                            S.op(("pool" if half == 1 else "dve"), lambda e, ei=ei, tid=tid, half=half: e.tensor_tensor(out=Eb[:, ei, :], in0=Eb[:, ei, :], in1=EB[:, tid, half * 512:(half + 1) * 512], op=ALU.mult),
                                 reads=[b_E[ei], b_EB], writes=[b_E[ei]])
                    yield

            def C_s(grp, u, qs, keyf, j):
                chunks = keyf(2 * u + j)
                run(C_s_gen(grp, u, qs, chunks, j))
                return chunks

            def C_pv(grp, u, qs, chunks, j):
                nck = len(chunks)
                for h in range(8):
                    pair, half = h // 2, h % 2
                    pb, bpb = pv[h // 4], b_pv[h // 4]
                    for ci, (kf, kb, vf, vb, tid) in enumerate(chunks):
                        ei = 2 * ci + half
                        S.op("pe", lambda e, ei=ei, pair=pair, vf=vf, h=h, pb=pb, ci=ci: e.matmul(
                            pb[:, (h % 4) * 65:(h % 4) * 65 + 65], lhsT=Eb[:, ei, pair * 128:(pair + 1) * 128], rhs=vf(h), start=(ci == 0), stop=(ci == nck - 1)),
                            reads=[b_E[ei], vb], writes=[bpb])
                for g2 in range(2):
                    pb, bpb = pv[g2], b_pv[g2]
                    pv3 = pb[:, 0:260].rearrange("p (h d) -> p h d", d=65)
                    S.op("dve", lambda e, pv3=pv3, g2=g2: e.reciprocal(out=rinv[:, 4 * g2:4 * g2 + 4], in_=pv3[:, :, 64]), reads=[bpb], writes=[b_rinv])
                    S.op("dve", lambda e, pv3=pv3, g2=g2: e.tensor_tensor(out=ob[:, 256 * g2:256 * g2 + 256].rearrange("p (h d) -> p h d", d=64), in0=pv3[:, :, 0:64],
                                                                            in1=rinv[:, 4 * g2:4 * g2 + 4].unsqueeze(2).to_broadcast([128, 4, 64]), op=ALU.mult),
                         reads=[bpb, b_rinv], writes=[b_ob])
                S.op("dve", lambda e: e.scalar_tensor_tensor(out=yct[:], in0=ob[:], scalar=0.5, in1=sgc[qs][:, j, :], op0=ALU.mult, op1=ALU.mult), reads=[b_ob, b_sgc[qs]], writes=[b_yct])

            def C_o(grp, u, qs, j):
                run(C_o_gen(grp, u, qs, j))

            def C_o_gen(grp, u, qs, j, pool=None):
                v = 0 if grp == "S" else 1
                yt, byt = yT[qs], b_yT[qs]
                tile = 2 * u + j
                tr, b_tr = mmt(pool)
                for a in range(4):
                    S.op("pe", lambda e, a=a: e.transpose(out=tr[:, a * 128:(a + 1) * 128], in_=yct[:, a * 128:(a + 1) * 128], identity=ident[:]), reads=[b_yct, b_id], writes=[b_tr])
                S.op("act", lambda e: e.activation(out=yt[:, 4:8, j * 128:(j + 1) * 128], in_=tr[:, 0:512].rearrange("p (a t) -> p a t", a=4), func=AF.Copy), reads=[b_tr], writes=[byt])
                yield
                ti = cnt["tU"] % 2
                cnt["tU"] += 1
                rd = [xbuf(grp, u)] if l > 0 else []
                S.dma("sp", lambda e: e.dma_start(out=tU[ti][:], in_=src_x(grp, tile)), reads=rd, writes=[b_tU[ti]], sem_key="xC%d" % ti)
                for n in range(2):
                    po, bpo = mm(pool)
                    for kc in range(8):
                        S.op("pe", lambda e, kc=kc, n=n, po=po: e.matmul(po[:, :], lhsT=yt[:, kc, j * 128:(j + 1) * 128], rhs=wo[:, kc, n * 512:(n + 1) * 512], start=(kc == 0), stop=(kc == 7)),
                             reads=[byt, b_wo], writes=[bpo])
                        if kc == 3:
                            yield
                    S.op("dve", lambda e, n=n, po=po: e.tensor_tensor(out=ob2[:], in0=po[:, :], in1=gate[:, v, n * 512:(n + 1) * 512], op=ALU.mult),
                         reads=[bpo, b_gate], writes=[b_ob2])
                    S.op("dve", lambda e, n=n: e.scalar_tensor_tensor(out=tU[ti][:, n * 512:(n + 1) * 512], in0=tU[ti][:, n * 512:(n + 1) * 512], scalar=ALPHA, in1=ob2[:], op0=ALU.mult, op1=ALU.add),
                         reads=[b_ob2, b_tU[ti]], writes=[b_tU[ti]])
                    yield
                for i in range(2):
                    S.op("dve", lambda e, i=i: e.bn_stats(out=stt[3][:, i, :], in_=tU[ti][:, i * 512:(i + 1) * 512]), reads=[b_tU[ti]], writes=[b_stt[3]])
                S.op("dve", lambda e: e.bn_aggr(out=mv[3][:], in_=stt[3][:].rearrange("p a b -> p (a b)")), reads=[b_stt[3]], writes=[b_mv[3]])
                ln_small(3)
                S.op("act", lambda e: e.activation(out=tU[ti][:], in_=tU[ti][:], func=AF.Identity, bias=nm[3][:], scale=rs[3][:]), reads=[b_tU[ti], b_nm[3], b_rs[3]], writes=[b_tU[ti]])
                S.op("pool", lambda e: e.tensor_tensor(out=tU[ti][:], in0=tU[ti][:], in1=lng[:], op=ALU.mult), reads=[b_tU[ti], b_lng], writes=[b_tU[ti]])
                S.op("pool", lambda e: e.tensor_tensor(out=tU[ti][:], in0=tU[ti][:], in1=lnb[:], op=ALU.add), reads=[b_tU[ti], b_lnb], writes=[b_tU[ti]])
                S.dma("sp", lambda e: e.dma_start(out=dst_x(grp, tile), in_=tU[ti][:]), reads=[b_tU[ti]], writes=([] if last else [xbuf(grp, u)]),
                      sem_key="xo%d" % ti, final=True)

            def ctx_chunks():
                out = []
                for a in range(2):
                    out.append((lambda pair, half, a=a: ckT[64 * half:64 * half + 64, pair, a * 128:(a + 1) * 128], b_ckT,
                                lambda h, a=a: cV[:, a, h, :], b_cV, None))
                return out

            nP = ns_units - l
            nR = ns_units - 1 - l

            def keyf_S(tile):
                out = []
                for (kt, tid) in attn_keys(tile):
                    ku, kj = kt // 2, kt % 2
                    ksl = ku % NK
                    out.append((lambda pair, half, ksl=ksl, kj=kj: KT[ksl][64 * half:64 * half + 64, pair, kj * 128:(kj + 1) * 128], b_KT[ksl],
                                lambda h, ksl=ksl, kj=kj: Vr[:, 2 * ksl + kj, h, :], b_V[2 * ksl + kj], tid))
                return ctx_chunks() + out

            def pipeline(grp, nP, nR, keyf_of, base, halo):
                def hs_(u): return (base + u) % NH
                def ks_(u): return (base + u) % NK
                def qs_(u): return (base + u) % 2
                def hprev(u): return hs_(u - 1) if (halo and u > 0) else None
                def hnext(u): return hs_(u + 1) if halo else None
                for u0 in range(min(3, nP)):
                    A_pre(grp, u0); A_pe(grp, u0, hs_(u0))
                B_kv(grp, 0, hs_(0), ks_(0)); B_q(grp, 0, hs_(0), qs_(0)); B_conv(grp, 0, hs_(0), qs_(0), None, hnext(0)); B_gm(grp, 0, hs_(0), qs_(0))
                pend = None
                for i in range(nP):
                    u1 = i + 1
                    full1 = u1 < nR
                    if i + 3 < nP:
                        xis_ = A_dma(grp, i + 3)
                    if u1 < nP:
                        B_kv(grp, u1, hs_(u1), ks_(u1))
                    if pend is not None:
                        C_o(*pend)
                        pend = None
                    if i + 3 < nP:
                        A_ln(grp, i + 3, xis_)
                    g_b = iter(())
                    if full1:
                        def g_b_f(u1=u1):
                            yield from B_q_gen(grp, u1, hs_(u1), qs_(u1), "G")
                            yield from B_conv_gen(grp, u1, hs_(u1), qs_(u1), hprev(u1), hnext(u1), "G")
                        g_b = g_b_f()
                    if i < nR:
                        kf = keyf_of(i)
                        ch0 = kf(2 * i)
                        zipper(C_s_gen(grp, i, qs_(i), ch0, 0, "S"), g_b, 1, 2)
                        C_pv(grp, i, qs_(i), ch0, 0)
                        ch1 = kf(2 * i + 1)
                        zipper(C_s_gen(grp, i, qs_(i), ch1, 1, "S"), C_o_gen(grp, i, qs_(i), 0, "G"), 2, 1)
                    else:
                        run(g_b)
                    if i + 3 < nP:
                        A_pe(grp, i + 3, hs_(i + 3))
                    if full1:
                        B_gm(grp, u1, hs_(u1), qs_(u1))
                    if i < nR:
                        C_pv(grp, i, qs_(i), ch1, 1)
                        pend = (grp, i, qs_(i), 1)
                if pend is not None:
                    C_o(*pend)

            def keyf_P_of(u):
                ks = (nP_s + u) % NK
                def keyf(tile):
                    out = []
                    for kj in range(2):
                        out.append((lambda pair, half, kj=kj: KT[ks][64 * half:64 * half + 64, pair, kj * 128:(kj + 1) * 128], b_KT[ks],
                                    lambda h, kj=kj: Vr[:, 2 * ks + kj, h, :], b_V[2 * ks + kj], None))
                    return out
                return keyf

            nP_s = nP
            if not SKIPS:
                pipeline("S", nP, nR, lambda u: keyf_S, 0, True)
            if not SKIPP:
                pipeline("P", n_pseq, n_pseq, keyf_P_of, nP_s, False)

        for l_ in range(depth):
            layer(l_)

        S.emit()
    return nc


def _tables(rpb, flip):
    reps = [(4, 2), (4, 3), (4, 4), (4, 5), (4, 6), (0, 2), (0, 3)]
    out = np.empty((DEPTH, 7, 128, 2, 4, 128), np.float32)
    p = np.arange(128)
    for tid, (t, u) in enumerate(reps):
        ql = t * 128 + p
        kl = u * 128 + p
        qg = 4095 - ql if flip else ql
        kg = 4095 - kl if flip else kl
        qr, qc = qg // 64, qg % 64
        kr, kc = kg // 64, kg % 64
        rs_ = np.clip(qr - 4, 0, 56)
        cs_ = np.clip(qc - 8, 0, 48)
        valid = ((kr[:, None] >= rs_[None, :]) & (kr[:, None] < rs_[None, :] + 8)
                 & (kc[:, None] >= cs_[None, :]) & (kc[:, None] < cs_[None, :] + 16))
        dr = np.clip(kr[:, None] - qr[None, :] + 7, 0, 14)
        dc = np.clip(kc[:, None] - qc[None, :], -15, 15) + 15
        g = rpb[:, :, dr, dc]
        g = np.where(valid[None, None], g, np.float32(NEG))
        g = g.transpose(0, 2, 1, 3).reshape(DEPTH, 128, 4, 2, 128).transpose(0, 1, 3, 2, 4)
        out[:, tid] = g
    return np.ascontiguousarray(out.reshape(DEPTH, 7, 128, 1024))


_NC_CACHE = {}


def kernel(x_prompt, x_sample, cache_k, cache_v, c, c_ctx, w_ada, b_ada, w_in, conv_w,
           gmlp_ln_g, gmlp_ln_b, w_spatial, b_spatial, rpb, w_out, ln_g, ln_b):
    f = lambda a: np.ascontiguousarray(np.asarray(a, dtype=np.float32))
    x_prompt, x_sample, cache_k, cache_v, c, c_ctx = map(f, (x_prompt, x_sample, cache_k, cache_v, c, c_ctx))
    w_ada, b_ada, w_in, conv_w, gmlp_ln_g, gmlp_ln_b = map(f, (w_ada, b_ada, w_in, conv_w, gmlp_ln_g, gmlp_ln_b))
    w_spatial, b_spatial, rpb, w_out, ln_g, ln_b = map(f, (w_spatial, b_spatial, rpb, w_out, ln_g, ln_b))
    if "nc" not in _NC_CACHE:
        _NC_CACHE["nc"] = build()
    nc = _NC_CACHE["nc"]
    ident = np.eye(128, dtype=np.float32)
    tabs = [_tables(rpb, 0), _tables(rpb, 1)]
    in_maps = []
    for i in range(8):
        b, flip = i // 2, i % 2
        xs = x_sample[b][::-1] if flip else x_sample[b]
        xp = x_prompt[4 * i:4 * i + 4]
        if flip:
            xp = xp[:, ::-1]
        ws = w_spatial[:, :, ::-1, ::-1] if flip else w_spatial
        bs = b_spatial[:, :, ::-1] if flip else b_spatial
        cwv = conv_w[:, ::-1, :] if flip else conv_w
        in_maps.append({
            "xp": f(xp.reshape(1024, D)),
            "xs": f(xs[0:3072]),
            "c2": f(np.stack([c[b], c_ctx])),
            "ckT": f(cache_k[b].reshape(DEPTH, 256, 512).transpose(0, 2, 1)),
            "cv": f(cache_v[b].reshape(DEPTH, 256, 512)),
            "w_ada": w_ada, "b_ada": b_ada, "w_in": w_in, "w_out": w_out,
            "conv_w": f(cwv), "gmlp_ln_g": gmlp_ln_g, "gmlp_ln_b": gmlp_ln_b,
            "wsT": f(ws.transpose(0, 3, 1, 2)), "bsT": f(bs.transpose(0, 2, 1)),
            "ebt": tabs[flip], "ln_g": ln_g, "ln_b": ln_b, "ident": ident,
        })
    res = run_bass_kernel_spmd(nc, in_maps, core_ids=list(range(8))).results
    y_prompt = np.empty((32, 256, D), np.float32)
    y_sample = np.empty((4, 4096, D), np.float32)
    new_k = np.empty((32, DEPTH, 256, 8, 64), np.float32)
    new_v = np.empty((32, DEPTH, 256, 8, 64), np.float32)
    for i in range(8):
        b, flip = i // 2, i % 2
        r = res[i]
        yp = np.asarray(r["yp"]).reshape(4, 256, D)
        nk = np.asarray(r["nk"]).reshape(4, DEPTH, 256, 8, 64)
        nv = np.asarray(r["nv"]).reshape(4, DEPTH, 256, 8, 64)
        ys = np.asarray(r["ys"])
        if flip:
            yp = yp[:, ::-1]
            nk = nk[:, :, ::-1]
            nv = nv[:, :, ::-1]
            y_sample[b, 2048:] = ys[::-1]
        else:
            y_sample[b, :2048] = ys
        y_prompt[4 * i:4 * i + 4] = yp
        new_k[4 * i:4 * i + 4] = nk
        new_v[4 * i:4 * i + 4] = nv
    return (y_prompt, y_sample, new_k, new_v)
```

```python
import contextlib
import numpy as np
import concourse.bass as bass
import concourse.mybir as mybir
from concourse.bass_utils import run_bass_kernel_spmd

F32 = mybir.dt.float32
BF16 = mybir.dt.bfloat16
ALU = mybir.AluOpType
AF = mybir.ActivationFunctionType

D = 1024
DEPTH = 4
ALPHA = (2 * DEPTH) ** 0.25
SCALE = 64 ** -0.5
NEG = -30000.0
ENGS = ("pe", "act", "dve", "pool", "sp")
MAXOPS = 10 ** 9
SCHEDULE = True
SKIPP = False
SKIPS = False


class Buf:
    __slots__ = ("name", "excl", "last_w", "readers")

    def __init__(self, name, excl=False):
        self.name = name
        self.excl = excl
        self.last_w = None
        self.readers = []


class Op:
    __slots__ = ("eng", "fn", "is_dma", "sem_key", "deps", "signal", "count", "dma_count", "idx", "reads", "writes",
                 "preds", "succs", "npred", "cost", "fin", "pos", "mode", "start")

    def __init__(self, eng, fn, is_dma, sem_key):
        self.eng = eng
        self.fn = fn
        self.is_dma = is_dma
        self.sem_key = sem_key
        self.deps = []
        self.signal = False
        self.count = 0
        self.dma_count = 0
        self.mode = 0


class _Rec:
    def __init__(self):
        self.name = None
        self.kw = {}
        self.args = ()

    def __getattr__(self, name):
        def f(*a, **k):
            self.name, self.args, self.kw = name, a, k
            return self
        return f


def _nfree(ap):
    n = 1
    for d in tuple(ap.shape)[1:]:
        n *= int(d)
    return n


def _is_psum_or_f32(ap):
    try:
        return ("psum" in str(ap.space).lower()) or (ap.dtype == F32)
    except Exception:
        return True


def _cost_ns(op):
    r = _Rec()
    try:
        op.fn(r)
    except Exception:
        return 300.0
    kw, a, nm_ = r.kw, r.args, r.name
    out = kw.get("out", a[0] if a else None)
    try:
        if op.is_dma:
            byts = _nfree(out) * int(tuple(out.shape)[0]) * 4
            return 2000.0 + byts / 200.0
        if op.eng == "pe":
            if nm_ == "transpose":
                return 75.0
            rhs = kw.get("rhs", a[2] if len(a) > 2 else None)
            lhsT = kw.get("lhsT", a[1] if len(a) > 1 else None)
            if int(tuple(lhsT.shape)[0]) <= 64:
                op.mode = 64
                return max(35.0, 10.0 + _nfree(rhs) * 0.3)
            return max(45.0, 30.0 + _nfree(rhs) * 0.45)
        n = _nfree(out)
        if op.eng == "act":
            return 200.0 + n / 1.15
        if op.eng == "pool":
            if nm_ == "memset":
                return 150.0 + n * 0.9
            return 250.0 + n * 2.2
        if nm_ == "reciprocal":
            return 1000.0
        if nm_ == "bn_aggr":
            return 200.0
        srcs = [kw.get(k) for k in ("in0", "in1", "in_") if kw.get(k) is not None]
        slow = any(_is_psum_or_f32(x) for x in srcs) or nm_ in ("bn_stats",)
        return 160.0 + n * (1.04 if slow else 0.55)
    except Exception:
        return 300.0


class Sched:
    def __init__(self, nc):
        self.nc = nc
        self.all = []
        self.ops = {e: [] for e in ENGS}
        self.dma_counts = {}
        self.dma_keys = []
        self.final_waits = []
        self.dve_token = Buf("dve_token")

    def _add(self, eng, fn, reads, writes, is_dma=False, sem_key=None):
        op = Op(eng, fn, is_dma, sem_key)
        if eng == "dve" and not is_dma:
            reads = reads + [self.dve_token]
        op.reads, op.writes = reads, writes
        op.idx = len(self.all)
        self.all.append(op)
        if is_dma:
            if sem_key not in self.dma_counts:
                self.dma_counts[sem_key] = 0
                self.dma_keys.append(sem_key)
            self.dma_counts[sem_key] += 16
            op.dma_count = self.dma_counts[sem_key]
        return op

    def op(self, eng, fn, reads=(), writes=(), excl_dve=False):
        w = list(writes)
        if excl_dve:
            w.append(self.dve_token)
        return self._add(eng, fn, list(reads), w)

    def dma(self, eng, fn, reads=(), writes=(), sem_key=None, final=False):
        o = self._add(eng, fn, list(reads), list(writes), is_dma=True, sem_key=sem_key)
        if final:
            self.final_waits.append(o)
        return o

    def _edges(self):
        last_key = {}
        for op in self.all:
            preds = {}
            for b in op.reads:
                if b.last_w is not None:
                    preds[id(b.last_w)] = b.last_w
                if b.excl:
                    for r in b.readers:
                        if r.eng != op.eng:
                            preds[id(r)] = r
            for b in op.writes:
                if b.last_w is not None:
                    preds[id(b.last_w)] = b.last_w
                for r in b.readers:
                    preds[id(r)] = r
            if op.is_dma:
                p = last_key.get(op.sem_key)
                if p is not None:
                    preds[id(p)] = p
                last_key[op.sem_key] = op
            preds.pop(id(op), None)
            for b in op.reads:
                b.readers.append(op)
            for b in op.writes:
                b.last_w = op
                b.readers = []
            op.preds = list(preds.values())
            op.succs = []
        for op in self.all:
            for p in op.preds:
                p.succs.append(op)

    def _schedule(self):
        SEM = 120.0
        for op in self.all:
            op.cost = _cost_ns(op)
            op.npred = len(op.preds)
            op.fin = None
        cand = {e: [] for e in ENGS}
        ready = {}
        for op in self.all:
            if op.npred == 0:
                cand[op.eng].append(op)
                ready[id(op)] = 0.0
        free = {e: 0.0 for e in ENGS}
        order = {e: [] for e in ENGS}
        pe_mode = [0]
        left = len(self.all)
        while left:
            best = None
            for e in ENGS:
                cl = cand[e]
                if not cl:
                    continue
                t = free[e]
                pick = None
                if e == "pe":
                    for o in cl:
                        if ready[id(o)] <= t and o.mode == pe_mode[0]:
                            if pick is None or o.idx < pick.idx:
                                pick = o
                if pick is None:
                    for o in cl:
                        if ready[id(o)] <= t:
                            if pick is None or o.idx < pick.idx:
                                pick = o
                if pick is not None:
                    st_ = t
                else:
                    for o in cl:
                        rt = ready[id(o)]
                        if pick is None or rt < ready[id(pick)] or (rt == ready[id(pick)] and o.idx < pick.idx):
                            pick = o
                    st_ = ready[id(pick)]
                if best is None or st_ < best[0] or (st_ == best[0] and pick.idx < best[1].idx):
                    best = (st_, pick)
            st_, o = best
            e = o.eng
            cand[e].remove(o)
            if e == "pe":
                if o.mode != pe_mode[0]:
                    st_ += 120.0
                pe_mode[0] = o.mode
            if o.is_dma:
                free[e] = st_ + 60.0
            else:
                free[e] = st_ + o.cost
            o.fin = st_ + o.cost
            o.start = st_
            o.pos = len(order[e])
            order[e].append(o)
            left -= 1
            for s_ in o.succs:
                s_.npred -= 1
                if s_.npred == 0:
                    pe_s = (s_.eng == "pe" and not s_.is_dma)
                    ready[id(s_)] = max((p.start if (pe_s and p.eng == "pe" and not p.is_dma) else p.fin + SEM) for p in s_.preds)
                    cand[s_.eng].append(s_)
        self.model_ns = max(o.fin for o in self.all)
        return order

    def emit(self):
        nc = self.nc
        self._edges()
        if SCHEDULE:
            self.ops = self._schedule()
        else:
            self.ops = {e: [o for o in self.all if o.eng == e] for e in ENGS}
            for e in ENGS:
                for i, o in enumerate(self.ops[e]):
                    o.pos = i
        for e in ENGS:
            for o in self.ops[e]:
                lastp = {}
                dl = []
                for p in o.preds:
                    if p.is_dma:
                        dl.append(p)
                        continue
                    if p.eng == "pe" and o.eng == "pe" and not o.is_dma:
                        continue
                    q = lastp.get(p.eng)
                    if q is None or p.pos > q.pos:
                        lastp[p.eng] = p
                o.deps = dl + list(lastp.values())
                for p in lastp.values():
                    p.signal = True
        for e in ENGS:
            c = 0
            for o in self.ops[e]:
                if (not o.is_dma) and o.signal:
                    c += 1
                    o.count = c
        with contextlib.ExitStack() as st:
            esem = {e: st.enter_context(nc.semaphore("s_" + e)) for e in ENGS}
            dsem = {k: st.enter_context(nc.semaphore("d_" + str(k))) for k in self.dma_keys}
            block = st.enter_context(nc.Block())
            engobj = {"pe": "tensor", "act": "scalar", "dve": "vector", "pool": "gpsimd", "sp": "sync"}

            def make(e):
                def body(eng):
                    waited = {}
                    for o in self.ops[e]:
                        for d in o.deps:
                            if d.is_dma:
                                s, v, key = dsem[d.sem_key], d.dma_count, ("d", d.sem_key)
                            else:
                                s, v, key = esem[d.eng], d.count, ("e", d.eng)
                            if waited.get(key, 0) >= v:
                                continue
                            waited[key] = v
                            eng.wait_ge(s, v)
                        ins = o.fn(eng)
                        if o.is_dma:
                            ins.then_inc(dsem[o.sem_key], 16)
                        elif o.signal:
                            ins.then_inc(esem[e], 1)
                    if e == "sp":
                        fin = {}
                        for o in self.final_waits:
                            fin[o.sem_key] = max(fin.get(o.sem_key, 0), o.dma_count)
                        for k, v in fin.items():
                            if waited.get(("d", k), 0) >= v:
                                continue
                            eng.wait_ge(dsem[k], v)

                return body

            for e in ENGS:
                getattr(block, engobj[e])(make(e))


def attn_keys(t):
    if t >= 2:
        return [(t + d, d + 2) for d in (-2, -1, 0, 1, 2)]
    if t == 0:
        return [(0, 2), (1, 3), (2, 5), (3, 6)]
    return [(0, 1), (1, 2), (2, 3), (3, 5)]


def build(depth=DEPTH, n_pseq=4, ns_units=12):
    nc = bass.Bass("TRN2", target_bir_lowering=False)

    def din(name, shape):
        return nc.dram_tensor(name, shape, F32, kind="ExternalInput").ap()

    def dout(name, shape):
        return nc.dram_tensor(name, shape, F32, kind="ExternalOutput").ap()

    xp_d = din("xp", [n_pseq * 256, D])
    xs_d = din("xs", [3072, D])
    c2_d = din("c2", [2, D])
    ckT_d = din("ckT", [DEPTH, 512, 256])
    cv_d = din("cv", [DEPTH, 256, 512])
    wada_d = din("w_ada", [DEPTH, D, 3 * D])
    bada_d = din("b_ada", [DEPTH, 3 * D])
    win_d = din("w_in", [DEPTH, D, 3840])
    wout_d = din("w_out", [DEPTH, D, D])
    convw_d = din("conv_w", [DEPTH, 3, 256])
    glg_d = din("gmlp_ln_g", [DEPTH, 256])
    glb_d = din("gmlp_ln_b", [DEPTH, 256])
    wsT_d = din("wsT", [DEPTH, 128, 4, 128])
    bsT_d = din("bsT", [DEPTH, 128, 4])
    ebt_d = din("ebt", [DEPTH, 7, 128, 1024])
    lng_d = din("ln_g", [DEPTH, D])
    lnb_d = din("ln_b", [DEPTH, D])
    id_d = din("ident", [128, 128])
    yp_d = dout("yp", [n_pseq * 256, D])
    ys_d = dout("ys", [2048, D])
    nk_d = dout("nk", [n_pseq, DEPTH, 256, 512])
    nv_d = dout("nv", [n_pseq, DEPTH, 256, 512])
    sxp_d = nc.dram_tensor("sxp", [n_pseq * 256, D], F32).ap()
    sxs_d = nc.dram_tensor("sxs", [3072, D], F32).ap()

    S = Sched(nc)
    st = contextlib.ExitStack()
    with st:
        def sb(name, shape, dt=F32):
            return st.enter_context(nc.sbuf_tensor(name, shape, dt))

        def ps(name, shape, dt=F32):
            return st.enter_context(nc.psum_tensor(name, shape, dt))

        wF = sb("wF", [128, 8, 1536], BF16); b_wF = Buf("wF"); b_wFq = Buf("wFq")
        wT = sb("wT", [128, 8, 2304], BF16); b_wTk = Buf("wTk"); b_wTv = Buf("wTv"); b_wTvu = Buf("wTvu"); b_wTgb = Buf("wTgb"); b_wTgc = Buf("wTgc")
        wo = sb("wo", [128, 8, 1024], BF16); b_wo = Buf("wo")
        NH = 4
        hT = [sb("hT%d" % i, [128, 8, 256], BF16) for i in range(NH)]; b_hT = [Buf("hT%d" % i) for i in range(NH)]
        xA = [sb("xA%d" % i, [128, D]) for i in range(2)]; b_xA = [Buf("xA%d" % i) for i in range(2)]
        xn = [sb("xn%d" % i, [128, D], BF16) for i in range(2)]; b_xn = [Buf("xn%d" % i) for i in range(2)]
        QT = [sb("QT%d" % i, [128, 4, 256], BF16) for i in range(2)]; b_QT = [Buf("QT%d" % i) for i in range(2)]
        NK = 3
        KT = [sb("KT%d" % i, [128, 4, 256], BF16) for i in range(NK)]; b_KT = [Buf("KT%d" % i) for i in range(NK)]
        Vr = sb("Vr", [128, 2 * NK, 8, 65], BF16); b_V = [Buf("V%d" % i) for i in range(2 * NK)]
        yT = [sb("yT%d" % i, [128, 8, 256], BF16) for i in range(2)]; b_yT = [Buf("yT%d" % i) for i in range(2)]
        sgc = [sb("sgc%d" % i, [128, 2, 512], BF16) for i in range(2)]; b_sgc = [Buf("sgc%d" % i) for i in range(2)]
        ktok = sb("ktok", [128, 512], BF16); b_ktok = Buf("ktok")
        ckT = sb("ckT_sb", [128, 4, 256], BF16); b_ckT = Buf("ckT")
        cV = sb("cV", [128, 2, 8, 65], BF16); b_cV = Buf("cV")
        EB = sb("EB", [128, 7, 1024], BF16); b_EB = Buf("EB")
        tU = [sb("tU%d" % i, [128, D]) for i in range(2)]; b_tU = [Buf("tU%d" % i) for i in range(2)]
        Eb = sb("Eb", [128, 14, 512], BF16); b_E = [Buf("E%d" % i) for i in range(14)]
        c2f = sb("c2f", [128, 2, 8]); b_c2f = Buf("c2f")
        c2t = sb("c2t", [128, 2, 8]); b_c2t = Buf("c2t")
        scT = sb("scT", [128, 8, 2], BF16); b_scT = Buf("scT")
        sc1p = sb("sc1p", [128, 2, 8]); b_sc1p = Buf("sc1p")
        shp = sb("shp", [128, 2, 8]); b_shp = Buf("shp")
        badc = sb("badc", [128, 16]); b_badc = Buf("badc")
        gate = sb("gate", [128, 2, D]); b_gate = Buf("gate")
        lng = sb("lng", [128, D]); b_lng = Buf("lng")
        lnb = sb("lnb", [128, D]); b_lnb = Buf("lnb")
        glg = sb("glg", [128, 256]); b_glg = Buf("glg")
        glb = sb("glb", [128, 256]); b_glb = Buf("glb")
        bsT = sb("bsT_sb", [128, 4]); b_bsT = Buf("bsT")
        bsb = sb("bsb", [128, 256]); b_bsb = Buf("bsb")
        rsw = sb("rsw", [128, 4]); b_rsw = Buf("rsw")
        ones1 = sb("ones1", [128, 2], BF16); b_ones1 = Buf("ones1")
        wsT = sb("wsT_sb", [128, 4, 128], BF16); b_wsT = Buf("wsT")
        cw = sb("cw", [128, 2, 3]); b_cw = Buf("cw")
        ident = sb("ident_sb", [128, 128], BF16); b_id = Buf("ident")
        nh = sb("nh", [128, 2]); b_nh = Buf("nh")
        stt = [sb("stt%d" % i, [128, 2, 6]) for i in range(4)]
        mv = [sb("mv%d" % i, [128, 2]) for i in range(4)]
        ve = [sb("ve%d" % i, [128, 1]) for i in range(4)]
        rs = [sb("rs%d" % i, [128, 1]) for i in range(4)]
        nm = [sb("nm%d" % i, [128, 1]) for i in range(4)]
        b_stt = [Buf("stt%d" % i) for i in range(4)]
        b_mv = [Buf("mv%d" % i) for i in range(4)]
        b_ve = [Buf("ve%d" % i) for i in range(4)]
        b_rs = [Buf("rs%d" % i) for i in range(4)]
        b_nm = [Buf("nm%d" % i) for i in range(4)]
        wkall = sb("wkall", [128, 4, 256])
        wk = {}
        for i_, nme in enumerate(("xa", "acc", "tg", "s2")):
            wk[nme] = (wkall[:, i_, :], Buf("wk_" + nme))
        wk["m"] = wk["xa"]
        tgc = wkall[:, 2:4, :].rearrange("p a t -> p (a t)")
        b_tgc_l = [wk["tg"][1], wk["s2"][1]]
        zb = sb("zb", [128, 258]); b_zb = Buf("zb")
        xh = sb("xh", [128, 4]); b_xh = Buf("xh")
        zh = sb("zh", [128, 4]); b_zh = Buf("zh")
        vn3 = sb("vn3", [128, 256], BF16); b_vn3 = Buf("vn3")
        ybt = sb("ybt", [128, 256], BF16); b_ybt = Buf("ybt")
        rinv = sb("rinv", [128, 8]); b_rinv = Buf("rinv")
        ob = sb("ob", [128, 512]); b_ob = Buf("ob")
        ob2 = sb("ob2", [128, 512]); b_ob2 = Buf("ob2")
        kst = ob2; b_kst = b_ob2
        vst = ob; b_vst = b_ob
        yct = sb("yct", [128, 512], BF16); b_yct = Buf("yct")

        NMM = 6
        mmb = [ps("mm%d" % i, [128, 512]) for i in range(NMM)]; b_mm = [Buf("mm%d" % i, True) for i in range(NMM)]
        pv = [ps("pv%d" % i, [128, 512]) for i in range(2)]; b_pv = [Buf("pv0", True), Buf("pv1", True)]
        mm_i = [0]

        pool_banks = {None: [0, 1, 2, 3, 4, 5], "S": [2, 3, 4, 5], "G": [0, 1]}
        pool_i = {None: 0, "S": 0, "G": 0}

        def mm(pool=None):
            bl = pool_banks[pool]
            i = bl[pool_i[pool] % len(bl)]
            pool_i[pool] += 1
            return mmb[i], b_mm[i]

        def mmt(pool=None):
            t, b = mm(pool)
            return t[:].bitcast(BF16), b

        def ln_small(site, eng_after="dve"):
            S.op("dve", lambda e: e.tensor_scalar(out=ve[site][:], in0=mv[site][:, 1:2], scalar1=1e-5, scalar2=None, op0=ALU.add),
                 reads=[b_mv[site]], writes=[b_ve[site]])
            S.op("pool", lambda e: e.tensor_tensor(out=rs[site][:], in0=ve[site][:], in1=nh[:, 0:1], op=ALU.pow),
                 reads=[b_ve[site], b_nh], writes=[b_rs[site]], excl_dve=True)
            S.op("dve", lambda e: e.scalar_tensor_tensor(out=nm[site][:], in0=mv[site][:, 0:1], scalar=-1.0, in1=rs[site][:], op0=ALU.mult, op1=ALU.mult),
                 reads=[b_mv[site], b_rs[site]], writes=[b_nm[site]])

        S.dma("pool", lambda e: e.dma_start(out=ident[:], in_=id_d), writes=[b_id], sem_key="c_id")
        S.op("pool", lambda e: e.memset(nh[:], -0.5), writes=[b_nh])
        S.op("pool", lambda e: e.memset(ones1[:], 1.0), writes=[b_ones1])
        S.op("pool", lambda e: e.memset(Vr[:].rearrange("p a h d -> p (a h d)"), 1.0), writes=b_V)
        S.op("pool", lambda e: e.memset(cV[:].rearrange("p a h d -> p (a h d)"), 1.0), writes=[b_cV])
        for v in range(2):
            S.dma("sp", lambda e, v=v: e.dma_start(out=c2f[:, v, :], in_=c2_d[v, :].rearrange("(k p) -> p k", p=128), allow_slow_non_contiguous=True), writes=[b_c2f], sem_key="c_c2")
        S.op("act", lambda e: e.activation(out=c2t[:], in_=c2f[:], func=AF.Tanh, scale=0.5), reads=[b_c2f], writes=[b_c2t])
        S.op("dve", lambda e: e.scalar_tensor_tensor(out=c2t[:], in0=c2t[:], scalar=1.0, in1=c2f[:], op0=ALU.add, op1=ALU.mult), reads=[b_c2f, b_c2t], writes=[b_c2t])
        S.op("dve", lambda e: e.tensor_scalar(out=scT[:].rearrange("p k v -> p v k"), in0=c2t[:], scalar1=0.5, scalar2=None, op0=ALU.mult), reads=[b_c2t], writes=[b_scT])
        b_sxs = [Buf("sxs%d" % u) for u in range(12)]
        b_sxp = [Buf("sxp%d" % u) for u in range(n_pseq)]

        def layer(l):
            last = (l == depth - 1)
            def wdma(dst, src, bufd, key):
                S.dma("pool", lambda e: e.dma_start(out=dst, in_=src.rearrange("(k p) n -> p k n", p=128)), writes=[bufd], sem_key=key)
            wdma(wT[:, :, 1792:2304], win_d[l][:, 2304:2816], b_wTk, "w_Tk")
            wdma(wT[:, :, 768:1280], win_d[l][:, 2816:3328], b_wTv, "w_Tv")
            for c_ in range(2):
                S.dma("sp", lambda e, c_=c_: e.dma_start(out=cw[:, c_, :], in_=convw_d[l][:, c_ * 128:(c_ + 1) * 128].rearrange("j p -> p j"), allow_slow_non_contiguous=True), writes=[b_cw], sem_key="p_cw")
            S.dma("sp", lambda e: e.dma_start(out=glg[:], in_=glg_d[l, :].partition_broadcast(128)), writes=[b_glg], sem_key="p_glg")
            S.dma("sp", lambda e: e.dma_start(out=glb[:], in_=glb_d[l, :].partition_broadcast(128)), writes=[b_glb], sem_key="p_glb")
            S.dma("sp", lambda e: e.dma_start(out=lng[:], in_=lng_d[l, :].partition_broadcast(128)), writes=[b_lng], sem_key="p_lng")
            S.dma("sp", lambda e: e.dma_start(out=lnb[:], in_=lnb_d[l, :].partition_broadcast(128)), writes=[b_lnb], sem_key="p_lnb")
            S.dma("sp", lambda e: e.dma_start(out=bsT[:], in_=bsT_d[l]), writes=[b_bsT], sem_key="p_bsT")
            S.op("dve", lambda e: e.tensor_copy(out=bsb[:].rearrange("p (g c) -> p g c", g=4), in_=bsT[:, :].unsqueeze(2).to_broadcast([128, 4, 64])), reads=[b_bsT], writes=[b_bsb])
            S.dma("pool", lambda e: e.dma_start(out=wsT[:], in_=wsT_d[l]), writes=[b_wsT], sem_key="p_wsT")
            prs, bprs = mm()
            for g in range(4):
                S.op("pe", lambda e, g=g, prs=prs: e.matmul(prs[:, g:g + 1], lhsT=wsT[:, g, :], rhs=ones1[:, 0:1], start=True, stop=True), reads=[b_wsT, b_ones1], writes=[bprs])
            S.op("dve", lambda e, prs=prs: e.tensor_copy(out=rsw[:], in_=prs[:, 0:4]), reads=[bprs], writes=[b_rsw])
            S.op("dve", lambda e: e.tensor_tensor(out=glb[:].rearrange("p (g c) -> p g c", g=4), in0=glb[:].rearrange("p (g c) -> p g c", g=4), in1=rsw[:, :].unsqueeze(2).to_broadcast([128, 4, 64]), op=ALU.mult),
                 reads=[b_glb, b_rsw], writes=[b_glb])
            S.op("dve", lambda e: e.tensor_tensor(out=bsb[:], in0=bsb[:], in1=glb[:], op=ALU.add), reads=[b_bsb, b_glb], writes=[b_bsb])
            S.dma("sp", lambda e: e.dma_start(out=badc[:], in_=bada_d[l, 0:2048].rearrange("(j p) -> p j", p=128), allow_slow_non_contiguous=True), writes=[b_badc], sem_key="p_badc")
            wad = Eb[:, 0:8, :]
            Ssil = Eb[:, 8:12, :].rearrange("p (v a) (b t) -> p v (a b) t", v=2, b=4)
            b_Ssil_l = b_E[8:12]
            for v in range(2):
                S.op("dve", lambda e, v=v: e.tensor_copy(out=Ssil[:, v], in_=scT[:, :, v:v + 1].to_broadcast([128, 8, 128])), reads=[b_scT], writes=b_Ssil_l)
            for ch in range(6):
                S.dma("pool", lambda e, ch=ch: e.dma_start(out=wad, in_=wada_d[l][:, ch * 512:(ch + 1) * 512].rearrange("(k p) n -> p k n", p=128)),
                      writes=b_E[0:8], sem_key="w_ada")
                if ch < 4:
                    pt, bpt = mm()
                    for jb in range(4):
                        for kc in range(8):
                            S.op("pe", lambda e, jb=jb, kc=kc, pt=pt: e.matmul(pt[:, jb * 2:jb * 2 + 2], lhsT=wad[:, kc, jb * 128:(jb + 1) * 128], rhs=scT[:, kc, :], start=(kc == 0), stop=(kc == 7)),
                                 reads=b_E[0:8] + [b_scT], writes=[bpt])
                    for v in range(2):
                        dst = (shp if ch < 2 else sc1p)
                        bd = (b_shp if ch < 2 else b_sc1p)
                        j0 = (ch % 2) * 4
                        S.op("dve", lambda e, v=v, dst=dst, j0=j0, pt=pt, ch=ch: e.scalar_tensor_tensor(
                            out=dst[:, v, j0:j0 + 4], in0=pt[:, v:8:2], scalar=(0.0 if ch < 2 else 1.0), in1=badc[:, ch * 4:ch * 4 + 4], op0=ALU.add, op1=ALU.add),
                            reads=[bpt, b_badc], writes=[bd])
                else:
                    half = ch - 4
                    S.dma("sp", lambda e, half=half: e.dma_start(out=tU[0][:, 0:512], in_=bada_d[l, 2048 + half * 512:2048 + (half + 1) * 512].partition_broadcast(128)), writes=[b_tU[0]], sem_key="p_bg")
                    for v in range(2):
                        pt, bpt = mm()
                        for kc in range(8):
                            S.op("pe", lambda e, v=v, kc=kc, pt=pt: e.matmul(pt[:, :], lhsT=Ssil[:, v, kc, :], rhs=wad[:, kc, :], start=(kc == 0), stop=(kc == 7)),
                                 reads=b_E[0:8] + b_Ssil_l, writes=[bpt])
                        S.op("dve", lambda e, v=v, pt=pt, half=half: e.tensor_tensor(out=gate[:, v, half * 512:(half + 1) * 512], in0=pt[:, :], in1=tU[0][:, 0:512], op=ALU.add),
                             reads=[bpt, b_tU[0]], writes=[b_gate])
            wdma(wF[:, :, 1024:1536], win_d[l][:, 1792:2304], b_wFq, "w_Fq")
            wdma(wF[:, :, 0:1024], win_d[l][:, 0:1024], b_wF, "w_F")
            wdma(wT[:, :, 0:256], win_d[l][:, 1280:1536], b_wTvu, "w_Tvu")
            wdma(wT[:, :, 256:512], win_d[l][:, 1024:1280], b_wTvu, "w_Tvu2")
            wdma(wT[:, :, 1280:1792], win_d[l][:, 3328:3840], b_wTgc, "w_Tgc")
            wdma(wT[:, :, 512:768], win_d[l][:, 1536:1792], b_wTgb, "w_Tgb")
            wdma(wo[:], wout_d[l], b_wo, "w_o")
            S.dma("pool", lambda e: e.dma_start(out=ckT[:], in_=ckT_d[l].rearrange("(a p) k -> p a k", p=128)), writes=[b_ckT], sem_key="c_k")
            for a in range(2):
                S.dma("pool", lambda e, a=a: e.dma_start(out=cV[:, a, :, 0:64], in_=cv_d[l][a * 128:(a + 1) * 128, :].rearrange("p (h d) -> p h d", d=64)), writes=[b_cV], sem_key="c_v")
            for tid in range(7):
                S.dma("sp", lambda e, tid=tid: e.dma_start(out=tU[tid % 2][:], in_=ebt_d[l, tid]), writes=[b_tU[tid % 2]], sem_key="p_eb%d" % (tid % 2))
                S.op("act", lambda e, tid=tid: e.activation(out=EB[:, tid, :], in_=tU[tid % 2][:], func=AF.Exp), reads=[b_tU[tid % 2]], writes=[b_EB])

            def src_x(grp, tile):
                if l == 0:
                    return (xs_d if grp == "S" else xp_d)[tile * 128:(tile + 1) * 128, :]
                return (sxs_d if grp == "S" else sxp_d)[tile * 128:(tile + 1) * 128, :]

            def dst_x(grp, tile):
                if last:
                    return (ys_d if grp == "S" else yp_d)[tile * 128:(tile + 1) * 128, :]
                return (sxs_d if grp == "S" else sxp_d)[tile * 128:(tile + 1) * 128, :]

            def xbuf(grp, u):
                return (b_sxs if grp == "S" else b_sxp)[u]

            cnt = {"xA": 0, "tU": 0}

            def A_dma(grp, u):
                xis = []
                for j in range(2):
                    tile = 2 * u + j
                    xi = cnt["xA"] % 2
                    cnt["xA"] += 1
                    rd = [xbuf(grp, u)] if l > 0 else []
                    S.dma("sp", lambda e, xi=xi, tile=tile: e.dma_start(out=xA[xi][:], in_=src_x(grp, tile)), reads=rd, writes=[b_xA[xi]], sem_key="xA%d" % xi)
                    xis.append(xi)
                return xis

            def A_ln(grp, u, xis):
                for j in range(2):
                    xi = xis[j]
                    for i in range(2):
                        S.op("dve", lambda e, i=i, xi=xi, j=j: e.bn_stats(out=stt[j][:, i, :], in_=xA[xi][:, i * 512:(i + 1) * 512]), reads=[b_xA[xi]], writes=[b_stt[j]])
                    S.op("dve", lambda e, j=j: e.bn_aggr(out=mv[j][:], in_=stt[j][:].rearrange("p a b -> p (a b)")), reads=[b_stt[j]], writes=[b_mv[j]])
                    ln_small(j)
                    S.op("act", lambda e, xi=xi, j=j: e.activation(out=xn[j][:], in_=xA[xi][:], func=AF.Identity, bias=nm[j][:], scale=rs[j][:]),
                         reads=[b_xA[xi], b_nm[j], b_rs[j]], writes=[b_xn[j]])

            def A_pre(grp, u):
                A_ln(grp, u, A_dma(grp, u))

            def A_pe(grp, u, hs):
                v = 0 if grp == "S" else 1
                for j in range(2):
                    tr, b_tr = mmt()
                    for kc in range(8):
                        S.op("pe", lambda e, kc=kc, j=j, tr=tr: e.transpose(out=tr[:, kc * 128:(kc + 1) * 128], in_=xn[j][:, kc * 128:(kc + 1) * 128], identity=ident[:]),
                             reads=[b_xn[j], b_id], writes=[b_tr])
                    for kc in range(8):
                        if kc % 2 == 0:
                            S.op("dve", lambda e, kc=kc, j=j, tr=tr: e.tensor_scalar(out=hT[hs][:, kc, j * 128:(j + 1) * 128], in0=tr[:, kc * 128:(kc + 1) * 128],
                                                                                      scalar1=sc1p[:, v, kc:kc + 1], scalar2=shp[:, v, kc:kc + 1], op0=ALU.mult, op1=ALU.add),
                                 reads=[b_tr, b_sc1p, b_shp], writes=[b_hT[hs]])
                    for kc in range(8):
                        if kc % 2 == 1:
                            S.op("act", lambda e, kc=kc, j=j, tr=tr: e.activation(out=hT[hs][:, kc, j * 128:(j + 1) * 128], in_=tr[:, kc * 128:(kc + 1) * 128], func=AF.Identity,
                                                                                    scale=sc1p[:, v, kc:kc + 1], bias=shp[:, v, kc:kc + 1]),
                                 reads=[b_tr, b_sc1p, b_shp], writes=[b_hT[hs]])

            def proj_T(hs, j, c0, c1):
                pt, bpt = mm()
                n = c1 - c0
                for kc in range(8):
                    S.op("pe", lambda e, kc=kc, pt=pt: e.matmul(pt[:, 0:n], lhsT=hT[hs][:, kc, j * 128:(j + 1) * 128], rhs=wT[:, kc, c0:c1], start=(kc == 0), stop=(kc == 7)),
                         reads=[b_hT[hs], {1792: b_wTk, 768: b_wTv, 0: b_wTvu, 512: b_wTgb, 1280: b_wTgc}[c0]], writes=[bpt])
                return pt, bpt

            def proj_F(hs, pt, bpt, off, cb):
                for kc in range(8):
                    S.op("pe", lambda e, kc=kc: e.matmul(pt[:, off:off + 256], lhsT=wF[:, kc, cb * 128:(cb + 1) * 128], rhs=hT[hs][:, kc, :], start=(kc == 0), stop=(kc == 7)),
                         reads=[b_hT[hs], (b_wFq if cb >= 8 else b_wF)], writes=[bpt])

            def B_kv(grp, u, hs, ks):
                with_out = (grp == "P")
                for j in range(2):
                    vslot = 2 * ks + j
                    pt, bpt = proj_T(hs, j, 1792, 2304)
                    S.op("act", lambda e, pt=pt: e.activation(out=ktok[:], in_=pt[:, :], func=AF.Copy), reads=[bpt], writes=[b_ktok])
                    if with_out:
                        S.op("dve", lambda e, pt=pt: e.tensor_copy(out=kst[:], in_=pt[:, :]), reads=[bpt], writes=[b_kst])
                        S.dma("sp", lambda e, j=j: e.dma_start(out=nk_d[u, l, j * 128:(j + 1) * 128, :], in_=kst[:]), reads=[b_kst], sem_key="o_k", final=True)
                    tr, b_tr = mmt()
                    for a in range(4):
                        S.op("pe", lambda e, a=a, tr=tr: e.transpose(out=tr[:, a * 128:(a + 1) * 128], in_=ktok[:, a * 128:(a + 1) * 128], identity=ident[:]),
                             reads=[b_ktok, b_id], writes=[b_tr])
                    S.op("dve", lambda e, j=j, tr=tr: e.tensor_copy(out=KT[ks][:, :, j * 128:(j + 1) * 128], in_=tr[:, 0:512].rearrange("p (a t) -> p a t", a=4)),
                         reads=[b_tr], writes=[b_KT[ks]])
                    pt, bpt = proj_T(hs, j, 768, 1280)
                    S.op("act", lambda e, pt=pt, vslot=vslot: e.activation(out=Vr[:, vslot, :, 0:64], in_=pt[:, :].rearrange("p (h d) -> p h d", d=64), func=AF.Copy),
                         reads=[bpt], writes=[b_V[vslot]])
                    if with_out:
                        S.op("dve", lambda e, pt=pt: e.tensor_copy(out=vst[:], in_=pt[:, :]), reads=[bpt], writes=[b_vst])
                        S.dma("sp", lambda e, j=j: e.dma_start(out=nv_d[u, l, j * 128:(j + 1) * 128, :], in_=vst[:]), reads=[b_vst], sem_key="o_v", final=True)

            def proj_F_g(hs, pt, bpt, off, cb):
                proj_F(hs, pt, bpt, off, cb)
                yield

            def B_q_gen(grp, u, hs, qs, pool=None):
                for half in range(2):
                    pq, bpq = mm(pool)
                    yield from proj_F_g(hs, pq, bpq, 0, 8 + 2 * half)
                    yield from proj_F_g(hs, pq, bpq, 256, 9 + 2 * half)
                    S.op("act", lambda e, pq=pq, half=half: e.activation(out=QT[qs][:, 2 * half:2 * half + 2, :], in_=pq[:, :].rearrange("p (a t) -> p a t", a=2), func=AF.Copy),
                         reads=[bpq], writes=[b_QT[qs]])

            def run(gen):
                for _ in gen:
                    pass

            def zipper(ga, gb, na=1, nb=2):
                da = db = False
                while not (da and db):
                    for _ in range(na):
                        if not da:
                            try:
                                next(ga)
                            except StopIteration:
                                da = True
                    for _ in range(nb):
                        if not db:
                            try:
                                next(gb)
                            except StopIteration:
                                db = True

            def B_q(grp, u, hs, qs):
                run(B_q_gen(grp, u, hs, qs))

            def B_conv_gen(grp, u, hs, qs, hs_prev, hs_next, pool=None):
                yt, byt = yT[qs], b_yT[qs]
                have_h = [hs_prev is not None, hs_next is not None]
                hal = pv[1][:, 384:392]
                b_halo = b_pv[1]
                for cbi, cb in enumerate((0, 1, 4, 5)):
                    for side in range(2):
                        if not have_h[side]:
                            continue
                        hsrc = hT[hs_prev][:, :, 255:256] if side == 0 else hT[hs_next][:, :, 0:1]
                        bsrc = b_hT[hs_prev] if side == 0 else b_hT[hs_next]
                        for kc in range(8):
                            S.op("pe", lambda e, kc=kc, cb=cb, hsrc=hsrc, col=cbi * 2 + side: e.matmul(hal[:, col:col + 1], lhsT=wF[:, kc, cb * 128:(cb + 1) * 128], rhs=hsrc[:, kc, :], start=(kc == 0), stop=(kc == 7)),
                                 reads=[bsrc, b_wF], writes=[b_halo])
                    yield
                S.op("pool", lambda e: e.memset(zh[:], 0.0), writes=[b_zh])
                for side in range(2):
                    if have_h[side]:
                        S.op("dve", lambda e, side=side: e.tensor_copy(out=xh[:, side:4:2], in_=hal[:, side:4:2]), reads=[b_halo], writes=[b_xh])
                        S.op("dve", lambda e, side=side: e.tensor_tensor(out=zh[:, side:4:2], in0=hal[:, 4 + side:8:2], in1=xh[:, side:4:2], op=ALU.mult), reads=[b_halo, b_xh], writes=[b_zh])
                for c in range(2):
                    p1, bp1 = mm(pool)
                    yield from proj_F_g(hs, p1, bp1, 0, 0 + c)
                    yield from proj_F_g(hs, p1, bp1, 256, 4 + c)
                    p2, bp2 = mm(pool)
                    yield from proj_F_g(hs, p2, bp2, 0, 2 + c)
                    yield from proj_F_g(hs, p2, bp2, 256, 6 + c)
                    xa_t, bxa = wk["xa"]; acc, bacc = wk["acc"]; tg, btg = wk["tg"]; s2, bs2 = wk["s2"]; m_, bm = wk["m"]
                    S.op("act", lambda e, p1=p1: e.activation(out=xa_t[:], in_=p1[:, 0:256], func=AF.Copy), reads=[bp1], writes=[bxa])
                    S.op("dve", lambda e, p1=p1: e.tensor_tensor(out=zb[:, 1:257], in0=p1[:, 256:512], in1=xa_t[:], op=ALU.mult), reads=[bp1, bxa], writes=[b_zb])
                    S.op("dve", lambda e, c=c: e.tensor_copy(out=zb[:, 0:258:257], in_=zh[:, 2 * c:2 * c + 2]), reads=[b_zh], writes=[b_zb])
                    S.op("act", lambda e, c=c: e.activation(out=acc[:], in_=zb[:, 1:257], func=AF.Identity, scale=cw[:, c, 1:2]), reads=[b_zb, b_cw], writes=[bacc])
                    S.op("dve", lambda e, c=c: e.scalar_tensor_tensor(out=acc[:], in0=zb[:, 0:256], scalar=cw[:, c, 0:1], in1=acc[:], op0=ALU.mult, op1=ALU.add), reads=[b_zb, b_cw, bacc], writes=[bacc])
                    S.op("dve", lambda e, c=c: e.scalar_tensor_tensor(out=acc[:], in0=zb[:, 2:258], scalar=cw[:, c, 2:3], in1=acc[:], op0=ALU.mult, op1=ALU.add), reads=[b_zb, b_cw, bacc], writes=[bacc])
                    S.op("act", lambda e, p2=p2: e.activation(out=tg[:], in_=p2[:, 256:512], func=AF.Tanh, scale=0.5), reads=[bp2], writes=[btg])
                    S.op("dve", lambda e, p2=p2: e.scalar_tensor_tensor(out=s2[:], in0=tg[:], scalar=1.0, in1=p2[:, 256:512], op0=ALU.add, op1=ALU.mult), reads=[btg, bp2], writes=[bs2])
                    S.op("dve", lambda e, p2=p2: e.tensor_tensor(out=m_[:], in0=p2[:, 0:256], in1=acc[:], op=ALU.mult), reads=[bp2, bacc], writes=[bm])
                    S.op("dve", lambda e, c=c: e.scalar_tensor_tensor(out=yt[:, c, :], in0=m_[:], scalar=0.5, in1=s2[:], op0=ALU.mult, op1=ALU.mult), reads=[bm, bs2], writes=[byt])
                    yield

            def B_conv(grp, u, hs, qs, hs_prev, hs_next):
                run(B_conv_gen(grp, u, hs, qs, hs_prev, hs_next))

            def B_gm(grp, u, hs, qs):
                yt, byt = yT[qs], b_yT[qs]
                for j in range(2):
                    xa_t, bxa = wk["xa"]; acc, bacc = wk["acc"]; tg, btg = wk["tg"]; s2, bs2 = wk["s2"]; m_, bm = wk["m"]
                    pvu, bpvu = proj_T(hs, j, 0, 512)
                    S.op("dve", lambda e, pvu=pvu: e.bn_stats(out=stt[2][:, 0, :], in_=pvu[:, 0:256]), reads=[bpvu], writes=[b_stt[2]])
                    S.op("dve", lambda e: e.bn_aggr(out=mv[2][:], in_=stt[2][:, 0, :]), reads=[b_stt[2]], writes=[b_mv[2]])
                    ln_small(2)
                    pgc, bpgc = proj_T(hs, j, 1280, 1792)
                    S.op("act", lambda e, pgc=pgc: e.activation(out=tgc[:], in_=pgc[:, :], func=AF.Tanh, scale=0.5), reads=[bpgc], writes=b_tgc_l)
                    S.op("dve", lambda e, pgc=pgc, j=j: e.scalar_tensor_tensor(out=sgc[qs][:, j, :], in0=tgc[:], scalar=1.0, in1=pgc[:, :], op0=ALU.add, op1=ALU.mult), reads=b_tgc_l + [bpgc], writes=[b_sgc[qs]])
                    pgb, bpgb = proj_T(hs, j, 512, 768)
                    S.op("act", lambda e, pvu=pvu: e.activation(out=vn3[:], in_=pvu[:, 0:256], func=AF.Identity, bias=nm[2][:], scale=rs[2][:]), reads=[bpvu, b_nm[2], b_rs[2]], writes=[b_vn3])
                    psv, bpsv = mm()
                    for g in range(4):
                        S.op("pe", lambda e, g=g, psv=psv: e.matmul(psv[:, g * 64:(g + 1) * 64], lhsT=wsT[:, g, :], rhs=vn3[:, g * 64:(g + 1) * 64], start=True, stop=True),
                             reads=[b_wsT, b_vn3], writes=[bpsv])
                    S.op("dve", lambda e, psv=psv: e.tensor_tensor(out=acc[:], in0=psv[:, 0:256], in1=glg[:], op=ALU.mult), reads=[bpsv, b_glg], writes=[bacc])
                    S.op("dve", lambda e: e.tensor_tensor(out=acc[:], in0=acc[:], in1=bsb[:], op=ALU.add), reads=[bacc, b_bsb], writes=[bacc])
                    S.op("dve", lambda e, pvu=pvu: e.tensor_tensor(out=m_[:], in0=pvu[:, 256:512], in1=acc[:], op=ALU.mult), reads=[bpvu, bacc], writes=[bm])
                    S.op("act", lambda e, pgb=pgb: e.activation(out=tg[:], in_=pgb[:, 0:256], func=AF.Tanh, scale=0.5), reads=[bpgb], writes=[btg])
                    S.op("dve", lambda e, pgb=pgb: e.scalar_tensor_tensor(out=s2[:], in0=tg[:], scalar=1.0, in1=pgb[:, 0:256], op0=ALU.add, op1=ALU.mult), reads=[btg, bpgb], writes=[bs2])
                    S.op("dve", lambda e: e.scalar_tensor_tensor(out=ybt[:], in0=m_[:], scalar=0.5, in1=s2[:], op0=ALU.mult, op1=ALU.mult), reads=[bm, bs2], writes=[b_ybt])
                    tr, b_tr = mmt()
                    for a in range(2):
                        S.op("pe", lambda e, a=a, tr=tr: e.transpose(out=tr[:, a * 128:(a + 1) * 128], in_=ybt[:, a * 128:(a + 1) * 128], identity=ident[:]), reads=[b_ybt, b_id], writes=[b_tr])
                    S.op("act", lambda e, j=j, tr=tr: e.activation(out=yt[:, 2:4, j * 128:(j + 1) * 128], in_=tr[:, 0:256].rearrange("p (a t) -> p a t", a=2), func=AF.Copy), reads=[b_tr], writes=[byt])

            def C_s_gen(grp, u, qs, chunks, j, pool=None):
                for ci, (kf, kb, vf, vb, tid) in enumerate(chunks):
                    SE, b_SE = mm(pool)
                    SO, b_SO = mm(pool)
                    for h in range(8):
                        pair, half = h // 2, h % 2
                        bank, bbank = (SE, b_SE) if half == 0 else (SO, b_SO)
                        S.op("pe", lambda e, kf=kf, pair=pair, half=half, bank=bank: e.matmul(
                            bank[:, pair * 128:(pair + 1) * 128], lhsT=kf(pair, half), rhs=QT[qs][64 * half:64 * half + 64, pair, j * 128:(j + 1) * 128], start=True, stop=True),
                            reads=[kb, b_QT[qs]], writes=[bbank])
                    for half in range(2):
                        bank, bbank = (SE, b_SE) if half == 0 else (SO, b_SO)
                        ei = 2 * ci + half
                        S.op("act", lambda e, bank=bank, ei=ei: e.activation(out=Eb[:, ei, :], in_=bank[:, :], func=AF.Exp, scale=SCALE), reads=[bbank], writes=[b_E[ei]])
                        if tid is not None:
                            S.op(("pool" if half == 1 else "dve"), lambda e, ei=ei, tid=tid, half=half: e.tensor_tensor(out=Eb[:, ei, :], in0=Eb[:, ei, :], in1=EB[:, tid, half * 512:(half + 1) * 512], op=ALU.mult),
                                 reads=[b_E[ei], b_EB], writes=[b_E[ei]])
                    yield

            def C_s(grp, u, qs, keyf, j):
                chunks = keyf(2 * u + j)
                run(C_s_gen(grp, u, qs, chunks, j))
                return chunks

            def C_pv(grp, u, qs, chunks, j):
                nck = len(chunks)
                for h in range(8):
                    pair, half = h // 2, h % 2
                    pb, bpb = pv[h // 4], b_pv[h // 4]
                    for ci, (kf, kb, vf, vb, tid) in enumerate(chunks):
                        ei = 2 * ci + half
                        S.op("pe", lambda e, ei=ei, pair=pair, vf=vf, h=h, pb=pb, ci=ci: e.matmul(
                            pb[:, (h % 4) * 65:(h % 4) * 65 + 65], lhsT=Eb[:, ei, pair * 128:(pair + 1) * 128], rhs=vf(h), start=(ci == 0), stop=(ci == nck - 1)),
                            reads=[b_E[ei], vb], writes=[bpb])
                for g2 in range(2):
                    pb, bpb = pv[g2], b_pv[g2]
                    pv3 = pb[:, 0:260].rearrange("p (h d) -> p h d", d=65)
                    S.op("dve", lambda e, pv3=pv3, g2=g2: e.reciprocal(out=rinv[:, 4 * g2:4 * g2 + 4], in_=pv3[:, :, 64]), reads=[bpb], writes=[b_rinv])
                    S.op("dve", lambda e, pv3=pv3, g2=g2: e.tensor_tensor(out=ob[:, 256 * g2:256 * g2 + 256].rearrange("p (h d) -> p h d", d=64), in0=pv3[:, :, 0:64],
                                                                            in1=rinv[:, 4 * g2:4 * g2 + 4].unsqueeze(2).to_broadcast([128, 4, 64]), op=ALU.mult),
                         reads=[bpb, b_rinv], writes=[b_ob])
                S.op("dve", lambda e: e.scalar_tensor_tensor(out=yct[:], in0=ob[:], scalar=0.5, in1=sgc[qs][:, j, :], op0=ALU.mult, op1=ALU.mult), reads=[b_ob, b_sgc[qs]], writes=[b_yct])

            def C_o(grp, u, qs, j):
                run(C_o_gen(grp, u, qs, j))

            def C_o_gen(grp, u, qs, j, pool=None):
                v = 0 if grp == "S" else 1
                yt, byt = yT[qs], b_yT[qs]
                tile = 2 * u + j
                tr, b_tr = mmt(pool)
                for a in range(4):
                    S.op("pe", lambda e, a=a: e.transpose(out=tr[:, a * 128:(a + 1) * 128], in_=yct[:, a * 128:(a + 1) * 128], identity=ident[:]), reads=[b_yct, b_id], writes=[b_tr])
                S.op("act", lambda e: e.activation(out=yt[:, 4:8, j * 128:(j + 1) * 128], in_=tr[:, 0:512].rearrange("p (a t) -> p a t", a=4), func=AF.Copy), reads=[b_tr], writes=[byt])
                yield
                ti = cnt["tU"] % 2
                cnt["tU"] += 1
                rd = [xbuf(grp, u)] if l > 0 else []
                S.dma("sp", lambda e: e.dma_start(out=tU[ti][:], in_=src_x(grp, tile)), reads=rd, writes=[b_tU[ti]], sem_key="xC%d" % ti)
                for n in range(2):
                    po, bpo = mm(pool)
                    for kc in range(8):
                        S.op("pe", lambda e, kc=kc, n=n, po=po: e.matmul(po[:, :], lhsT=yt[:, kc, j * 128:(j + 1) * 128], rhs=wo[:, kc, n * 512:(n + 1) * 512], start=(kc == 0), stop=(kc == 7)),
                             reads=[byt, b_wo], writes=[bpo])
                        if kc == 3:
                            yield
                    S.op("dve", lambda e, n=n, po=po: e.tensor_tensor(out=ob2[:], in0=po[:, :], in1=gate[:, v, n * 512:(n + 1) * 512], op=ALU.mult),
                         reads=[bpo, b_gate], writes=[b_ob2])
                    S.op("dve", lambda e, n=n: e.scalar_tensor_tensor(out=tU[ti][:, n * 512:(n + 1) * 512], in0=tU[ti][:, n * 512:(n + 1) * 512], scalar=ALPHA, in1=ob2[:], op0=ALU.mult, op1=ALU.add),
                         reads=[b_ob2, b_tU[ti]], writes=[b_tU[ti]])
                    yield
                for i in range(2):
                    S.op("dve", lambda e, i=i: e.bn_stats(out=stt[3][:, i, :], in_=tU[ti][:, i * 512:(i + 1) * 512]), reads=[b_tU[ti]], writes=[b_stt[3]])
                S.op("dve", lambda e: e.bn_aggr(out=mv[3][:], in_=stt[3][:].rearrange("p a b -> p (a b)")), reads=[b_stt[3]], writes=[b_mv[3]])
                ln_small(3)
                S.op("act", lambda e: e.activation(out=tU[ti][:], in_=tU[ti][:], func=AF.Identity, bias=nm[3][:], scale=rs[3][:]), reads=[b_tU[ti], b_nm[3], b_rs[3]], writes=[b_tU[ti]])
                S.op("pool", lambda e: e.tensor_tensor(out=tU[ti][:], in0=tU[ti][:], in1=lng[:], op=ALU.mult), reads=[b_tU[ti], b_lng], writes=[b_tU[ti]])
                S.op("pool", lambda e: e.tensor_tensor(out=tU[ti][:], in0=tU[ti][:], in1=lnb[:], op=ALU.add), reads=[b_tU[ti], b_lnb], writes=[b_tU[ti]])
                S.dma("sp", lambda e: e.dma_start(out=dst_x(grp, tile), in_=tU[ti][:]), reads=[b_tU[ti]], writes=([] if last else [xbuf(grp, u)]),
                      sem_key="xo%d" % ti, final=True)

            def ctx_chunks():
                out = []
                for a in range(2):
                    out.append((lambda pair, half, a=a: ckT[64 * half:64 * half + 64, pair, a * 128:(a + 1) * 128], b_ckT,
                                lambda h, a=a: cV[:, a, h, :], b_cV, None))
                return out

            nP = ns_units - l
            nR = ns_units - 1 - l

            def keyf_S(tile):
                out = []
                for (kt, tid) in attn_keys(tile):
                    ku, kj = kt // 2, kt % 2
                    ksl = ku % NK
                    out.append((lambda pair, half, ksl=ksl, kj=kj: KT[ksl][64 * half:64 * half + 64, pair, kj * 128:(kj + 1) * 128], b_KT[ksl],
                                lambda h, ksl=ksl, kj=kj: Vr[:, 2 * ksl + kj, h, :], b_V[2 * ksl + kj], tid))
                return ctx_chunks() + out

            def pipeline(grp, nP, nR, keyf_of, base, halo):
                def hs_(u): return (base + u) % NH
                def ks_(u): return (base + u) % NK
                def qs_(u): return (base + u) % 2
                def hprev(u): return hs_(u - 1) if (halo and u > 0) else None
                def hnext(u): return hs_(u + 1) if halo else None
                for u0 in range(min(3, nP)):
                    A_pre(grp, u0); A_pe(grp, u0, hs_(u0))
                B_kv(grp, 0, hs_(0), ks_(0)); B_q(grp, 0, hs_(0), qs_(0)); B_conv(grp, 0, hs_(0), qs_(0), None, hnext(0)); B_gm(grp, 0, hs_(0), qs_(0))
                pend = None
                for i in range(nP):
                    u1 = i + 1
                    full1 = u1 < nR
                    if i + 3 < nP:
                        xis_ = A_dma(grp, i + 3)
                    if u1 < nP:
                        B_kv(grp, u1, hs_(u1), ks_(u1))
                    if pend is not None:
                        C_o(*pend)
                        pend = None
                    if i + 3 < nP:
                        A_ln(grp, i + 3, xis_)
                    g_b = iter(())
                    if full1:
                        def g_b_f(u1=u1):
                            yield from B_q_gen(grp, u1, hs_(u1), qs_(u1), "G")
                            yield from B_conv_gen(grp, u1, hs_(u1), qs_(u1), hprev(u1), hnext(u1), "G")
                        g_b = g_b_f()
                    if i < nR:
                        kf = keyf_of(i)
                        ch0 = kf(2 * i)
                        zipper(C_s_gen(grp, i, qs_(i), ch0, 0, "S"), g_b, 1, 2)
                        C_pv(grp, i, qs_(i), ch0, 0)
                        ch1 = kf(2 * i + 1)
                        zipper(C_s_gen(grp, i, qs_(i), ch1, 1, "S"), C_o_gen(grp, i, qs_(i), 0, "G"), 2, 1)
                    else:
                        run(g_b)
                    if i + 3 < nP:
                        A_pe(grp, i + 3, hs_(i + 3))
                    if full1:
                        B_gm(grp, u1, hs_(u1), qs_(u1))
                    if i < nR:
                        C_pv(grp, i, qs_(i), ch1, 1)
                        pend = (grp, i, qs_(i), 1)
                if pend is not None:
                    C_o(*pend)

            def keyf_P_of(u):
                ks = (nP_s + u) % NK
                def keyf(tile):
                    out = []
                    for kj in range(2):
                        out.append((lambda pair, half, kj=kj: KT[ks][64 * half:64 * half + 64, pair, kj * 128:(kj + 1) * 128], b_KT[ks],
                                    lambda h, kj=kj: Vr[:, 2 * ks + kj, h, :], b_V[2 * ks + kj], None))
                    return out
                return keyf

            nP_s = nP
            if not SKIPS:
                pipeline("S", nP, nR, lambda u: keyf_S, 0, True)
            if not SKIPP:
                pipeline("P", n_pseq, n_pseq, keyf_P_of, nP_s, False)

        for l_ in range(depth):
            layer(l_)

        S.emit()
    return nc


def _tables(rpb, flip):
    reps = [(4, 2), (4, 3), (4, 4), (4, 5), (4, 6), (0, 2), (0, 3)]
    out = np.empty((DEPTH, 7, 128, 2, 4, 128), np.float32)
    p = np.arange(128)
    for tid, (t, u) in enumerate(reps):
        ql = t * 128 + p
        kl = u * 128 + p
        qg = 4095 - ql if flip else ql
        kg = 4095 - kl if flip else kl
        qr, qc = qg // 64, qg % 64
        kr, kc = kg // 64, kg % 64
        rs_ = np.clip(qr - 4, 0, 56)
        cs_ = np.clip(qc - 8, 0, 48)
        valid = ((kr[:, None] >= rs_[None, :]) & (kr[:, None] < rs_[None, :] + 8)
                 & (kc[:, None] >= cs_[None, :]) & (kc[:, None] < cs_[None, :] + 16))
        dr = np.clip(kr[:, None] - qr[None, :] + 7, 0, 14)
        dc = np.clip(kc[:, None] - qc[None, :], -15, 15) + 15
        g = rpb[:, :, dr, dc]
        g = np.where(valid[None, None], g, np.float32(NEG))
        g = g.transpose(0, 2, 1, 3).reshape(DEPTH, 128, 4, 2, 128).transpose(0, 1, 3, 2, 4)
        out[:, tid] = g
    return np.ascontiguousarray(out.reshape(DEPTH, 7, 128, 1024))


_NC_CACHE = {}


def kernel(x_prompt, x_sample, cache_k, cache_v, c, c_ctx, w_ada, b_ada, w_in, conv_w,
           gmlp_ln_g, gmlp_ln_b, w_spatial, b_spatial, rpb, w_out, ln_g, ln_b):
    f = lambda a: np.ascontiguousarray(np.asarray(a, dtype=np.float32))
    x_prompt, x_sample, cache_k, cache_v, c, c_ctx = map(f, (x_prompt, x_sample, cache_k, cache_v, c, c_ctx))
    w_ada, b_ada, w_in, conv_w, gmlp_ln_g, gmlp_ln_b = map(f, (w_ada, b_ada, w_in, conv_w, gmlp_ln_g, gmlp_ln_b))
    w_spatial, b_spatial, rpb, w_out, ln_g, ln_b = map(f, (w_spatial, b_spatial, rpb, w_out, ln_g, ln_b))
    if "nc" not in _NC_CACHE:
        _NC_CACHE["nc"] = build()
    nc = _NC_CACHE["nc"]
    ident = np.eye(128, dtype=np.float32)
    tabs = [_tables(rpb, 0), _tables(rpb, 1)]
    in_maps = []
    for i in range(8):
        b, flip = i // 2, i % 2
        xs = x_sample[b][::-1] if flip else x_sample[b]
        xp = x_prompt[4 * i:4 * i + 4]
        if flip:
            xp = xp[:, ::-1]
        ws = w_spatial[:, :, ::-1, ::-1] if flip else w_spatial
        bs = b_spatial[:, :, ::-1] if flip else b_spatial
        cwv = conv_w[:, ::-1, :] if flip else conv_w
        in_maps.append({
            "xp": f(xp.reshape(1024, D)),
            "xs": f(xs[0:3072]),
            "c2": f(np.stack([c[b], c_ctx])),
            "ckT": f(cache_k[b].reshape(DEPTH, 256, 512).transpose(0, 2, 1)),
            "cv": f(cache_v[b].reshape(DEPTH, 256, 512)),
            "w_ada": w_ada, "b_ada": b_ada, "w_in": w_in, "w_out": w_out,
            "conv_w": f(cwv), "gmlp_ln_g": gmlp_ln_g, "gmlp_ln_b": gmlp_ln_b,
            "wsT": f(ws.transpose(0, 3, 1, 2)), "bsT": f(bs.transpose(0, 2, 1)),
            "ebt": tabs[flip], "ln_g": ln_g, "ln_b": ln_b, "ident": ident,
        })
    res = run_bass_kernel_spmd(nc, in_maps, core_ids=list(range(8))).results
    y_prompt = np.empty((32, 256, D), np.float32)
    y_sample = np.empty((4, 4096, D), np.float32)
    new_k = np.empty((32, DEPTH, 256, 8, 64), np.float32)
    new_v = np.empty((32, DEPTH, 256, 8, 64), np.float32)
    for i in range(8):
        b, flip = i // 2, i % 2
        r = res[i]
        yp = np.asarray(r["yp"]).reshape(4, 256, D)
        nk = np.asarray(r["nk"]).reshape(4, DEPTH, 256, 8, 64)
        nv = np.asarray(r["nv"]).reshape(4, DEPTH, 256, 8, 64)
        ys = np.asarray(r["ys"])
        if flip:
            yp = yp[:, ::-1]
            nk = nk[:, :, ::-1]
            nv = nv[:, :, ::-1]
            y_sample[b, 2048:] = ys[::-1]
        else:
            y_sample[b, :2048] = ys
        y_prompt[4 * i:4 * i + 4] = yp
        new_k[4 * i:4 * i + 4] = nk
        new_v[4 * i:4 * i + 4] = nv
    return (y_prompt, y_sample, new_k, new_v)
```

```python
import contextlib
import numpy as np
import concourse.bass as bass
import concourse.mybir as mybir
from concourse.bass_utils import run_bass_kernel_spmd

F32 = mybir.dt.float32
BF16 = mybir.dt.bfloat16
ALU = mybir.AluOpType
AF = mybir.ActivationFunctionType

D = 1024
DEPTH = 4
ALPHA = (2 * DEPTH) ** 0.25
SCALE = 64 ** -0.5
NEG = -30000.0
ENGS = ("pe", "act", "dve", "pool", "sp")
MAXOPS = 10 ** 9
SCHEDULE = True
PRIO_W = 0.0
TOKEN_ALL = False
SKIPP = False
SKIPS = False


class Buf:
    __slots__ = ("name", "excl", "last_w", "readers")

    def __init__(self, name, excl=False):
        self.name = name
        self.excl = excl
        self.last_w = None
        self.readers = []


class Op:
    __slots__ = ("eng", "fn", "is_dma", "sem_key", "deps", "signal", "count", "dma_count", "idx", "reads", "writes",
                 "preds", "succs", "npred", "cost", "fin", "pos", "mode", "start", "prio")

    def __init__(self, eng, fn, is_dma, sem_key):
        self.eng = eng
        self.fn = fn
        self.is_dma = is_dma
        self.sem_key = sem_key
        self.deps = []
        self.signal = False
        self.count = 0
        self.dma_count = 0
        self.mode = 0


class _Rec:
    def __init__(self):
        self.name = None
        self.kw = {}
        self.args = ()

    def __getattr__(self, name):
        def f(*a, **k):
            self.name, self.args, self.kw = name, a, k
            return self
        return f


def _nfree(ap):
    n = 1
    for d in tuple(ap.shape)[1:]:
        n *= int(d)
    return n


def _is_psum_or_f32(ap):
    try:
        return ("psum" in str(ap.space).lower()) or (ap.dtype == F32)
    except Exception:
        return True


def _cost_ns(op):
    r = _Rec()
    try:
        op.fn(r)
    except Exception:
        return 300.0
    kw, a, nm_ = r.kw, r.args, r.name
    out = kw.get("out", a[0] if a else None)
    try:
        if op.is_dma:
            byts = _nfree(out) * int(tuple(out.shape)[0]) * 4
            return 2000.0 + byts / 200.0
        if op.eng == "pe":
            if nm_ == "transpose":
                return 75.0
            rhs = kw.get("rhs", a[2] if len(a) > 2 else None)
            lhsT = kw.get("lhsT", a[1] if len(a) > 1 else None)
            if int(tuple(lhsT.shape)[0]) <= 64:
                op.mode = 64
                return max(35.0, 10.0 + _nfree(rhs) * 0.3)
            return max(45.0, 30.0 + _nfree(rhs) * 0.45)
        n = _nfree(out)
        if op.eng == "act":
            return 200.0 + n / 1.15
        if op.eng == "pool":
            if nm_ == "memset":
                return 150.0 + n * 0.9
            return 250.0 + n * 2.2
        if nm_ == "reciprocal":
            return 1000.0
        if nm_ == "bn_aggr":
            return 200.0
        srcs = [kw.get(k) for k in ("in0", "in1", "in_") if kw.get(k) is not None]
        slow = any(_is_psum_or_f32(x) for x in srcs) or nm_ in ("bn_stats",)
        return 160.0 + n * (1.04 if slow else 0.55)
    except Exception:
        return 300.0


class Sched:
    def __init__(self, nc):
        self.nc = nc
        self.all = []
        self.ops = {e: [] for e in ENGS}
        self.dma_counts = {}
        self.dma_keys = []
        self.final_waits = []
        self.dve_token = Buf("dve_token")

    def _add(self, eng, fn, reads, writes, is_dma=False, sem_key=None):
        op = Op(eng, fn, is_dma, sem_key)
        if eng == "dve" and not is_dma:
            r_ = _Rec()
            try:
                fn(r_)
            except Exception:
                pass
            if TOKEN_ALL or r_.name not in ("tensor_tensor", "scalar_tensor_tensor"):
                reads = reads + [self.dve_token]
        op.reads, op.writes = reads, writes
        op.idx = len(self.all)
        self.all.append(op)
        if is_dma:
            if sem_key not in self.dma_counts:
                self.dma_counts[sem_key] = 0
                self.dma_keys.append(sem_key)
            self.dma_counts[sem_key] += 16
            op.dma_count = self.dma_counts[sem_key]
        return op

    def op(self, eng, fn, reads=(), writes=(), excl_dve=False):
        w = list(writes)
        if excl_dve:
            w.append(self.dve_token)
        return self._add(eng, fn, list(reads), w)

    def dma(self, eng, fn, reads=(), writes=(), sem_key=None, final=False):
        o = self._add(eng, fn, list(reads), list(writes), is_dma=True, sem_key=sem_key)
        if final:
            self.final_waits.append(o)
        return o

    def _edges(self):
        last_key = {}
        for op in self.all:
            preds = {}
            for b in op.reads:
                if b.last_w is not None:
                    preds[id(b.last_w)] = b.last_w
                if b.excl:
                    for r in b.readers:
                        if r.eng != op.eng:
                            preds[id(r)] = r
            for b in op.writes:
                if b.last_w is not None:
                    preds[id(b.last_w)] = b.last_w
                for r in b.readers:
                    preds[id(r)] = r
            if op.is_dma:
                p = last_key.get(op.sem_key)
                if p is not None:
                    preds[id(p)] = p
                last_key[op.sem_key] = op
            preds.pop(id(op), None)
            for b in op.reads:
                b.readers.append(op)
            for b in op.writes:
                b.last_w = op
                b.readers = []
            op.preds = list(preds.values())
            op.succs = []
        for op in self.all:
            for p in op.preds:
                p.succs.append(op)

    def _schedule(self):
        SEM = 120.0
        for op in self.all:
            op.cost = _cost_ns(op)
            op.npred = len(op.preds)
            op.fin = None
        rank = {}
        for op in reversed(self.all):
            r_ = 0.0
            for s_ in op.succs:
                r2 = rank[id(s_)]
                if r2 > r_:
                    r_ = r2
            rank[id(op)] = r_ + op.cost + SEM
        tot = max(rank.values())
        n_all = float(len(self.all))
        for op in self.all:
            op.prio = op.idx / n_all - PRIO_W * rank[id(op)] / tot
        cand = {e: [] for e in ENGS}
        ready = {}
        for op in self.all:
            if op.npred == 0:
                cand[op.eng].append(op)
                ready[id(op)] = 0.0
        free = {e: 0.0 for e in ENGS}
        order = {e: [] for e in ENGS}
        pe_mode = [0]
        left = len(self.all)
        while left:
            best = None
            for e in ENGS:
                cl = cand[e]
                if not cl:
                    continue
                t = free[e]
                pick = None
                if e == "pe":
                    for o in cl:
                        if ready[id(o)] <= t and o.mode == pe_mode[0]:
                            if pick is None or o.prio < pick.prio:
                                pick = o
                if pick is None:
                    for o in cl:
                        if ready[id(o)] <= t:
                            if pick is None or o.prio < pick.prio:
                                pick = o
                if pick is not None:
                    st_ = t
                else:
                    for o in cl:
                        rt = ready[id(o)]
                        if pick is None or rt < ready[id(pick)] or (rt == ready[id(pick)] and o.idx < pick.idx):
                            pick = o
                    st_ = ready[id(pick)]
                if best is None or st_ < best[0] or (st_ == best[0] and pick.idx < best[1].idx):
                    best = (st_, pick)
            st_, o = best
            e = o.eng
            cand[e].remove(o)
            if e == "pe":
                if o.mode != pe_mode[0]:
                    st_ += 120.0
                pe_mode[0] = o.mode
            if o.is_dma:
                free[e] = st_ + 60.0
            else:
                free[e] = st_ + o.cost
            o.fin = st_ + o.cost
            o.start = st_
            o.pos = len(order[e])
            order[e].append(o)
            left -= 1
            for s_ in o.succs:
                s_.npred -= 1
                if s_.npred == 0:
                    pe_s = (s_.eng == "pe" and not s_.is_dma)
                    ready[id(s_)] = max((p.start if (pe_s and p.eng == "pe" and not p.is_dma) else p.fin + SEM) for p in s_.preds)
                    cand[s_.eng].append(s_)
        self.model_ns = max(o.fin for o in self.all)
        return order

    def emit(self):
        nc = self.nc
        self._edges()
        if SCHEDULE:
            self.ops = self._schedule()
        else:
            self.ops = {e: [o for o in self.all if o.eng == e] for e in ENGS}
            for e in ENGS:
                for i, o in enumerate(self.ops[e]):
                    o.pos = i
        for e in ENGS:
            for o in self.ops[e]:
                lastp = {}
                dl = []
                for p in o.preds:
                    if p.is_dma:
                        dl.append(p)
                        continue
                    if p.eng == "pe" and o.eng == "pe" and not o.is_dma:
                        continue
                    q = lastp.get(p.eng)
                    if q is None or p.pos > q.pos:
                        lastp[p.eng] = p
                o.deps = dl + list(lastp.values())
                for p in lastp.values():
                    p.signal = True
        for e in ENGS:
            c = 0
            for o in self.ops[e]:
                if (not o.is_dma) and o.signal:
                    c += 1
                    o.count = c
        with contextlib.ExitStack() as st:
            esem = {e: st.enter_context(nc.semaphore("s_" + e)) for e in ENGS}
            dsem = {k: st.enter_context(nc.semaphore("d_" + str(k))) for k in self.dma_keys}
            block = st.enter_context(nc.Block())
            engobj = {"pe": "tensor", "act": "scalar", "dve": "vector", "pool": "gpsimd", "sp": "sync"}

            def make(e):
                def body(eng):
                    waited = {}
                    for o in self.ops[e]:
                        for d in o.deps:
                            if d.is_dma:
                                s, v, key = dsem[d.sem_key], d.dma_count, ("d", d.sem_key)
                            else:
                                s, v, key = esem[d.eng], d.count, ("e", d.eng)
                            if waited.get(key, 0) >= v:
                                continue
                            waited[key] = v
                            eng.wait_ge(s, v)
                        ins = o.fn(eng)
                        if o.is_dma:
                            ins.then_inc(dsem[o.sem_key], 16)
                        elif o.signal:
                            ins.then_inc(esem[e], 1)
                    if e == "sp":
                        fin = {}
                        for o in self.final_waits:
                            fin[o.sem_key] = max(fin.get(o.sem_key, 0), o.dma_count)
                        for k, v in fin.items():
                            if waited.get(("d", k), 0) >= v:
                                continue
                            eng.wait_ge(dsem[k], v)

                return body

            for e in ENGS:
                getattr(block, engobj[e])(make(e))


def attn_keys(t):
    if t >= 2:
        return [(t + d, d + 2) for d in (-2, -1, 0, 1, 2)]
    if t == 0:
        return [(0, 2), (1, 3), (2, 5), (3, 6)]
    return [(0, 1), (1, 2), (2, 3), (3, 5)]


def build(depth=DEPTH, n_pseq=4, ns_units=12):
    nc = bass.Bass("TRN2", target_bir_lowering=False)

    def din(name, shape):
        return nc.dram_tensor(name, shape, F32, kind="ExternalInput").ap()

    def dout(name, shape):
        return nc.dram_tensor(name, shape, F32, kind="ExternalOutput").ap()

    xp_d = din("xp", [n_pseq * 256, D])
    xs_d = din("xs", [3072, D])
    c2_d = din("c2", [2, D])
    ckT_d = din("ckT", [DEPTH, 512, 256])
    cv_d = din("cv", [DEPTH, 256, 512])
    wada_d = din("w_ada", [DEPTH, D, 3 * D])
    bada_d = din("b_ada", [DEPTH, 3 * D])
    win_d = din("w_in", [DEPTH, D, 3840])
    wout_d = din("w_out", [DEPTH, D, D])
    convw_d = din("conv_w", [DEPTH, 3, 256])
    glg_d = din("gmlp_ln_g", [DEPTH, 256])
    glb_d = din("gmlp_ln_b", [DEPTH, 256])
    wsT_d = din("wsT", [DEPTH, 128, 4, 128])
    bsT_d = din("bsT", [DEPTH, 128, 4])
    ebt_d = din("ebt", [DEPTH, 7, 128, 1024])
    lng_d = din("ln_g", [DEPTH, D])
    lnb_d = din("ln_b", [DEPTH, D])
    id_d = din("ident", [128, 128])
    yp_d = dout("yp", [n_pseq * 256, D])
    ys_d = dout("ys", [2048, D])
    nk_d = dout("nk", [n_pseq, DEPTH, 256, 512])
    nv_d = dout("nv", [n_pseq, DEPTH, 256, 512])
    sxp_d = nc.dram_tensor("sxp", [n_pseq * 256, D], F32).ap()
    sxs_d = nc.dram_tensor("sxs", [3072, D], F32).ap()

    S = Sched(nc)
    st = contextlib.ExitStack()
    with st:
        def sb(name, shape, dt=F32):
            return st.enter_context(nc.sbuf_tensor(name, shape, dt))

        def ps(name, shape, dt=F32):
            return st.enter_context(nc.psum_tensor(name, shape, dt))

        wF = sb("wF", [128, 8, 1536], BF16); b_wF = Buf("wF"); b_wFq = Buf("wFq")
        wT = sb("wT", [128, 8, 2304], BF16); b_wTk = Buf("wTk"); b_wTv = Buf("wTv"); b_wTvu = Buf("wTvu"); b_wTgb = Buf("wTgb"); b_wTgc = Buf("wTgc")
        wo = sb("wo", [128, 8, 1024], BF16); b_wo = Buf("wo")
        NH = 4
        hT = [sb("hT%d" % i, [128, 8, 256], BF16) for i in range(NH)]; b_hT = [Buf("hT%d" % i) for i in range(NH)]
        xA = [sb("xA%d" % i, [128, D]) for i in range(2)]; b_xA = [Buf("xA%d" % i) for i in range(2)]
        xn = [sb("xn%d" % i, [128, D], BF16) for i in range(2)]; b_xn = [Buf("xn%d" % i) for i in range(2)]
        QT = [sb("QT%d" % i, [128, 4, 256], BF16) for i in range(2)]; b_QT = [Buf("QT%d" % i) for i in range(2)]
        NK = 3
        KT = [sb("KT%d" % i, [128, 4, 256], BF16) for i in range(NK)]; b_KT = [Buf("KT%d" % i) for i in range(NK)]
        Vr = sb("Vr", [128, 2 * NK, 8, 65], BF16); b_V = [Buf("V%d" % i) for i in range(2 * NK)]
        yT = [sb("yT%d" % i, [128, 8, 256], BF16) for i in range(2)]; b_yT = [Buf("yT%d" % i) for i in range(2)]
        sgc = [sb("sgc%d" % i, [128, 2, 512], BF16) for i in range(2)]; b_sgc = [Buf("sgc%d" % i) for i in range(2)]
        ktok = sb("ktok", [128, 512], BF16); b_ktok = Buf("ktok")
        ckT = sb("ckT_sb", [128, 4, 256], BF16); b_ckT = Buf("ckT")
        cV = sb("cV", [128, 2, 8, 65], BF16); b_cV = Buf("cV")
        EB = sb("EB", [128, 7, 1024], BF16); b_EB = Buf("EB")
        tU = [sb("tU%d" % i, [128, D]) for i in range(2)]; b_tU = [Buf("tU%d" % i) for i in range(2)]
        Eb = sb("Eb", [128, 14, 512], BF16); b_E = [Buf("E%d" % i) for i in range(14)]
        c2f = sb("c2f", [128, 2, 8]); b_c2f = Buf("c2f")
        c2t = sb("c2t", [128, 2, 8]); b_c2t = Buf("c2t")
        scT = sb("scT", [128, 8, 2], BF16); b_scT = Buf("scT")
        sc1p = sb("sc1p", [128, 2, 8]); b_sc1p = Buf("sc1p")
        shp = sb("shp", [128, 2, 8]); b_shp = Buf("shp")
        badc = sb("badc", [128, 16]); b_badc = Buf("badc")
        gate = sb("gate", [128, 2, D]); b_gate = Buf("gate")
        lng = sb("lng", [128, D]); b_lng = Buf("lng")
        lnb = sb("lnb", [128, D]); b_lnb = Buf("lnb")
        glg = sb("glg", [128, 256]); b_glg = Buf("glg")
        glb = sb("glb", [128, 256]); b_glb = Buf("glb")
        bsT = sb("bsT_sb", [128, 4]); b_bsT = Buf("bsT")
        bsb = sb("bsb", [128, 256]); b_bsb = Buf("bsb")
        rsw = sb("rsw", [128, 4]); b_rsw = Buf("rsw")
        ones1 = sb("ones1", [128, 2], BF16); b_ones1 = Buf("ones1")
        wsT = sb("wsT_sb", [128, 4, 128], BF16); b_wsT = Buf("wsT")
        cw = sb("cw", [128, 2, 3]); b_cw = Buf("cw")
        ident = sb("ident_sb", [128, 128], BF16); b_id = Buf("ident")
        nh = sb("nh", [128, 2]); b_nh = Buf("nh")
        stt = [sb("stt%d" % i, [128, 2, 6]) for i in range(4)]
        mv = [sb("mv%d" % i, [128, 2]) for i in range(4)]
        ve = [sb("ve%d" % i, [128, 1]) for i in range(4)]
        rs = [sb("rs%d" % i, [128, 1]) for i in range(4)]
        nm = [sb("nm%d" % i, [128, 1]) for i in range(4)]
        b_stt = [Buf("stt%d" % i) for i in range(4)]
        b_mv = [Buf("mv%d" % i) for i in range(4)]
        b_ve = [Buf("ve%d" % i) for i in range(4)]
        b_rs = [Buf("rs%d" % i) for i in range(4)]
        b_nm = [Buf("nm%d" % i) for i in range(4)]
        wkall = sb("wkall", [128, 4, 256])
        wk = {}
        for i_, nme in enumerate(("xa", "acc", "tg", "s2")):
            wk[nme] = (wkall[:, i_, :], Buf("wk_" + nme))
        wk["m"] = wk["xa"]
        tgc = wkall[:, 2:4, :].rearrange("p a t -> p (a t)")
        b_tgc_l = [wk["tg"][1], wk["s2"][1]]
        zb = sb("zb", [128, 258]); b_zb = Buf("zb")
        xh = sb("xh", [128, 4]); b_xh = Buf("xh")
        zh = sb("zh", [128, 4]); b_zh = Buf("zh")
        vn3 = sb("vn3", [128, 256], BF16); b_vn3 = Buf("vn3")
        ybt = sb("ybt", [128, 256], BF16); b_ybt = Buf("ybt")
        rinv = sb("rinv", [128, 8]); b_rinv = Buf("rinv")
        ob = sb("ob", [128, 512]); b_ob = Buf("ob")
        ob2 = sb("ob2", [128, 512]); b_ob2 = Buf("ob2")
        kst = ob2; b_kst = b_ob2
        vst = ob; b_vst = b_ob
        yct = sb("yct", [128, 512], BF16); b_yct = Buf("yct")

        NMM = 6
        mmb = [ps("mm%d" % i, [128, 512]) for i in range(NMM)]; b_mm = [Buf("mm%d" % i, True) for i in range(NMM)]
        pv = [ps("pv%d" % i, [128, 512]) for i in range(2)]; b_pv = [Buf("pv0", True), Buf("pv1", True)]
        mm_i = [0]

        pool_banks = {None: [0, 1, 2, 3, 4, 5], "S": [2, 3, 4, 5], "G": [0, 1]}
        pool_i = {None: 0, "S": 0, "G": 0}

        def mm(pool=None):
            bl = pool_banks[pool]
            i = bl[pool_i[pool] % len(bl)]
            pool_i[pool] += 1
            return mmb[i], b_mm[i]

        def mmt(pool=None):
            t, b = mm(pool)
            return t[:].bitcast(BF16), b

        def ln_small(site, eng_after="dve"):
            S.op("dve", lambda e: e.tensor_scalar(out=ve[site][:], in0=mv[site][:, 1:2], scalar1=1e-5, scalar2=None, op0=ALU.add),
                 reads=[b_mv[site]], writes=[b_ve[site]])
            S.op("pool", lambda e: e.tensor_tensor(out=rs[site][:], in0=ve[site][:], in1=nh[:, 0:1], op=ALU.pow),
                 reads=[b_ve[site], b_nh], writes=[b_rs[site]], excl_dve=True)
            S.op("dve", lambda e: e.scalar_tensor_tensor(out=nm[site][:], in0=mv[site][:, 0:1], scalar=-1.0, in1=rs[site][:], op0=ALU.mult, op1=ALU.mult),
                 reads=[b_mv[site], b_rs[site]], writes=[b_nm[site]])

        S.dma("pool", lambda e: e.dma_start(out=ident[:], in_=id_d), writes=[b_id], sem_key="c_id")
        S.op("pool", lambda e: e.memset(nh[:], -0.5), writes=[b_nh])
        S.op("pool", lambda e: e.memset(ones1[:], 1.0), writes=[b_ones1])
        S.op("pool", lambda e: e.memset(Vr[:].rearrange("p a h d -> p (a h d)"), 1.0), writes=b_V)
        S.op("pool", lambda e: e.memset(cV[:].rearrange("p a h d -> p (a h d)"), 1.0), writes=[b_cV])
        for v in range(2):
            S.dma("sp", lambda e, v=v: e.dma_start(out=c2f[:, v, :], in_=c2_d[v, :].rearrange("(k p) -> p k", p=128), allow_slow_non_contiguous=True), writes=[b_c2f], sem_key="c_c2")
        S.op("act", lambda e: e.activation(out=c2t[:], in_=c2f[:], func=AF.Tanh, scale=0.5), reads=[b_c2f], writes=[b_c2t])
        S.op("dve", lambda e: e.scalar_tensor_tensor(out=c2t[:], in0=c2t[:], scalar=1.0, in1=c2f[:], op0=ALU.add, op1=ALU.mult), reads=[b_c2f, b_c2t], writes=[b_c2t])
        S.op("dve", lambda e: e.tensor_scalar(out=scT[:].rearrange("p k v -> p v k"), in0=c2t[:], scalar1=0.5, scalar2=None, op0=ALU.mult), reads=[b_c2t], writes=[b_scT])
        b_sxs = [Buf("sxs%d" % u) for u in range(12)]
        b_sxp = [Buf("sxp%d" % u) for u in range(n_pseq)]

        def layer(l):
            last = (l == depth - 1)
            def wdma(dst, src, bufd, key):
                S.dma("pool", lambda e: e.dma_start(out=dst, in_=src.rearrange("(k p) n -> p k n", p=128)), writes=[bufd], sem_key=key)
            wdma(wT[:, :, 1792:2304], win_d[l][:, 2304:2816], b_wTk, "w_Tk")
            wdma(wT[:, :, 768:1280], win_d[l][:, 2816:3328], b_wTv, "w_Tv")
            for c_ in range(2):
                S.dma("sp", lambda e, c_=c_: e.dma_start(out=cw[:, c_, :], in_=convw_d[l][:, c_ * 128:(c_ + 1) * 128].rearrange("j p -> p j"), allow_slow_non_contiguous=True), writes=[b_cw], sem_key="p_cw")
            S.dma("sp", lambda e: e.dma_start(out=glg[:], in_=glg_d[l, :].partition_broadcast(128)), writes=[b_glg], sem_key="p_glg")
            S.dma("sp", lambda e: e.dma_start(out=glb[:], in_=glb_d[l, :].partition_broadcast(128)), writes=[b_glb], sem_key="p_glb")
            S.dma("sp", lambda e: e.dma_start(out=lng[:], in_=lng_d[l, :].partition_broadcast(128)), writes=[b_lng], sem_key="p_lng")
            S.dma("sp", lambda e: e.dma_start(out=lnb[:], in_=lnb_d[l, :].partition_broadcast(128)), writes=[b_lnb], sem_key="p_lnb")
            S.dma("sp", lambda e: e.dma_start(out=bsT[:], in_=bsT_d[l]), writes=[b_bsT], sem_key="p_bsT")
            S.op("dve", lambda e: e.tensor_copy(out=bsb[:].rearrange("p (g c) -> p g c", g=4), in_=bsT[:, :].unsqueeze(2).to_broadcast([128, 4, 64])), reads=[b_bsT], writes=[b_bsb])
            S.dma("pool", lambda e: e.dma_start(out=wsT[:], in_=wsT_d[l]), writes=[b_wsT], sem_key="p_wsT")
            prs, bprs = mm()
            for g in range(4):
                S.op("pe", lambda e, g=g, prs=prs: e.matmul(prs[:, g:g + 1], lhsT=wsT[:, g, :], rhs=ones1[:, 0:1], start=True, stop=True), reads=[b_wsT, b_ones1], writes=[bprs])
            S.op("dve", lambda e, prs=prs: e.tensor_copy(out=rsw[:], in_=prs[:, 0:4]), reads=[bprs], writes=[b_rsw])
            S.op("dve", lambda e: e.tensor_tensor(out=glb[:].rearrange("p (g c) -> p g c", g=4), in0=glb[:].rearrange("p (g c) -> p g c", g=4), in1=rsw[:, :].unsqueeze(2).to_broadcast([128, 4, 64]), op=ALU.mult),
                 reads=[b_glb, b_rsw], writes=[b_glb])
            S.op("dve", lambda e: e.tensor_tensor(out=bsb[:], in0=bsb[:], in1=glb[:], op=ALU.add), reads=[b_bsb, b_glb], writes=[b_bsb])
            S.dma("sp", lambda e: e.dma_start(out=badc[:], in_=bada_d[l, 0:2048].rearrange("(j p) -> p j", p=128), allow_slow_non_contiguous=True), writes=[b_badc], sem_key="p_badc")
            wad = Eb[:, 0:8, :]
            Ssil = Eb[:, 8:12, :].rearrange("p (v a) (b t) -> p v (a b) t", v=2, b=4)
            b_Ssil_l = b_E[8:12]
            for v in range(2):
                S.op("dve", lambda e, v=v: e.tensor_copy(out=Ssil[:, v], in_=scT[:, :, v:v + 1].to_broadcast([128, 8, 128])), reads=[b_scT], writes=b_Ssil_l)
            for ch in range(6):
                S.dma("pool", lambda e, ch=ch: e.dma_start(out=wad, in_=wada_d[l][:, ch * 512:(ch + 1) * 512].rearrange("(k p) n -> p k n", p=128)),
                      writes=b_E[0:8], sem_key="w_ada")
                if ch < 4:
                    pt, bpt = mm()
                    for jb in range(4):
                        for kc in range(8):
                            S.op("pe", lambda e, jb=jb, kc=kc, pt=pt: e.matmul(pt[:, jb * 2:jb * 2 + 2], lhsT=wad[:, kc, jb * 128:(jb + 1) * 128], rhs=scT[:, kc, :], start=(kc == 0), stop=(kc == 7)),
                                 reads=b_E[0:8] + [b_scT], writes=[bpt])
                    for v in range(2):
                        dst = (shp if ch < 2 else sc1p)
                        bd = (b_shp if ch < 2 else b_sc1p)
                        j0 = (ch % 2) * 4
                        S.op("dve", lambda e, v=v, dst=dst, j0=j0, pt=pt, ch=ch: e.scalar_tensor_tensor(
                            out=dst[:, v, j0:j0 + 4], in0=pt[:, v:8:2], scalar=(0.0 if ch < 2 else 1.0), in1=badc[:, ch * 4:ch * 4 + 4], op0=ALU.add, op1=ALU.add),
                            reads=[bpt, b_badc], writes=[bd])
                else:
                    half = ch - 4
                    S.dma("sp", lambda e, half=half: e.dma_start(out=tU[0][:, 0:512], in_=bada_d[l, 2048 + half * 512:2048 + (half + 1) * 512].partition_broadcast(128)), writes=[b_tU[0]], sem_key="p_bg")
                    for v in range(2):
                        pt, bpt = mm()
                        for kc in range(8):
                            S.op("pe", lambda e, v=v, kc=kc, pt=pt: e.matmul(pt[:, :], lhsT=Ssil[:, v, kc, :], rhs=wad[:, kc, :], start=(kc == 0), stop=(kc == 7)),
                                 reads=b_E[0:8] + b_Ssil_l, writes=[bpt])
                        S.op("dve", lambda e, v=v, pt=pt, half=half: e.tensor_tensor(out=gate[:, v, half * 512:(half + 1) * 512], in0=pt[:, :], in1=tU[0][:, 0:512], op=ALU.add),
                             reads=[bpt, b_tU[0]], writes=[b_gate])
            wdma(wF[:, :, 1024:1536], win_d[l][:, 1792:2304], b_wFq, "w_Fq")
            wdma(wF[:, :, 0:1024], win_d[l][:, 0:1024], b_wF, "w_F")
            wdma(wT[:, :, 0:256], win_d[l][:, 1280:1536], b_wTvu, "w_Tvu")
            wdma(wT[:, :, 256:512], win_d[l][:, 1024:1280], b_wTvu, "w_Tvu2")
            wdma(wT[:, :, 1280:1792], win_d[l][:, 3328:3840], b_wTgc, "w_Tgc")
            wdma(wT[:, :, 512:768], win_d[l][:, 1536:1792], b_wTgb, "w_Tgb")
            wdma(wo[:], wout_d[l], b_wo, "w_o")
            S.dma("pool", lambda e: e.dma_start(out=ckT[:], in_=ckT_d[l].rearrange("(a p) k -> p a k", p=128)), writes=[b_ckT], sem_key="c_k")
            for a in range(2):
                S.dma("pool", lambda e, a=a: e.dma_start(out=cV[:, a, :, 0:64], in_=cv_d[l][a * 128:(a + 1) * 128, :].rearrange("p (h d) -> p h d", d=64)), writes=[b_cV], sem_key="c_v")
            for tid in range(7):
                S.dma("sp", lambda e, tid=tid: e.dma_start(out=tU[tid % 2][:], in_=ebt_d[l, tid]), writes=[b_tU[tid % 2]], sem_key="p_eb%d" % (tid % 2))
                S.op("act", lambda e, tid=tid: e.activation(out=EB[:, tid, :], in_=tU[tid % 2][:], func=AF.Exp), reads=[b_tU[tid % 2]], writes=[b_EB])

            def src_x(grp, tile):
                if l == 0:
                    return (xs_d if grp == "S" else xp_d)[tile * 128:(tile + 1) * 128, :]
                return (sxs_d if grp == "S" else sxp_d)[tile * 128:(tile + 1) * 128, :]

            def dst_x(grp, tile):
                if last:
                    return (ys_d if grp == "S" else yp_d)[tile * 128:(tile + 1) * 128, :]
                return (sxs_d if grp == "S" else sxp_d)[tile * 128:(tile + 1) * 128, :]

            def xbuf(grp, u):
                return (b_sxs if grp == "S" else b_sxp)[u]

            cnt = {"xA": 0, "tU": 0}

            def A_dma(grp, u):
                xis = []
                for j in range(2):
                    tile = 2 * u + j
                    xi = cnt["xA"] % 2
                    cnt["xA"] += 1
                    rd = [xbuf(grp, u)] if l > 0 else []
                    S.dma("sp", lambda e, xi=xi, tile=tile: e.dma_start(out=xA[xi][:], in_=src_x(grp, tile)), reads=rd, writes=[b_xA[xi]], sem_key="xA%d" % xi)
                    xis.append(xi)
                return xis

            def A_ln(grp, u, xis):
                for j in range(2):
                    xi = xis[j]
                    for i in range(2):
                        S.op("dve", lambda e, i=i, xi=xi, j=j: e.bn_stats(out=stt[j][:, i, :], in_=xA[xi][:, i * 512:(i + 1) * 512]), reads=[b_xA[xi]], writes=[b_stt[j]])
                    S.op("dve", lambda e, j=j: e.bn_aggr(out=mv[j][:], in_=stt[j][:].rearrange("p a b -> p (a b)")), reads=[b_stt[j]], writes=[b_mv[j]])
                    ln_small(j)
                    S.op("act", lambda e, xi=xi, j=j: e.activation(out=xn[j][:], in_=xA[xi][:], func=AF.Identity, bias=nm[j][:], scale=rs[j][:]),
                         reads=[b_xA[xi], b_nm[j], b_rs[j]], writes=[b_xn[j]])

            def A_pre(grp, u):
                A_ln(grp, u, A_dma(grp, u))

            def A_pe(grp, u, hs):
                v = 0 if grp == "S" else 1
                for j in range(2):
                    tr, b_tr = mmt()
                    for kc in range(8):
                        S.op("pe", lambda e, kc=kc, j=j, tr=tr: e.transpose(out=tr[:, kc * 128:(kc + 1) * 128], in_=xn[j][:, kc * 128:(kc + 1) * 128], identity=ident[:]),
                             reads=[b_xn[j], b_id], writes=[b_tr])
                    for kc in range(8):
                        if kc % 2 == 0:
                            S.op("dve", lambda e, kc=kc, j=j, tr=tr: e.tensor_scalar(out=hT[hs][:, kc, j * 128:(j + 1) * 128], in0=tr[:, kc * 128:(kc + 1) * 128],
                                                                                      scalar1=sc1p[:, v, kc:kc + 1], scalar2=shp[:, v, kc:kc + 1], op0=ALU.mult, op1=ALU.add),
                                 reads=[b_tr, b_sc1p, b_shp], writes=[b_hT[hs]])
                    for kc in range(8):
                        if kc % 2 == 1:
                            S.op("act", lambda e, kc=kc, j=j, tr=tr: e.activation(out=hT[hs][:, kc, j * 128:(j + 1) * 128], in_=tr[:, kc * 128:(kc + 1) * 128], func=AF.Identity,
                                                                                    scale=sc1p[:, v, kc:kc + 1], bias=shp[:, v, kc:kc + 1]),
                                 reads=[b_tr, b_sc1p, b_shp], writes=[b_hT[hs]])

            def proj_T(hs, j, c0, c1):
                pt, bpt = mm()
                n = c1 - c0
                for kc in range(8):
                    S.op("pe", lambda e, kc=kc, pt=pt: e.matmul(pt[:, 0:n], lhsT=hT[hs][:, kc, j * 128:(j + 1) * 128], rhs=wT[:, kc, c0:c1], start=(kc == 0), stop=(kc == 7)),
                         reads=[b_hT[hs], {1792: b_wTk, 768: b_wTv, 0: b_wTvu, 512: b_wTgb, 1280: b_wTgc}[c0]], writes=[bpt])
                return pt, bpt

            def proj_F(hs, pt, bpt, off, cb):
                for kc in range(8):
                    S.op("pe", lambda e, kc=kc: e.matmul(pt[:, off:off + 256], lhsT=wF[:, kc, cb * 128:(cb + 1) * 128], rhs=hT[hs][:, kc, :], start=(kc == 0), stop=(kc == 7)),
                         reads=[b_hT[hs], (b_wFq if cb >= 8 else b_wF)], writes=[bpt])

            def B_kv(grp, u, hs, ks):
                with_out = (grp == "P")
                for j in range(2):
                    vslot = 2 * ks + j
                    pt, bpt = proj_T(hs, j, 1792, 2304)
                    S.op("act", lambda e, pt=pt: e.activation(out=ktok[:], in_=pt[:, :], func=AF.Copy), reads=[bpt], writes=[b_ktok])
                    if with_out:
                        S.op("dve", lambda e, pt=pt: e.tensor_copy(out=kst[:], in_=pt[:, :]), reads=[bpt], writes=[b_kst])
                        S.dma("sp", lambda e, j=j: e.dma_start(out=nk_d[u, l, j * 128:(j + 1) * 128, :], in_=kst[:]), reads=[b_kst], sem_key="o_k", final=True)
                    tr, b_tr = mmt()
                    for a in range(4):
                        S.op("pe", lambda e, a=a, tr=tr: e.transpose(out=tr[:, a * 128:(a + 1) * 128], in_=ktok[:, a * 128:(a + 1) * 128], identity=ident[:]),
                             reads=[b_ktok, b_id], writes=[b_tr])
                    S.op("dve", lambda e, j=j, tr=tr: e.tensor_copy(out=KT[ks][:, :, j * 128:(j + 1) * 128], in_=tr[:, 0:512].rearrange("p (a t) -> p a t", a=4)),
                         reads=[b_tr], writes=[b_KT[ks]])
                    pt, bpt = proj_T(hs, j, 768, 1280)
                    S.op("act", lambda e, pt=pt, vslot=vslot: e.activation(out=Vr[:, vslot, :, 0:64], in_=pt[:, :].rearrange("p (h d) -> p h d", d=64), func=AF.Copy),
                         reads=[bpt], writes=[b_V[vslot]])
                    if with_out:
                        S.op("dve", lambda e, pt=pt: e.tensor_copy(out=vst[:], in_=pt[:, :]), reads=[bpt], writes=[b_vst])
                        S.dma("sp", lambda e, j=j: e.dma_start(out=nv_d[u, l, j * 128:(j + 1) * 128, :], in_=vst[:]), reads=[b_vst], sem_key="o_v", final=True)

            def proj_F_g(hs, pt, bpt, off, cb):
                proj_F(hs, pt, bpt, off, cb)
                yield

            def B_q_gen(grp, u, hs, qs, pool=None):
                for half in range(2):
                    pq, bpq = mm(pool)
                    yield from proj_F_g(hs, pq, bpq, 0, 8 + 2 * half)
                    yield from proj_F_g(hs, pq, bpq, 256, 9 + 2 * half)
                    S.op("act", lambda e, pq=pq, half=half: e.activation(out=QT[qs][:, 2 * half:2 * half + 2, :], in_=pq[:, :].rearrange("p (a t) -> p a t", a=2), func=AF.Copy),
                         reads=[bpq], writes=[b_QT[qs]])

            def run(gen):
                for _ in gen:
                    pass

            def zipper(ga, gb, na=1, nb=2):
                da = db = False
                while not (da and db):
                    for _ in range(na):
                        if not da:
                            try:
                                next(ga)
                            except StopIteration:
                                da = True
                    for _ in range(nb):
                        if not db:
                            try:
                                next(gb)
                            except StopIteration:
                                db = True

            def B_q(grp, u, hs, qs):
                run(B_q_gen(grp, u, hs, qs))

            def B_conv_gen(grp, u, hs, qs, hs_prev, hs_next, pool=None):
                yt, byt = yT[qs], b_yT[qs]
                have_h = [hs_prev is not None, hs_next is not None]
                hal = pv[1][:, 384:392]
                b_halo = b_pv[1]
                for cbi, cb in enumerate((0, 1, 4, 5)):
                    for side in range(2):
                        if not have_h[side]:
                            continue
                        hsrc = hT[hs_prev][:, :, 255:256] if side == 0 else hT[hs_next][:, :, 0:1]
                        bsrc = b_hT[hs_prev] if side == 0 else b_hT[hs_next]
                        for kc in range(8):
                            S.op("pe", lambda e, kc=kc, cb=cb, hsrc=hsrc, col=cbi * 2 + side: e.matmul(hal[:, col:col + 1], lhsT=wF[:, kc, cb * 128:(cb + 1) * 128], rhs=hsrc[:, kc, :], start=(kc == 0), stop=(kc == 7)),
                                 reads=[bsrc, b_wF], writes=[b_halo])
                    yield
                S.op("pool", lambda e: e.memset(zh[:], 0.0), writes=[b_zh])
                for side in range(2):
                    if have_h[side]:
                        S.op("dve", lambda e, side=side: e.tensor_copy(out=xh[:, side:4:2], in_=hal[:, side:4:2]), reads=[b_halo], writes=[b_xh])
                        S.op("dve", lambda e, side=side: e.tensor_tensor(out=zh[:, side:4:2], in0=hal[:, 4 + side:8:2], in1=xh[:, side:4:2], op=ALU.mult), reads=[b_halo, b_xh], writes=[b_zh])
                for c in range(2):
                    p1, bp1 = mm(pool)
                    yield from proj_F_g(hs, p1, bp1, 0, 0 + c)
                    yield from proj_F_g(hs, p1, bp1, 256, 4 + c)
                    p2, bp2 = mm(pool)
                    yield from proj_F_g(hs, p2, bp2, 0, 2 + c)
                    yield from proj_F_g(hs, p2, bp2, 256, 6 + c)
                    xa_t, bxa = wk["xa"]; acc, bacc = wk["acc"]; tg, btg = wk["tg"]; s2, bs2 = wk["s2"]; m_, bm = wk["m"]
                    S.op("act", lambda e, p1=p1: e.activation(out=xa_t[:], in_=p1[:, 0:256], func=AF.Copy), reads=[bp1], writes=[bxa])
                    S.op("dve", lambda e, p1=p1: e.tensor_tensor(out=zb[:, 1:257], in0=p1[:, 256:512], in1=xa_t[:], op=ALU.mult), reads=[bp1, bxa], writes=[b_zb])
                    S.op("dve", lambda e, c=c: e.tensor_copy(out=zb[:, 0:258:257], in_=zh[:, 2 * c:2 * c + 2]), reads=[b_zh], writes=[b_zb])
                    S.op("act", lambda e, c=c: e.activation(out=acc[:], in_=zb[:, 1:257], func=AF.Identity, scale=cw[:, c, 1:2]), reads=[b_zb, b_cw], writes=[bacc])
                    S.op("dve", lambda e, c=c: e.scalar_tensor_tensor(out=acc[:], in0=zb[:, 0:256], scalar=cw[:, c, 0:1], in1=acc[:], op0=ALU.mult, op1=ALU.add), reads=[b_zb, b_cw, bacc], writes=[bacc])
                    S.op("dve", lambda e, c=c: e.scalar_tensor_tensor(out=acc[:], in0=zb[:, 2:258], scalar=cw[:, c, 2:3], in1=acc[:], op0=ALU.mult, op1=ALU.add), reads=[b_zb, b_cw, bacc], writes=[bacc])
                    S.op("act", lambda e, p2=p2: e.activation(out=tg[:], in_=p2[:, 256:512], func=AF.Tanh, scale=0.5), reads=[bp2], writes=[btg])
                    S.op("dve", lambda e, p2=p2: e.scalar_tensor_tensor(out=s2[:], in0=tg[:], scalar=1.0, in1=p2[:, 256:512], op0=ALU.add, op1=ALU.mult), reads=[btg, bp2], writes=[bs2])
                    S.op("dve", lambda e, p2=p2: e.tensor_tensor(out=m_[:], in0=p2[:, 0:256], in1=acc[:], op=ALU.mult), reads=[bp2, bacc], writes=[bm])
                    S.op("dve", lambda e, c=c: e.scalar_tensor_tensor(out=yt[:, c, :], in0=m_[:], scalar=0.5, in1=s2[:], op0=ALU.mult, op1=ALU.mult), reads=[bm, bs2], writes=[byt])
                    yield

            def B_conv(grp, u, hs, qs, hs_prev, hs_next):
                run(B_conv_gen(grp, u, hs, qs, hs_prev, hs_next))

            def B_gm(grp, u, hs, qs):
                yt, byt = yT[qs], b_yT[qs]
                for j in range(2):
                    xa_t, bxa = wk["xa"]; acc, bacc = wk["acc"]; tg, btg = wk["tg"]; s2, bs2 = wk["s2"]; m_, bm = wk["m"]
                    pvu, bpvu = proj_T(hs, j, 0, 512)
                    S.op("dve", lambda e, pvu=pvu: e.bn_stats(out=stt[2][:, 0, :], in_=pvu[:, 0:256]), reads=[bpvu], writes=[b_stt[2]])
                    S.op("dve", lambda e: e.bn_aggr(out=mv[2][:], in_=stt[2][:, 0, :]), reads=[b_stt[2]], writes=[b_mv[2]])
                    ln_small(2)
                    pgc, bpgc = proj_T(hs, j, 1280, 1792)
                    S.op("act", lambda e, pgc=pgc: e.activation(out=tgc[:], in_=pgc[:, :], func=AF.Tanh, scale=0.5), reads=[bpgc], writes=b_tgc_l)
                    S.op("dve", lambda e, pgc=pgc, j=j: e.scalar_tensor_tensor(out=sgc[qs][:, j, :], in0=tgc[:], scalar=1.0, in1=pgc[:, :], op0=ALU.add, op1=ALU.mult), reads=b_tgc_l + [bpgc], writes=[b_sgc[qs]])
                    pgb, bpgb = proj_T(hs, j, 512, 768)
                    S.op("act", lambda e, pvu=pvu: e.activation(out=vn3[:], in_=pvu[:, 0:256], func=AF.Identity, bias=nm[2][:], scale=rs[2][:]), reads=[bpvu, b_nm[2], b_rs[2]], writes=[b_vn3])
                    psv, bpsv = mm()
                    for g in range(4):
                        S.op("pe", lambda e, g=g, psv=psv: e.matmul(psv[:, g * 64:(g + 1) * 64], lhsT=wsT[:, g, :], rhs=vn3[:, g * 64:(g + 1) * 64], start=True, stop=True),
                             reads=[b_wsT, b_vn3], writes=[bpsv])
                    S.op("dve", lambda e, psv=psv: e.tensor_tensor(out=acc[:], in0=psv[:, 0:256], in1=glg[:], op=ALU.mult), reads=[bpsv, b_glg], writes=[bacc])
                    S.op("dve", lambda e: e.tensor_tensor(out=acc[:], in0=acc[:], in1=bsb[:], op=ALU.add), reads=[bacc, b_bsb], writes=[bacc])
                    S.op("dve", lambda e, pvu=pvu: e.tensor_tensor(out=m_[:], in0=pvu[:, 256:512], in1=acc[:], op=ALU.mult), reads=[bpvu, bacc], writes=[bm])
                    S.op("act", lambda e, pgb=pgb: e.activation(out=tg[:], in_=pgb[:, 0:256], func=AF.Tanh, scale=0.5), reads=[bpgb], writes=[btg])
                    S.op("dve", lambda e, pgb=pgb: e.scalar_tensor_tensor(out=s2[:], in0=tg[:], scalar=1.0, in1=pgb[:, 0:256], op0=ALU.add, op1=ALU.mult), reads=[btg, bpgb], writes=[bs2])
                    S.op("dve", lambda e: e.scalar_tensor_tensor(out=ybt[:], in0=m_[:], scalar=0.5, in1=s2[:], op0=ALU.mult, op1=ALU.mult), reads=[bm, bs2], writes=[b_ybt])
                    tr, b_tr = mmt()
                    for a in range(2):
                        S.op("pe", lambda e, a=a, tr=tr: e.transpose(out=tr[:, a * 128:(a + 1) * 128], in_=ybt[:, a * 128:(a + 1) * 128], identity=ident[:]), reads=[b_ybt, b_id], writes=[b_tr])
                    S.op("act", lambda e, j=j, tr=tr: e.activation(out=yt[:, 2:4, j * 128:(j + 1) * 128], in_=tr[:, 0:256].rearrange("p (a t) -> p a t", a=2), func=AF.Copy), reads=[b_tr], writes=[byt])

            def C_s_gen(grp, u, qs, chunks, j, pool=None):
                for ci, (kf, kb, vf, vb, tid) in enumerate(chunks):
                    SE, b_SE = mm(pool)
                    SO, b_SO = mm(pool)
                    for h in range(8):
                        pair, half = h // 2, h % 2
                        bank, bbank = (SE, b_SE) if half == 0 else (SO, b_SO)
                        S.op("pe", lambda e, kf=kf, pair=pair, half=half, bank=bank: e.matmul(
                            bank[:, pair * 128:(pair + 1) * 128], lhsT=kf(pair, half), rhs=QT[qs][64 * half:64 * half + 64, pair, j * 128:(j + 1) * 128], start=True, stop=True),
                            reads=[kb, b_QT[qs]], writes=[bbank])
                    for half in range(2):
                        bank, bbank = (SE, b_SE) if half == 0 else (SO, b_SO)
                        ei = 2 * ci + half
                        S.op("act", lambda e, bank=bank, ei=ei: e.activation(out=Eb[:, ei, :], in_=bank[:, :], func=AF.Exp, scale=SCALE), reads=[bbank], writes=[b_E[ei]])
                        if tid is not None:
                            S.op(("pool" if half == 1 else "dve"), lambda e, ei=ei, tid=tid, half=half: e.tensor_tensor(out=Eb[:, ei, :], in0=Eb[:, ei, :], in1=EB[:, tid, half * 512:(half + 1) * 512], op=ALU.mult),
                                 reads=[b_E[ei], b_EB], writes=[b_E[ei]])
                    yield

            def C_s(grp, u, qs, keyf, j):
                chunks = keyf(2 * u + j)
                run(C_s_gen(grp, u, qs, chunks, j))
                return chunks

            def C_pv(grp, u, qs, chunks, j):
                nck = len(chunks)
                for h in range(8):
                    pair, half = h // 2, h % 2
                    pb, bpb = pv[h // 4], b_pv[h // 4]
                    for ci, (kf, kb, vf, vb, tid) in enumerate(chunks):
                        ei = 2 * ci + half
                        S.op("pe", lambda e, ei=ei, pair=pair, vf=vf, h=h, pb=pb, ci=ci: e.matmul(
                            pb[:, (h % 4) * 65:(h % 4) * 65 + 65], lhsT=Eb[:, ei, pair * 128:(pair + 1) * 128], rhs=vf(h), start=(ci == 0), stop=(ci == nck - 1)),
                            reads=[b_E[ei], vb], writes=[bpb])
                for g2 in range(2):
                    pb, bpb = pv[g2], b_pv[g2]
                    pv3 = pb[:, 0:260].rearrange("p (h d) -> p h d", d=65)
                    S.op("dve", lambda e, pv3=pv3, g2=g2: e.reciprocal(out=rinv[:, 4 * g2:4 * g2 + 4], in_=pv3[:, :, 64]), reads=[bpb], writes=[b_rinv])
                    S.op("dve", lambda e, pv3=pv3, g2=g2: e.tensor_tensor(out=ob[:, 256 * g2:256 * g2 + 256].rearrange("p (h d) -> p h d", d=64), in0=pv3[:, :, 0:64],
                                                                            in1=rinv[:, 4 * g2:4 * g2 + 4].unsqueeze(2).to_broadcast([128, 4, 64]), op=ALU.mult),
                         reads=[bpb, b_rinv], writes=[b_ob])
                S.op("dve", lambda e: e.scalar_tensor_tensor(out=yct[:], in0=ob[:], scalar=0.5, in1=sgc[qs][:, j, :], op0=ALU.mult, op1=ALU.mult), reads=[b_ob, b_sgc[qs]], writes=[b_yct])

            def C_o(grp, u, qs, j):
                run(C_o_gen(grp, u, qs, j))

            def C_o_gen(grp, u, qs, j, pool=None):
                v = 0 if grp == "S" else 1
                yt, byt = yT[qs], b_yT[qs]
                tile = 2 * u + j
                tr, b_tr = mmt(pool)
                for a in range(4):
                    S.op("pe", lambda e, a=a: e.transpose(out=tr[:, a * 128:(a + 1) * 128], in_=yct[:, a * 128:(a + 1) * 128], identity=ident[:]), reads=[b_yct, b_id], writes=[b_tr])
                S.op("act", lambda e: e.activation(out=yt[:, 4:8, j * 128:(j + 1) * 128], in_=tr[:, 0:512].rearrange("p (a t) -> p a t", a=4), func=AF.Copy), reads=[b_tr], writes=[byt])
                yield
                ti = cnt["tU"] % 2
                cnt["tU"] += 1
                rd = [xbuf(grp, u)] if l > 0 else []
                S.dma("sp", lambda e: e.dma_start(out=tU[ti][:], in_=src_x(grp, tile)), reads=rd, writes=[b_tU[ti]], sem_key="xC%d" % ti)
                for n in range(2):
                    po, bpo = mm(pool)
                    for kc in range(8):
                        S.op("pe", lambda e, kc=kc, n=n, po=po: e.matmul(po[:, :], lhsT=yt[:, kc, j * 128:(j + 1) * 128], rhs=wo[:, kc, n * 512:(n + 1) * 512], start=(kc == 0), stop=(kc == 7)),
                             reads=[byt, b_wo], writes=[bpo])
                        if kc == 3:
                            yield
                    S.op("dve", lambda e, n=n, po=po: e.tensor_tensor(out=ob2[:], in0=po[:, :], in1=gate[:, v, n * 512:(n + 1) * 512], op=ALU.mult),
                         reads=[bpo, b_gate], writes=[b_ob2])
                    S.op("dve", lambda e, n=n: e.scalar_tensor_tensor(out=tU[ti][:, n * 512:(n + 1) * 512], in0=tU[ti][:, n * 512:(n + 1) * 512], scalar=ALPHA, in1=ob2[:], op0=ALU.mult, op1=ALU.add),
                         reads=[b_ob2, b_tU[ti]], writes=[b_tU[ti]])
                    yield
                for i in range(2):
                    S.op("dve", lambda e, i=i: e.bn_stats(out=stt[3][:, i, :], in_=tU[ti][:, i * 512:(i + 1) * 512]), reads=[b_tU[ti]], writes=[b_stt[3]])
                S.op("dve", lambda e: e.bn_aggr(out=mv[3][:], in_=stt[3][:].rearrange("p a b -> p (a b)")), reads=[b_stt[3]], writes=[b_mv[3]])
                ln_small(3)
                S.op("act", lambda e: e.activation(out=tU[ti][:], in_=tU[ti][:], func=AF.Identity, bias=nm[3][:], scale=rs[3][:]), reads=[b_tU[ti], b_nm[3], b_rs[3]], writes=[b_tU[ti]])
                S.op("pool", lambda e: e.tensor_tensor(out=tU[ti][:], in0=tU[ti][:], in1=lng[:], op=ALU.mult), reads=[b_tU[ti], b_lng], writes=[b_tU[ti]])
                S.op("pool", lambda e: e.tensor_tensor(out=tU[ti][:], in0=tU[ti][:], in1=lnb[:], op=ALU.add), reads=[b_tU[ti], b_lnb], writes=[b_tU[ti]])
                S.dma("sp", lambda e: e.dma_start(out=dst_x(grp, tile), in_=tU[ti][:]), reads=[b_tU[ti]], writes=([] if last else [xbuf(grp, u)]),
                      sem_key="xo%d" % ti, final=True)

            def ctx_chunks():
                out = []
                for a in range(2):
                    out.append((lambda pair, half, a=a: ckT[64 * half:64 * half + 64, pair, a * 128:(a + 1) * 128], b_ckT,
                                lambda h, a=a: cV[:, a, h, :], b_cV, None))
                return out

            nP = ns_units - l
            nR = ns_units - 1 - l

            def keyf_S(tile):
                out = []
                for (kt, tid) in attn_keys(tile):
                    ku, kj = kt // 2, kt % 2
                    ksl = ku % NK
                    out.append((lambda pair, half, ksl=ksl, kj=kj: KT[ksl][64 * half:64 * half + 64, pair, kj * 128:(kj + 1) * 128], b_KT[ksl],
                                lambda h, ksl=ksl, kj=kj: Vr[:, 2 * ksl + kj, h, :], b_V[2 * ksl + kj], tid))
                return ctx_chunks() + out

            def pipeline(grp, nP, nR, keyf_of, base, halo):
                def hs_(u): return (base + u) % NH
                def ks_(u): return (base + u) % NK
                def qs_(u): return (base + u) % 2
                def hprev(u): return hs_(u - 1) if (halo and u > 0) else None
                def hnext(u): return hs_(u + 1) if halo else None
                for u0 in range(min(3, nP)):
                    A_pre(grp, u0); A_pe(grp, u0, hs_(u0))
                B_kv(grp, 0, hs_(0), ks_(0)); B_q(grp, 0, hs_(0), qs_(0)); B_conv(grp, 0, hs_(0), qs_(0), None, hnext(0)); B_gm(grp, 0, hs_(0), qs_(0))
                pend = None
                for i in range(nP):
                    u1 = i + 1
                    full1 = u1 < nR
                    if i + 3 < nP:
                        xis_ = A_dma(grp, i + 3)
                    if u1 < nP:
                        B_kv(grp, u1, hs_(u1), ks_(u1))
                    if pend is not None:
                        C_o(*pend)
                        pend = None
                    if i + 3 < nP:
                        A_ln(grp, i + 3, xis_)
                    g_b = iter(())
                    if full1:
                        def g_b_f(u1=u1):
                            yield from B_q_gen(grp, u1, hs_(u1), qs_(u1), "G")
                            yield from B_conv_gen(grp, u1, hs_(u1), qs_(u1), hprev(u1), hnext(u1), "G")
                        g_b = g_b_f()
                    if i < nR:
                        kf = keyf_of(i)
                        ch0 = kf(2 * i)
                        zipper(C_s_gen(grp, i, qs_(i), ch0, 0, "S"), g_b, 1, 2)
                        C_pv(grp, i, qs_(i), ch0, 0)
                        ch1 = kf(2 * i + 1)
                        zipper(C_s_gen(grp, i, qs_(i), ch1, 1, "S"), C_o_gen(grp, i, qs_(i), 0, "G"), 2, 1)
                    else:
                        run(g_b)
                    if i + 3 < nP:
                        A_pe(grp, i + 3, hs_(i + 3))
                    if full1:
                        B_gm(grp, u1, hs_(u1), qs_(u1))
                    if i < nR:
                        C_pv(grp, i, qs_(i), ch1, 1)
                        pend = (grp, i, qs_(i), 1)
                if pend is not None:
                    C_o(*pend)

            def keyf_P_of(u):
                ks = (nP_s + u) % NK
                def keyf(tile):
                    out = []
                    for kj in range(2):
                        out.append((lambda pair, half, kj=kj: KT[ks][64 * half:64 * half + 64, pair, kj * 128:(kj + 1) * 128], b_KT[ks],
                                    lambda h, kj=kj: Vr[:, 2 * ks + kj, h, :], b_V[2 * ks + kj], None))
                    return out
                return keyf

            nP_s = nP
            if not SKIPS:
                pipeline("S", nP, nR, lambda u: keyf_S, 0, True)
            if not SKIPP:
                pipeline("P", n_pseq, n_pseq, keyf_P_of, nP_s, False)

        for l_ in range(depth):
            layer(l_)

        S.emit()
    return nc


def _tables(rpb, flip):
    reps = [(4, 2), (4, 3), (4, 4), (4, 5), (4, 6), (0, 2), (0, 3)]
    out = np.empty((DEPTH, 7, 128, 2, 4, 128), np.float32)
    p = np.arange(128)
    for tid, (t, u) in enumerate(reps):
        ql = t * 128 + p
        kl = u * 128 + p
        qg = 4095 - ql if flip else ql
        kg = 4095 - kl if flip else kl
        qr, qc = qg // 64, qg % 64
        kr, kc = kg // 64, kg % 64
        rs_ = np.clip(qr - 4, 0, 56)
        cs_ = np.clip(qc - 8, 0, 48)
        valid = ((kr[:, None] >= rs_[None, :]) & (kr[:, None] < rs_[None, :] + 8)
                 & (kc[:, None] >= cs_[None, :]) & (kc[:, None] < cs_[None, :] + 16))
        dr = np.clip(kr[:, None] - qr[None, :] + 7, 0, 14)
        dc = np.clip(kc[:, None] - qc[None, :], -15, 15) + 15
        g = rpb[:, :, dr, dc]
        g = np.where(valid[None, None], g, np.float32(NEG))
        g = g.transpose(0, 2, 1, 3).reshape(DEPTH, 128, 4, 2, 128).transpose(0, 1, 3, 2, 4)
        out[:, tid] = g
    return np.ascontiguousarray(out.reshape(DEPTH, 7, 128, 1024))


_NC_CACHE = {}


def kernel(x_prompt, x_sample, cache_k, cache_v, c, c_ctx, w_ada, b_ada, w_in, conv_w,
           gmlp_ln_g, gmlp_ln_b, w_spatial, b_spatial, rpb, w_out, ln_g, ln_b):
    f = lambda a: np.ascontiguousarray(np.asarray(a, dtype=np.float32))
    x_prompt, x_sample, cache_k, cache_v, c, c_ctx = map(f, (x_prompt, x_sample, cache_k, cache_v, c, c_ctx))
    w_ada, b_ada, w_in, conv_w, gmlp_ln_g, gmlp_ln_b = map(f, (w_ada, b_ada, w_in, conv_w, gmlp_ln_g, gmlp_ln_b))
    w_spatial, b_spatial, rpb, w_out, ln_g, ln_b = map(f, (w_spatial, b_spatial, rpb, w_out, ln_g, ln_b))
    if "nc" not in _NC_CACHE:
        _NC_CACHE["nc"] = build()
    nc = _NC_CACHE["nc"]
    ident = np.eye(128, dtype=np.float32)
    tabs = [_tables(rpb, 0), _tables(rpb, 1)]
    in_maps = []
    for i in range(8):
        b, flip = i // 2, i % 2
        xs = x_sample[b][::-1] if flip else x_sample[b]
        xp = x_prompt[4 * i:4 * i + 4]
        if flip:
            xp = xp[:, ::-1]
        ws = w_spatial[:, :, ::-1, ::-1] if flip else w_spatial
        bs = b_spatial[:, :, ::-1] if flip else b_spatial
        cwv = conv_w[:, ::-1, :] if flip else conv_w
        in_maps.append({
            "xp": f(xp.reshape(1024, D)),
            "xs": f(xs[0:3072]),
            "c2": f(np.stack([c[b], c_ctx])),
            "ckT": f(cache_k[b].reshape(DEPTH, 256, 512).transpose(0, 2, 1)),
            "cv": f(cache_v[b].reshape(DEPTH, 256, 512)),
            "w_ada": w_ada, "b_ada": b_ada, "w_in": w_in, "w_out": w_out,
            "conv_w": f(cwv), "gmlp_ln_g": gmlp_ln_g, "gmlp_ln_b": gmlp_ln_b,
            "wsT": f(ws.transpose(0, 3, 1, 2)), "bsT": f(bs.transpose(0, 2, 1)),
            "ebt": tabs[flip], "ln_g": ln_g, "ln_b": ln_b, "ident": ident,
        })
    res = run_bass_kernel_spmd(nc, in_maps, core_ids=list(range(8))).results
    y_prompt = np.empty((32, 256, D), np.float32)
    y_sample = np.empty((4, 4096, D), np.float32)
    new_k = np.empty((32, DEPTH, 256, 8, 64), np.float32)
    new_v = np.empty((32, DEPTH, 256, 8, 64), np.float32)
    for i in range(8):
        b, flip = i // 2, i % 2
        r = res[i]
        yp = np.asarray(r["yp"]).reshape(4, 256, D)
        nk = np.asarray(r["nk"]).reshape(4, DEPTH, 256, 8, 64)
        nv = np.asarray(r["nv"]).reshape(4, DEPTH, 256, 8, 64)
        ys = np.asarray(r["ys"])
        if flip:
            yp = yp[:, ::-1]
            nk = nk[:, :, ::-1]
            nv = nv[:, :, ::-1]
            y_sample[b, 2048:] = ys[::-1]
        else:
            y_sample[b, :2048] = ys
        y_prompt[4 * i:4 * i + 4] = yp
        new_k[4 * i:4 * i + 4] = nk
        new_v[4 * i:4 * i + 4] = nv
    return (y_prompt, y_sample, new_k, new_v)
```

```python
import contextlib
import numpy as np
import concourse.bass as bass
import concourse.mybir as mybir
from concourse.bass_utils import run_bass_kernel_spmd

F32 = mybir.dt.float32
BF16 = mybir.dt.bfloat16
ALU = mybir.AluOpType
AF = mybir.ActivationFunctionType

D = 1024
DEPTH = 4
ALPHA = (2 * DEPTH) ** 0.25
SCALE = 64 ** -0.5
NEG = -30000.0
ENGS = ("pe", "act", "dve", "pool", "sp")
MAXOPS = 10 ** 9
SCHEDULE = True
PRIO_W = 0.0
TOKEN_ALL = False
SKIPP = False
SKIPS = False


class Buf:
    __slots__ = ("name", "excl", "last_w", "readers")

    def __init__(self, name, excl=False):
        self.name = name
        self.excl = excl
        self.last_w = None
        self.readers = []


class Op:
    __slots__ = ("eng", "fn", "is_dma", "sem_key", "deps", "signal", "count", "dma_count", "idx", "reads", "writes",
                 "preds", "succs", "npred", "cost", "fin", "pos", "mode", "start", "prio")

    def __init__(self, eng, fn, is_dma, sem_key):
        self.eng = eng
        self.fn = fn
        self.is_dma = is_dma
        self.sem_key = sem_key
        self.deps = []
        self.signal = False
        self.count = 0
        self.dma_count = 0
        self.mode = 0


class _Rec:
    def __init__(self):
        self.name = None
        self.kw = {}
        self.args = ()

    def __getattr__(self, name):
        def f(*a, **k):
            self.name, self.args, self.kw = name, a, k
            return self
        return f


def _nfree(ap):
    n = 1
    for d in tuple(ap.shape)[1:]:
        n *= int(d)
    return n


def _is_psum_or_f32(ap):
    try:
        return ("psum" in str(ap.space).lower()) or (ap.dtype == F32)
    except Exception:
        return True


def _cost_ns(op):
    r = _Rec()
    try:
        op.fn(r)
    except Exception:
        return 300.0
    kw, a, nm_ = r.kw, r.args, r.name
    out = kw.get("out", a[0] if a else None)
    try:
        if op.is_dma:
            byts = _nfree(out) * int(tuple(out.shape)[0]) * 4
            return 2000.0 + byts / 200.0
        if op.eng == "pe":
            if nm_ == "transpose":
                return 75.0
            rhs = kw.get("rhs", a[2] if len(a) > 2 else None)
            lhsT = kw.get("lhsT", a[1] if len(a) > 1 else None)
            if int(tuple(lhsT.shape)[0]) <= 64:
                op.mode = 64
                return max(35.0, 15.0 + _nfree(rhs) * 0.5)
            return max(45.0, 30.0 + _nfree(rhs) * 0.45)
        n = _nfree(out)
        if op.eng == "act":
            if kw.get("func") == AF.Exp:
                return 150.0 + n * 0.69
            return 240.0 + n * 0.95
        if op.eng == "pool":
            if nm_ == "memset":
                return 150.0 + n * 0.9
            if n <= 8:
                return 480.0
            return 250.0 + n * 2.1
        if nm_ == "reciprocal":
            return 1000.0
        if nm_ == "bn_aggr":
            return 250.0
        if nm_ == "bn_stats":
            return 160.0 + _nfree(kw.get("in_")) * 1.0
        if n <= 8:
            return 550.0
        srcs = [kw.get(k) for k in ("in0", "in1", "in_") if kw.get(k) is not None]
        slow = any(_is_psum_or_f32(x) for x in srcs)
        if nm_ in ("tensor_tensor", "scalar_tensor_tensor"):
            return 160.0 + n * (1.04 if slow else 1.5)
        return 200.0 + n * (1.04 if slow else 0.55)
    except Exception:
        return 300.0


class Sched:
    def __init__(self, nc):
        self.nc = nc
        self.all = []
        self.ops = {e: [] for e in ENGS}
        self.dma_counts = {}
        self.dma_keys = []
        self.final_waits = []
        self.dve_token = Buf("dve_token")

    def _add(self, eng, fn, reads, writes, is_dma=False, sem_key=None):
        op = Op(eng, fn, is_dma, sem_key)
        if eng == "dve" and not is_dma:
            r_ = _Rec()
            try:
                fn(r_)
            except Exception:
                pass
            if TOKEN_ALL or r_.name not in ("tensor_tensor", "scalar_tensor_tensor"):
                reads = reads + [self.dve_token]
        op.reads, op.writes = reads, writes
        op.idx = len(self.all)
        self.all.append(op)
        if is_dma:
            if sem_key not in self.dma_counts:
                self.dma_counts[sem_key] = 0
                self.dma_keys.append(sem_key)
            self.dma_counts[sem_key] += 16
            op.dma_count = self.dma_counts[sem_key]
        return op

    def op(self, eng, fn, reads=(), writes=(), excl_dve=False):
        w = list(writes)
        if excl_dve:
            w.append(self.dve_token)
        return self._add(eng, fn, list(reads), w)

    def dma(self, eng, fn, reads=(), writes=(), sem_key=None, final=False):
        o = self._add(eng, fn, list(reads), list(writes), is_dma=True, sem_key=sem_key)
        if final:
            self.final_waits.append(o)
        return o

    def _edges(self):
        last_key = {}
        for op in self.all:
            preds = {}
            for b in op.reads:
                if b.last_w is not None:
                    preds[id(b.last_w)] = b.last_w
                if b.excl:
                    for r in b.readers:
                        if r.eng != op.eng:
                            preds[id(r)] = r
            for b in op.writes:
                if b.last_w is not None:
                    preds[id(b.last_w)] = b.last_w
                for r in b.readers:
                    preds[id(r)] = r
            if op.is_dma:
                p = last_key.get(op.sem_key)
                if p is not None:
                    preds[id(p)] = p
                last_key[op.sem_key] = op
            preds.pop(id(op), None)
            for b in op.reads:
                b.readers.append(op)
            for b in op.writes:
                b.last_w = op
                b.readers = []
            op.preds = list(preds.values())
            op.succs = []
        for op in self.all:
            for p in op.preds:
                p.succs.append(op)

    def _schedule(self):
        SEM = 120.0
        for op in self.all:
            op.cost = _cost_ns(op)
            op.npred = len(op.preds)
            op.fin = None
        rank = {}
        for op in reversed(self.all):
            r_ = 0.0
            for s_ in op.succs:
                r2 = rank[id(s_)]
                if r2 > r_:
                    r_ = r2
            rank[id(op)] = r_ + op.cost + SEM
        tot = max(rank.values())
        n_all = float(len(self.all))
        for op in self.all:
            op.prio = op.idx / n_all - PRIO_W * rank[id(op)] / tot
        cand = {e: [] for e in ENGS}
        ready = {}
        for op in self.all:
            if op.npred == 0:
                cand[op.eng].append(op)
                ready[id(op)] = 0.0
        free = {e: 0.0 for e in ENGS}
        order = {e: [] for e in ENGS}
        pe_mode = [0]
        left = len(self.all)
        while left:
            best = None
            for e in ENGS:
                cl = cand[e]
                if not cl:
                    continue
                t = free[e]
                pick = None
                if e == "pe":
                    for o in cl:
                        if ready[id(o)] <= t and o.mode == pe_mode[0]:
                            if pick is None or o.prio < pick.prio:
                                pick = o
                if pick is None:
                    for o in cl:
                        if ready[id(o)] <= t:
                            if pick is None or o.prio < pick.prio:
                                pick = o
                if pick is not None:
                    st_ = t
                else:
                    for o in cl:
                        rt = ready[id(o)]
                        if pick is None or rt < ready[id(pick)] or (rt == ready[id(pick)] and o.idx < pick.idx):
                            pick = o
                    st_ = ready[id(pick)]
                if best is None or st_ < best[0] or (st_ == best[0] and pick.idx < best[1].idx):
                    best = (st_, pick)
            st_, o = best
            e = o.eng
            cand[e].remove(o)
            if e == "pe":
                if o.mode != pe_mode[0]:
                    st_ += 120.0
                pe_mode[0] = o.mode
            if o.is_dma:
                free[e] = st_ + 60.0
            else:
                free[e] = st_ + o.cost
            o.fin = st_ + o.cost
            o.start = st_
            o.pos = len(order[e])
            order[e].append(o)
            left -= 1
            for s_ in o.succs:
                s_.npred -= 1
                if s_.npred == 0:
                    pe_s = (s_.eng == "pe" and not s_.is_dma)
                    ready[id(s_)] = max((p.start if (pe_s and p.eng == "pe" and not p.is_dma) else p.fin + SEM) for p in s_.preds)
                    cand[s_.eng].append(s_)
        self.model_ns = max(o.fin for o in self.all)
        return order

    def emit(self):
        nc = self.nc
        self._edges()
        if SCHEDULE:
            self.ops = self._schedule()
        else:
            self.ops = {e: [o for o in self.all if o.eng == e] for e in ENGS}
            for e in ENGS:
                for i, o in enumerate(self.ops[e]):
                    o.pos = i
        for e in ENGS:
            for o in self.ops[e]:
                lastp = {}
                dl = []
                for p in o.preds:
                    if p.is_dma:
                        dl.append(p)
                        continue
                    if p.eng == "pe" and o.eng == "pe" and not o.is_dma:
                        continue
                    q = lastp.get(p.eng)
                    if q is None or p.pos > q.pos:
                        lastp[p.eng] = p
                o.deps = dl + list(lastp.values())
                for p in lastp.values():
                    p.signal = True
        for e in ENGS:
            c = 0
            for o in self.ops[e]:
                if (not o.is_dma) and o.signal:
                    c += 1
                    o.count = c
        with contextlib.ExitStack() as st:
            esem = {e: st.enter_context(nc.semaphore("s_" + e)) for e in ENGS}
            dsem = {k: st.enter_context(nc.semaphore("d_" + str(k))) for k in self.dma_keys}
            block = st.enter_context(nc.Block())
            engobj = {"pe": "tensor", "act": "scalar", "dve": "vector", "pool": "gpsimd", "sp": "sync"}

            def make(e):
                def body(eng):
                    waited = {}
                    for o in self.ops[e]:
                        for d in o.deps:
                            if d.is_dma:
                                s, v, key = dsem[d.sem_key], d.dma_count, ("d", d.sem_key)
                            else:
                                s, v, key = esem[d.eng], d.count, ("e", d.eng)
                            if waited.get(key, 0) >= v:
                                continue
                            waited[key] = v
                            eng.wait_ge(s, v)
                        ins = o.fn(eng)
                        if o.is_dma:
                            ins.then_inc(dsem[o.sem_key], 16)
                        elif o.signal:
                            ins.then_inc(esem[e], 1)
                    if e == "sp":
                        fin = {}
                        for o in self.final_waits:
                            fin[o.sem_key] = max(fin.get(o.sem_key, 0), o.dma_count)
                        for k, v in fin.items():
                            if waited.get(("d", k), 0) >= v:
                                continue
                            eng.wait_ge(dsem[k], v)

                return body

            for e in ENGS:
                getattr(block, engobj[e])(make(e))


def attn_keys(t):
    if t >= 2:
        return [(t + d, d + 2) for d in (-2, -1, 0, 1, 2)]
    if t == 0:
        return [(0, 2), (1, 3), (2, 5), (3, 6)]
    return [(0, 1), (1, 2), (2, 3), (3, 5)]


def build(depth=DEPTH, n_pseq=4, ns_units=12):
    nc = bass.Bass("TRN2", target_bir_lowering=False)

    def din(name, shape):
        return nc.dram_tensor(name, shape, F32, kind="ExternalInput").ap()

    def dout(name, shape):
        return nc.dram_tensor(name, shape, F32, kind="ExternalOutput").ap()

    xp_d = din("xp", [n_pseq * 256, D])
    xs_d = din("xs", [3072, D])
    c2_d = din("c2", [2, D])
    ckT_d = din("ckT", [DEPTH, 512, 256])
    cv_d = din("cv", [DEPTH, 256, 512])
    wada_d = din("w_ada", [DEPTH, D, 3 * D])
    bada_d = din("b_ada", [DEPTH, 3 * D])
    win_d = din("w_in", [DEPTH, D, 3840])
    wout_d = din("w_out", [DEPTH, D, D])
    convw_d = din("conv_w", [DEPTH, 3, 256])
    glg_d = din("gmlp_ln_g", [DEPTH, 256])
    glb_d = din("gmlp_ln_b", [DEPTH, 256])
    wsT_d = din("wsT", [DEPTH, 128, 4, 128])
    bsT_d = din("bsT", [DEPTH, 128, 4])
    ebt_d = din("ebt", [DEPTH, 7, 128, 1024])
    lng_d = din("ln_g", [DEPTH, D])
    lnb_d = din("ln_b", [DEPTH, D])
    id_d = din("ident", [128, 128])
    yp_d = dout("yp", [n_pseq * 256, D])
    ys_d = dout("ys", [2048, D])
    nk_d = dout("nk", [n_pseq, DEPTH, 256, 512])
    nv_d = dout("nv", [n_pseq, DEPTH, 256, 512])
    sxp_d = nc.dram_tensor("sxp", [n_pseq * 256, D], F32).ap()
    sxs_d = nc.dram_tensor("sxs", [3072, D], F32).ap()

    S = Sched(nc)
    st = contextlib.ExitStack()
    with st:
        def sb(name, shape, dt=F32):
            return st.enter_context(nc.sbuf_tensor(name, shape, dt))

        def ps(name, shape, dt=F32):
            return st.enter_context(nc.psum_tensor(name, shape, dt))

        wF = sb("wF", [128, 8, 1536], BF16); b_wF = Buf("wF"); b_wFq = Buf("wFq")
        wT = sb("wT", [128, 8, 2304], BF16); b_wTk = Buf("wTk"); b_wTv = Buf("wTv"); b_wTvu = Buf("wTvu"); b_wTgb = Buf("wTgb"); b_wTgc = Buf("wTgc")
        wo = sb("wo", [128, 8, 1024], BF16); b_wo = Buf("wo")
        NH = 4
        hT = [sb("hT%d" % i, [128, 8, 256], BF16) for i in range(NH)]; b_hT = [Buf("hT%d" % i) for i in range(NH)]
        xA = [sb("xA%d" % i, [128, D]) for i in range(2)]; b_xA = [Buf("xA%d" % i) for i in range(2)]
        xn = [sb("xn%d" % i, [128, D], BF16) for i in range(2)]; b_xn = [Buf("xn%d" % i) for i in range(2)]
        QT = [sb("QT%d" % i, [128, 4, 256], BF16) for i in range(2)]; b_QT = [Buf("QT%d" % i) for i in range(2)]
        NK = 3
        KT = [sb("KT%d" % i, [128, 4, 256], BF16) for i in range(NK)]; b_KT = [Buf("KT%d" % i) for i in range(NK)]
        Vr = sb("Vr", [128, 2 * NK, 8, 65], BF16); b_V = [Buf("V%d" % i) for i in range(2 * NK)]
        yT = [sb("yT%d" % i, [128, 8, 256], BF16) for i in range(2)]; b_yT = [Buf("yT%d" % i) for i in range(2)]
        sgc = [sb("sgc%d" % i, [128, 2, 512], BF16) for i in range(2)]; b_sgc = [Buf("sgc%d" % i) for i in range(2)]
        ktok = sb("ktok", [128, 512], BF16); b_ktok = Buf("ktok")
        ckT = sb("ckT_sb", [128, 4, 256], BF16); b_ckT = Buf("ckT")
        cV = sb("cV", [128, 2, 8, 65], BF16); b_cV = Buf("cV")
        EB = sb("EB", [128, 7, 1024], BF16); b_EB = Buf("EB")
        tU = [sb("tU%d" % i, [128, D]) for i in range(2)]; b_tU = [Buf("tU%d" % i) for i in range(2)]
        Eb = sb("Eb", [128, 14, 512], BF16); b_E = [Buf("E%d" % i) for i in range(14)]
        c2f = sb("c2f", [128, 2, 8]); b_c2f = Buf("c2f")
        c2t = sb("c2t", [128, 2, 8]); b_c2t = Buf("c2t")
        scT = sb("scT", [128, 8, 2], BF16); b_scT = Buf("scT")
        sc1p = sb("sc1p", [128, 2, 8]); b_sc1p = Buf("sc1p")
        shp = sb("shp", [128, 2, 8]); b_shp = Buf("shp")
        badc = sb("badc", [128, 16]); b_badc = Buf("badc")
        gate = sb("gate", [128, 2, D]); b_gate = Buf("gate")
        lng = sb("lng", [128, D]); b_lng = Buf("lng")
        lnb = sb("lnb", [128, D]); b_lnb = Buf("lnb")
        glg = sb("glg", [128, 256]); b_glg = Buf("glg")
        glb = sb("glb", [128, 256]); b_glb = Buf("glb")
        bsT = sb("bsT_sb", [128, 4]); b_bsT = Buf("bsT")
        bsb = sb("bsb", [128, 256]); b_bsb = Buf("bsb")
        rsw = sb("rsw", [128, 4]); b_rsw = Buf("rsw")
        ones1 = sb("ones1", [128, 2], BF16); b_ones1 = Buf("ones1")
        wsT = sb("wsT_sb", [128, 4, 128], BF16); b_wsT = Buf("wsT")
        cw = sb("cw", [128, 2, 3]); b_cw = Buf("cw")
        ident = sb("ident_sb", [128, 128], BF16); b_id = Buf("ident")
        nh = sb("nh", [128, 2]); b_nh = Buf("nh")
        stt = [sb("stt%d" % i, [128, 2, 6]) for i in range(4)]
        mv = [sb("mv%d" % i, [128, 2]) for i in range(4)]
        ve = [sb("ve%d" % i, [128, 1]) for i in range(4)]
        rs = [sb("rs%d" % i, [128, 1]) for i in range(4)]
        nm = [sb("nm%d" % i, [128, 1]) for i in range(4)]
        b_stt = [Buf("stt%d" % i) for i in range(4)]
        b_mv = [Buf("mv%d" % i) for i in range(4)]
        b_ve = [Buf("ve%d" % i) for i in range(4)]
        b_rs = [Buf("rs%d" % i) for i in range(4)]
        b_nm = [Buf("nm%d" % i) for i in range(4)]
        wkall = sb("wkall", [128, 4, 256])
        wk = {}
        for i_, nme in enumerate(("xa", "acc", "tg", "s2")):
            wk[nme] = (wkall[:, i_, :], Buf("wk_" + nme))
        wk["m"] = wk["xa"]
        tgc = wkall[:, 2:4, :].rearrange("p a t -> p (a t)")
        b_tgc_l = [wk["tg"][1], wk["s2"][1]]
        zb = sb("zb", [128, 258]); b_zb = Buf("zb")
        xh = sb("xh", [128, 4]); b_xh = Buf("xh")
        zh = sb("zh", [128, 4]); b_zh = Buf("zh")
        vn3 = sb("vn3", [128, 256], BF16); b_vn3 = Buf("vn3")
        ybt = sb("ybt", [128, 256], BF16); b_ybt = Buf("ybt")
        rinv = sb("rinv", [128, 8]); b_rinv = Buf("rinv")
        ob = sb("ob", [128, 512]); b_ob = Buf("ob")
        ob2 = sb("ob2", [128, 512]); b_ob2 = Buf("ob2")
        kst = ob2; b_kst = b_ob2
        vst = ob; b_vst = b_ob
        yct = sb("yct", [128, 512], BF16); b_yct = Buf("yct")

        NMM = 6
        mmb = [ps("mm%d" % i, [128, 512]) for i in range(NMM)]; b_mm = [Buf("mm%d" % i, True) for i in range(NMM)]
        pv = [ps("pv%d" % i, [128, 512]) for i in range(2)]; b_pv = [Buf("pv0", True), Buf("pv1", True)]
        mm_i = [0]

        pool_banks = {None: [0, 1, 2, 3, 4, 5], "S": [2, 3, 4, 5], "G": [0, 1]}
        pool_i = {None: 0, "S": 0, "G": 0}

        def mm(pool=None):
            bl = pool_banks[pool]
            i = bl[pool_i[pool] % len(bl)]
            pool_i[pool] += 1
            return mmb[i], b_mm[i]

        def mmt(pool=None):
            t, b = mm(pool)
            return t[:].bitcast(BF16), b

        def ln_small(site, eng_after="dve"):
            S.op("dve", lambda e: e.tensor_scalar(out=ve[site][:], in0=mv[site][:, 1:2], scalar1=1e-5, scalar2=None, op0=ALU.add),
                 reads=[b_mv[site]], writes=[b_ve[site]])
            S.op("pool", lambda e: e.tensor_tensor(out=rs[site][:], in0=ve[site][:], in1=nh[:, 0:1], op=ALU.pow),
                 reads=[b_ve[site], b_nh], writes=[b_rs[site]], excl_dve=True)
            S.op("dve", lambda e: e.scalar_tensor_tensor(out=nm[site][:], in0=mv[site][:, 0:1], scalar=-1.0, in1=rs[site][:], op0=ALU.mult, op1=ALU.mult),
                 reads=[b_mv[site], b_rs[site]], writes=[b_nm[site]])

        S.dma("pool", lambda e: e.dma_start(out=ident[:], in_=id_d), writes=[b_id], sem_key="c_id")
        S.op("pool", lambda e: e.memset(nh[:], -0.5), writes=[b_nh])
        S.op("pool", lambda e: e.memset(ones1[:], 1.0), writes=[b_ones1])
        S.op("pool", lambda e: e.memset(Vr[:].rearrange("p a h d -> p (a h d)"), 1.0), writes=b_V)
        S.op("pool", lambda e: e.memset(cV[:].rearrange("p a h d -> p (a h d)"), 1.0), writes=[b_cV])
        for v in range(2):
            S.dma("sp", lambda e, v=v: e.dma_start(out=c2f[:, v, :], in_=c2_d[v, :].rearrange("(k p) -> p k", p=128), allow_slow_non_contiguous=True), writes=[b_c2f], sem_key="c_c2")
        S.op("act", lambda e: e.activation(out=c2t[:], in_=c2f[:], func=AF.Tanh, scale=0.5), reads=[b_c2f], writes=[b_c2t])
        S.op("dve", lambda e: e.scalar_tensor_tensor(out=c2t[:], in0=c2t[:], scalar=1.0, in1=c2f[:], op0=ALU.add, op1=ALU.mult), reads=[b_c2f, b_c2t], writes=[b_c2t])
        S.op("dve", lambda e: e.tensor_scalar(out=scT[:].rearrange("p k v -> p v k"), in0=c2t[:], scalar1=0.5, scalar2=None, op0=ALU.mult), reads=[b_c2t], writes=[b_scT])
        b_sxs = [Buf("sxs%d" % u) for u in range(12)]
        b_sxp = [Buf("sxp%d" % u) for u in range(n_pseq)]

        def layer(l):
            last = (l == depth - 1)
            def wdma(dst, src, bufd, key):
                S.dma("pool", lambda e: e.dma_start(out=dst, in_=src.rearrange("(k p) n -> p k n", p=128)), writes=[bufd], sem_key=key)
            wdma(wT[:, :, 1792:2304], win_d[l][:, 2304:2816], b_wTk, "w_Tk")
            wdma(wT[:, :, 768:1280], win_d[l][:, 2816:3328], b_wTv, "w_Tv")
            for c_ in range(2):
                S.dma("sp", lambda e, c_=c_: e.dma_start(out=cw[:, c_, :], in_=convw_d[l][:, c_ * 128:(c_ + 1) * 128].rearrange("j p -> p j"), allow_slow_non_contiguous=True), writes=[b_cw], sem_key="p_cw")
            S.dma("sp", lambda e: e.dma_start(out=glg[:], in_=glg_d[l, :].partition_broadcast(128)), writes=[b_glg], sem_key="p_glg")
            S.dma("sp", lambda e: e.dma_start(out=glb[:], in_=glb_d[l, :].partition_broadcast(128)), writes=[b_glb], sem_key="p_glb")
            S.dma("sp", lambda e: e.dma_start(out=lng[:], in_=lng_d[l, :].partition_broadcast(128)), writes=[b_lng], sem_key="p_lng")
            S.dma("sp", lambda e: e.dma_start(out=lnb[:], in_=lnb_d[l, :].partition_broadcast(128)), writes=[b_lnb], sem_key="p_lnb")
            S.dma("sp", lambda e: e.dma_start(out=bsT[:], in_=bsT_d[l]), writes=[b_bsT], sem_key="p_bsT")
            S.op("dve", lambda e: e.tensor_copy(out=bsb[:].rearrange("p (g c) -> p g c", g=4), in_=bsT[:, :].unsqueeze(2).to_broadcast([128, 4, 64])), reads=[b_bsT], writes=[b_bsb])
            S.dma("pool", lambda e: e.dma_start(out=wsT[:], in_=wsT_d[l]), writes=[b_wsT], sem_key="p_wsT")
            prs, bprs = mm()
            for g in range(4):
                S.op("pe", lambda e, g=g, prs=prs: e.matmul(prs[:, g:g + 1], lhsT=wsT[:, g, :], rhs=ones1[:, 0:1], start=True, stop=True), reads=[b_wsT, b_ones1], writes=[bprs])
            S.op("dve", lambda e, prs=prs: e.tensor_copy(out=rsw[:], in_=prs[:, 0:4]), reads=[bprs], writes=[b_rsw])
            S.op("dve", lambda e: e.tensor_tensor(out=glb[:].rearrange("p (g c) -> p g c", g=4), in0=glb[:].rearrange("p (g c) -> p g c", g=4), in1=rsw[:, :].unsqueeze(2).to_broadcast([128, 4, 64]), op=ALU.mult),
                 reads=[b_glb, b_rsw], writes=[b_glb])
            S.op("dve", lambda e: e.tensor_tensor(out=bsb[:], in0=bsb[:], in1=glb[:], op=ALU.add), reads=[b_bsb, b_glb], writes=[b_bsb])
            S.dma("sp", lambda e: e.dma_start(out=badc[:], in_=bada_d[l, 0:2048].rearrange("(j p) -> p j", p=128), allow_slow_non_contiguous=True), writes=[b_badc], sem_key="p_badc")
            wad = Eb[:, 0:8, :]
            Ssil = Eb[:, 8:12, :].rearrange("p (v a) (b t) -> p v (a b) t", v=2, b=4)
            b_Ssil_l = b_E[8:12]
            for v in range(2):
                S.op("dve", lambda e, v=v: e.tensor_copy(out=Ssil[:, v], in_=scT[:, :, v:v + 1].to_broadcast([128, 8, 128])), reads=[b_scT], writes=b_Ssil_l)
            for ch in range(6):
                S.dma("pool", lambda e, ch=ch: e.dma_start(out=wad, in_=wada_d[l][:, ch * 512:(ch + 1) * 512].rearrange("(k p) n -> p k n", p=128)),
                      writes=b_E[0:8], sem_key="w_ada")
                if ch < 4:
                    pt, bpt = mm()
                    for jb in range(4):
                        for kc in range(8):
                            S.op("pe", lambda e, jb=jb, kc=kc, pt=pt: e.matmul(pt[:, jb * 2:jb * 2 + 2], lhsT=wad[:, kc, jb * 128:(jb + 1) * 128], rhs=scT[:, kc, :], start=(kc == 0), stop=(kc == 7)),
                                 reads=b_E[0:8] + [b_scT], writes=[bpt])
                    for v in range(2):
                        dst = (shp if ch < 2 else sc1p)
                        bd = (b_shp if ch < 2 else b_sc1p)
                        j0 = (ch % 2) * 4
                        S.op("dve", lambda e, v=v, dst=dst, j0=j0, pt=pt, ch=ch: e.scalar_tensor_tensor(
                            out=dst[:, v, j0:j0 + 4], in0=pt[:, v:8:2], scalar=(0.0 if ch < 2 else 1.0), in1=badc[:, ch * 4:ch * 4 + 4], op0=ALU.add, op1=ALU.add),
                            reads=[bpt, b_badc], writes=[bd])
                else:
                    half = ch - 4
                    S.dma("sp", lambda e, half=half: e.dma_start(out=tU[0][:, 0:512], in_=bada_d[l, 2048 + half * 512:2048 + (half + 1) * 512].partition_broadcast(128)), writes=[b_tU[0]], sem_key="p_bg")
                    for v in range(2):
                        pt, bpt = mm()
                        for kc in range(8):
                            S.op("pe", lambda e, v=v, kc=kc, pt=pt: e.matmul(pt[:, :], lhsT=Ssil[:, v, kc, :], rhs=wad[:, kc, :], start=(kc == 0), stop=(kc == 7)),
                                 reads=b_E[0:8] + b_Ssil_l, writes=[bpt])
                        S.op("dve", lambda e, v=v, pt=pt, half=half: e.tensor_tensor(out=gate[:, v, half * 512:(half + 1) * 512], in0=pt[:, :], in1=tU[0][:, 0:512], op=ALU.add),
                             reads=[bpt, b_tU[0]], writes=[b_gate])
            wdma(wF[:, :, 1024:1536], win_d[l][:, 1792:2304], b_wFq, "w_Fq")
            wdma(wF[:, :, 0:1024], win_d[l][:, 0:1024], b_wF, "w_F")
            wdma(wT[:, :, 0:256], win_d[l][:, 1280:1536], b_wTvu, "w_Tvu")
            wdma(wT[:, :, 256:512], win_d[l][:, 1024:1280], b_wTvu, "w_Tvu2")
            wdma(wT[:, :, 1280:1792], win_d[l][:, 3328:3840], b_wTgc, "w_Tgc")
            wdma(wT[:, :, 512:768], win_d[l][:, 1536:1792], b_wTgb, "w_Tgb")
            wdma(wo[:], wout_d[l], b_wo, "w_o")
            S.dma("pool", lambda e: e.dma_start(out=ckT[:], in_=ckT_d[l].rearrange("(a p) k -> p a k", p=128)), writes=[b_ckT], sem_key="c_k")
            for a in range(2):
                S.dma("pool", lambda e, a=a: e.dma_start(out=cV[:, a, :, 0:64], in_=cv_d[l][a * 128:(a + 1) * 128, :].rearrange("p (h d) -> p h d", d=64)), writes=[b_cV], sem_key="c_v")
            for tid in range(7):
                S.dma("sp", lambda e, tid=tid: e.dma_start(out=tU[tid % 2][:], in_=ebt_d[l, tid]), writes=[b_tU[tid % 2]], sem_key="p_eb%d" % (tid % 2))
                S.op("act", lambda e, tid=tid: e.activation(out=EB[:, tid, :], in_=tU[tid % 2][:], func=AF.Exp), reads=[b_tU[tid % 2]], writes=[b_EB])

            def src_x(grp, tile):
                if l == 0:
                    return (xs_d if grp == "S" else xp_d)[tile * 128:(tile + 1) * 128, :]
                return (sxs_d if grp == "S" else sxp_d)[tile * 128:(tile + 1) * 128, :]

            def dst_x(grp, tile):
                if last:
                    return (ys_d if grp == "S" else yp_d)[tile * 128:(tile + 1) * 128, :]
                return (sxs_d if grp == "S" else sxp_d)[tile * 128:(tile + 1) * 128, :]

            def xbuf(grp, u):
                return (b_sxs if grp == "S" else b_sxp)[u]

            cnt = {"xA": 0, "tU": 0}

            def A_dma(grp, u):
                xis = []
                for j in range(2):
                    tile = 2 * u + j
                    xi = cnt["xA"] % 2
                    cnt["xA"] += 1
                    rd = [xbuf(grp, u)] if l > 0 else []
                    S.dma("sp", lambda e, xi=xi, tile=tile: e.dma_start(out=xA[xi][:], in_=src_x(grp, tile)), reads=rd, writes=[b_xA[xi]], sem_key="xA%d" % xi)
                    xis.append(xi)
                return xis

            def A_ln(grp, u, xis):
                for j in range(2):
                    xi = xis[j]
                    for i in range(2):
                        S.op("dve", lambda e, i=i, xi=xi, j=j: e.bn_stats(out=stt[j][:, i, :], in_=xA[xi][:, i * 512:(i + 1) * 512]), reads=[b_xA[xi]], writes=[b_stt[j]])
                    S.op("dve", lambda e, j=j: e.bn_aggr(out=mv[j][:], in_=stt[j][:].rearrange("p a b -> p (a b)")), reads=[b_stt[j]], writes=[b_mv[j]])
                    ln_small(j)
                    S.op("act", lambda e, xi=xi, j=j: e.activation(out=xn[j][:], in_=xA[xi][:], func=AF.Identity, bias=nm[j][:], scale=rs[j][:]),
                         reads=[b_xA[xi], b_nm[j], b_rs[j]], writes=[b_xn[j]])

            def A_pre(grp, u):
                A_ln(grp, u, A_dma(grp, u))

            def A_pe(grp, u, hs):
                v = 0 if grp == "S" else 1
                for j in range(2):
                    tr, b_tr = mmt()
                    for kc in range(8):
                        S.op("pe", lambda e, kc=kc, j=j, tr=tr: e.transpose(out=tr[:, kc * 128:(kc + 1) * 128], in_=xn[j][:, kc * 128:(kc + 1) * 128], identity=ident[:]),
                             reads=[b_xn[j], b_id], writes=[b_tr])
                    for kc in range(8):
                        if kc % 2 == 0:
                            S.op("dve", lambda e, kc=kc, j=j, tr=tr: e.tensor_scalar(out=hT[hs][:, kc, j * 128:(j + 1) * 128], in0=tr[:, kc * 128:(kc + 1) * 128],
                                                                                      scalar1=sc1p[:, v, kc:kc + 1], scalar2=shp[:, v, kc:kc + 1], op0=ALU.mult, op1=ALU.add),
                                 reads=[b_tr, b_sc1p, b_shp], writes=[b_hT[hs]])
                    for kc in range(8):
                        if kc % 2 == 1:
                            S.op("act", lambda e, kc=kc, j=j, tr=tr: e.activation(out=hT[hs][:, kc, j * 128:(j + 1) * 128], in_=tr[:, kc * 128:(kc + 1) * 128], func=AF.Identity,
                                                                                    scale=sc1p[:, v, kc:kc + 1], bias=shp[:, v, kc:kc + 1]),
                                 reads=[b_tr, b_sc1p, b_shp], writes=[b_hT[hs]])

            def proj_T(hs, j, c0, c1):
                pt, bpt = mm()
                n = c1 - c0
                for kc in range(8):
                    S.op("pe", lambda e, kc=kc, pt=pt: e.matmul(pt[:, 0:n], lhsT=hT[hs][:, kc, j * 128:(j + 1) * 128], rhs=wT[:, kc, c0:c1], start=(kc == 0), stop=(kc == 7)),
                         reads=[b_hT[hs], {1792: b_wTk, 768: b_wTv, 0: b_wTvu, 512: b_wTgb, 1280: b_wTgc}[c0]], writes=[bpt])
                return pt, bpt

            def proj_F(hs, pt, bpt, off, cb):
                for kc in range(8):
                    S.op("pe", lambda e, kc=kc: e.matmul(pt[:, off:off + 256], lhsT=wF[:, kc, cb * 128:(cb + 1) * 128], rhs=hT[hs][:, kc, :], start=(kc == 0), stop=(kc == 7)),
                         reads=[b_hT[hs], (b_wFq if cb >= 8 else b_wF)], writes=[bpt])

            def B_kv(grp, u, hs, ks):
                with_out = (grp == "P")
                for j in range(2):
                    vslot = 2 * ks + j
                    pt, bpt = proj_T(hs, j, 1792, 2304)
                    S.op("act", lambda e, pt=pt: e.activation(out=ktok[:], in_=pt[:, :], func=AF.Copy), reads=[bpt], writes=[b_ktok])
                    if with_out:
                        S.op("dve", lambda e, pt=pt: e.tensor_copy(out=kst[:], in_=pt[:, :]), reads=[bpt], writes=[b_kst])
                        S.dma("sp", lambda e, j=j: e.dma_start(out=nk_d[u, l, j * 128:(j + 1) * 128, :], in_=kst[:]), reads=[b_kst], sem_key="o_k", final=True)
                    tr, b_tr = mmt()
                    for a in range(4):
                        S.op("pe", lambda e, a=a, tr=tr: e.transpose(out=tr[:, a * 128:(a + 1) * 128], in_=ktok[:, a * 128:(a + 1) * 128], identity=ident[:]),
                             reads=[b_ktok, b_id], writes=[b_tr])
                    S.op("dve", lambda e, j=j, tr=tr: e.tensor_copy(out=KT[ks][:, :, j * 128:(j + 1) * 128], in_=tr[:, 0:512].rearrange("p (a t) -> p a t", a=4)),
                         reads=[b_tr], writes=[b_KT[ks]])
                    pt, bpt = proj_T(hs, j, 768, 1280)
                    S.op("act", lambda e, pt=pt, vslot=vslot: e.activation(out=Vr[:, vslot, :, 0:64], in_=pt[:, :].rearrange("p (h d) -> p h d", d=64), func=AF.Copy),
                         reads=[bpt], writes=[b_V[vslot]])
                    if with_out:
                        S.op("dve", lambda e, pt=pt: e.tensor_copy(out=vst[:], in_=pt[:, :]), reads=[bpt], writes=[b_vst])
                        S.dma("sp", lambda e, j=j: e.dma_start(out=nv_d[u, l, j * 128:(j + 1) * 128, :], in_=vst[:]), reads=[b_vst], sem_key="o_v", final=True)

            def proj_F_g(hs, pt, bpt, off, cb):
                proj_F(hs, pt, bpt, off, cb)
                yield

            def B_q_gen(grp, u, hs, qs, pool=None):
                for half in range(2):
                    pq, bpq = mm(pool)
                    yield from proj_F_g(hs, pq, bpq, 0, 8 + 2 * half)
                    yield from proj_F_g(hs, pq, bpq, 256, 9 + 2 * half)
                    S.op("act", lambda e, pq=pq, half=half: e.activation(out=QT[qs][:, 2 * half:2 * half + 2, :], in_=pq[:, :].rearrange("p (a t) -> p a t", a=2), func=AF.Copy),
                         reads=[bpq], writes=[b_QT[qs]])

            def run(gen):
                for _ in gen:
                    pass

            def zipper(ga, gb, na=1, nb=2):
                da = db = False
                while not (da and db):
                    for _ in range(na):
                        if not da:
                            try:
                                next(ga)
                            except StopIteration:
                                da = True
                    for _ in range(nb):
                        if not db:
                            try:
                                next(gb)
                            except StopIteration:
                                db = True

            def B_q(grp, u, hs, qs):
                run(B_q_gen(grp, u, hs, qs))

            def B_conv_gen(grp, u, hs, qs, hs_prev, hs_next, pool=None):
                yt, byt = yT[qs], b_yT[qs]
                have_h = [hs_prev is not None, hs_next is not None]
                hal = pv[1][:, 384:392]
                b_halo = b_pv[1]
                for cbi, cb in enumerate((0, 1, 4, 5)):
                    for side in range(2):
                        if not have_h[side]:
                            continue
                        hsrc = hT[hs_prev][:, :, 255:256] if side == 0 else hT[hs_next][:, :, 0:1]
                        bsrc = b_hT[hs_prev] if side == 0 else b_hT[hs_next]
                        for kc in range(8):
                            S.op("pe", lambda e, kc=kc, cb=cb, hsrc=hsrc, col=cbi * 2 + side: e.matmul(hal[:, col:col + 1], lhsT=wF[:, kc, cb * 128:(cb + 1) * 128], rhs=hsrc[:, kc, :], start=(kc == 0), stop=(kc == 7)),
                                 reads=[bsrc, b_wF], writes=[b_halo])
                    yield
                S.op("pool", lambda e: e.memset(zh[:], 0.0), writes=[b_zh])
                for side in range(2):
                    if have_h[side]:
                        S.op("dve", lambda e, side=side: e.tensor_copy(out=xh[:, side:4:2], in_=hal[:, side:4:2]), reads=[b_halo], writes=[b_xh])
                        S.op("dve", lambda e, side=side: e.tensor_tensor(out=zh[:, side:4:2], in0=hal[:, 4 + side:8:2], in1=xh[:, side:4:2], op=ALU.mult), reads=[b_halo, b_xh], writes=[b_zh])
                for c in range(2):
                    p1, bp1 = mm(pool)
                    yield from proj_F_g(hs, p1, bp1, 0, 0 + c)
                    yield from proj_F_g(hs, p1, bp1, 256, 4 + c)
                    p2, bp2 = mm(pool)
                    yield from proj_F_g(hs, p2, bp2, 0, 2 + c)
                    yield from proj_F_g(hs, p2, bp2, 256, 6 + c)
                    xa_t, bxa = wk["xa"]; acc, bacc = wk["acc"]; tg, btg = wk["tg"]; s2, bs2 = wk["s2"]; m_, bm = wk["m"]
                    S.op("act", lambda e, p1=p1: e.activation(out=xa_t[:], in_=p1[:, 0:256], func=AF.Copy), reads=[bp1], writes=[bxa])
                    S.op("dve", lambda e, p1=p1: e.tensor_tensor(out=zb[:, 1:257], in0=p1[:, 256:512], in1=xa_t[:], op=ALU.mult), reads=[bp1, bxa], writes=[b_zb])
                    S.op("dve", lambda e, c=c: e.tensor_copy(out=zb[:, 0:258:257], in_=zh[:, 2 * c:2 * c + 2]), reads=[b_zh], writes=[b_zb])
                    S.op("act", lambda e, c=c: e.activation(out=acc[:], in_=zb[:, 1:257], func=AF.Identity, scale=cw[:, c, 1:2]), reads=[b_zb, b_cw], writes=[bacc])
                    S.op("dve", lambda e, c=c: e.scalar_tensor_tensor(out=acc[:], in0=zb[:, 0:256], scalar=cw[:, c, 0:1], in1=acc[:], op0=ALU.mult, op1=ALU.add), reads=[b_zb, b_cw, bacc], writes=[bacc])
                    S.op("dve", lambda e, c=c: e.scalar_tensor_tensor(out=acc[:], in0=zb[:, 2:258], scalar=cw[:, c, 2:3], in1=acc[:], op0=ALU.mult, op1=ALU.add), reads=[b_zb, b_cw, bacc], writes=[bacc])
                    S.op("act", lambda e, p2=p2: e.activation(out=tg[:], in_=p2[:, 256:512], func=AF.Tanh, scale=0.5), reads=[bp2], writes=[btg])
                    S.op("dve", lambda e, p2=p2: e.scalar_tensor_tensor(out=s2[:], in0=tg[:], scalar=1.0, in1=p2[:, 256:512], op0=ALU.add, op1=ALU.mult), reads=[btg, bp2], writes=[bs2])
                    S.op("dve", lambda e, p2=p2: e.tensor_tensor(out=m_[:], in0=p2[:, 0:256], in1=acc[:], op=ALU.mult), reads=[bp2, bacc], writes=[bm])
                    S.op("dve", lambda e, c=c: e.scalar_tensor_tensor(out=yt[:, c, :], in0=m_[:], scalar=0.5, in1=s2[:], op0=ALU.mult, op1=ALU.mult), reads=[bm, bs2], writes=[byt])
                    yield

            def B_conv(grp, u, hs, qs, hs_prev, hs_next):
                run(B_conv_gen(grp, u, hs, qs, hs_prev, hs_next))

            def B_gm(grp, u, hs, qs):
                yt, byt = yT[qs], b_yT[qs]
                for j in range(2):
                    xa_t, bxa = wk["xa"]; acc, bacc = wk["acc"]; tg, btg = wk["tg"]; s2, bs2 = wk["s2"]; m_, bm = wk["m"]
                    pvu, bpvu = proj_T(hs, j, 0, 512)
                    S.op("dve", lambda e, pvu=pvu: e.bn_stats(out=stt[2][:, 0, :], in_=pvu[:, 0:256]), reads=[bpvu], writes=[b_stt[2]])
                    S.op("dve", lambda e: e.bn_aggr(out=mv[2][:], in_=stt[2][:, 0, :]), reads=[b_stt[2]], writes=[b_mv[2]])
                    ln_small(2)
                    pgc, bpgc = proj_T(hs, j, 1280, 1792)
                    S.op("act", lambda e, pgc=pgc: e.activation(out=tgc[:], in_=pgc[:, :], func=AF.Tanh, scale=0.5), reads=[bpgc], writes=b_tgc_l)
                    S.op("dve", lambda e, pgc=pgc, j=j: e.scalar_tensor_tensor(out=sgc[qs][:, j, :], in0=tgc[:], scalar=1.0, in1=pgc[:, :], op0=ALU.add, op1=ALU.mult), reads=b_tgc_l + [bpgc], writes=[b_sgc[qs]])
                    pgb, bpgb = proj_T(hs, j, 512, 768)
                    S.op("act", lambda e, pvu=pvu: e.activation(out=vn3[:], in_=pvu[:, 0:256], func=AF.Identity, bias=nm[2][:], scale=rs[2][:]), reads=[bpvu, b_nm[2], b_rs[2]], writes=[b_vn3])
                    psv, bpsv = mm()
                    for g in range(4):
                        S.op("pe", lambda e, g=g, psv=psv: e.matmul(psv[:, g * 64:(g + 1) * 64], lhsT=wsT[:, g, :], rhs=vn3[:, g * 64:(g + 1) * 64], start=True, stop=True),
                             reads=[b_wsT, b_vn3], writes=[bpsv])
                    S.op("dve", lambda e, psv=psv: e.tensor_tensor(out=acc[:], in0=psv[:, 0:256], in1=glg[:], op=ALU.mult), reads=[bpsv, b_glg], writes=[bacc])
                    S.op("dve", lambda e: e.tensor_tensor(out=acc[:], in0=acc[:], in1=bsb[:], op=ALU.add), reads=[bacc, b_bsb], writes=[bacc])
                    S.op("dve", lambda e, pvu=pvu: e.tensor_tensor(out=m_[:], in0=pvu[:, 256:512], in1=acc[:], op=ALU.mult), reads=[bpvu, bacc], writes=[bm])
                    S.op("act", lambda e, pgb=pgb: e.activation(out=tg[:], in_=pgb[:, 0:256], func=AF.Tanh, scale=0.5), reads=[bpgb], writes=[btg])
                    S.op("dve", lambda e, pgb=pgb: e.scalar_tensor_tensor(out=s2[:], in0=tg[:], scalar=1.0, in1=pgb[:, 0:256], op0=ALU.add, op1=ALU.mult), reads=[btg, bpgb], writes=[bs2])
                    S.op("dve", lambda e: e.scalar_tensor_tensor(out=ybt[:], in0=m_[:], scalar=0.5, in1=s2[:], op0=ALU.mult, op1=ALU.mult), reads=[bm, bs2], writes=[b_ybt])
                    tr, b_tr = mmt()
                    for a in range(2):
                        S.op("pe", lambda e, a=a, tr=tr: e.transpose(out=tr[:, a * 128:(a + 1) * 128], in_=ybt[:, a * 128:(a + 1) * 128], identity=ident[:]), reads=[b_ybt, b_id], writes=[b_tr])
                    S.op("act", lambda e, j=j, tr=tr: e.activation(out=yt[:, 2:4, j * 128:(j + 1) * 128], in_=tr[:, 0:256].rearrange("p (a t) -> p a t", a=2), func=AF.Copy), reads=[b_tr], writes=[byt])

            def C_s_gen(grp, u, qs, chunks, j, pool=None):
                for ci, (kf, kb, vf, vb, tid) in enumerate(chunks):
                    SE, b_SE = mm(pool)
                    SO, b_SO = mm(pool)
                    for h in range(8):
                        pair, half = h // 2, h % 2
                        bank, bbank = (SE, b_SE) if half == 0 else (SO, b_SO)
                        S.op("pe", lambda e, kf=kf, pair=pair, half=half, bank=bank: e.matmul(
                            bank[:, pair * 128:(pair + 1) * 128], lhsT=kf(pair, half), rhs=QT[qs][64 * half:64 * half + 64, pair, j * 128:(j + 1) * 128], start=True, stop=True),
                            reads=[kb, b_QT[qs]], writes=[bbank])
                    for half in range(2):
                        bank, bbank = (SE, b_SE) if half == 0 else (SO, b_SO)
                        ei = 2 * ci + half
                        S.op("act", lambda e, bank=bank, ei=ei: e.activation(out=Eb[:, ei, :], in_=bank[:, :], func=AF.Exp, scale=SCALE), reads=[bbank], writes=[b_E[ei]])
                        if tid is not None:
                            S.op(("pool" if half == 1 else "dve"), lambda e, ei=ei, tid=tid, half=half: e.tensor_tensor(out=Eb[:, ei, :], in0=Eb[:, ei, :], in1=EB[:, tid, half * 512:(half + 1) * 512], op=ALU.mult),
                                 reads=[b_E[ei], b_EB], writes=[b_E[ei]])
                    yield

            def C_s(grp, u, qs, keyf, j):
                chunks = keyf(2 * u + j)
                run(C_s_gen(grp, u, qs, chunks, j))
                return chunks

            def C_pv(grp, u, qs, chunks, j):
                nck = len(chunks)
                for h in range(8):
                    pair, half = h // 2, h % 2
                    pb, bpb = pv[h // 4], b_pv[h // 4]
                    for ci, (kf, kb, vf, vb, tid) in enumerate(chunks):
                        ei = 2 * ci + half
                        S.op("pe", lambda e, ei=ei, pair=pair, vf=vf, h=h, pb=pb, ci=ci: e.matmul(
                            pb[:, (h % 4) * 65:(h % 4) * 65 + 65], lhsT=Eb[:, ei, pair * 128:(pair + 1) * 128], rhs=vf(h), start=(ci == 0), stop=(ci == nck - 1)),
                            reads=[b_E[ei], vb], writes=[bpb])
                for g2 in range(2):
                    pb, bpb = pv[g2], b_pv[g2]
                    pv3 = pb[:, 0:260].rearrange("p (h d) -> p h d", d=65)
                    S.op("dve", lambda e, pv3=pv3, g2=g2: e.reciprocal(out=rinv[:, 4 * g2:4 * g2 + 4], in_=pv3[:, :, 64]), reads=[bpb], writes=[b_rinv])
                    S.op("dve", lambda e, pv3=pv3, g2=g2: e.tensor_tensor(out=ob[:, 256 * g2:256 * g2 + 256].rearrange("p (h d) -> p h d", d=64), in0=pv3[:, :, 0:64],
                                                                            in1=rinv[:, 4 * g2:4 * g2 + 4].unsqueeze(2).to_broadcast([128, 4, 64]), op=ALU.mult),
                         reads=[bpb, b_rinv], writes=[b_ob])
                S.op("dve", lambda e: e.scalar_tensor_tensor(out=yct[:], in0=ob[:], scalar=0.5, in1=sgc[qs][:, j, :], op0=ALU.mult, op1=ALU.mult), reads=[b_ob, b_sgc[qs]], writes=[b_yct])

            def C_o(grp, u, qs, j):
                run(C_o_gen(grp, u, qs, j))

            def C_o_gen(grp, u, qs, j, pool=None):
                v = 0 if grp == "S" else 1
                yt, byt = yT[qs], b_yT[qs]
                tile = 2 * u + j
                tr, b_tr = mmt(pool)
                for a in range(4):
                    S.op("pe", lambda e, a=a: e.transpose(out=tr[:, a * 128:(a + 1) * 128], in_=yct[:, a * 128:(a + 1) * 128], identity=ident[:]), reads=[b_yct, b_id], writes=[b_tr])
                S.op("act", lambda e: e.activation(out=yt[:, 4:8, j * 128:(j + 1) * 128], in_=tr[:, 0:512].rearrange("p (a t) -> p a t", a=4), func=AF.Copy), reads=[b_tr], writes=[byt])
                yield
                ti = cnt["tU"] % 2
                cnt["tU"] += 1
                rd = [xbuf(grp, u)] if l > 0 else []
                S.dma("sp", lambda e: e.dma_start(out=tU[ti][:], in_=src_x(grp, tile)), reads=rd, writes=[b_tU[ti]], sem_key="xC%d" % ti)
                for n in range(2):
                    po, bpo = mm(pool)
                    for kc in range(8):
                        S.op("pe", lambda e, kc=kc, n=n, po=po: e.matmul(po[:, :], lhsT=yt[:, kc, j * 128:(j + 1) * 128], rhs=wo[:, kc, n * 512:(n + 1) * 512], start=(kc == 0), stop=(kc == 7)),
                             reads=[byt, b_wo], writes=[bpo])
                        if kc == 3:
                            yield
                    S.op("dve", lambda e, n=n, po=po: e.tensor_tensor(out=ob2[:], in0=po[:, :], in1=gate[:, v, n * 512:(n + 1) * 512], op=ALU.mult),
                         reads=[bpo, b_gate], writes=[b_ob2])
                    S.op("dve", lambda e, n=n: e.scalar_tensor_tensor(out=tU[ti][:, n * 512:(n + 1) * 512], in0=tU[ti][:, n * 512:(n + 1) * 512], scalar=ALPHA, in1=ob2[:], op0=ALU.mult, op1=ALU.add),
                         reads=[b_ob2, b_tU[ti]], writes=[b_tU[ti]])
                    yield
                for i in range(2):
                    S.op("dve", lambda e, i=i: e.bn_stats(out=stt[3][:, i, :], in_=tU[ti][:, i * 512:(i + 1) * 512]), reads=[b_tU[ti]], writes=[b_stt[3]])
                S.op("dve", lambda e: e.bn_aggr(out=mv[3][:], in_=stt[3][:].rearrange("p a b -> p (a b)")), reads=[b_stt[3]], writes=[b_mv[3]])
                ln_small(3)
                S.op("act", lambda e: e.activation(out=tU[ti][:], in_=tU[ti][:], func=AF.Identity, bias=nm[3][:], scale=rs[3][:]), reads=[b_tU[ti], b_nm[3], b_rs[3]], writes=[b_tU[ti]])
                S.op("pool", lambda e: e.tensor_tensor(out=tU[ti][:], in0=tU[ti][:], in1=lng[:], op=ALU.mult), reads=[b_tU[ti], b_lng], writes=[b_tU[ti]])
                S.op("pool", lambda e: e.tensor_tensor(out=tU[ti][:], in0=tU[ti][:], in1=lnb[:], op=ALU.add), reads=[b_tU[ti], b_lnb], writes=[b_tU[ti]])
                S.dma("sp", lambda e: e.dma_start(out=dst_x(grp, tile), in_=tU[ti][:]), reads=[b_tU[ti]], writes=([] if last else [xbuf(grp, u)]),
                      sem_key="xo%d" % ti, final=True)

            def ctx_chunks():
                out = []
                for a in range(2):
                    out.append((lambda pair, half, a=a: ckT[64 * half:64 * half + 64, pair, a * 128:(a + 1) * 128], b_ckT,
                                lambda h, a=a: cV[:, a, h, :], b_cV, None))
                return out

            nP = ns_units - l
            nR = ns_units - 1 - l

            def keyf_S(tile):
                out = []
                for (kt, tid) in attn_keys(tile):
                    ku, kj = kt // 2, kt % 2
                    ksl = ku % NK
                    out.append((lambda pair, half, ksl=ksl, kj=kj: KT[ksl][64 * half:64 * half + 64, pair, kj * 128:(kj + 1) * 128], b_KT[ksl],
                                lambda h, ksl=ksl, kj=kj: Vr[:, 2 * ksl + kj, h, :], b_V[2 * ksl + kj], tid))
                return ctx_chunks() + out

            def pipeline(grp, nP, nR, keyf_of, base, halo):
                def hs_(u): return (base + u) % NH
                def ks_(u): return (base + u) % NK
                def qs_(u): return (base + u) % 2
                def hprev(u): return hs_(u - 1) if (halo and u > 0) else None
                def hnext(u): return hs_(u + 1) if halo else None
                for u0 in range(min(3, nP)):
                    A_pre(grp, u0); A_pe(grp, u0, hs_(u0))
                B_kv(grp, 0, hs_(0), ks_(0)); B_q(grp, 0, hs_(0), qs_(0)); B_conv(grp, 0, hs_(0), qs_(0), None, hnext(0)); B_gm(grp, 0, hs_(0), qs_(0))
                pend = None
                for i in range(nP):
                    u1 = i + 1
                    full1 = u1 < nR
                    if i + 3 < nP:
                        xis_ = A_dma(grp, i + 3)
                    if u1 < nP:
                        B_kv(grp, u1, hs_(u1), ks_(u1))
                    if pend is not None:
                        C_o(*pend)
                        pend = None
                    if i + 3 < nP:
                        A_ln(grp, i + 3, xis_)
                    g_b = iter(())
                    if full1:
                        def g_b_f(u1=u1):
                            yield from B_q_gen(grp, u1, hs_(u1), qs_(u1), "G")
                            yield from B_conv_gen(grp, u1, hs_(u1), qs_(u1), hprev(u1), hnext(u1), "G")
                        g_b = g_b_f()
                    if i < nR:
                        kf = keyf_of(i)
                        ch0 = kf(2 * i)
                        zipper(C_s_gen(grp, i, qs_(i), ch0, 0, "S"), g_b, 1, 2)
                        C_pv(grp, i, qs_(i), ch0, 0)
                        ch1 = kf(2 * i + 1)
                        zipper(C_s_gen(grp, i, qs_(i), ch1, 1, "S"), C_o_gen(grp, i, qs_(i), 0, "G"), 2, 1)
                    else:
                        run(g_b)
                    if i + 3 < nP:
                        A_pe(grp, i + 3, hs_(i + 3))
                    if full1:
                        B_gm(grp, u1, hs_(u1), qs_(u1))
                    if i < nR:
                        C_pv(grp, i, qs_(i), ch1, 1)
                        pend = (grp, i, qs_(i), 1)
                if pend is not None:
                    C_o(*pend)

            def keyf_P_of(u):
                ks = (nP_s + u) % NK
                def keyf(tile):
                    out = []
                    for kj in range(2):
                        out.append((lambda pair, half, kj=kj: KT[ks][64 * half:64 * half + 64, pair, kj * 128:(kj + 1) * 128], b_KT[ks],
                                    lambda h, kj=kj: Vr[:, 2 * ks + kj, h, :], b_V[2 * ks + kj], None))
                    return out
                return keyf

            nP_s = nP
            if not SKIPS:
                pipeline("S", nP, nR, lambda u: keyf_S, 0, True)
            if not SKIPP:
                pipeline("P", n_pseq, n_pseq, keyf_P_of, nP_s, False)

        for l_ in range(depth):
            layer(l_)

        S.emit()
    return nc


def _tables(rpb, flip):
    reps = [(4, 2), (4, 3), (4, 4), (4, 5), (4, 6), (0, 2), (0, 3)]
    out = np.empty((DEPTH, 7, 128, 2, 4, 128), np.float32)
    p = np.arange(128)
    for tid, (t, u) in enumerate(reps):
        ql = t * 128 + p
        kl = u * 128 + p
        qg = 4095 - ql if flip else ql
        kg = 4095 - kl if flip else kl
        qr, qc = qg // 64, qg % 64
        kr, kc = kg // 64, kg % 64
        rs_ = np.clip(qr - 4, 0, 56)
        cs_ = np.clip(qc - 8, 0, 48)
        valid = ((kr[:, None] >= rs_[None, :]) & (kr[:, None] < rs_[None, :] + 8)
                 & (kc[:, None] >= cs_[None, :]) & (kc[:, None] < cs_[None, :] + 16))
        dr = np.clip(kr[:, None] - qr[None, :] + 7, 0, 14)
        dc = np.clip(kc[:, None] - qc[None, :], -15, 15) + 15
        g = rpb[:, :, dr, dc]
        g = np.where(valid[None, None], g, np.float32(NEG))
        g = g.transpose(0, 2, 1, 3).reshape(DEPTH, 128, 4, 2, 128).transpose(0, 1, 3, 2, 4)
        out[:, tid] = g
    return np.ascontiguousarray(out.reshape(DEPTH, 7, 128, 1024))


_NC_CACHE = {}


def kernel(x_prompt, x_sample, cache_k, cache_v, c, c_ctx, w_ada, b_ada, w_in, conv_w,
           gmlp_ln_g, gmlp_ln_b, w_spatial, b_spatial, rpb, w_out, ln_g, ln_b):
    f = lambda a: np.ascontiguousarray(np.asarray(a, dtype=np.float32))
    x_prompt, x_sample, cache_k, cache_v, c, c_ctx = map(f, (x_prompt, x_sample, cache_k, cache_v, c, c_ctx))
    w_ada, b_ada, w_in, conv_w, gmlp_ln_g, gmlp_ln_b = map(f, (w_ada, b_ada, w_in, conv_w, gmlp_ln_g, gmlp_ln_b))
    w_spatial, b_spatial, rpb, w_out, ln_g, ln_b = map(f, (w_spatial, b_spatial, rpb, w_out, ln_g, ln_b))
    if "nc" not in _NC_CACHE:
        _NC_CACHE["nc"] = build()
    nc = _NC_CACHE["nc"]
    ident = np.eye(128, dtype=np.float32)
    tabs = [_tables(rpb, 0), _tables(rpb, 1)]
    in_maps = []
    for i in range(8):
        b, flip = i // 2, i % 2
        xs = x_sample[b][::-1] if flip else x_sample[b]
        xp = x_prompt[4 * i:4 * i + 4]
        if flip:
            xp = xp[:, ::-1]
        ws = w_spatial[:, :, ::-1, ::-1] if flip else w_spatial
        bs = b_spatial[:, :, ::-1] if flip else b_spatial
        cwv = conv_w[:, ::-1, :] if flip else conv_w
        in_maps.append({
            "xp": f(xp.reshape(1024, D)),
            "xs": f(xs[0:3072]),
            "c2": f(np.stack([c[b], c_ctx])),
            "ckT": f(cache_k[b].reshape(DEPTH, 256, 512).transpose(0, 2, 1)),
            "cv": f(cache_v[b].reshape(DEPTH, 256, 512)),
            "w_ada": w_ada, "b_ada": b_ada, "w_in": w_in, "w_out": w_out,
            "conv_w": f(cwv), "gmlp_ln_g": gmlp_ln_g, "gmlp_ln_b": gmlp_ln_b,
            "wsT": f(ws.transpose(0, 3, 1, 2)), "bsT": f(bs.transpose(0, 2, 1)),
            "ebt": tabs[flip], "ln_g": ln_g, "ln_b": ln_b, "ident": ident,
        })
    res = run_bass_kernel_spmd(nc, in_maps, core_ids=list(range(8))).results
    y_prompt = np.empty((32, 256, D), np.float32)
    y_sample = np.empty((4, 4096, D), np.float32)
    new_k = np.empty((32, DEPTH, 256, 8, 64), np.float32)
    new_v = np.empty((32, DEPTH, 256, 8, 64), np.float32)
    for i in range(8):
        b, flip = i // 2, i % 2
        r = res[i]
        yp = np.asarray(r["yp"]).reshape(4, 256, D)
        nk = np.asarray(r["nk"]).reshape(4, DEPTH, 256, 8, 64)
        nv = np.asarray(r["nv"]).reshape(4, DEPTH, 256, 8, 64)
        ys = np.asarray(r["ys"])
        if flip:
            yp = yp[:, ::-1]
            nk = nk[:, :, ::-1]
            nv = nv[:, :, ::-1]
            y_sample[b, 2048:] = ys[::-1]
        else:
            y_sample[b, :2048] = ys
        y_prompt[4 * i:4 * i + 4] = yp
        new_k[4 * i:4 * i + 4] = nk
        new_v[4 * i:4 * i + 4] = nv
    return (y_prompt, y_sample, new_k, new_v)
```

```python
import contextlib
import numpy as np
import concourse.bass as bass
import concourse.mybir as mybir
from concourse.bass_utils import run_bass_kernel_spmd

F32 = mybir.dt.float32
BF16 = mybir.dt.bfloat16
ALU = mybir.AluOpType
AF = mybir.ActivationFunctionType

D = 1024
DEPTH = 4
ALPHA = (2 * DEPTH) ** 0.25
SCALE = 64 ** -0.5
NEG = -30000.0
ENGS = ("pe", "act", "dve", "pool", "sp")
MAXOPS = 10 ** 9
SCHEDULE = True
PRIO_W = 3.0
TOKEN_ALL = False
EB_MODE = "split"


def _eb_eng(ci, half):
    if EB_MODE == "dve":
        return "dve"
    if EB_MODE == "pool":
        return "pool"
    if EB_MODE == "p3":
        return "pool" if (half == 1 and ci % 2 == 0) else "dve"
    return "pool" if half == 1 else "dve"
SKIPP = False
SKIPS = False


class Buf:
    __slots__ = ("name", "excl", "last_w", "readers")

    def __init__(self, name, excl=False):
        self.name = name
        self.excl = excl
        self.last_w = None
        self.readers = []


class Op:
    __slots__ = ("eng", "fn", "is_dma", "sem_key", "deps", "signal", "count", "dma_count", "idx", "reads", "writes",
                 "preds", "succs", "npred", "cost", "fin", "pos", "mode", "start", "prio")

    def __init__(self, eng, fn, is_dma, sem_key):
        self.eng = eng
        self.fn = fn
        self.is_dma = is_dma
        self.sem_key = sem_key
        self.deps = []
        self.signal = False
        self.count = 0
        self.dma_count = 0
        self.mode = 0


class _Rec:
    def __init__(self):
        self.name = None
        self.kw = {}
        self.args = ()

    def __getattr__(self, name):
        def f(*a, **k):
            self.name, self.args, self.kw = name, a, k
            return self
        return f


def _nfree(ap):
    n = 1
    for d in tuple(ap.shape)[1:]:
        n *= int(d)
    return n


def _is_psum_or_f32(ap):
    try:
        return ("psum" in str(ap.space).lower()) or (ap.dtype == F32)
    except Exception:
        return True


def _cost_ns(op):
    r = _Rec()
    try:
        op.fn(r)
    except Exception:
        return 300.0
    kw, a, nm_ = r.kw, r.args, r.name
    out = kw.get("out", a[0] if a else None)
    try:
        if op.is_dma:
            byts = _nfree(out) * int(tuple(out.shape)[0]) * 4
            return 2000.0 + byts / 200.0
        if op.eng == "pe":
            if nm_ == "transpose":
                return 75.0
            rhs = kw.get("rhs", a[2] if len(a) > 2 else None)
            lhsT = kw.get("lhsT", a[1] if len(a) > 1 else None)
            if int(tuple(lhsT.shape)[0]) <= 64:
                op.mode = 64
                return max(35.0, 15.0 + _nfree(rhs) * 0.5)
            return max(45.0, 30.0 + _nfree(rhs) * 0.45)
        n = _nfree(out)
        if op.eng == "act":
            if kw.get("func") == AF.Exp:
                return 150.0 + n * 0.69
            return 240.0 + n * 0.95
        if op.eng == "pool":
            if nm_ == "memset":
                return 150.0 + n * 0.9
            if n <= 8:
                return 480.0
            return 250.0 + n * 2.1
        if nm_ == "reciprocal":
            return 1000.0
        if nm_ == "bn_aggr":
            return 250.0
        if nm_ == "bn_stats":
            return 160.0 + _nfree(kw.get("in_")) * 1.0
        if n <= 8:
            return 550.0
        srcs = [kw.get(k) for k in ("in0", "in1", "in_") if kw.get(k) is not None]
        slow = any(_is_psum_or_f32(x) for x in srcs)
        if nm_ in ("tensor_tensor", "scalar_tensor_tensor"):
            return 160.0 + n * (1.04 if slow else 1.5)
        return 200.0 + n * (1.04 if slow else 0.55)
    except Exception:
        return 300.0


class Sched:
    def __init__(self, nc):
        self.nc = nc
        self.all = []
        self.ops = {e: [] for e in ENGS}
        self.dma_counts = {}
        self.dma_keys = []
        self.final_waits = []
        self.dve_token = Buf("dve_token")

    def _add(self, eng, fn, reads, writes, is_dma=False, sem_key=None):
        op = Op(eng, fn, is_dma, sem_key)
        if eng == "dve" and not is_dma:
            r_ = _Rec()
            try:
                fn(r_)
            except Exception:
                pass
            if TOKEN_ALL or r_.name not in ("tensor_tensor", "scalar_tensor_tensor"):
                reads = reads + [self.dve_token]
        op.reads, op.writes = reads, writes
        op.idx = len(self.all)
        self.all.append(op)
        if is_dma:
            if sem_key not in self.dma_counts:
                self.dma_counts[sem_key] = 0
                self.dma_keys.append(sem_key)
            self.dma_counts[sem_key] += 16
            op.dma_count = self.dma_counts[sem_key]
        return op

    def op(self, eng, fn, reads=(), writes=(), excl_dve=False):
        w = list(writes)
        if excl_dve:
            w.append(self.dve_token)
        return self._add(eng, fn, list(reads), w)

    def dma(self, eng, fn, reads=(), writes=(), sem_key=None, final=False):
        o = self._add(eng, fn, list(reads), list(writes), is_dma=True, sem_key=sem_key)
        if final:
            self.final_waits.append(o)
        return o

    def _edges(self):
        last_key = {}
        for op in self.all:
            preds = {}
            for b in op.reads:
                if b.last_w is not None:
                    preds[id(b.last_w)] = b.last_w
                if b.excl:
                    for r in b.readers:
                        if r.eng != op.eng:
                            preds[id(r)] = r
            for b in op.writes:
                if b.last_w is not None:
                    preds[id(b.last_w)] = b.last_w
                for r in b.readers:
                    preds[id(r)] = r
            if op.is_dma:
                p = last_key.get(op.sem_key)
                if p is not None:
                    preds[id(p)] = p
                last_key[op.sem_key] = op
            preds.pop(id(op), None)
            for b in op.reads:
                b.readers.append(op)
            for b in op.writes:
                b.last_w = op
                b.readers = []
            op.preds = list(preds.values())
            op.succs = []
        for op in self.all:
            for p in op.preds:
                p.succs.append(op)

    def _schedule(self):
        SEM = 120.0
        for op in self.all:
            op.cost = _cost_ns(op)
            op.npred = len(op.preds)
            op.fin = None
        rank = {}
        for op in reversed(self.all):
            r_ = 0.0
            for s_ in op.succs:
                r2 = rank[id(s_)]
                if r2 > r_:
                    r_ = r2
            rank[id(op)] = r_ + op.cost + SEM
        tot = max(rank.values())
        n_all = float(len(self.all))
        for op in self.all:
            op.prio = op.idx / n_all - PRIO_W * rank[id(op)] / tot
        cand = {e: [] for e in ENGS}
        ready = {}
        for op in self.all:
            if op.npred == 0:
                cand[op.eng].append(op)
                ready[id(op)] = 0.0
        free = {e: 0.0 for e in ENGS}
        order = {e: [] for e in ENGS}
        pe_mode = [0]
        left = len(self.all)
        while left:
            best = None
            for e in ENGS:
                cl = cand[e]
                if not cl:
                    continue
                t = free[e]
                pick = None
                if e == "pe":
                    for o in cl:
                        if ready[id(o)] <= t and o.mode == pe_mode[0]:
                            if pick is None or o.prio < pick.prio:
                                pick = o
                if pick is None:
                    for o in cl:
                        if ready[id(o)] <= t:
                            if pick is None or o.prio < pick.prio:
                                pick = o
                if pick is not None:
                    st_ = t
                else:
                    for o in cl:
                        rt = ready[id(o)]
                        if pick is None or rt < ready[id(pick)] or (rt == ready[id(pick)] and o.idx < pick.idx):
                            pick = o
                    st_ = ready[id(pick)]
                if best is None or st_ < best[0] or (st_ == best[0] and pick.idx < best[1].idx):
                    best = (st_, pick)
            st_, o = best
            e = o.eng
            cand[e].remove(o)
            if e == "pe":
                if o.mode != pe_mode[0]:
                    st_ += 120.0
                pe_mode[0] = o.mode
            if o.is_dma:
                free[e] = st_ + 60.0
            else:
                free[e] = st_ + o.cost
            o.fin = st_ + o.cost
            o.start = st_
            o.pos = len(order[e])
            order[e].append(o)
            left -= 1
            for s_ in o.succs:
                s_.npred -= 1
                if s_.npred == 0:
                    pe_s = (s_.eng == "pe" and not s_.is_dma)
                    ready[id(s_)] = max((p.start if (pe_s and p.eng == "pe" and not p.is_dma) else p.fin + SEM) for p in s_.preds)
                    cand[s_.eng].append(s_)
        self.model_ns = max(o.fin for o in self.all)
        return order

    def emit(self):
        nc = self.nc
        self._edges()
        if SCHEDULE:
            self.ops = self._schedule()
        else:
            self.ops = {e: [o for o in self.all if o.eng == e] for e in ENGS}
            for e in ENGS:
                for i, o in enumerate(self.ops[e]):
                    o.pos = i
        for e in ENGS:
            for o in self.ops[e]:
                lastp = {}
                dl = []
                for p in o.preds:
                    if p.is_dma:
                        dl.append(p)
                        continue
                    if p.eng == "pe" and o.eng == "pe" and not o.is_dma:
                        continue
                    q = lastp.get(p.eng)
                    if q is None or p.pos > q.pos:
                        lastp[p.eng] = p
                o.deps = dl + list(lastp.values())
                for p in lastp.values():
                    p.signal = True
        for e in ENGS:
            c = 0
            for o in self.ops[e]:
                if (not o.is_dma) and o.signal:
                    c += 1
                    o.count = c
        with contextlib.ExitStack() as st:
            esem = {e: st.enter_context(nc.semaphore("s_" + e)) for e in ENGS}
            dsem = {k: st.enter_context(nc.semaphore("d_" + str(k))) for k in self.dma_keys}
            block = st.enter_context(nc.Block())
            engobj = {"pe": "tensor", "act": "scalar", "dve": "vector", "pool": "gpsimd", "sp": "sync"}

            def make(e):
                def body(eng):
                    waited = {}
                    for o in self.ops[e]:
                        for d in o.deps:
                            if d.is_dma:
                                s, v, key = dsem[d.sem_key], d.dma_count, ("d", d.sem_key)
                            else:
                                s, v, key = esem[d.eng], d.count, ("e", d.eng)
                            if waited.get(key, 0) >= v:
                                continue
                            waited[key] = v
                            eng.wait_ge(s, v)
                        ins = o.fn(eng)
                        if o.is_dma:
                            ins.then_inc(dsem[o.sem_key], 16)
                        elif o.signal:
                            ins.then_inc(esem[e], 1)
                    if e == "sp":
                        fin = {}
                        for o in self.final_waits:
                            fin[o.sem_key] = max(fin.get(o.sem_key, 0), o.dma_count)
                        for k, v in fin.items():
                            if waited.get(("d", k), 0) >= v:
                                continue
                            eng.wait_ge(dsem[k], v)

                return body

            for e in ENGS:
                getattr(block, engobj[e])(make(e))


def attn_keys(t):
    if t >= 2:
        return [(t + d, d + 2) for d in (-2, -1, 0, 1, 2)]
    if t == 0:
        return [(0, 2), (1, 3), (2, 5), (3, 6)]
    return [(0, 1), (1, 2), (2, 3), (3, 5)]


def build(depth=DEPTH, n_pseq=4, ns_units=12):
    nc = bass.Bass("TRN2", target_bir_lowering=False)

    def din(name, shape):
        return nc.dram_tensor(name, shape, F32, kind="ExternalInput").ap()

    def dout(name, shape):
        return nc.dram_tensor(name, shape, F32, kind="ExternalOutput").ap()

    xp_d = din("xp", [n_pseq * 256, D])
    xs_d = din("xs", [3072, D])
    c2_d = din("c2", [2, D])
    ckT_d = din("ckT", [DEPTH, 512, 256])
    cv_d = din("cv", [DEPTH, 256, 512])
    wada_d = din("w_ada", [DEPTH, D, 3 * D])
    bada_d = din("b_ada", [DEPTH, 3 * D])
    win_d = din("w_in", [DEPTH, D, 3840])
    wout_d = din("w_out", [DEPTH, D, D])
    convw_d = din("conv_w", [DEPTH, 3, 256])
    glg_d = din("gmlp_ln_g", [DEPTH, 256])
    glb_d = din("gmlp_ln_b", [DEPTH, 256])
    wsT_d = din("wsT", [DEPTH, 128, 4, 128])
    bsT_d = din("bsT", [DEPTH, 128, 4])
    ebt_d = din("ebt", [DEPTH, 7, 128, 1024])
    lng_d = din("ln_g", [DEPTH, D])
    lnb_d = din("ln_b", [DEPTH, D])
    id_d = din("ident", [128, 128])
    yp_d = dout("yp", [n_pseq * 256, D])
    ys_d = dout("ys", [2048, D])
    nk_d = dout("nk", [n_pseq, DEPTH, 256, 512])
    nv_d = dout("nv", [n_pseq, DEPTH, 256, 512])
    sxp_d = nc.dram_tensor("sxp", [n_pseq * 256, D], F32).ap()
    sxs_d = nc.dram_tensor("sxs", [3072, D], F32).ap()

    S = Sched(nc)
    st = contextlib.ExitStack()
    with st:
        def sb(name, shape, dt=F32):
            return st.enter_context(nc.sbuf_tensor(name, shape, dt))

        def ps(name, shape, dt=F32):
            return st.enter_context(nc.psum_tensor(name, shape, dt))

        wF = sb("wF", [128, 8, 1536], BF16); b_wF = Buf("wF"); b_wFq = Buf("wFq")
        wT = sb("wT", [128, 8, 2304], BF16); b_wTk = Buf("wTk"); b_wTv = Buf("wTv"); b_wTvu = Buf("wTvu"); b_wTgb = Buf("wTgb"); b_wTgc = Buf("wTgc")
        wo = sb("wo", [128, 8, 1024], BF16); b_wo = Buf("wo")
        NH = 4
        hT = [sb("hT%d" % i, [128, 8, 256], BF16) for i in range(NH)]; b_hT = [Buf("hT%d" % i) for i in range(NH)]
        xA = [sb("xA%d" % i, [128, D]) for i in range(2)]; b_xA = [Buf("xA%d" % i) for i in range(2)]
        xn = [sb("xn%d" % i, [128, D], BF16) for i in range(2)]; b_xn = [Buf("xn%d" % i) for i in range(2)]
        QT = [sb("QT%d" % i, [128, 4, 256], BF16) for i in range(2)]; b_QT = [Buf("QT%d" % i) for i in range(2)]
        NK = 3
        KT = [sb("KT%d" % i, [128, 4, 256], BF16) for i in range(NK)]; b_KT = [Buf("KT%d" % i) for i in range(NK)]
        Vr = sb("Vr", [128, 2 * NK, 8, 65], BF16); b_V = [Buf("V%d" % i) for i in range(2 * NK)]
        yT = [sb("yT%d" % i, [128, 8, 256], BF16) for i in range(2)]; b_yT = [Buf("yT%d" % i) for i in range(2)]
        sgc = [sb("sgc%d" % i, [128, 2, 512], BF16) for i in range(2)]; b_sgc = [Buf("sgc%d" % i) for i in range(2)]
        ktok = sb("ktok", [128, 512], BF16); b_ktok = Buf("ktok")
        ckT = sb("ckT_sb", [128, 4, 256], BF16); b_ckT = Buf("ckT")
        cV = sb("cV", [128, 2, 8, 65], BF16); b_cV = Buf("cV")
        EB = sb("EB", [128, 7, 1024], BF16); b_EB = Buf("EB")
        tU = [sb("tU%d" % i, [128, D]) for i in range(2)]; b_tU = [Buf("tU%d" % i) for i in range(2)]
        Eb = sb("Eb", [128, 14, 512], BF16); b_E = [Buf("E%d" % i) for i in range(14)]
        c2f = sb("c2f", [128, 2, 8]); b_c2f = Buf("c2f")
        c2t = sb("c2t", [128, 2, 8]); b_c2t = Buf("c2t")
        scT = sb("scT", [128, 8, 2], BF16); b_scT = Buf("scT")
        sc1p = sb("sc1p", [128, 2, 8]); b_sc1p = Buf("sc1p")
        shp = sb("shp", [128, 2, 8]); b_shp = Buf("shp")
        badc = sb("badc", [128, 16]); b_badc = Buf("badc")
        gate = sb("gate", [128, 2, D]); b_gate = Buf("gate")
        lng = sb("lng", [128, D]); b_lng = Buf("lng")
        lnb = sb("lnb", [128, D]); b_lnb = Buf("lnb")
        glg = sb("glg", [128, 256]); b_glg = Buf("glg")
        glb = sb("glb", [128, 256]); b_glb = Buf("glb")
        bsT = sb("bsT_sb", [128, 4]); b_bsT = Buf("bsT")
        bsb = sb("bsb", [128, 256]); b_bsb = Buf("bsb")
        rsw = sb("rsw", [128, 4]); b_rsw = Buf("rsw")
        ones1 = sb("ones1", [128, 2], BF16); b_ones1 = Buf("ones1")
        wsT = sb("wsT_sb", [128, 4, 128], BF16); b_wsT = Buf("wsT")
        cw = sb("cw", [128, 2, 3]); b_cw = Buf("cw")
        ident = sb("ident_sb", [128, 128], BF16); b_id = Buf("ident")
        nh = sb("nh", [128, 2]); b_nh = Buf("nh")
        stt = [sb("stt%d" % i, [128, 2, 6]) for i in range(4)]
        mv = [sb("mv%d" % i, [128, 2]) for i in range(4)]
        ve = [sb("ve%d" % i, [128, 1]) for i in range(4)]
        rs = [sb("rs%d" % i, [128, 1]) for i in range(4)]
        nm = [sb("nm%d" % i, [128, 1]) for i in range(4)]
        b_stt = [Buf("stt%d" % i) for i in range(4)]
        b_mv = [Buf("mv%d" % i) for i in range(4)]
        b_ve = [Buf("ve%d" % i) for i in range(4)]
        b_rs = [Buf("rs%d" % i) for i in range(4)]
        b_nm = [Buf("nm%d" % i) for i in range(4)]
        wkall = sb("wkall", [128, 4, 256])
        wk = {}
        for i_, nme in enumerate(("xa", "acc", "tg", "s2")):
            wk[nme] = (wkall[:, i_, :], Buf("wk_" + nme))
        wk["m"] = wk["xa"]
        tgc = wkall[:, 2:4, :].rearrange("p a t -> p (a t)")
        b_tgc_l = [wk["tg"][1], wk["s2"][1]]
        zb = sb("zb", [128, 258]); b_zb = Buf("zb")
        xh = sb("xh", [128, 4]); b_xh = Buf("xh")
        zh = sb("zh", [128, 4]); b_zh = Buf("zh")
        vn3 = sb("vn3", [128, 256], BF16); b_vn3 = Buf("vn3")
        ybt = sb("ybt", [128, 256], BF16); b_ybt = Buf("ybt")
        rinv = sb("rinv", [128, 8]); b_rinv = Buf("rinv")
        ob = sb("ob", [128, 512]); b_ob = Buf("ob")
        ob2 = sb("ob2", [128, 512]); b_ob2 = Buf("ob2")
        kst = ob2; b_kst = b_ob2
        vst = ob; b_vst = b_ob
        yct = sb("yct", [128, 512], BF16); b_yct = Buf("yct")

        NMM = 6
        mmb = [ps("mm%d" % i, [128, 512]) for i in range(NMM)]; b_mm = [Buf("mm%d" % i, True) for i in range(NMM)]
        pv = [ps("pv%d" % i, [128, 512]) for i in range(2)]; b_pv = [Buf("pv0", True), Buf("pv1", True)]
        mm_i = [0]

        pool_banks = {None: [0, 1, 2, 3, 4, 5], "S": [2, 3, 4, 5], "G": [0, 1]}
        pool_i = {None: 0, "S": 0, "G": 0}

        def mm(pool=None):
            bl = pool_banks[pool]
            i = bl[pool_i[pool] % len(bl)]
            pool_i[pool] += 1
            return mmb[i], b_mm[i]

        def mmt(pool=None):
            t, b = mm(pool)
            return t[:].bitcast(BF16), b

        def ln_small(site, eng_after="dve"):
            S.op("dve", lambda e: e.tensor_scalar(out=ve[site][:], in0=mv[site][:, 1:2], scalar1=1e-5, scalar2=None, op0=ALU.add),
                 reads=[b_mv[site]], writes=[b_ve[site]])
            S.op("pool", lambda e: e.tensor_tensor(out=rs[site][:], in0=ve[site][:], in1=nh[:, 0:1], op=ALU.pow),
                 reads=[b_ve[site], b_nh], writes=[b_rs[site]], excl_dve=True)
            S.op("dve", lambda e: e.scalar_tensor_tensor(out=nm[site][:], in0=mv[site][:, 0:1], scalar=-1.0, in1=rs[site][:], op0=ALU.mult, op1=ALU.mult),
                 reads=[b_mv[site], b_rs[site]], writes=[b_nm[site]])

        S.dma("pool", lambda e: e.dma_start(out=ident[:], in_=id_d), writes=[b_id], sem_key="c_id")
        S.op("pool", lambda e: e.memset(nh[:], -0.5), writes=[b_nh])
        S.op("pool", lambda e: e.memset(ones1[:], 1.0), writes=[b_ones1])
        S.op("pool", lambda e: e.memset(Vr[:].rearrange("p a h d -> p (a h d)"), 1.0), writes=b_V)
        S.op("pool", lambda e: e.memset(cV[:].rearrange("p a h d -> p (a h d)"), 1.0), writes=[b_cV])
        for v in range(2):
            S.dma("sp", lambda e, v=v: e.dma_start(out=c2f[:, v, :], in_=c2_d[v, :].rearrange("(k p) -> p k", p=128), allow_slow_non_contiguous=True), writes=[b_c2f], sem_key="c_c2")
        S.op("act", lambda e: e.activation(out=c2t[:], in_=c2f[:], func=AF.Tanh, scale=0.5), reads=[b_c2f], writes=[b_c2t])
        S.op("dve", lambda e: e.scalar_tensor_tensor(out=c2t[:], in0=c2t[:], scalar=1.0, in1=c2f[:], op0=ALU.add, op1=ALU.mult), reads=[b_c2f, b_c2t], writes=[b_c2t])
        S.op("dve", lambda e: e.tensor_scalar(out=scT[:].rearrange("p k v -> p v k"), in0=c2t[:], scalar1=0.5, scalar2=None, op0=ALU.mult), reads=[b_c2t], writes=[b_scT])
        b_sxs = [Buf("sxs%d" % u) for u in range(12)]
        b_sxp = [Buf("sxp%d" % u) for u in range(n_pseq)]

        def layer(l):
            last = (l == depth - 1)
            def wdma(dst, src, bufd, key):
                S.dma("pool", lambda e: e.dma_start(out=dst, in_=src.rearrange("(k p) n -> p k n", p=128)), writes=[bufd], sem_key=key)
            wdma(wT[:, :, 1792:2304], win_d[l][:, 2304:2816], b_wTk, "w_Tk")
            wdma(wT[:, :, 768:1280], win_d[l][:, 2816:3328], b_wTv, "w_Tv")
            for c_ in range(2):
                S.dma("sp", lambda e, c_=c_: e.dma_start(out=cw[:, c_, :], in_=convw_d[l][:, c_ * 128:(c_ + 1) * 128].rearrange("j p -> p j"), allow_slow_non_contiguous=True), writes=[b_cw], sem_key="p_cw")
            S.dma("sp", lambda e: e.dma_start(out=glg[:], in_=glg_d[l, :].partition_broadcast(128)), writes=[b_glg], sem_key="p_glg")
            S.dma("sp", lambda e: e.dma_start(out=glb[:], in_=glb_d[l, :].partition_broadcast(128)), writes=[b_glb], sem_key="p_glb")
            S.dma("sp", lambda e: e.dma_start(out=lng[:], in_=lng_d[l, :].partition_broadcast(128)), writes=[b_lng], sem_key="p_lng")
            S.dma("sp", lambda e: e.dma_start(out=lnb[:], in_=lnb_d[l, :].partition_broadcast(128)), writes=[b_lnb], sem_key="p_lnb")
            S.dma("sp", lambda e: e.dma_start(out=bsT[:], in_=bsT_d[l]), writes=[b_bsT], sem_key="p_bsT")
            S.op("dve", lambda e: e.tensor_copy(out=bsb[:].rearrange("p (g c) -> p g c", g=4), in_=bsT[:, :].unsqueeze(2).to_broadcast([128, 4, 64])), reads=[b_bsT], writes=[b_bsb])
            S.dma("pool", lambda e: e.dma_start(out=wsT[:], in_=wsT_d[l]), writes=[b_wsT], sem_key="p_wsT")
            prs, bprs = mm()
            for g in range(4):
                S.op("pe", lambda e, g=g, prs=prs: e.matmul(prs[:, g:g + 1], lhsT=wsT[:, g, :], rhs=ones1[:, 0:1], start=True, stop=True), reads=[b_wsT, b_ones1], writes=[bprs])
            S.op("dve", lambda e, prs=prs: e.tensor_copy(out=rsw[:], in_=prs[:, 0:4]), reads=[bprs], writes=[b_rsw])
            S.op("dve", lambda e: e.tensor_tensor(out=glb[:].rearrange("p (g c) -> p g c", g=4), in0=glb[:].rearrange("p (g c) -> p g c", g=4), in1=rsw[:, :].unsqueeze(2).to_broadcast([128, 4, 64]), op=ALU.mult),
                 reads=[b_glb, b_rsw], writes=[b_glb])
            S.op("dve", lambda e: e.tensor_tensor(out=bsb[:], in0=bsb[:], in1=glb[:], op=ALU.add), reads=[b_bsb, b_glb], writes=[b_bsb])
            S.dma("sp", lambda e: e.dma_start(out=badc[:], in_=bada_d[l, 0:2048].rearrange("(j p) -> p j", p=128), allow_slow_non_contiguous=True), writes=[b_badc], sem_key="p_badc")
            wad = Eb[:, 0:8, :]
            Ssil = Eb[:, 8:12, :].rearrange("p (v a) (b t) -> p v (a b) t", v=2, b=4)
            b_Ssil_l = b_E[8:12]
            for v in range(2):
                S.op("dve", lambda e, v=v: e.tensor_copy(out=Ssil[:, v], in_=scT[:, :, v:v + 1].to_broadcast([128, 8, 128])), reads=[b_scT], writes=b_Ssil_l)
            for ch in range(6):
                S.dma("pool", lambda e, ch=ch: e.dma_start(out=wad, in_=wada_d[l][:, ch * 512:(ch + 1) * 512].rearrange("(k p) n -> p k n", p=128)),
                      writes=b_E[0:8], sem_key="w_ada")
                if ch < 4:
                    pt, bpt = mm()
                    for jb in range(4):
                        for kc in range(8):
                            S.op("pe", lambda e, jb=jb, kc=kc, pt=pt: e.matmul(pt[:, jb * 2:jb * 2 + 2], lhsT=wad[:, kc, jb * 128:(jb + 1) * 128], rhs=scT[:, kc, :], start=(kc == 0), stop=(kc == 7)),
                                 reads=b_E[0:8] + [b_scT], writes=[bpt])
                    for v in range(2):
                        dst = (shp if ch < 2 else sc1p)
                        bd = (b_shp if ch < 2 else b_sc1p)
                        j0 = (ch % 2) * 4
                        S.op("dve", lambda e, v=v, dst=dst, j0=j0, pt=pt, ch=ch: e.scalar_tensor_tensor(
                            out=dst[:, v, j0:j0 + 4], in0=pt[:, v:8:2], scalar=(0.0 if ch < 2 else 1.0), in1=badc[:, ch * 4:ch * 4 + 4], op0=ALU.add, op1=ALU.add),
                            reads=[bpt, b_badc], writes=[bd])
                else:
                    half = ch - 4
                    S.dma("sp", lambda e, half=half: e.dma_start(out=tU[0][:, 0:512], in_=bada_d[l, 2048 + half * 512:2048 + (half + 1) * 512].partition_broadcast(128)), writes=[b_tU[0]], sem_key="p_bg")
                    for v in range(2):
                        pt, bpt = mm()
                        for kc in range(8):
                            S.op("pe", lambda e, v=v, kc=kc, pt=pt: e.matmul(pt[:, :], lhsT=Ssil[:, v, kc, :], rhs=wad[:, kc, :], start=(kc == 0), stop=(kc == 7)),
                                 reads=b_E[0:8] + b_Ssil_l, writes=[bpt])
                        S.op("dve", lambda e, v=v, pt=pt, half=half: e.tensor_tensor(out=gate[:, v, half * 512:(half + 1) * 512], in0=pt[:, :], in1=tU[0][:, 0:512], op=ALU.add),
                             reads=[bpt, b_tU[0]], writes=[b_gate])
            wdma(wF[:, :, 1024:1536], win_d[l][:, 1792:2304], b_wFq, "w_Fq")
            wdma(wF[:, :, 0:1024], win_d[l][:, 0:1024], b_wF, "w_F")
            wdma(wT[:, :, 0:256], win_d[l][:, 1280:1536], b_wTvu, "w_Tvu")
            wdma(wT[:, :, 256:512], win_d[l][:, 1024:1280], b_wTvu, "w_Tvu2")
            wdma(wT[:, :, 1280:1792], win_d[l][:, 3328:3840], b_wTgc, "w_Tgc")
            wdma(wT[:, :, 512:768], win_d[l][:, 1536:1792], b_wTgb, "w_Tgb")
            wdma(wo[:], wout_d[l], b_wo, "w_o")
            S.dma("pool", lambda e: e.dma_start(out=ckT[:], in_=ckT_d[l].rearrange("(a p) k -> p a k", p=128)), writes=[b_ckT], sem_key="c_k")
            for a in range(2):
                S.dma("pool", lambda e, a=a: e.dma_start(out=cV[:, a, :, 0:64], in_=cv_d[l][a * 128:(a + 1) * 128, :].rearrange("p (h d) -> p h d", d=64)), writes=[b_cV], sem_key="c_v")
            for tid in range(7):
                S.dma("sp", lambda e, tid=tid: e.dma_start(out=tU[tid % 2][:], in_=ebt_d[l, tid]), writes=[b_tU[tid % 2]], sem_key="p_eb%d" % (tid % 2))
                S.op("act", lambda e, tid=tid: e.activation(out=EB[:, tid, :], in_=tU[tid % 2][:], func=AF.Exp), reads=[b_tU[tid % 2]], writes=[b_EB])

            def src_x(grp, tile):
                if l == 0:
                    return (xs_d if grp == "S" else xp_d)[tile * 128:(tile + 1) * 128, :]
                return (sxs_d if grp == "S" else sxp_d)[tile * 128:(tile + 1) * 128, :]

            def dst_x(grp, tile):
                if last:
                    return (ys_d if grp == "S" else yp_d)[tile * 128:(tile + 1) * 128, :]
                return (sxs_d if grp == "S" else sxp_d)[tile * 128:(tile + 1) * 128, :]

            def xbuf(grp, u):
                return (b_sxs if grp == "S" else b_sxp)[u]

            cnt = {"xA": 0, "tU": 0}

            def A_dma(grp, u):
                xis = []
                for j in range(2):
                    tile = 2 * u + j
                    xi = cnt["xA"] % 2
                    cnt["xA"] += 1
                    rd = [xbuf(grp, u)] if l > 0 else []
                    S.dma("sp", lambda e, xi=xi, tile=tile: e.dma_start(out=xA[xi][:], in_=src_x(grp, tile)), reads=rd, writes=[b_xA[xi]], sem_key="xA%d" % xi)
                    xis.append(xi)
                return xis

            def A_ln(grp, u, xis):
                for j in range(2):
                    xi = xis[j]
                    for i in range(2):
                        S.op("dve", lambda e, i=i, xi=xi, j=j: e.bn_stats(out=stt[j][:, i, :], in_=xA[xi][:, i * 512:(i + 1) * 512]), reads=[b_xA[xi]], writes=[b_stt[j]])
                    S.op("dve", lambda e, j=j: e.bn_aggr(out=mv[j][:], in_=stt[j][:].rearrange("p a b -> p (a b)")), reads=[b_stt[j]], writes=[b_mv[j]])
                    ln_small(j)
                    S.op("act", lambda e, xi=xi, j=j: e.activation(out=xn[j][:], in_=xA[xi][:], func=AF.Identity, bias=nm[j][:], scale=rs[j][:]),
                         reads=[b_xA[xi], b_nm[j], b_rs[j]], writes=[b_xn[j]])

            def A_pre(grp, u):
                A_ln(grp, u, A_dma(grp, u))

            def A_pe(grp, u, hs):
                v = 0 if grp == "S" else 1
                for j in range(2):
                    tr, b_tr = mmt()
                    for kc in range(8):
                        S.op("pe", lambda e, kc=kc, j=j, tr=tr: e.transpose(out=tr[:, kc * 128:(kc + 1) * 128], in_=xn[j][:, kc * 128:(kc + 1) * 128], identity=ident[:]),
                             reads=[b_xn[j], b_id], writes=[b_tr])
                    for kc in range(8):
                        if kc % 2 == 0:
                            S.op("dve", lambda e, kc=kc, j=j, tr=tr: e.tensor_scalar(out=hT[hs][:, kc, j * 128:(j + 1) * 128], in0=tr[:, kc * 128:(kc + 1) * 128],
                                                                                      scalar1=sc1p[:, v, kc:kc + 1], scalar2=shp[:, v, kc:kc + 1], op0=ALU.mult, op1=ALU.add),
                                 reads=[b_tr, b_sc1p, b_shp], writes=[b_hT[hs]])
                    for kc in range(8):
                        if kc % 2 == 1:
                            S.op("act", lambda e, kc=kc, j=j, tr=tr: e.activation(out=hT[hs][:, kc, j * 128:(j + 1) * 128], in_=tr[:, kc * 128:(kc + 1) * 128], func=AF.Identity,
                                                                                    scale=sc1p[:, v, kc:kc + 1], bias=shp[:, v, kc:kc + 1]),
                                 reads=[b_tr, b_sc1p, b_shp], writes=[b_hT[hs]])

            def proj_T(hs, j, c0, c1):
                pt, bpt = mm()
                n = c1 - c0
                for kc in range(8):
                    S.op("pe", lambda e, kc=kc, pt=pt: e.matmul(pt[:, 0:n], lhsT=hT[hs][:, kc, j * 128:(j + 1) * 128], rhs=wT[:, kc, c0:c1], start=(kc == 0), stop=(kc == 7)),
                         reads=[b_hT[hs], {1792: b_wTk, 768: b_wTv, 0: b_wTvu, 512: b_wTgb, 1280: b_wTgc}[c0]], writes=[bpt])
                return pt, bpt

            def proj_F(hs, pt, bpt, off, cb):
                for kc in range(8):
                    S.op("pe", lambda e, kc=kc: e.matmul(pt[:, off:off + 256], lhsT=wF[:, kc, cb * 128:(cb + 1) * 128], rhs=hT[hs][:, kc, :], start=(kc == 0), stop=(kc == 7)),
                         reads=[b_hT[hs], (b_wFq if cb >= 8 else b_wF)], writes=[bpt])

            def B_kv(grp, u, hs, ks):
                with_out = (grp == "P")
                for j in range(2):
                    vslot = 2 * ks + j
                    pt, bpt = proj_T(hs, j, 1792, 2304)
                    S.op("act", lambda e, pt=pt: e.activation(out=ktok[:], in_=pt[:, :], func=AF.Copy), reads=[bpt], writes=[b_ktok])
                    if with_out:
                        S.op("dve", lambda e, pt=pt: e.tensor_copy(out=kst[:], in_=pt[:, :]), reads=[bpt], writes=[b_kst])
                        S.dma("sp", lambda e, j=j: e.dma_start(out=nk_d[u, l, j * 128:(j + 1) * 128, :], in_=kst[:]), reads=[b_kst], sem_key="o_k", final=True)
                    tr, b_tr = mmt()
                    for a in range(4):
                        S.op("pe", lambda e, a=a, tr=tr: e.transpose(out=tr[:, a * 128:(a + 1) * 128], in_=ktok[:, a * 128:(a + 1) * 128], identity=ident[:]),
                             reads=[b_ktok, b_id], writes=[b_tr])
                    S.op("dve", lambda e, j=j, tr=tr: e.tensor_copy(out=KT[ks][:, :, j * 128:(j + 1) * 128], in_=tr[:, 0:512].rearrange("p (a t) -> p a t", a=4)),
                         reads=[b_tr], writes=[b_KT[ks]])
                    pt, bpt = proj_T(hs, j, 768, 1280)
                    S.op("act", lambda e, pt=pt, vslot=vslot: e.activation(out=Vr[:, vslot, :, 0:64], in_=pt[:, :].rearrange("p (h d) -> p h d", d=64), func=AF.Copy),
                         reads=[bpt], writes=[b_V[vslot]])
                    if with_out:
                        S.op("dve", lambda e, pt=pt: e.tensor_copy(out=vst[:], in_=pt[:, :]), reads=[bpt], writes=[b_vst])
                        S.dma("sp", lambda e, j=j: e.dma_start(out=nv_d[u, l, j * 128:(j + 1) * 128, :], in_=vst[:]), reads=[b_vst], sem_key="o_v", final=True)

            def proj_F_g(hs, pt, bpt, off, cb):
                proj_F(hs, pt, bpt, off, cb)
                yield

            def B_q_gen(grp, u, hs, qs, pool=None):
                for half in range(2):
                    pq, bpq = mm(pool)
                    yield from proj_F_g(hs, pq, bpq, 0, 8 + 2 * half)
                    yield from proj_F_g(hs, pq, bpq, 256, 9 + 2 * half)
                    S.op("act", lambda e, pq=pq, half=half: e.activation(out=QT[qs][:, 2 * half:2 * half + 2, :], in_=pq[:, :].rearrange("p (a t) -> p a t", a=2), func=AF.Copy),
                         reads=[bpq], writes=[b_QT[qs]])

            def run(gen):
                for _ in gen:
                    pass

            def zipper(ga, gb, na=1, nb=2):
                da = db = False
                while not (da and db):
                    for _ in range(na):
                        if not da:
                            try:
                                next(ga)
                            except StopIteration:
                                da = True
                    for _ in range(nb):
                        if not db:
                            try:
                                next(gb)
                            except StopIteration:
                                db = True

            def B_q(grp, u, hs, qs):
                run(B_q_gen(grp, u, hs, qs))

            def B_conv_gen(grp, u, hs, qs, hs_prev, hs_next, pool=None):
                yt, byt = yT[qs], b_yT[qs]
                have_h = [hs_prev is not None, hs_next is not None]
                hal = pv[1][:, 384:392]
                b_halo = b_pv[1]
                for cbi, cb in enumerate((0, 1, 4, 5)):
                    for side in range(2):
                        if not have_h[side]:
                            continue
                        hsrc = hT[hs_prev][:, :, 255:256] if side == 0 else hT[hs_next][:, :, 0:1]
                        bsrc = b_hT[hs_prev] if side == 0 else b_hT[hs_next]
                        for kc in range(8):
                            S.op("pe", lambda e, kc=kc, cb=cb, hsrc=hsrc, col=cbi * 2 + side: e.matmul(hal[:, col:col + 1], lhsT=wF[:, kc, cb * 128:(cb + 1) * 128], rhs=hsrc[:, kc, :], start=(kc == 0), stop=(kc == 7)),
                                 reads=[bsrc, b_wF], writes=[b_halo])
                    yield
                S.op("pool", lambda e: e.memset(zh[:], 0.0), writes=[b_zh])
                for side in range(2):
                    if have_h[side]:
                        S.op("dve", lambda e, side=side: e.tensor_copy(out=xh[:, side:4:2], in_=hal[:, side:4:2]), reads=[b_halo], writes=[b_xh])
                        S.op("dve", lambda e, side=side: e.tensor_tensor(out=zh[:, side:4:2], in0=hal[:, 4 + side:8:2], in1=xh[:, side:4:2], op=ALU.mult), reads=[b_halo, b_xh], writes=[b_zh])
                for c in range(2):
                    p1, bp1 = mm(pool)
                    yield from proj_F_g(hs, p1, bp1, 0, 0 + c)
                    yield from proj_F_g(hs, p1, bp1, 256, 4 + c)
                    p2, bp2 = mm(pool)
                    yield from proj_F_g(hs, p2, bp2, 0, 2 + c)
                    yield from proj_F_g(hs, p2, bp2, 256, 6 + c)
                    xa_t, bxa = wk["xa"]; acc, bacc = wk["acc"]; tg, btg = wk["tg"]; s2, bs2 = wk["s2"]; m_, bm = wk["m"]
                    S.op("act", lambda e, p1=p1: e.activation(out=xa_t[:], in_=p1[:, 0:256], func=AF.Copy), reads=[bp1], writes=[bxa])
                    S.op("dve", lambda e, p1=p1: e.tensor_tensor(out=zb[:, 1:257], in0=p1[:, 256:512], in1=xa_t[:], op=ALU.mult), reads=[bp1, bxa], writes=[b_zb])
                    S.op("dve", lambda e, c=c: e.tensor_copy(out=zb[:, 0:258:257], in_=zh[:, 2 * c:2 * c + 2]), reads=[b_zh], writes=[b_zb])
                    S.op("act", lambda e, c=c: e.activation(out=acc[:], in_=zb[:, 1:257], func=AF.Identity, scale=cw[:, c, 1:2]), reads=[b_zb, b_cw], writes=[bacc])
                    S.op("dve", lambda e, c=c: e.scalar_tensor_tensor(out=acc[:], in0=zb[:, 0:256], scalar=cw[:, c, 0:1], in1=acc[:], op0=ALU.mult, op1=ALU.add), reads=[b_zb, b_cw, bacc], writes=[bacc])
                    S.op("dve", lambda e, c=c: e.scalar_tensor_tensor(out=acc[:], in0=zb[:, 2:258], scalar=cw[:, c, 2:3], in1=acc[:], op0=ALU.mult, op1=ALU.add), reads=[b_zb, b_cw, bacc], writes=[bacc])
                    S.op("act", lambda e, p2=p2: e.activation(out=tg[:], in_=p2[:, 256:512], func=AF.Tanh, scale=0.5), reads=[bp2], writes=[btg])
                    S.op("dve", lambda e, p2=p2: e.scalar_tensor_tensor(out=s2[:], in0=tg[:], scalar=1.0, in1=p2[:, 256:512], op0=ALU.add, op1=ALU.mult), reads=[btg, bp2], writes=[bs2])
                    S.op("dve", lambda e, p2=p2: e.tensor_tensor(out=m_[:], in0=p2[:, 0:256], in1=acc[:], op=ALU.mult), reads=[bp2, bacc], writes=[bm])
                    S.op("dve", lambda e, c=c: e.scalar_tensor_tensor(out=yt[:, c, :], in0=m_[:], scalar=0.5, in1=s2[:], op0=ALU.mult, op1=ALU.mult), reads=[bm, bs2], writes=[byt])
                    yield

            def B_conv(grp, u, hs, qs, hs_prev, hs_next):
                run(B_conv_gen(grp, u, hs, qs, hs_prev, hs_next))

            def B_gm(grp, u, hs, qs):
                yt, byt = yT[qs], b_yT[qs]
                for j in range(2):
                    xa_t, bxa = wk["xa"]; acc, bacc = wk["acc"]; tg, btg = wk["tg"]; s2, bs2 = wk["s2"]; m_, bm = wk["m"]
                    pvu, bpvu = proj_T(hs, j, 0, 512)
                    S.op("dve", lambda e, pvu=pvu: e.bn_stats(out=stt[2][:, 0, :], in_=pvu[:, 0:256]), reads=[bpvu], writes=[b_stt[2]])
                    S.op("dve", lambda e: e.bn_aggr(out=mv[2][:], in_=stt[2][:, 0, :]), reads=[b_stt[2]], writes=[b_mv[2]])
                    ln_small(2)
                    pgc, bpgc = proj_T(hs, j, 1280, 1792)
                    S.op("act", lambda e, pgc=pgc: e.activation(out=tgc[:], in_=pgc[:, :], func=AF.Tanh, scale=0.5), reads=[bpgc], writes=b_tgc_l)
                    S.op("dve", lambda e, pgc=pgc, j=j: e.scalar_tensor_tensor(out=sgc[qs][:, j, :], in0=tgc[:], scalar=1.0, in1=pgc[:, :], op0=ALU.add, op1=ALU.mult), reads=b_tgc_l + [bpgc], writes=[b_sgc[qs]])
                    pgb, bpgb = proj_T(hs, j, 512, 768)
                    S.op("act", lambda e, pvu=pvu: e.activation(out=vn3[:], in_=pvu[:, 0:256], func=AF.Identity, bias=nm[2][:], scale=rs[2][:]), reads=[bpvu, b_nm[2], b_rs[2]], writes=[b_vn3])
                    psv, bpsv = mm()
                    for g in range(4):
                        S.op("pe", lambda e, g=g, psv=psv: e.matmul(psv[:, g * 64:(g + 1) * 64], lhsT=wsT[:, g, :], rhs=vn3[:, g * 64:(g + 1) * 64], start=True, stop=True),
                             reads=[b_wsT, b_vn3], writes=[bpsv])
                    S.op("dve", lambda e, psv=psv: e.tensor_tensor(out=acc[:], in0=psv[:, 0:256], in1=glg[:], op=ALU.mult), reads=[bpsv, b_glg], writes=[bacc])
                    S.op("dve", lambda e: e.tensor_tensor(out=acc[:], in0=acc[:], in1=bsb[:], op=ALU.add), reads=[bacc, b_bsb], writes=[bacc])
                    S.op("dve", lambda e, pvu=pvu: e.tensor_tensor(out=m_[:], in0=pvu[:, 256:512], in1=acc[:], op=ALU.mult), reads=[bpvu, bacc], writes=[bm])
                    S.op("act", lambda e, pgb=pgb: e.activation(out=tg[:], in_=pgb[:, 0:256], func=AF.Tanh, scale=0.5), reads=[bpgb], writes=[btg])
                    S.op("dve", lambda e, pgb=pgb: e.scalar_tensor_tensor(out=s2[:], in0=tg[:], scalar=1.0, in1=pgb[:, 0:256], op0=ALU.add, op1=ALU.mult), reads=[btg, bpgb], writes=[bs2])
                    S.op("dve", lambda e: e.scalar_tensor_tensor(out=ybt[:], in0=m_[:], scalar=0.5, in1=s2[:], op0=ALU.mult, op1=ALU.mult), reads=[bm, bs2], writes=[b_ybt])
                    tr, b_tr = mmt()
                    for a in range(2):
                        S.op("pe", lambda e, a=a, tr=tr: e.transpose(out=tr[:, a * 128:(a + 1) * 128], in_=ybt[:, a * 128:(a + 1) * 128], identity=ident[:]), reads=[b_ybt, b_id], writes=[b_tr])
                    S.op("act", lambda e, j=j, tr=tr: e.activation(out=yt[:, 2:4, j * 128:(j + 1) * 128], in_=tr[:, 0:256].rearrange("p (a t) -> p a t", a=2), func=AF.Copy), reads=[b_tr], writes=[byt])

            def C_s_gen(grp, u, qs, chunks, j, pool=None):
                for ci, (kf, kb, vf, vb, tid) in enumerate(chunks):
                    SE, b_SE = mm(pool)
                    SO, b_SO = mm(pool)
                    for h in range(8):
                        pair, half = h // 2, h % 2
                        bank, bbank = (SE, b_SE) if half == 0 else (SO, b_SO)
                        S.op("pe", lambda e, kf=kf, pair=pair, half=half, bank=bank: e.matmul(
                            bank[:, pair * 128:(pair + 1) * 128], lhsT=kf(pair, half), rhs=QT[qs][64 * half:64 * half + 64, pair, j * 128:(j + 1) * 128], start=True, stop=True),
                            reads=[kb, b_QT[qs]], writes=[bbank])
                    for half in range(2):
                        bank, bbank = (SE, b_SE) if half == 0 else (SO, b_SO)
                        ei = 2 * ci + half
                        S.op("act", lambda e, bank=bank, ei=ei: e.activation(out=Eb[:, ei, :], in_=bank[:, :], func=AF.Exp, scale=SCALE), reads=[bbank], writes=[b_E[ei]])
                        if tid is not None:
                            S.op(_eb_eng(ci, half), lambda e, ei=ei, tid=tid, half=half: e.tensor_tensor(out=Eb[:, ei, :], in0=Eb[:, ei, :], in1=EB[:, tid, half * 512:(half + 1) * 512], op=ALU.mult),
                                 reads=[b_E[ei], b_EB], writes=[b_E[ei]])
                    yield

            def C_s(grp, u, qs, keyf, j):
                chunks = keyf(2 * u + j)
                run(C_s_gen(grp, u, qs, chunks, j))
                return chunks

            def C_pv(grp, u, qs, chunks, j):
                nck = len(chunks)
                for h in range(8):
                    pair, half = h // 2, h % 2
                    pb, bpb = pv[h // 4], b_pv[h // 4]
                    for ci, (kf, kb, vf, vb, tid) in enumerate(chunks):
                        ei = 2 * ci + half
                        S.op("pe", lambda e, ei=ei, pair=pair, vf=vf, h=h, pb=pb, ci=ci: e.matmul(
                            pb[:, (h % 4) * 65:(h % 4) * 65 + 65], lhsT=Eb[:, ei, pair * 128:(pair + 1) * 128], rhs=vf(h), start=(ci == 0), stop=(ci == nck - 1)),
                            reads=[b_E[ei], vb], writes=[bpb])
                for g2 in range(2):
                    pb, bpb = pv[g2], b_pv[g2]
                    pv3 = pb[:, 0:260].rearrange("p (h d) -> p h d", d=65)
                    S.op("dve", lambda e, pv3=pv3, g2=g2: e.reciprocal(out=rinv[:, 4 * g2:4 * g2 + 4], in_=pv3[:, :, 64]), reads=[bpb], writes=[b_rinv])
                    S.op("dve", lambda e, pv3=pv3, g2=g2: e.tensor_tensor(out=ob[:, 256 * g2:256 * g2 + 256].rearrange("p (h d) -> p h d", d=64), in0=pv3[:, :, 0:64],
                                                                            in1=rinv[:, 4 * g2:4 * g2 + 4].unsqueeze(2).to_broadcast([128, 4, 64]), op=ALU.mult),
                         reads=[bpb, b_rinv], writes=[b_ob])
                S.op("dve", lambda e: e.scalar_tensor_tensor(out=yct[:], in0=ob[:], scalar=0.5, in1=sgc[qs][:, j, :], op0=ALU.mult, op1=ALU.mult), reads=[b_ob, b_sgc[qs]], writes=[b_yct])

            def C_o(grp, u, qs, j):
                run(C_o_gen(grp, u, qs, j))

            def C_o_gen(grp, u, qs, j, pool=None):
                v = 0 if grp == "S" else 1
                yt, byt = yT[qs], b_yT[qs]
                tile = 2 * u + j
                tr, b_tr = mmt(pool)
                for a in range(4):
                    S.op("pe", lambda e, a=a: e.transpose(out=tr[:, a * 128:(a + 1) * 128], in_=yct[:, a * 128:(a + 1) * 128], identity=ident[:]), reads=[b_yct, b_id], writes=[b_tr])
                S.op("act", lambda e: e.activation(out=yt[:, 4:8, j * 128:(j + 1) * 128], in_=tr[:, 0:512].rearrange("p (a t) -> p a t", a=4), func=AF.Copy), reads=[b_tr], writes=[byt])
                yield
                ti = cnt["tU"] % 2
                cnt["tU"] += 1
                rd = [xbuf(grp, u)] if l > 0 else []
                S.dma("sp", lambda e: e.dma_start(out=tU[ti][:], in_=src_x(grp, tile)), reads=rd, writes=[b_tU[ti]], sem_key="xC%d" % ti)
                for n in range(2):
                    po, bpo = mm(pool)
                    for kc in range(8):
                        S.op("pe", lambda e, kc=kc, n=n, po=po: e.matmul(po[:, :], lhsT=yt[:, kc, j * 128:(j + 1) * 128], rhs=wo[:, kc, n * 512:(n + 1) * 512], start=(kc == 0), stop=(kc == 7)),
                             reads=[byt, b_wo], writes=[bpo])
                        if kc == 3:
                            yield
                    S.op("dve", lambda e, n=n, po=po: e.tensor_tensor(out=ob2[:], in0=po[:, :], in1=gate[:, v, n * 512:(n + 1) * 512], op=ALU.mult),
                         reads=[bpo, b_gate], writes=[b_ob2])
                    S.op("dve", lambda e, n=n: e.scalar_tensor_tensor(out=tU[ti][:, n * 512:(n + 1) * 512], in0=tU[ti][:, n * 512:(n + 1) * 512], scalar=ALPHA, in1=ob2[:], op0=ALU.mult, op1=ALU.add),
                         reads=[b_ob2, b_tU[ti]], writes=[b_tU[ti]])
                    yield
                for i in range(2):
                    S.op("dve", lambda e, i=i: e.bn_stats(out=stt[3][:, i, :], in_=tU[ti][:, i * 512:(i + 1) * 512]), reads=[b_tU[ti]], writes=[b_stt[3]])
                S.op("dve", lambda e: e.bn_aggr(out=mv[3][:], in_=stt[3][:].rearrange("p a b -> p (a b)")), reads=[b_stt[3]], writes=[b_mv[3]])
                ln_small(3)
                S.op("act", lambda e: e.activation(out=tU[ti][:], in_=tU[ti][:], func=AF.Identity, bias=nm[3][:], scale=rs[3][:]), reads=[b_tU[ti], b_nm[3], b_rs[3]], writes=[b_tU[ti]])
                S.op("pool", lambda e: e.tensor_tensor(out=tU[ti][:], in0=tU[ti][:], in1=lng[:], op=ALU.mult), reads=[b_tU[ti], b_lng], writes=[b_tU[ti]])
                S.op("pool", lambda e: e.tensor_tensor(out=tU[ti][:], in0=tU[ti][:], in1=lnb[:], op=ALU.add), reads=[b_tU[ti], b_lnb], writes=[b_tU[ti]])
                S.dma("sp", lambda e: e.dma_start(out=dst_x(grp, tile), in_=tU[ti][:]), reads=[b_tU[ti]], writes=([] if last else [xbuf(grp, u)]),
                      sem_key="xo%d" % ti, final=True)

            def ctx_chunks():
                out = []
                for a in range(2):
                    out.append((lambda pair, half, a=a: ckT[64 * half:64 * half + 64, pair, a * 128:(a + 1) * 128], b_ckT,
                                lambda h, a=a: cV[:, a, h, :], b_cV, None))
                return out

            nP = ns_units - l
            nR = ns_units - 1 - l

            def keyf_S(tile):
                out = []
                for (kt, tid) in attn_keys(tile):
                    ku, kj = kt // 2, kt % 2
                    ksl = ku % NK
                    out.append((lambda pair, half, ksl=ksl, kj=kj: KT[ksl][64 * half:64 * half + 64, pair, kj * 128:(kj + 1) * 128], b_KT[ksl],
                                lambda h, ksl=ksl, kj=kj: Vr[:, 2 * ksl + kj, h, :], b_V[2 * ksl + kj], tid))
                return ctx_chunks() + out

            def pipeline(grp, nP, nR, keyf_of, base, halo):
                def hs_(u): return (base + u) % NH
                def ks_(u): return (base + u) % NK
                def qs_(u): return (base + u) % 2
                def hprev(u): return hs_(u - 1) if (halo and u > 0) else None
                def hnext(u): return hs_(u + 1) if halo else None
                for u0 in range(min(3, nP)):
                    A_pre(grp, u0); A_pe(grp, u0, hs_(u0))
                B_kv(grp, 0, hs_(0), ks_(0)); B_q(grp, 0, hs_(0), qs_(0)); B_conv(grp, 0, hs_(0), qs_(0), None, hnext(0)); B_gm(grp, 0, hs_(0), qs_(0))
                pend = None
                for i in range(nP):
                    u1 = i + 1
                    full1 = u1 < nR
                    if i + 3 < nP:
                        xis_ = A_dma(grp, i + 3)
                    if u1 < nP:
                        B_kv(grp, u1, hs_(u1), ks_(u1))
                    if pend is not None:
                        C_o(*pend)
                        pend = None
                    if i + 3 < nP:
                        A_ln(grp, i + 3, xis_)
                    g_b = iter(())
                    if full1:
                        def g_b_f(u1=u1):
                            yield from B_q_gen(grp, u1, hs_(u1), qs_(u1), "G")
                            yield from B_conv_gen(grp, u1, hs_(u1), qs_(u1), hprev(u1), hnext(u1), "G")
                        g_b = g_b_f()
                    if i < nR:
                        kf = keyf_of(i)
                        ch0 = kf(2 * i)
                        zipper(C_s_gen(grp, i, qs_(i), ch0, 0, "S"), g_b, 1, 2)
                        C_pv(grp, i, qs_(i), ch0, 0)
                        ch1 = kf(2 * i + 1)
                        zipper(C_s_gen(grp, i, qs_(i), ch1, 1, "S"), C_o_gen(grp, i, qs_(i), 0, "G"), 2, 1)
                    else:
                        run(g_b)
                    if i + 3 < nP:
                        A_pe(grp, i + 3, hs_(i + 3))
                    if full1:
                        B_gm(grp, u1, hs_(u1), qs_(u1))
                    if i < nR:
                        C_pv(grp, i, qs_(i), ch1, 1)
                        pend = (grp, i, qs_(i), 1)
                if pend is not None:
                    C_o(*pend)

            def keyf_P_of(u):
                ks = (nP_s + u) % NK
                def keyf(tile):
                    out = []
                    for kj in range(2):
                        out.append((lambda pair, half, kj=kj: KT[ks][64 * half:64 * half + 64, pair, kj * 128:(kj + 1) * 128], b_KT[ks],
                                    lambda h, kj=kj: Vr[:, 2 * ks + kj, h, :], b_V[2 * ks + kj], None))
                    return out
                return keyf

            nP_s = nP
            if not SKIPS:
                pipeline("S", nP, nR, lambda u: keyf_S, 0, True)
            if not SKIPP:
                pipeline("P", n_pseq, n_pseq, keyf_P_of, nP_s, False)

        for l_ in range(depth):
            layer(l_)

        S.emit()
    return nc


def _tables(rpb, flip):
    reps = [(4, 2), (4, 3), (4, 4), (4, 5), (4, 6), (0, 2), (0, 3)]
    out = np.empty((DEPTH, 7, 128, 2, 4, 128), np.float32)
    p = np.arange(128)
    for tid, (t, u) in enumerate(reps):
        ql = t * 128 + p
        kl = u * 128 + p
        qg = 4095 - ql if flip else ql
        kg = 4095 - kl if flip else kl
        qr, qc = qg // 64, qg % 64
        kr, kc = kg // 64, kg % 64
        rs_ = np.clip(qr - 4, 0, 56)
        cs_ = np.clip(qc - 8, 0, 48)
        valid = ((kr[:, None] >= rs_[None, :]) & (kr[:, None] < rs_[None, :] + 8)
                 & (kc[:, None] >= cs_[None, :]) & (kc[:, None] < cs_[None, :] + 16))
        dr = np.clip(kr[:, None] - qr[None, :] + 7, 0, 14)
        dc = np.clip(kc[:, None] - qc[None, :], -15, 15) + 15
        g = rpb[:, :, dr, dc]
        g = np.where(valid[None, None], g, np.float32(NEG))
        g = g.transpose(0, 2, 1, 3).reshape(DEPTH, 128, 4, 2, 128).transpose(0, 1, 3, 2, 4)
        out[:, tid] = g
    return np.ascontiguousarray(out.reshape(DEPTH, 7, 128, 1024))


_NC_CACHE = {}


def kernel(x_prompt, x_sample, cache_k, cache_v, c, c_ctx, w_ada, b_ada, w_in, conv_w,
           gmlp_ln_g, gmlp_ln_b, w_spatial, b_spatial, rpb, w_out, ln_g, ln_b):
    f = lambda a: np.ascontiguousarray(np.asarray(a, dtype=np.float32))
    x_prompt, x_sample, cache_k, cache_v, c, c_ctx = map(f, (x_prompt, x_sample, cache_k, cache_v, c, c_ctx))
    w_ada, b_ada, w_in, conv_w, gmlp_ln_g, gmlp_ln_b = map(f, (w_ada, b_ada, w_in, conv_w, gmlp_ln_g, gmlp_ln_b))
    w_spatial, b_spatial, rpb, w_out, ln_g, ln_b = map(f, (w_spatial, b_spatial, rpb, w_out, ln_g, ln_b))
    if "nc" not in _NC_CACHE:
        _NC_CACHE["nc"] = build()
    nc = _NC_CACHE["nc"]
    ident = np.eye(128, dtype=np.float32)
    tabs = [_tables(rpb, 0), _tables(rpb, 1)]
    in_maps = []
    for i in range(8):
        b, flip = i // 2, i % 2
        xs = x_sample[b][::-1] if flip else x_sample[b]
        xp = x_prompt[4 * i:4 * i + 4]
        if flip:
            xp = xp[:, ::-1]
        ws = w_spatial[:, :, ::-1, ::-1] if flip else w_spatial
        bs = b_spatial[:, :, ::-1] if flip else b_spatial
        cwv = conv_w[:, ::-1, :] if flip else conv_w
        in_maps.append({
            "xp": f(xp.reshape(1024, D)),
            "xs": f(xs[0:3072]),
            "c2": f(np.stack([c[b], c_ctx])),
            "ckT": f(cache_k[b].reshape(DEPTH, 256, 512).transpose(0, 2, 1)),
            "cv": f(cache_v[b].reshape(DEPTH, 256, 512)),
            "w_ada": w_ada, "b_ada": b_ada, "w_in": w_in, "w_out": w_out,
            "conv_w": f(cwv), "gmlp_ln_g": gmlp_ln_g, "gmlp_ln_b": gmlp_ln_b,
            "wsT": f(ws.transpose(0, 3, 1, 2)), "bsT": f(bs.transpose(0, 2, 1)),
            "ebt": tabs[flip], "ln_g": ln_g, "ln_b": ln_b, "ident": ident,
        })
    res = run_bass_kernel_spmd(nc, in_maps, core_ids=list(range(8))).results
    y_prompt = np.empty((32, 256, D), np.float32)
    y_sample = np.empty((4, 4096, D), np.float32)
    new_k = np.empty((32, DEPTH, 256, 8, 64), np.float32)
    new_v = np.empty((32, DEPTH, 256, 8, 64), np.float32)
    for i in range(8):
        b, flip = i // 2, i % 2
        r = res[i]
        yp = np.asarray(r["yp"]).reshape(4, 256, D)
        nk = np.asarray(r["nk"]).reshape(4, DEPTH, 256, 8, 64)
        nv = np.asarray(r["nv"]).reshape(4, DEPTH, 256, 8, 64)
        ys = np.asarray(r["ys"])
        if flip:
            yp = yp[:, ::-1]
            nk = nk[:, :, ::-1]
            nv = nv[:, :, ::-1]
            y_sample[b, 2048:] = ys[::-1]
        else:
            y_sample[b, :2048] = ys
        y_prompt[4 * i:4 * i + 4] = yp
        new_k[4 * i:4 * i + 4] = nk
        new_v[4 * i:4 * i + 4] = nv
    return (y_prompt, y_sample, new_k, new_v)
```

```python
import contextlib
import numpy as np
import concourse.bass as bass
import concourse.mybir as mybir
from concourse.bass_utils import run_bass_kernel_spmd

F32 = mybir.dt.float32
BF16 = mybir.dt.bfloat16
ALU = mybir.AluOpType
AF = mybir.ActivationFunctionType

D = 1024
DEPTH = 4
ALPHA = (2 * DEPTH) ** 0.25
SCALE = 64 ** -0.5
NEG = -30000.0
ENGS = ("pe", "act", "dve", "pool", "sp")
MAXOPS = 10 ** 9
SCHEDULE = True
PRIO_W = 3.0
TOKEN_ALL = False
EB_MODE = "split"
TOKEN_PSUM_EXEMPT = True
UNZIP = False


def _eb_eng(ci, half):
    if EB_MODE == "dve":
        return "dve"
    if EB_MODE == "pool":
        return "pool"
    if EB_MODE == "p3":
        return "pool" if (half == 1 and ci % 2 == 0) else "dve"
    return "pool" if half == 1 else "dve"
SKIPP = False
SKIPS = False


class Buf:
    __slots__ = ("name", "excl", "last_w", "readers")

    def __init__(self, name, excl=False):
        self.name = name
        self.excl = excl
        self.last_w = None
        self.readers = []


class Op:
    __slots__ = ("eng", "fn", "is_dma", "sem_key", "deps", "signal", "count", "dma_count", "idx", "reads", "writes",
                 "preds", "succs", "npred", "cost", "fin", "pos", "mode", "start", "prio")

    def __init__(self, eng, fn, is_dma, sem_key):
        self.eng = eng
        self.fn = fn
        self.is_dma = is_dma
        self.sem_key = sem_key
        self.deps = []
        self.signal = False
        self.count = 0
        self.dma_count = 0
        self.mode = 0


class _Rec:
    def __init__(self):
        self.name = None
        self.kw = {}
        self.args = ()

    def __getattr__(self, name):
        def f(*a, **k):
            self.name, self.args, self.kw = name, a, k
            return self
        return f


def _nfree(ap):
    n = 1
    for d in tuple(ap.shape)[1:]:
        n *= int(d)
    return n


def _is_psum_or_f32(ap):
    try:
        return ("psum" in str(ap.space).lower()) or (ap.dtype == F32)
    except Exception:
        return True


def _cost_ns(op):
    r = _Rec()
    try:
        op.fn(r)
    except Exception:
        return 300.0
    kw, a, nm_ = r.kw, r.args, r.name
    out = kw.get("out", a[0] if a else None)
    try:
        if op.is_dma:
            byts = _nfree(out) * int(tuple(out.shape)[0]) * 4
            return 2000.0 + byts / 200.0
        if op.eng == "pe":
            if nm_ == "transpose":
                return 75.0
            rhs = kw.get("rhs", a[2] if len(a) > 2 else None)
            lhsT = kw.get("lhsT", a[1] if len(a) > 1 else None)
            if int(tuple(lhsT.shape)[0]) <= 64:
                op.mode = 64
                return max(35.0, 15.0 + _nfree(rhs) * 0.5)
            return max(45.0, 30.0 + _nfree(rhs) * 0.45)
        n = _nfree(out)
        if op.eng == "act":
            if kw.get("func") == AF.Exp:
                return 150.0 + n * 0.69
            return 240.0 + n * 0.95
        if op.eng == "pool":
            if nm_ == "memset":
                return 150.0 + n * 0.9
            if n <= 8:
                return 480.0
            return 250.0 + n * 2.1
        if nm_ == "reciprocal":
            return 1000.0
        if nm_ == "bn_aggr":
            return 250.0
        if nm_ == "bn_stats":
            return 160.0 + _nfree(kw.get("in_")) * 1.0
        if n <= 8:
            return 550.0
        srcs = [kw.get(k) for k in ("in0", "in1", "in_") if kw.get(k) is not None]
        slow = any(_is_psum_or_f32(x) for x in srcs)
        if nm_ in ("tensor_tensor", "scalar_tensor_tensor"):
            return 160.0 + n * (1.04 if slow else 1.5)
        return 200.0 + n * (1.04 if slow else 0.55)
    except Exception:
        return 300.0


class Sched:
    def __init__(self, nc):
        self.nc = nc
        self.all = []
        self.ops = {e: [] for e in ENGS}
        self.dma_counts = {}
        self.dma_keys = []
        self.final_waits = []
        self.dve_token = Buf("dve_token")

    def _add(self, eng, fn, reads, writes, is_dma=False, sem_key=None):
        op = Op(eng, fn, is_dma, sem_key)
        if eng == "dve" and not is_dma:
            r_ = _Rec()
            try:
                fn(r_)
            except Exception:
                pass
            psrc = False
            if TOKEN_PSUM_EXEMPT:
                for k_ in ("in0", "in1", "in_"):
                    x_ = r_.kw.get(k_)
                    if x_ is not None and "psum" in str(getattr(x_, "space", "")).lower():
                        psrc = True
            if TOKEN_ALL or (r_.name not in ("tensor_tensor", "scalar_tensor_tensor") and not psrc):
                reads = reads + [self.dve_token]
        op.reads, op.writes = reads, writes
        op.idx = len(self.all)
        self.all.append(op)
        if is_dma:
            if sem_key not in self.dma_counts:
                self.dma_counts[sem_key] = 0
                self.dma_keys.append(sem_key)
            self.dma_counts[sem_key] += 16
            op.dma_count = self.dma_counts[sem_key]
        return op

    def op(self, eng, fn, reads=(), writes=(), excl_dve=False):
        w = list(writes)
        if excl_dve:
            w.append(self.dve_token)
        return self._add(eng, fn, list(reads), w)

    def dma(self, eng, fn, reads=(), writes=(), sem_key=None, final=False):
        o = self._add(eng, fn, list(reads), list(writes), is_dma=True, sem_key=sem_key)
        if final:
            self.final_waits.append(o)
        return o

    def _edges(self):
        last_key = {}
        for op in self.all:
            preds = {}
            for b in op.reads:
                if b.last_w is not None:
                    preds[id(b.last_w)] = b.last_w
                if b.excl:
                    for r in b.readers:
                        if r.eng != op.eng:
                            preds[id(r)] = r
            for b in op.writes:
                if b.last_w is not None:
                    preds[id(b.last_w)] = b.last_w
                for r in b.readers:
                    preds[id(r)] = r
            if op.is_dma:
                p = last_key.get(op.sem_key)
                if p is not None:
                    preds[id(p)] = p
                last_key[op.sem_key] = op
            preds.pop(id(op), None)
            for b in op.reads:
                b.readers.append(op)
            for b in op.writes:
                b.last_w = op
                b.readers = []
            op.preds = list(preds.values())
            op.succs = []
        for op in self.all:
            for p in op.preds:
                p.succs.append(op)

    def _schedule(self):
        SEM = 120.0
        for op in self.all:
            op.cost = _cost_ns(op)
            op.npred = len(op.preds)
            op.fin = None
        rank = {}
        for op in reversed(self.all):
            r_ = 0.0
            for s_ in op.succs:
                r2 = rank[id(s_)]
                if r2 > r_:
                    r_ = r2
            rank[id(op)] = r_ + op.cost + SEM
        tot = max(rank.values())
        n_all = float(len(self.all))
        for op in self.all:
            op.prio = op.idx / n_all - PRIO_W * rank[id(op)] / tot
        cand = {e: [] for e in ENGS}
        ready = {}
        for op in self.all:
            if op.npred == 0:
                cand[op.eng].append(op)
                ready[id(op)] = 0.0
        free = {e: 0.0 for e in ENGS}
        order = {e: [] for e in ENGS}
        pe_mode = [0]
        left = len(self.all)
        while left:
            best = None
            for e in ENGS:
                cl = cand[e]
                if not cl:
                    continue
                t = free[e]
                pick = None
                if e == "pe":
                    for o in cl:
                        if ready[id(o)] <= t and o.mode == pe_mode[0]:
                            if pick is None or o.prio < pick.prio:
                                pick = o
                if pick is None:
                    for o in cl:
                        if ready[id(o)] <= t:
                            if pick is None or o.prio < pick.prio:
                                pick = o
                if pick is not None:
                    st_ = t
                else:
                    for o in cl:
                        rt = ready[id(o)]
                        if pick is None or rt < ready[id(pick)] or (rt == ready[id(pick)] and o.idx < pick.idx):
                            pick = o
                    st_ = ready[id(pick)]
                if best is None or st_ < best[0] or (st_ == best[0] and pick.idx < best[1].idx):
                    best = (st_, pick)
            st_, o = best
            e = o.eng
            cand[e].remove(o)
            if e == "pe":
                if o.mode != pe_mode[0]:
                    st_ += 120.0
                pe_mode[0] = o.mode
            if o.is_dma:
                free[e] = st_ + 60.0
            else:
                free[e] = st_ + o.cost
            o.fin = st_ + o.cost
            o.start = st_
            o.pos = len(order[e])
            order[e].append(o)
            left -= 1
            for s_ in o.succs:
                s_.npred -= 1
                if s_.npred == 0:
                    pe_s = (s_.eng == "pe" and not s_.is_dma)
                    ready[id(s_)] = max((p.start if (pe_s and p.eng == "pe" and not p.is_dma) else p.fin + SEM) for p in s_.preds)
                    cand[s_.eng].append(s_)
        self.model_ns = max(o.fin for o in self.all)
        return order

    def emit(self):
        nc = self.nc
        self._edges()
        if SCHEDULE:
            self.ops = self._schedule()
        else:
            self.ops = {e: [o for o in self.all if o.eng == e] for e in ENGS}
            for e in ENGS:
                for i, o in enumerate(self.ops[e]):
                    o.pos = i
        for e in ENGS:
            for o in self.ops[e]:
                lastp = {}
                dl = []
                for p in o.preds:
                    if p.is_dma:
                        dl.append(p)
                        continue
                    if p.eng == "pe" and o.eng == "pe" and not o.is_dma:
                        continue
                    q = lastp.get(p.eng)
                    if q is None or p.pos > q.pos:
                        lastp[p.eng] = p
                o.deps = dl + list(lastp.values())
                for p in lastp.values():
                    p.signal = True
        for e in ENGS:
            c = 0
            for o in self.ops[e]:
                if (not o.is_dma) and o.signal:
                    c += 1
                    o.count = c
        with contextlib.ExitStack() as st:
            esem = {e: st.enter_context(nc.semaphore("s_" + e)) for e in ENGS}
            dsem = {k: st.enter_context(nc.semaphore("d_" + str(k))) for k in self.dma_keys}
            block = st.enter_context(nc.Block())
            engobj = {"pe": "tensor", "act": "scalar", "dve": "vector", "pool": "gpsimd", "sp": "sync"}

            def make(e):
                def body(eng):
                    waited = {}
                    for o in self.ops[e]:
                        for d in o.deps:
                            if d.is_dma:
                                s, v, key = dsem[d.sem_key], d.dma_count, ("d", d.sem_key)
                            else:
                                s, v, key = esem[d.eng], d.count, ("e", d.eng)
                            if waited.get(key, 0) >= v:
                                continue
                            waited[key] = v
                            eng.wait_ge(s, v)
                        ins = o.fn(eng)
                        if o.is_dma:
                            ins.then_inc(dsem[o.sem_key], 16)
                        elif o.signal:
                            ins.then_inc(esem[e], 1)
                    if e == "sp":
                        fin = {}
                        for o in self.final_waits:
                            fin[o.sem_key] = max(fin.get(o.sem_key, 0), o.dma_count)
                        for k, v in fin.items():
                            if waited.get(("d", k), 0) >= v:
                                continue
                            eng.wait_ge(dsem[k], v)

                return body

            for e in ENGS:
                getattr(block, engobj[e])(make(e))


def attn_keys(t):
    if t >= 2:
        return [(t + d, d + 2) for d in (-2, -1, 0, 1, 2)]
    if t == 0:
        return [(0, 2), (1, 3), (2, 5), (3, 6)]
    return [(0, 1), (1, 2), (2, 3), (3, 5)]


def build(depth=DEPTH, n_pseq=4, ns_units=12):
    nc = bass.Bass("TRN2", target_bir_lowering=False)

    def din(name, shape):
        return nc.dram_tensor(name, shape, F32, kind="ExternalInput").ap()

    def dout(name, shape):
        return nc.dram_tensor(name, shape, F32, kind="ExternalOutput").ap()

    xp_d = din("xp", [n_pseq * 256, D])
    xs_d = din("xs", [3072, D])
    c2_d = din("c2", [2, D])
    ckT_d = din("ckT", [DEPTH, 512, 256])
    cv_d = din("cv", [DEPTH, 256, 512])
    wada_d = din("w_ada", [DEPTH, D, 3 * D])
    bada_d = din("b_ada", [DEPTH, 3 * D])
    win_d = din("w_in", [DEPTH, D, 3840])
    wout_d = din("w_out", [DEPTH, D, D])
    convw_d = din("conv_w", [DEPTH, 3, 256])
    glg_d = din("gmlp_ln_g", [DEPTH, 256])
    glb_d = din("gmlp_ln_b", [DEPTH, 256])
    wsT_d = din("wsT", [DEPTH, 128, 4, 128])
    bsT_d = din("bsT", [DEPTH, 128, 4])
    ebt_d = din("ebt", [DEPTH, 7, 128, 1024])
    lng_d = din("ln_g", [DEPTH, D])
    lnb_d = din("ln_b", [DEPTH, D])
    id_d = din("ident", [128, 128])
    yp_d = dout("yp", [n_pseq * 256, D])
    ys_d = dout("ys", [2048, D])
    nk_d = dout("nk", [n_pseq, DEPTH, 256, 512])
    nv_d = dout("nv", [n_pseq, DEPTH, 256, 512])
    sxp_d = nc.dram_tensor("sxp", [n_pseq * 256, D], F32).ap()
    sxs_d = nc.dram_tensor("sxs", [3072, D], F32).ap()

    S = Sched(nc)
    st = contextlib.ExitStack()
    with st:
        def sb(name, shape, dt=F32):
            return st.enter_context(nc.sbuf_tensor(name, shape, dt))

        def ps(name, shape, dt=F32):
            return st.enter_context(nc.psum_tensor(name, shape, dt))

        wF = sb("wF", [128, 8, 1536], BF16); b_wF = Buf("wF"); b_wFq = Buf("wFq")
        wT = sb("wT", [128, 8, 2304], BF16); b_wTk = Buf("wTk"); b_wTv = Buf("wTv"); b_wTvu = Buf("wTvu"); b_wTgb = Buf("wTgb"); b_wTgc = Buf("wTgc")
        wo = sb("wo", [128, 8, 1024], BF16); b_wo = Buf("wo")
        NH = 4
        hT = [sb("hT%d" % i, [128, 8, 256], BF16) for i in range(NH)]; b_hT = [Buf("hT%d" % i) for i in range(NH)]
        xA = [sb("xA%d" % i, [128, D]) for i in range(2)]; b_xA = [Buf("xA%d" % i) for i in range(2)]
        xn = [sb("xn%d" % i, [128, D], BF16) for i in range(2)]; b_xn = [Buf("xn%d" % i) for i in range(2)]
        QT = [sb("QT%d" % i, [128, 4, 256], BF16) for i in range(2)]; b_QT = [Buf("QT%d" % i) for i in range(2)]
        NK = 3
        KT = [sb("KT%d" % i, [128, 4, 256], BF16) for i in range(NK)]; b_KT = [Buf("KT%d" % i) for i in range(NK)]
        Vr = sb("Vr", [128, 2 * NK, 8, 65], BF16); b_V = [Buf("V%d" % i) for i in range(2 * NK)]
        yT = [sb("yT%d" % i, [128, 8, 256], BF16) for i in range(2)]; b_yT = [Buf("yT%d" % i) for i in range(2)]
        sgc = [sb("sgc%d" % i, [128, 2, 512], BF16) for i in range(2)]; b_sgc = [Buf("sgc%d" % i) for i in range(2)]
        ktok = sb("ktok", [128, 512], BF16); b_ktok = Buf("ktok")
        ckT = sb("ckT_sb", [128, 4, 256], BF16); b_ckT = Buf("ckT")
        cV = sb("cV", [128, 2, 8, 65], BF16); b_cV = Buf("cV")
        EB = sb("EB", [128, 7, 1024], BF16); b_EB = Buf("EB")
        tU = [sb("tU%d" % i, [128, D]) for i in range(2)]; b_tU = [Buf("tU%d" % i) for i in range(2)]
        Eb = sb("Eb", [128, 14, 512], BF16); b_E = [Buf("E%d" % i) for i in range(14)]
        c2f = sb("c2f", [128, 2, 8]); b_c2f = Buf("c2f")
        c2t = sb("c2t", [128, 2, 8]); b_c2t = Buf("c2t")
        scT = sb("scT", [128, 8, 2], BF16); b_scT = Buf("scT")
        sc1p = sb("sc1p", [128, 2, 8]); b_sc1p = Buf("sc1p")
        shp = sb("shp", [128, 2, 8]); b_shp = Buf("shp")
        badc = sb("badc", [128, 16]); b_badc = Buf("badc")
        gate = sb("gate", [128, 2, D]); b_gate = Buf("gate")
        lng = sb("lng", [128, D]); b_lng = Buf("lng")
        lnb = sb("lnb", [128, D]); b_lnb = Buf("lnb")
        glg = sb("glg", [128, 256]); b_glg = Buf("glg")
        glb = sb("glb", [128, 256]); b_glb = Buf("glb")
        bsT = sb("bsT_sb", [128, 4]); b_bsT = Buf("bsT")
        bsb = sb("bsb", [128, 256]); b_bsb = Buf("bsb")
        rsw = sb("rsw", [128, 4]); b_rsw = Buf("rsw")
        ones1 = sb("ones1", [128, 2], BF16); b_ones1 = Buf("ones1")
        wsT = sb("wsT_sb", [128, 4, 128], BF16); b_wsT = Buf("wsT")
        cw = sb("cw", [128, 2, 3]); b_cw = Buf("cw")
        ident = sb("ident_sb", [128, 128], BF16); b_id = Buf("ident")
        nh = sb("nh", [128, 2]); b_nh = Buf("nh")
        stt = [sb("stt%d" % i, [128, 2, 6]) for i in range(4)]
        mv = [sb("mv%d" % i, [128, 2]) for i in range(4)]
        ve = [sb("ve%d" % i, [128, 1]) for i in range(4)]
        rs = [sb("rs%d" % i, [128, 1]) for i in range(4)]
        nm = [sb("nm%d" % i, [128, 1]) for i in range(4)]
        b_stt = [Buf("stt%d" % i) for i in range(4)]
        b_mv = [Buf("mv%d" % i) for i in range(4)]
        b_ve = [Buf("ve%d" % i) for i in range(4)]
        b_rs = [Buf("rs%d" % i) for i in range(4)]
        b_nm = [Buf("nm%d" % i) for i in range(4)]
        wkall = sb("wkall", [128, 4, 256])
        wk = {}
        for i_, nme in enumerate(("xa", "acc", "tg", "s2")):
            wk[nme] = (wkall[:, i_, :], Buf("wk_" + nme))
        wk["m"] = wk["xa"]
        tgc = wkall[:, 2:4, :].rearrange("p a t -> p (a t)")
        b_tgc_l = [wk["tg"][1], wk["s2"][1]]
        zb = sb("zb", [128, 258]); b_zb = Buf("zb")
        xh = sb("xh", [128, 4]); b_xh = Buf("xh")
        zh = sb("zh", [128, 4]); b_zh = Buf("zh")
        vn3 = sb("vn3", [128, 256], BF16); b_vn3 = Buf("vn3")
        ybt = sb("ybt", [128, 256], BF16); b_ybt = Buf("ybt")
        rinv = sb("rinv", [128, 8]); b_rinv = Buf("rinv")
        ob = sb("ob", [128, 512]); b_ob = Buf("ob")
        ob2 = sb("ob2", [128, 512]); b_ob2 = Buf("ob2")
        kst = ob2; b_kst = b_ob2
        vst = ob; b_vst = b_ob
        yct = sb("yct", [128, 512], BF16); b_yct = Buf("yct")

        NMM = 6
        mmb = [ps("mm%d" % i, [128, 512]) for i in range(NMM)]; b_mm = [Buf("mm%d" % i, True) for i in range(NMM)]
        pv = [ps("pv%d" % i, [128, 512]) for i in range(2)]; b_pv = [Buf("pv0", True), Buf("pv1", True)]
        mm_i = [0]

        pool_banks = {None: [0, 1, 2, 3, 4, 5], "S": [2, 3, 4, 5], "G": [0, 1]}
        pool_i = {None: 0, "S": 0, "G": 0}

        def mm(pool=None):
            if UNZIP and pool != "Gm":
                pool = None
            bl = pool_banks[pool]
            i = bl[pool_i[pool] % len(bl)]
            pool_i[pool] += 1
            return mmb[i], b_mm[i]

        def mmt(pool=None):
            t, b = mm(pool)
            return t[:].bitcast(BF16), b

        def ln_small(site, eng_after="dve"):
            S.op("dve", lambda e: e.tensor_scalar(out=ve[site][:], in0=mv[site][:, 1:2], scalar1=1e-5, scalar2=None, op0=ALU.add),
                 reads=[b_mv[site]], writes=[b_ve[site]])
            S.op("pool", lambda e: e.tensor_tensor(out=rs[site][:], in0=ve[site][:], in1=nh[:, 0:1], op=ALU.pow),
                 reads=[b_ve[site], b_nh], writes=[b_rs[site]], excl_dve=True)
            S.op("dve", lambda e: e.scalar_tensor_tensor(out=nm[site][:], in0=mv[site][:, 0:1], scalar=-1.0, in1=rs[site][:], op0=ALU.mult, op1=ALU.mult),
                 reads=[b_mv[site], b_rs[site]], writes=[b_nm[site]])

        S.dma("pool", lambda e: e.dma_start(out=ident[:], in_=id_d), writes=[b_id], sem_key="c_id")
        S.op("pool", lambda e: e.memset(nh[:], -0.5), writes=[b_nh])
        S.op("pool", lambda e: e.memset(ones1[:], 1.0), writes=[b_ones1])
        S.op("pool", lambda e: e.memset(Vr[:].rearrange("p a h d -> p (a h d)"), 1.0), writes=b_V)
        S.op("pool", lambda e: e.memset(cV[:].rearrange("p a h d -> p (a h d)"), 1.0), writes=[b_cV])
        for v in range(2):
            S.dma("sp", lambda e, v=v: e.dma_start(out=c2f[:, v, :], in_=c2_d[v, :].rearrange("(k p) -> p k", p=128), allow_slow_non_contiguous=True), writes=[b_c2f], sem_key="c_c2")
        S.op("act", lambda e: e.activation(out=c2t[:], in_=c2f[:], func=AF.Tanh, scale=0.5), reads=[b_c2f], writes=[b_c2t])
        S.op("dve", lambda e: e.scalar_tensor_tensor(out=c2t[:], in0=c2t[:], scalar=1.0, in1=c2f[:], op0=ALU.add, op1=ALU.mult), reads=[b_c2f, b_c2t], writes=[b_c2t])
        S.op("dve", lambda e: e.tensor_scalar(out=scT[:].rearrange("p k v -> p v k"), in0=c2t[:], scalar1=0.5, scalar2=None, op0=ALU.mult), reads=[b_c2t], writes=[b_scT])
        b_sxs = [Buf("sxs%d" % u) for u in range(12)]
        b_sxp = [Buf("sxp%d" % u) for u in range(n_pseq)]

        def layer(l):
            last = (l == depth - 1)
            def wdma(dst, src, bufd, key):
                S.dma("pool", lambda e: e.dma_start(out=dst, in_=src.rearrange("(k p) n -> p k n", p=128)), writes=[bufd], sem_key=key)
            wdma(wT[:, :, 1792:2304], win_d[l][:, 2304:2816], b_wTk, "w_Tk")
            wdma(wT[:, :, 768:1280], win_d[l][:, 2816:3328], b_wTv, "w_Tv")
            for c_ in range(2):
                S.dma("sp", lambda e, c_=c_: e.dma_start(out=cw[:, c_, :], in_=convw_d[l][:, c_ * 128:(c_ + 1) * 128].rearrange("j p -> p j"), allow_slow_non_contiguous=True), writes=[b_cw], sem_key="p_cw")
            S.dma("sp", lambda e: e.dma_start(out=glg[:], in_=glg_d[l, :].partition_broadcast(128)), writes=[b_glg], sem_key="p_glg")
            S.dma("sp", lambda e: e.dma_start(out=glb[:], in_=glb_d[l, :].partition_broadcast(128)), writes=[b_glb], sem_key="p_glb")
            S.dma("sp", lambda e: e.dma_start(out=lng[:], in_=lng_d[l, :].partition_broadcast(128)), writes=[b_lng], sem_key="p_lng")
            S.dma("sp", lambda e: e.dma_start(out=lnb[:], in_=lnb_d[l, :].partition_broadcast(128)), writes=[b_lnb], sem_key="p_lnb")
            S.dma("sp", lambda e: e.dma_start(out=bsT[:], in_=bsT_d[l]), writes=[b_bsT], sem_key="p_bsT")
            S.op("dve", lambda e: e.tensor_copy(out=bsb[:].rearrange("p (g c) -> p g c", g=4), in_=bsT[:, :].unsqueeze(2).to_broadcast([128, 4, 64])), reads=[b_bsT], writes=[b_bsb])
            S.dma("pool", lambda e: e.dma_start(out=wsT[:], in_=wsT_d[l]), writes=[b_wsT], sem_key="p_wsT")
            prs, bprs = mm()
            for g in range(4):
                S.op("pe", lambda e, g=g, prs=prs: e.matmul(prs[:, g:g + 1], lhsT=wsT[:, g, :], rhs=ones1[:, 0:1], start=True, stop=True), reads=[b_wsT, b_ones1], writes=[bprs])
            S.op("dve", lambda e, prs=prs: e.tensor_copy(out=rsw[:], in_=prs[:, 0:4]), reads=[bprs], writes=[b_rsw])
            S.op("dve", lambda e: e.tensor_tensor(out=glb[:].rearrange("p (g c) -> p g c", g=4), in0=glb[:].rearrange("p (g c) -> p g c", g=4), in1=rsw[:, :].unsqueeze(2).to_broadcast([128, 4, 64]), op=ALU.mult),
                 reads=[b_glb, b_rsw], writes=[b_glb])
            S.op("dve", lambda e: e.tensor_tensor(out=bsb[:], in0=bsb[:], in1=glb[:], op=ALU.add), reads=[b_bsb, b_glb], writes=[b_bsb])
            S.dma("sp", lambda e: e.dma_start(out=badc[:], in_=bada_d[l, 0:2048].rearrange("(j p) -> p j", p=128), allow_slow_non_contiguous=True), writes=[b_badc], sem_key="p_badc")
            wad = Eb[:, 0:8, :]
            Ssil = Eb[:, 8:12, :].rearrange("p (v a) (b t) -> p v (a b) t", v=2, b=4)
            b_Ssil_l = b_E[8:12]
            for v in range(2):
                S.op("dve", lambda e, v=v: e.tensor_copy(out=Ssil[:, v], in_=scT[:, :, v:v + 1].to_broadcast([128, 8, 128])), reads=[b_scT], writes=b_Ssil_l)
            for ch in range(6):
                S.dma("pool", lambda e, ch=ch: e.dma_start(out=wad, in_=wada_d[l][:, ch * 512:(ch + 1) * 512].rearrange("(k p) n -> p k n", p=128)),
                      writes=b_E[0:8], sem_key="w_ada")
                if ch < 4:
                    pt, bpt = mm()
                    for jb in range(4):
                        for kc in range(8):
                            S.op("pe", lambda e, jb=jb, kc=kc, pt=pt: e.matmul(pt[:, jb * 2:jb * 2 + 2], lhsT=wad[:, kc, jb * 128:(jb + 1) * 128], rhs=scT[:, kc, :], start=(kc == 0), stop=(kc == 7)),
                                 reads=b_E[0:8] + [b_scT], writes=[bpt])
                    for v in range(2):
                        dst = (shp if ch < 2 else sc1p)
                        bd = (b_shp if ch < 2 else b_sc1p)
                        j0 = (ch % 2) * 4
                        S.op("dve", lambda e, v=v, dst=dst, j0=j0, pt=pt, ch=ch: e.scalar_tensor_tensor(
                            out=dst[:, v, j0:j0 + 4], in0=pt[:, v:8:2], scalar=(0.0 if ch < 2 else 1.0), in1=badc[:, ch * 4:ch * 4 + 4], op0=ALU.add, op1=ALU.add),
                            reads=[bpt, b_badc], writes=[bd])
                else:
                    half = ch - 4
                    S.dma("sp", lambda e, half=half: e.dma_start(out=tU[0][:, 0:512], in_=bada_d[l, 2048 + half * 512:2048 + (half + 1) * 512].partition_broadcast(128)), writes=[b_tU[0]], sem_key="p_bg")
                    for v in range(2):
                        pt, bpt = mm()
                        for kc in range(8):
                            S.op("pe", lambda e, v=v, kc=kc, pt=pt: e.matmul(pt[:, :], lhsT=Ssil[:, v, kc, :], rhs=wad[:, kc, :], start=(kc == 0), stop=(kc == 7)),
                                 reads=b_E[0:8] + b_Ssil_l, writes=[bpt])
                        S.op("dve", lambda e, v=v, pt=pt, half=half: e.tensor_tensor(out=gate[:, v, half * 512:(half + 1) * 512], in0=pt[:, :], in1=tU[0][:, 0:512], op=ALU.add),
                             reads=[bpt, b_tU[0]], writes=[b_gate])
            wdma(wF[:, :, 1024:1536], win_d[l][:, 1792:2304], b_wFq, "w_Fq")
            wdma(wF[:, :, 0:1024], win_d[l][:, 0:1024], b_wF, "w_F")
            wdma(wT[:, :, 0:256], win_d[l][:, 1280:1536], b_wTvu, "w_Tvu")
            wdma(wT[:, :, 256:512], win_d[l][:, 1024:1280], b_wTvu, "w_Tvu2")
            wdma(wT[:, :, 1280:1792], win_d[l][:, 3328:3840], b_wTgc, "w_Tgc")
            wdma(wT[:, :, 512:768], win_d[l][:, 1536:1792], b_wTgb, "w_Tgb")
            wdma(wo[:], wout_d[l], b_wo, "w_o")
            S.dma("pool", lambda e: e.dma_start(out=ckT[:], in_=ckT_d[l].rearrange("(a p) k -> p a k", p=128)), writes=[b_ckT], sem_key="c_k")
            for a in range(2):
                S.dma("pool", lambda e, a=a: e.dma_start(out=cV[:, a, :, 0:64], in_=cv_d[l][a * 128:(a + 1) * 128, :].rearrange("p (h d) -> p h d", d=64)), writes=[b_cV], sem_key="c_v")
            for tid in range(7):
                S.dma("sp", lambda e, tid=tid: e.dma_start(out=tU[tid % 2][:], in_=ebt_d[l, tid]), writes=[b_tU[tid % 2]], sem_key="p_eb%d" % (tid % 2))
                S.op("act", lambda e, tid=tid: e.activation(out=EB[:, tid, :], in_=tU[tid % 2][:], func=AF.Exp), reads=[b_tU[tid % 2]], writes=[b_EB])

            def src_x(grp, tile):
                if l == 0:
                    return (xs_d if grp == "S" else xp_d)[tile * 128:(tile + 1) * 128, :]
                return (sxs_d if grp == "S" else sxp_d)[tile * 128:(tile + 1) * 128, :]

            def dst_x(grp, tile):
                if last:
                    return (ys_d if grp == "S" else yp_d)[tile * 128:(tile + 1) * 128, :]
                return (sxs_d if grp == "S" else sxp_d)[tile * 128:(tile + 1) * 128, :]

            def xbuf(grp, u):
                return (b_sxs if grp == "S" else b_sxp)[u]

            cnt = {"xA": 0, "tU": 0}

            def A_dma(grp, u):
                xis = []
                for j in range(2):
                    tile = 2 * u + j
                    xi = cnt["xA"] % 2
                    cnt["xA"] += 1
                    rd = [xbuf(grp, u)] if l > 0 else []
                    S.dma("sp", lambda e, xi=xi, tile=tile: e.dma_start(out=xA[xi][:], in_=src_x(grp, tile)), reads=rd, writes=[b_xA[xi]], sem_key="xA%d" % xi)
                    xis.append(xi)
                return xis

            def A_ln(grp, u, xis):
                for j in range(2):
                    xi = xis[j]
                    for i in range(2):
                        S.op("dve", lambda e, i=i, xi=xi, j=j: e.bn_stats(out=stt[j][:, i, :], in_=xA[xi][:, i * 512:(i + 1) * 512]), reads=[b_xA[xi]], writes=[b_stt[j]])
                    S.op("dve", lambda e, j=j: e.bn_aggr(out=mv[j][:], in_=stt[j][:].rearrange("p a b -> p (a b)")), reads=[b_stt[j]], writes=[b_mv[j]])
                    ln_small(j)
                    S.op("act", lambda e, xi=xi, j=j: e.activation(out=xn[j][:], in_=xA[xi][:], func=AF.Identity, bias=nm[j][:], scale=rs[j][:]),
                         reads=[b_xA[xi], b_nm[j], b_rs[j]], writes=[b_xn[j]])

            def A_pre(grp, u):
                A_ln(grp, u, A_dma(grp, u))

            def A_pe(grp, u, hs):
                v = 0 if grp == "S" else 1
                for j in range(2):
                    tr, b_tr = mmt()
                    for kc in range(8):
                        S.op("pe", lambda e, kc=kc, j=j, tr=tr: e.transpose(out=tr[:, kc * 128:(kc + 1) * 128], in_=xn[j][:, kc * 128:(kc + 1) * 128], identity=ident[:]),
                             reads=[b_xn[j], b_id], writes=[b_tr])
                    for kc in range(8):
                        if kc % 2 == 0:
                            S.op("dve", lambda e, kc=kc, j=j, tr=tr: e.tensor_scalar(out=hT[hs][:, kc, j * 128:(j + 1) * 128], in0=tr[:, kc * 128:(kc + 1) * 128],
                                                                                      scalar1=sc1p[:, v, kc:kc + 1], scalar2=shp[:, v, kc:kc + 1], op0=ALU.mult, op1=ALU.add),
                                 reads=[b_tr, b_sc1p, b_shp], writes=[b_hT[hs]])
                    for kc in range(8):
                        if kc % 2 == 1:
                            S.op("act", lambda e, kc=kc, j=j, tr=tr: e.activation(out=hT[hs][:, kc, j * 128:(j + 1) * 128], in_=tr[:, kc * 128:(kc + 1) * 128], func=AF.Identity,
                                                                                    scale=sc1p[:, v, kc:kc + 1], bias=shp[:, v, kc:kc + 1]),
                                 reads=[b_tr, b_sc1p, b_shp], writes=[b_hT[hs]])

            def proj_T(hs, j, c0, c1):
                pt, bpt = mm()
                n = c1 - c0
                for kc in range(8):
                    S.op("pe", lambda e, kc=kc, pt=pt: e.matmul(pt[:, 0:n], lhsT=hT[hs][:, kc, j * 128:(j + 1) * 128], rhs=wT[:, kc, c0:c1], start=(kc == 0), stop=(kc == 7)),
                         reads=[b_hT[hs], {1792: b_wTk, 768: b_wTv, 0: b_wTvu, 512: b_wTgb, 1280: b_wTgc}[c0]], writes=[bpt])
                return pt, bpt

            def proj_F(hs, pt, bpt, off, cb):
                for kc in range(8):
                    S.op("pe", lambda e, kc=kc: e.matmul(pt[:, off:off + 256], lhsT=wF[:, kc, cb * 128:(cb + 1) * 128], rhs=hT[hs][:, kc, :], start=(kc == 0), stop=(kc == 7)),
                         reads=[b_hT[hs], (b_wFq if cb >= 8 else b_wF)], writes=[bpt])

            def B_kv(grp, u, hs, ks):
                with_out = (grp == "P")
                for j in range(2):
                    vslot = 2 * ks + j
                    pt, bpt = proj_T(hs, j, 1792, 2304)
                    S.op("act", lambda e, pt=pt: e.activation(out=ktok[:], in_=pt[:, :], func=AF.Copy), reads=[bpt], writes=[b_ktok])
                    if with_out:
                        S.op("dve", lambda e, pt=pt: e.tensor_copy(out=kst[:], in_=pt[:, :]), reads=[bpt], writes=[b_kst])
                        S.dma("sp", lambda e, j=j: e.dma_start(out=nk_d[u, l, j * 128:(j + 1) * 128, :], in_=kst[:]), reads=[b_kst], sem_key="o_k", final=True)
                    tr, b_tr = mmt()
                    for a in range(4):
                        S.op("pe", lambda e, a=a, tr=tr: e.transpose(out=tr[:, a * 128:(a + 1) * 128], in_=ktok[:, a * 128:(a + 1) * 128], identity=ident[:]),
                             reads=[b_ktok, b_id], writes=[b_tr])
                    S.op("dve", lambda e, j=j, tr=tr: e.tensor_copy(out=KT[ks][:, :, j * 128:(j + 1) * 128], in_=tr[:, 0:512].rearrange("p (a t) -> p a t", a=4)),
                         reads=[b_tr], writes=[b_KT[ks]])
                    pt, bpt = proj_T(hs, j, 768, 1280)
                    S.op("act", lambda e, pt=pt, vslot=vslot: e.activation(out=Vr[:, vslot, :, 0:64], in_=pt[:, :].rearrange("p (h d) -> p h d", d=64), func=AF.Copy),
                         reads=[bpt], writes=[b_V[vslot]])
                    if with_out:
                        S.op("dve", lambda e, pt=pt: e.tensor_copy(out=vst[:], in_=pt[:, :]), reads=[bpt], writes=[b_vst])
                        S.dma("sp", lambda e, j=j: e.dma_start(out=nv_d[u, l, j * 128:(j + 1) * 128, :], in_=vst[:]), reads=[b_vst], sem_key="o_v", final=True)

            def proj_F_g(hs, pt, bpt, off, cb):
                proj_F(hs, pt, bpt, off, cb)
                yield

            def B_q_gen(grp, u, hs, qs, pool=None):
                for half in range(2):
                    pq, bpq = mm(pool)
                    yield from proj_F_g(hs, pq, bpq, 0, 8 + 2 * half)
                    yield from proj_F_g(hs, pq, bpq, 256, 9 + 2 * half)
                    S.op("act", lambda e, pq=pq, half=half: e.activation(out=QT[qs][:, 2 * half:2 * half + 2, :], in_=pq[:, :].rearrange("p (a t) -> p a t", a=2), func=AF.Copy),
                         reads=[bpq], writes=[b_QT[qs]])

            def run(gen):
                for _ in gen:
                    pass

            def zipper(ga, gb, na=1, nb=2):
                if UNZIP:
                    run(ga)
                    run(gb)
                    return
                da = db = False
                while not (da and db):
                    for _ in range(na):
                        if not da:
                            try:
                                next(ga)
                            except StopIteration:
                                da = True
                    for _ in range(nb):
                        if not db:
                            try:
                                next(gb)
                            except StopIteration:
                                db = True

            def B_q(grp, u, hs, qs):
                run(B_q_gen(grp, u, hs, qs))

            def B_conv_gen(grp, u, hs, qs, hs_prev, hs_next, pool=None):
                yt, byt = yT[qs], b_yT[qs]
                have_h = [hs_prev is not None, hs_next is not None]
                hal = pv[1][:, 384:392]
                b_halo = b_pv[1]
                for cbi, cb in enumerate((0, 1, 4, 5)):
                    for side in range(2):
                        if not have_h[side]:
                            continue
                        hsrc = hT[hs_prev][:, :, 255:256] if side == 0 else hT[hs_next][:, :, 0:1]
                        bsrc = b_hT[hs_prev] if side == 0 else b_hT[hs_next]
                        for kc in range(8):
                            S.op("pe", lambda e, kc=kc, cb=cb, hsrc=hsrc, col=cbi * 2 + side: e.matmul(hal[:, col:col + 1], lhsT=wF[:, kc, cb * 128:(cb + 1) * 128], rhs=hsrc[:, kc, :], start=(kc == 0), stop=(kc == 7)),
                                 reads=[bsrc, b_wF], writes=[b_halo])
                    yield
                S.op("pool", lambda e: e.memset(zh[:], 0.0), writes=[b_zh])
                for side in range(2):
                    if have_h[side]:
                        S.op("dve", lambda e, side=side: e.tensor_copy(out=xh[:, side:4:2], in_=hal[:, side:4:2]), reads=[b_halo], writes=[b_xh])
                        S.op("dve", lambda e, side=side: e.tensor_tensor(out=zh[:, side:4:2], in0=hal[:, 4 + side:8:2], in1=xh[:, side:4:2], op=ALU.mult), reads=[b_halo, b_xh], writes=[b_zh])
                for c in range(2):
                    p1, bp1 = mm(pool)
                    yield from proj_F_g(hs, p1, bp1, 0, 0 + c)
                    yield from proj_F_g(hs, p1, bp1, 256, 4 + c)
                    p2, bp2 = mm(pool)
                    yield from proj_F_g(hs, p2, bp2, 0, 2 + c)
                    yield from proj_F_g(hs, p2, bp2, 256, 6 + c)
                    xa_t, bxa = wk["xa"]; acc, bacc = wk["acc"]; tg, btg = wk["tg"]; s2, bs2 = wk["s2"]; m_, bm = wk["m"]
                    S.op("act", lambda e, p1=p1: e.activation(out=xa_t[:], in_=p1[:, 0:256], func=AF.Copy), reads=[bp1], writes=[bxa])
                    S.op("dve", lambda e, p1=p1: e.tensor_tensor(out=zb[:, 1:257], in0=p1[:, 256:512], in1=xa_t[:], op=ALU.mult), reads=[bp1, bxa], writes=[b_zb])
                    S.op("dve", lambda e, c=c: e.tensor_copy(out=zb[:, 0:258:257], in_=zh[:, 2 * c:2 * c + 2]), reads=[b_zh], writes=[b_zb])
                    S.op("act", lambda e, c=c: e.activation(out=acc[:], in_=zb[:, 1:257], func=AF.Identity, scale=cw[:, c, 1:2]), reads=[b_zb, b_cw], writes=[bacc])
                    S.op("dve", lambda e, c=c: e.scalar_tensor_tensor(out=acc[:], in0=zb[:, 0:256], scalar=cw[:, c, 0:1], in1=acc[:], op0=ALU.mult, op1=ALU.add), reads=[b_zb, b_cw, bacc], writes=[bacc])
                    S.op("dve", lambda e, c=c: e.scalar_tensor_tensor(out=acc[:], in0=zb[:, 2:258], scalar=cw[:, c, 2:3], in1=acc[:], op0=ALU.mult, op1=ALU.add), reads=[b_zb, b_cw, bacc], writes=[bacc])
                    S.op("act", lambda e, p2=p2: e.activation(out=tg[:], in_=p2[:, 256:512], func=AF.Tanh, scale=0.5), reads=[bp2], writes=[btg])
                    S.op("dve", lambda e, p2=p2: e.scalar_tensor_tensor(out=s2[:], in0=tg[:], scalar=1.0, in1=p2[:, 256:512], op0=ALU.add, op1=ALU.mult), reads=[btg, bp2], writes=[bs2])
                    S.op("dve", lambda e, p2=p2: e.tensor_tensor(out=m_[:], in0=p2[:, 0:256], in1=acc[:], op=ALU.mult), reads=[bp2, bacc], writes=[bm])
                    S.op("dve", lambda e, c=c: e.scalar_tensor_tensor(out=yt[:, c, :], in0=m_[:], scalar=0.5, in1=s2[:], op0=ALU.mult, op1=ALU.mult), reads=[bm, bs2], writes=[byt])
                    yield

            def B_conv(grp, u, hs, qs, hs_prev, hs_next):
                run(B_conv_gen(grp, u, hs, qs, hs_prev, hs_next))

            def B_gm(grp, u, hs, qs):
                yt, byt = yT[qs], b_yT[qs]
                for j in range(2):
                    xa_t, bxa = wk["xa"]; acc, bacc = wk["acc"]; tg, btg = wk["tg"]; s2, bs2 = wk["s2"]; m_, bm = wk["m"]
                    pvu, bpvu = proj_T(hs, j, 0, 512)
                    S.op("dve", lambda e, pvu=pvu: e.bn_stats(out=stt[2][:, 0, :], in_=pvu[:, 0:256]), reads=[bpvu], writes=[b_stt[2]])
                    S.op("dve", lambda e: e.bn_aggr(out=mv[2][:], in_=stt[2][:, 0, :]), reads=[b_stt[2]], writes=[b_mv[2]])
                    ln_small(2)
                    pgc, bpgc = proj_T(hs, j, 1280, 1792)
                    S.op("act", lambda e, pgc=pgc: e.activation(out=tgc[:], in_=pgc[:, :], func=AF.Tanh, scale=0.5), reads=[bpgc], writes=b_tgc_l)
                    S.op("dve", lambda e, pgc=pgc, j=j: e.scalar_tensor_tensor(out=sgc[qs][:, j, :], in0=tgc[:], scalar=1.0, in1=pgc[:, :], op0=ALU.add, op1=ALU.mult), reads=b_tgc_l + [bpgc], writes=[b_sgc[qs]])
                    pgb, bpgb = proj_T(hs, j, 512, 768)
                    S.op("act", lambda e, pvu=pvu: e.activation(out=vn3[:], in_=pvu[:, 0:256], func=AF.Identity, bias=nm[2][:], scale=rs[2][:]), reads=[bpvu, b_nm[2], b_rs[2]], writes=[b_vn3])
                    psv, bpsv = mm()
                    for g in range(4):
                        S.op("pe", lambda e, g=g, psv=psv: e.matmul(psv[:, g * 64:(g + 1) * 64], lhsT=wsT[:, g, :], rhs=vn3[:, g * 64:(g + 1) * 64], start=True, stop=True),
                             reads=[b_wsT, b_vn3], writes=[bpsv])
                    S.op("dve", lambda e, psv=psv: e.tensor_tensor(out=acc[:], in0=psv[:, 0:256], in1=glg[:], op=ALU.mult), reads=[bpsv, b_glg], writes=[bacc])
                    S.op("dve", lambda e: e.tensor_tensor(out=acc[:], in0=acc[:], in1=bsb[:], op=ALU.add), reads=[bacc, b_bsb], writes=[bacc])
                    S.op("dve", lambda e, pvu=pvu: e.tensor_tensor(out=m_[:], in0=pvu[:, 256:512], in1=acc[:], op=ALU.mult), reads=[bpvu, bacc], writes=[bm])
                    S.op("act", lambda e, pgb=pgb: e.activation(out=tg[:], in_=pgb[:, 0:256], func=AF.Tanh, scale=0.5), reads=[bpgb], writes=[btg])
                    S.op("dve", lambda e, pgb=pgb: e.scalar_tensor_tensor(out=s2[:], in0=tg[:], scalar=1.0, in1=pgb[:, 0:256], op0=ALU.add, op1=ALU.mult), reads=[btg, bpgb], writes=[bs2])
                    S.op("dve", lambda e: e.scalar_tensor_tensor(out=ybt[:], in0=m_[:], scalar=0.5, in1=s2[:], op0=ALU.mult, op1=ALU.mult), reads=[bm, bs2], writes=[b_ybt])
                    tr, b_tr = mmt()
                    for a in range(2):
                        S.op("pe", lambda e, a=a, tr=tr: e.transpose(out=tr[:, a * 128:(a + 1) * 128], in_=ybt[:, a * 128:(a + 1) * 128], identity=ident[:]), reads=[b_ybt, b_id], writes=[b_tr])
                    S.op("act", lambda e, j=j, tr=tr: e.activation(out=yt[:, 2:4, j * 128:(j + 1) * 128], in_=tr[:, 0:256].rearrange("p (a t) -> p a t", a=2), func=AF.Copy), reads=[b_tr], writes=[byt])

            def C_s_gen(grp, u, qs, chunks, j, pool=None):
                for ci, (kf, kb, vf, vb, tid) in enumerate(chunks):
                    SE, b_SE = mm(pool)
                    SO, b_SO = mm(pool)
                    for h in range(8):
                        pair, half = h // 2, h % 2
                        bank, bbank = (SE, b_SE) if half == 0 else (SO, b_SO)
                        S.op("pe", lambda e, kf=kf, pair=pair, half=half, bank=bank: e.matmul(
                            bank[:, pair * 128:(pair + 1) * 128], lhsT=kf(pair, half), rhs=QT[qs][64 * half:64 * half + 64, pair, j * 128:(j + 1) * 128], start=True, stop=True),
                            reads=[kb, b_QT[qs]], writes=[bbank])
                    for half in range(2):
                        bank, bbank = (SE, b_SE) if half == 0 else (SO, b_SO)
                        ei = 2 * ci + half
                        S.op("act", lambda e, bank=bank, ei=ei: e.activation(out=Eb[:, ei, :], in_=bank[:, :], func=AF.Exp, scale=SCALE), reads=[bbank], writes=[b_E[ei]])
                        if tid is not None:
                            S.op(_eb_eng(ci, half), lambda e, ei=ei, tid=tid, half=half: e.tensor_tensor(out=Eb[:, ei, :], in0=Eb[:, ei, :], in1=EB[:, tid, half * 512:(half + 1) * 512], op=ALU.mult),
                                 reads=[b_E[ei], b_EB], writes=[b_E[ei]])
                    yield

            def C_s(grp, u, qs, keyf, j):
                chunks = keyf(2 * u + j)
                run(C_s_gen(grp, u, qs, chunks, j))
                return chunks

            def C_pv(grp, u, qs, chunks, j):
                nck = len(chunks)
                for h in range(8):
                    pair, half = h // 2, h % 2
                    pb, bpb = pv[h // 4], b_pv[h // 4]
                    for ci, (kf, kb, vf, vb, tid) in enumerate(chunks):
                        ei = 2 * ci + half
                        S.op("pe", lambda e, ei=ei, pair=pair, vf=vf, h=h, pb=pb, ci=ci: e.matmul(
                            pb[:, (h % 4) * 65:(h % 4) * 65 + 65], lhsT=Eb[:, ei, pair * 128:(pair + 1) * 128], rhs=vf(h), start=(ci == 0), stop=(ci == nck - 1)),
                            reads=[b_E[ei], vb], writes=[bpb])
                for g2 in range(2):
                    pb, bpb = pv[g2], b_pv[g2]
                    pv3 = pb[:, 0:260].rearrange("p (h d) -> p h d", d=65)
                    S.op("dve", lambda e, pv3=pv3, g2=g2: e.reciprocal(out=rinv[:, 4 * g2:4 * g2 + 4], in_=pv3[:, :, 64]), reads=[bpb], writes=[b_rinv])
                    S.op("dve", lambda e, pv3=pv3, g2=g2: e.tensor_tensor(out=ob[:, 256 * g2:256 * g2 + 256].rearrange("p (h d) -> p h d", d=64), in0=pv3[:, :, 0:64],
                                                                            in1=rinv[:, 4 * g2:4 * g2 + 4].unsqueeze(2).to_broadcast([128, 4, 64]), op=ALU.mult),
                         reads=[bpb, b_rinv], writes=[b_ob])
                S.op("dve", lambda e: e.scalar_tensor_tensor(out=yct[:], in0=ob[:], scalar=0.5, in1=sgc[qs][:, j, :], op0=ALU.mult, op1=ALU.mult), reads=[b_ob, b_sgc[qs]], writes=[b_yct])

            def C_o(grp, u, qs, j):
                run(C_o_gen(grp, u, qs, j))

            def C_o_gen(grp, u, qs, j, pool=None):
                v = 0 if grp == "S" else 1
                yt, byt = yT[qs], b_yT[qs]
                tile = 2 * u + j
                tr, b_tr = mmt(pool)
                for a in range(4):
                    S.op("pe", lambda e, a=a: e.transpose(out=tr[:, a * 128:(a + 1) * 128], in_=yct[:, a * 128:(a + 1) * 128], identity=ident[:]), reads=[b_yct, b_id], writes=[b_tr])
                S.op("act", lambda e: e.activation(out=yt[:, 4:8, j * 128:(j + 1) * 128], in_=tr[:, 0:512].rearrange("p (a t) -> p a t", a=4), func=AF.Copy), reads=[b_tr], writes=[byt])
                yield
                ti = cnt["tU"] % 2
                cnt["tU"] += 1
                rd = [xbuf(grp, u)] if l > 0 else []
                S.dma("sp", lambda e: e.dma_start(out=tU[ti][:], in_=src_x(grp, tile)), reads=rd, writes=[b_tU[ti]], sem_key="xC%d" % ti)
                for n in range(2):
                    po, bpo = mm(pool)
                    for kc in range(8):
                        S.op("pe", lambda e, kc=kc, n=n, po=po: e.matmul(po[:, :], lhsT=yt[:, kc, j * 128:(j + 1) * 128], rhs=wo[:, kc, n * 512:(n + 1) * 512], start=(kc == 0), stop=(kc == 7)),
                             reads=[byt, b_wo], writes=[bpo])
                        if kc == 3:
                            yield
                    S.op("dve", lambda e, n=n, po=po: e.tensor_tensor(out=ob2[:], in0=po[:, :], in1=gate[:, v, n * 512:(n + 1) * 512], op=ALU.mult),
                         reads=[bpo, b_gate], writes=[b_ob2])
                    S.op("dve", lambda e, n=n: e.scalar_tensor_tensor(out=tU[ti][:, n * 512:(n + 1) * 512], in0=tU[ti][:, n * 512:(n + 1) * 512], scalar=ALPHA, in1=ob2[:], op0=ALU.mult, op1=ALU.add),
                         reads=[b_ob2, b_tU[ti]], writes=[b_tU[ti]])
                    yield
                for i in range(2):
                    S.op("dve", lambda e, i=i: e.bn_stats(out=stt[3][:, i, :], in_=tU[ti][:, i * 512:(i + 1) * 512]), reads=[b_tU[ti]], writes=[b_stt[3]])
                S.op("dve", lambda e: e.bn_aggr(out=mv[3][:], in_=stt[3][:].rearrange("p a b -> p (a b)")), reads=[b_stt[3]], writes=[b_mv[3]])
                ln_small(3)
                S.op("act", lambda e: e.activation(out=tU[ti][:], in_=tU[ti][:], func=AF.Identity, bias=nm[3][:], scale=rs[3][:]), reads=[b_tU[ti], b_nm[3], b_rs[3]], writes=[b_tU[ti]])
                S.op("pool", lambda e: e.tensor_tensor(out=tU[ti][:], in0=tU[ti][:], in1=lng[:], op=ALU.mult), reads=[b_tU[ti], b_lng], writes=[b_tU[ti]])
                S.op("pool", lambda e: e.tensor_tensor(out=tU[ti][:], in0=tU[ti][:], in1=lnb[:], op=ALU.add), reads=[b_tU[ti], b_lnb], writes=[b_tU[ti]])
                S.dma("sp", lambda e: e.dma_start(out=dst_x(grp, tile), in_=tU[ti][:]), reads=[b_tU[ti]], writes=([] if last else [xbuf(grp, u)]),
                      sem_key="xo%d" % ti, final=True)

            def ctx_chunks():
                out = []
                for a in range(2):
                    out.append((lambda pair, half, a=a: ckT[64 * half:64 * half + 64, pair, a * 128:(a + 1) * 128], b_ckT,
                                lambda h, a=a: cV[:, a, h, :], b_cV, None))
                return out

            nP = ns_units - l
            nR = ns_units - 1 - l

            def keyf_S(tile):
                out = []
                for (kt, tid) in attn_keys(tile):
                    ku, kj = kt // 2, kt % 2
                    ksl = ku % NK
                    out.append((lambda pair, half, ksl=ksl, kj=kj: KT[ksl][64 * half:64 * half + 64, pair, kj * 128:(kj + 1) * 128], b_KT[ksl],
                                lambda h, ksl=ksl, kj=kj: Vr[:, 2 * ksl + kj, h, :], b_V[2 * ksl + kj], tid))
                return ctx_chunks() + out

            def pipeline(grp, nP, nR, keyf_of, base, halo):
                def hs_(u): return (base + u) % NH
                def ks_(u): return (base + u) % NK
                def qs_(u): return (base + u) % 2
                def hprev(u): return hs_(u - 1) if (halo and u > 0) else None
                def hnext(u): return hs_(u + 1) if halo else None
                for u0 in range(min(3, nP)):
                    A_pre(grp, u0); A_pe(grp, u0, hs_(u0))
                B_kv(grp, 0, hs_(0), ks_(0)); B_q(grp, 0, hs_(0), qs_(0)); B_conv(grp, 0, hs_(0), qs_(0), None, hnext(0)); B_gm(grp, 0, hs_(0), qs_(0))
                pend = None
                for i in range(nP):
                    u1 = i + 1
                    full1 = u1 < nR
                    if i + 3 < nP:
                        xis_ = A_dma(grp, i + 3)
                    if u1 < nP:
                        B_kv(grp, u1, hs_(u1), ks_(u1))
                    if pend is not None:
                        C_o(*pend)
                        pend = None
                    if i + 3 < nP:
                        A_ln(grp, i + 3, xis_)
                    g_b = iter(())
                    if full1:
                        def g_b_f(u1=u1):
                            yield from B_q_gen(grp, u1, hs_(u1), qs_(u1), "G")
                            yield from B_conv_gen(grp, u1, hs_(u1), qs_(u1), hprev(u1), hnext(u1), "G")
                        g_b = g_b_f()
                    if i < nR:
                        kf = keyf_of(i)
                        ch0 = kf(2 * i)
                        zipper(C_s_gen(grp, i, qs_(i), ch0, 0, "S"), g_b, 1, 2)
                        C_pv(grp, i, qs_(i), ch0, 0)
                        ch1 = kf(2 * i + 1)
                        zipper(C_s_gen(grp, i, qs_(i), ch1, 1, "S"), C_o_gen(grp, i, qs_(i), 0, "G"), 2, 1)
                    else:
                        run(g_b)
                    if i + 3 < nP:
                        A_pe(grp, i + 3, hs_(i + 3))
                    if full1:
                        B_gm(grp, u1, hs_(u1), qs_(u1))
                    if i < nR:
                        C_pv(grp, i, qs_(i), ch1, 1)
                        pend = (grp, i, qs_(i), 1)
                if pend is not None:
                    C_o(*pend)

            def keyf_P_of(u):
                ks = (nP_s + u) % NK
                def keyf(tile):
                    out = []
                    for kj in range(2):
                        out.append((lambda pair, half, kj=kj: KT[ks][64 * half:64 * half + 64, pair, kj * 128:(kj + 1) * 128], b_KT[ks],
                                    lambda h, kj=kj: Vr[:, 2 * ks + kj, h, :], b_V[2 * ks + kj], None))
                    return out
                return keyf

            nP_s = nP
            if not SKIPS:
                pipeline("S", nP, nR, lambda u: keyf_S, 0, True)
            if not SKIPP:
                pipeline("P", n_pseq, n_pseq, keyf_P_of, nP_s, False)

        for l_ in range(depth):
            layer(l_)

        S.emit()
    return nc


def _tables(rpb, flip):
    reps = [(4, 2), (4, 3), (4, 4), (4, 5), (4, 6), (0, 2), (0, 3)]
    out = np.empty((DEPTH, 7, 128, 2, 4, 128), np.float32)
    p = np.arange(128)
    for tid, (t, u) in enumerate(reps):
        ql = t * 128 + p
        kl = u * 128 + p
        qg = 4095 - ql if flip else ql
        kg = 4095 - kl if flip else kl
        qr, qc = qg // 64, qg % 64
        kr, kc = kg // 64, kg % 64
        rs_ = np.clip(qr - 4, 0, 56)
        cs_ = np.clip(qc - 8, 0, 48)
        valid = ((kr[:, None] >= rs_[None, :]) & (kr[:, None] < rs_[None, :] + 8)
                 & (kc[:, None] >= cs_[None, :]) & (kc[:, None] < cs_[None, :] + 16))
        dr = np.clip(kr[:, None] - qr[None, :] + 7, 0, 14)
        dc = np.clip(kc[:, None] - qc[None, :], -15, 15) + 15
        g = rpb[:, :, dr, dc]
        g = np.where(valid[None, None], g, np.float32(NEG))
        g = g.transpose(0, 2, 1, 3).reshape(DEPTH, 128, 4, 2, 128).transpose(0, 1, 3, 2, 4)
        out[:, tid] = g
    return np.ascontiguousarray(out.reshape(DEPTH, 7, 128, 1024))


_NC_CACHE = {}


def kernel(x_prompt, x_sample, cache_k, cache_v, c, c_ctx, w_ada, b_ada, w_in, conv_w,
           gmlp_ln_g, gmlp_ln_b, w_spatial, b_spatial, rpb, w_out, ln_g, ln_b):
    f = lambda a: np.ascontiguousarray(np.asarray(a, dtype=np.float32))
    x_prompt, x_sample, cache_k, cache_v, c, c_ctx = map(f, (x_prompt, x_sample, cache_k, cache_v, c, c_ctx))
    w_ada, b_ada, w_in, conv_w, gmlp_ln_g, gmlp_ln_b = map(f, (w_ada, b_ada, w_in, conv_w, gmlp_ln_g, gmlp_ln_b))
    w_spatial, b_spatial, rpb, w_out, ln_g, ln_b = map(f, (w_spatial, b_spatial, rpb, w_out, ln_g, ln_b))
    if "nc" not in _NC_CACHE:
        _NC_CACHE["nc"] = build()
    nc = _NC_CACHE["nc"]
    ident = np.eye(128, dtype=np.float32)
    tabs = [_tables(rpb, 0), _tables(rpb, 1)]
    in_maps = []
    for i in range(8):
        b, flip = i // 2, i % 2
        xs = x_sample[b][::-1] if flip else x_sample[b]
        xp = x_prompt[4 * i:4 * i + 4]
        if flip:
            xp = xp[:, ::-1]
        ws = w_spatial[:, :, ::-1, ::-1] if flip else w_spatial
        bs = b_spatial[:, :, ::-1] if flip else b_spatial
        cwv = conv_w[:, ::-1, :] if flip else conv_w
        in_maps.append({
            "xp": f(xp.reshape(1024, D)),
            "xs": f(xs[0:3072]),
            "c2": f(np.stack([c[b], c_ctx])),
            "ckT": f(cache_k[b].reshape(DEPTH, 256, 512).transpose(0, 2, 1)),
            "cv": f(cache_v[b].reshape(DEPTH, 256, 512)),
            "w_ada": w_ada, "b_ada": b_ada, "w_in": w_in, "w_out": w_out,
            "conv_w": f(cwv), "gmlp_ln_g": gmlp_ln_g, "gmlp_ln_b": gmlp_ln_b,
            "wsT": f(ws.transpose(0, 3, 1, 2)), "bsT": f(bs.transpose(0, 2, 1)),
            "ebt": tabs[flip], "ln_g": ln_g, "ln_b": ln_b, "ident": ident,
        })
    res = run_bass_kernel_spmd(nc, in_maps, core_ids=list(range(8))).results
    y_prompt = np.empty((32, 256, D), np.float32)
    y_sample = np.empty((4, 4096, D), np.float32)
    new_k = np.empty((32, DEPTH, 256, 8, 64), np.float32)
    new_v = np.empty((32, DEPTH, 256, 8, 64), np.float32)
    for i in range(8):
        b, flip = i // 2, i % 2
        r = res[i]
        yp = np.asarray(r["yp"]).reshape(4, 256, D)
        nk = np.asarray(r["nk"]).reshape(4, DEPTH, 256, 8, 64)
        nv = np.asarray(r["nv"]).reshape(4, DEPTH, 256, 8, 64)
        ys = np.asarray(r["ys"])
        if flip:
            yp = yp[:, ::-1]
            nk = nk[:, :, ::-1]
            nv = nv[:, :, ::-1]
            y_sample[b, 2048:] = ys[::-1]
        else:
            y_sample[b, :2048] = ys
        y_prompt[4 * i:4 * i + 4] = yp
        new_k[4 * i:4 * i + 4] = nk
        new_v[4 * i:4 * i + 4] = nv
    return (y_prompt, y_sample, new_k, new_v)
```

```python
import contextlib
import numpy as np
import concourse.bass as bass
import concourse.mybir as mybir
from concourse.bass_utils import run_bass_kernel_spmd

F32 = mybir.dt.float32
BF16 = mybir.dt.bfloat16
ALU = mybir.AluOpType
AF = mybir.ActivationFunctionType

D = 1024
DEPTH = 4
ALPHA = (2 * DEPTH) ** 0.25
SCALE = 64 ** -0.5
NEG = -30000.0
ENGS = ("pe", "act", "dve", "pool", "sp")
MAXOPS = 10 ** 9
SCHEDULE = True
PRIO_W = 3.0
TOKEN_ALL = False
EB_MODE = "split"
TOKEN_PSUM_EXEMPT = True
UNZIP = True


def _eb_eng(ci, half):
    if EB_MODE == "dve":
        return "dve"
    if EB_MODE == "pool":
        return "pool"
    if EB_MODE == "p3":
        return "pool" if (half == 1 and ci % 2 == 0) else "dve"
    return "pool" if half == 1 else "dve"
SKIPP = False
SKIPS = False


class Buf:
    __slots__ = ("name", "excl", "last_w", "readers")

    def __init__(self, name, excl=False):
        self.name = name
        self.excl = excl
        self.last_w = None
        self.readers = []


class Op:
    __slots__ = ("eng", "fn", "is_dma", "sem_key", "deps", "signal", "count", "dma_count", "idx", "reads", "writes",
                 "preds", "succs", "npred", "cost", "fin", "pos", "mode", "start", "prio")

    def __init__(self, eng, fn, is_dma, sem_key):
        self.eng = eng
        self.fn = fn
        self.is_dma = is_dma
        self.sem_key = sem_key
        self.deps = []
        self.signal = False
        self.count = 0
        self.dma_count = 0
        self.mode = 0


class _Rec:
    def __init__(self):
        self.name = None
        self.kw = {}
        self.args = ()

    def __getattr__(self, name):
        def f(*a, **k):
            self.name, self.args, self.kw = name, a, k
            return self
        return f


def _nfree(ap):
    n = 1
    for d in tuple(ap.shape)[1:]:
        n *= int(d)
    return n


def _is_psum_or_f32(ap):
    try:
        return ("psum" in str(ap.space).lower()) or (ap.dtype == F32)
    except Exception:
        return True


def _cost_ns(op):
    r = _Rec()
    try:
        op.fn(r)
    except Exception:
        return 300.0
    kw, a, nm_ = r.kw, r.args, r.name
    out = kw.get("out", a[0] if a else None)
    try:
        if op.is_dma:
            byts = _nfree(out) * int(tuple(out.shape)[0]) * 4
            return 2000.0 + byts / 200.0
        if op.eng == "pe":
            if nm_ == "transpose":
                return 75.0
            rhs = kw.get("rhs", a[2] if len(a) > 2 else None)
            lhsT = kw.get("lhsT", a[1] if len(a) > 1 else None)
            if int(tuple(lhsT.shape)[0]) <= 64:
                op.mode = 64
                return max(35.0, 15.0 + _nfree(rhs) * 0.5)
            return max(45.0, 30.0 + _nfree(rhs) * 0.45)
        n = _nfree(out)
        if op.eng == "act":
            if kw.get("func") == AF.Exp:
                return 150.0 + n * 0.69
            return 240.0 + n * 0.95
        if op.eng == "pool":
            if nm_ == "memset":
                return 150.0 + n * 0.9
            if n <= 8:
                return 480.0
            return 250.0 + n * 2.1
        if nm_ == "reciprocal":
            return 1000.0
        if nm_ == "bn_aggr":
            return 250.0
        if nm_ == "bn_stats":
            return 160.0 + _nfree(kw.get("in_")) * 1.0
        if n <= 8:
            return 550.0
        srcs = [kw.get(k) for k in ("in0", "in1", "in_") if kw.get(k) is not None]
        slow = any(_is_psum_or_f32(x) for x in srcs)
        if nm_ in ("tensor_tensor", "scalar_tensor_tensor"):
            return 160.0 + n * (1.04 if slow else 1.5)
        return 200.0 + n * (1.04 if slow else 0.55)
    except Exception:
        return 300.0


class Sched:
    def __init__(self, nc):
        self.nc = nc
        self.all = []
        self.ops = {e: [] for e in ENGS}
        self.dma_counts = {}
        self.dma_keys = []
        self.final_waits = []
        self.dve_token = Buf("dve_token")

    def _add(self, eng, fn, reads, writes, is_dma=False, sem_key=None):
        op = Op(eng, fn, is_dma, sem_key)
        if eng == "dve" and not is_dma:
            r_ = _Rec()
            try:
                fn(r_)
            except Exception:
                pass
            psrc = False
            if TOKEN_PSUM_EXEMPT:
                for k_ in ("in0", "in1", "in_"):
                    x_ = r_.kw.get(k_)
                    if x_ is not None and "psum" in str(getattr(x_, "space", "")).lower():
                        psrc = True
            if TOKEN_ALL or (r_.name not in ("tensor_tensor", "scalar_tensor_tensor") and not psrc):
                reads = reads + [self.dve_token]
        op.reads, op.writes = reads, writes
        op.idx = len(self.all)
        self.all.append(op)
        if is_dma:
            if sem_key not in self.dma_counts:
                self.dma_counts[sem_key] = 0
                self.dma_keys.append(sem_key)
            self.dma_counts[sem_key] += 16
            op.dma_count = self.dma_counts[sem_key]
        return op

    def op(self, eng, fn, reads=(), writes=(), excl_dve=False):
        w = list(writes)
        if excl_dve:
            w.append(self.dve_token)
        return self._add(eng, fn, list(reads), w)

    def dma(self, eng, fn, reads=(), writes=(), sem_key=None, final=False):
        o = self._add(eng, fn, list(reads), list(writes), is_dma=True, sem_key=sem_key)
        if final:
            self.final_waits.append(o)
        return o

    def _edges(self):
        last_key = {}
        for op in self.all:
            preds = {}
            for b in op.reads:
                if b.last_w is not None:
                    preds[id(b.last_w)] = b.last_w
                if b.excl:
                    for r in b.readers:
                        if r.eng != op.eng:
                            preds[id(r)] = r
            for b in op.writes:
                if b.last_w is not None:
                    preds[id(b.last_w)] = b.last_w
                for r in b.readers:
                    preds[id(r)] = r
            if op.is_dma:
                p = last_key.get(op.sem_key)
                if p is not None:
                    preds[id(p)] = p
                last_key[op.sem_key] = op
            preds.pop(id(op), None)
            for b in op.reads:
                b.readers.append(op)
            for b in op.writes:
                b.last_w = op
                b.readers = []
            op.preds = list(preds.values())
            op.succs = []
        for op in self.all:
            for p in op.preds:
                p.succs.append(op)

    def _schedule(self):
        SEM = 120.0
        for op in self.all:
            op.cost = _cost_ns(op)
            op.npred = len(op.preds)
            op.fin = None
        rank = {}
        for op in reversed(self.all):
            r_ = 0.0
            for s_ in op.succs:
                r2 = rank[id(s_)]
                if r2 > r_:
                    r_ = r2
            rank[id(op)] = r_ + op.cost + SEM
        tot = max(rank.values())
        n_all = float(len(self.all))
        for op in self.all:
            op.prio = op.idx / n_all - PRIO_W * rank[id(op)] / tot
        cand = {e: [] for e in ENGS}
        ready = {}
        for op in self.all:
            if op.npred == 0:
                cand[op.eng].append(op)
                ready[id(op)] = 0.0
        free = {e: 0.0 for e in ENGS}
        order = {e: [] for e in ENGS}
        pe_mode = [0]
        left = len(self.all)
        while left:
            best = None
            for e in ENGS:
                cl = cand[e]
                if not cl:
                    continue
                t = free[e]
                pick = None
                if e == "pe":
                    for o in cl:
                        if ready[id(o)] <= t and o.mode == pe_mode[0]:
                            if pick is None or o.prio < pick.prio:
                                pick = o
                if pick is None:
                    for o in cl:
                        if ready[id(o)] <= t:
                            if pick is None or o.prio < pick.prio:
                                pick = o
                if pick is not None:
                    st_ = t
                else:
                    for o in cl:
                        rt = ready[id(o)]
                        if pick is None or rt < ready[id(pick)] or (rt == ready[id(pick)] and o.idx < pick.idx):
                            pick = o
                    st_ = ready[id(pick)]
                if best is None or st_ < best[0] or (st_ == best[0] and pick.idx < best[1].idx):
                    best = (st_, pick)
            st_, o = best
            e = o.eng
            cand[e].remove(o)
            if e == "pe":
                if o.mode != pe_mode[0]:
                    st_ += 120.0
                pe_mode[0] = o.mode
            if o.is_dma:
                free[e] = st_ + 60.0
            else:
                free[e] = st_ + o.cost
            o.fin = st_ + o.cost
            o.start = st_
            o.pos = len(order[e])
            order[e].append(o)
            left -= 1
            for s_ in o.succs:
                s_.npred -= 1
                if s_.npred == 0:
                    pe_s = (s_.eng == "pe" and not s_.is_dma)
                    ready[id(s_)] = max((p.start if (pe_s and p.eng == "pe" and not p.is_dma) else p.fin + SEM) for p in s_.preds)
                    cand[s_.eng].append(s_)
        self.model_ns = max(o.fin for o in self.all)
        return order

    def emit(self):
        nc = self.nc
        self._edges()
        if SCHEDULE:
            self.ops = self._schedule()
        else:
            self.ops = {e: [o for o in self.all if o.eng == e] for e in ENGS}
            for e in ENGS:
                for i, o in enumerate(self.ops[e]):
                    o.pos = i
        for e in ENGS:
            for o in self.ops[e]:
                lastp = {}
                dl = []
                for p in o.preds:
                    if p.is_dma:
                        dl.append(p)
                        continue
                    if p.eng == "pe" and o.eng == "pe" and not o.is_dma:
                        continue
                    q = lastp.get(p.eng)
                    if q is None or p.pos > q.pos:
                        lastp[p.eng] = p
                o.deps = dl + list(lastp.values())
                for p in lastp.values():
                    p.signal = True
        for e in ENGS:
            c = 0
            for o in self.ops[e]:
                if (not o.is_dma) and o.signal:
                    c += 1
                    o.count = c
        with contextlib.ExitStack() as st:
            esem = {e: st.enter_context(nc.semaphore("s_" + e)) for e in ENGS}
            dsem = {k: st.enter_context(nc.semaphore("d_" + str(k))) for k in self.dma_keys}
            block = st.enter_context(nc.Block())
            engobj = {"pe": "tensor", "act": "scalar", "dve": "vector", "pool": "gpsimd", "sp": "sync"}

            def make(e):
                def body(eng):
                    waited = {}
                    for o in self.ops[e]:
                        for d in o.deps:
                            if d.is_dma:
                                s, v, key = dsem[d.sem_key], d.dma_count, ("d", d.sem_key)
                            else:
                                s, v, key = esem[d.eng], d.count, ("e", d.eng)
                            if waited.get(key, 0) >= v:
                                continue
                            waited[key] = v
                            eng.wait_ge(s, v)
                        ins = o.fn(eng)
                        if o.is_dma:
                            ins.then_inc(dsem[o.sem_key], 16)
                        elif o.signal:
                            ins.then_inc(esem[e], 1)
                    if e == "sp":
                        fin = {}
                        for o in self.final_waits:
                            fin[o.sem_key] = max(fin.get(o.sem_key, 0), o.dma_count)
                        for k, v in fin.items():
                            if waited.get(("d", k), 0) >= v:
                                continue
                            eng.wait_ge(dsem[k], v)

                return body

            for e in ENGS:
                getattr(block, engobj[e])(make(e))


def attn_keys(t):
    if t >= 2:
        return [(t + d, d + 2) for d in (-2, -1, 0, 1, 2)]
    if t == 0:
        return [(0, 2), (1, 3), (2, 5), (3, 6)]
    return [(0, 1), (1, 2), (2, 3), (3, 5)]


def build(depth=DEPTH, n_pseq=4, ns_units=12):
    nc = bass.Bass("TRN2", target_bir_lowering=False)

    def din(name, shape):
        return nc.dram_tensor(name, shape, F32, kind="ExternalInput").ap()

    def dout(name, shape):
        return nc.dram_tensor(name, shape, F32, kind="ExternalOutput").ap()

    xp_d = din("xp", [n_pseq * 256, D])
    xs_d = din("xs", [3072, D])
    c2_d = din("c2", [2, D])
    ckT_d = din("ckT", [DEPTH, 512, 256])
    cv_d = din("cv", [DEPTH, 256, 512])
    wada_d = din("w_ada", [DEPTH, D, 3 * D])
    bada_d = din("b_ada", [DEPTH, 3 * D])
    win_d = din("w_in", [DEPTH, D, 3840])
    wout_d = din("w_out", [DEPTH, D, D])
    convw_d = din("conv_w", [DEPTH, 3, 256])
    glg_d = din("gmlp_ln_g", [DEPTH, 256])
    glb_d = din("gmlp_ln_b", [DEPTH, 256])
    wsT_d = din("wsT", [DEPTH, 128, 4, 128])
    bsT_d = din("bsT", [DEPTH, 128, 4])
    ebt_d = din("ebt", [DEPTH, 7, 128, 1024])
    lng_d = din("ln_g", [DEPTH, D])
    lnb_d = din("ln_b", [DEPTH, D])
    id_d = din("ident", [128, 128])
    yp_d = dout("yp", [n_pseq * 256, D])
    ys_d = dout("ys", [2048, D])
    nk_d = dout("nk", [n_pseq, DEPTH, 256, 512])
    nv_d = dout("nv", [n_pseq, DEPTH, 256, 512])
    sxp_d = nc.dram_tensor("sxp", [n_pseq * 256, D], F32).ap()
    sxs_d = nc.dram_tensor("sxs", [3072, D], F32).ap()

    S = Sched(nc)
    st = contextlib.ExitStack()
    with st:
        def sb(name, shape, dt=F32):
            return st.enter_context(nc.sbuf_tensor(name, shape, dt))

        def ps(name, shape, dt=F32):
            return st.enter_context(nc.psum_tensor(name, shape, dt))

        wF = sb("wF", [128, 8, 1536], BF16); b_wF = Buf("wF"); b_wFq = Buf("wFq")
        wT = sb("wT", [128, 8, 2304], BF16); b_wTk = Buf("wTk"); b_wTv = Buf("wTv"); b_wTvu = Buf("wTvu"); b_wTgb = Buf("wTgb"); b_wTgc = Buf("wTgc")
        wo = sb("wo", [128, 8, 1024], BF16); b_wo = Buf("wo")
        NH = 4
        hT = [sb("hT%d" % i, [128, 8, 256], BF16) for i in range(NH)]; b_hT = [Buf("hT%d" % i) for i in range(NH)]
        xA = [sb("xA%d" % i, [128, D]) for i in range(2)]; b_xA = [Buf("xA%d" % i) for i in range(2)]
        xn = [sb("xn%d" % i, [128, D], BF16) for i in range(2)]; b_xn = [Buf("xn%d" % i) for i in range(2)]
        QT = [sb("QT%d" % i, [128, 4, 256], BF16) for i in range(2)]; b_QT = [Buf("QT%d" % i) for i in range(2)]
        NK = 3
        KT = [sb("KT%d" % i, [128, 4, 256], BF16) for i in range(NK)]; b_KT = [Buf("KT%d" % i) for i in range(NK)]
        Vr = sb("Vr", [128, 2 * NK, 8, 65], BF16); b_V = [Buf("V%d" % i) for i in range(2 * NK)]
        yT = [sb("yT%d" % i, [128, 8, 256], BF16) for i in range(2)]; b_yT = [Buf("yT%d" % i) for i in range(2)]
        sgc = [sb("sgc%d" % i, [128, 2, 512], BF16) for i in range(2)]; b_sgc = [Buf("sgc%d" % i) for i in range(2)]
        ktok = sb("ktok", [128, 512], BF16); b_ktok = Buf("ktok")
        ckT = sb("ckT_sb", [128, 4, 256], BF16); b_ckT = Buf("ckT")
        cV = sb("cV", [128, 2, 8, 65], BF16); b_cV = Buf("cV")
        EB = sb("EB", [128, 7, 1024], BF16); b_EB = Buf("EB")
        tU = [sb("tU%d" % i, [128, D]) for i in range(2)]; b_tU = [Buf("tU%d" % i) for i in range(2)]
        Eb = sb("Eb", [128, 14, 512], BF16); b_E = [Buf("E%d" % i) for i in range(14)]
        c2f = sb("c2f", [128, 2, 8]); b_c2f = Buf("c2f")
        c2t = sb("c2t", [128, 2, 8]); b_c2t = Buf("c2t")
        scT = sb("scT", [128, 8, 2], BF16); b_scT = Buf("scT")
        sc1p = sb("sc1p", [128, 2, 8]); b_sc1p = Buf("sc1p")
        shp = sb("shp", [128, 2, 8]); b_shp = Buf("shp")
        badc = sb("badc", [128, 16]); b_badc = Buf("badc")
        gate = sb("gate", [128, 2, D]); b_gate = Buf("gate")
        lng = sb("lng", [128, D]); b_lng = Buf("lng")
        lnb = sb("lnb", [128, D]); b_lnb = Buf("lnb")
        glg = sb("glg", [128, 256]); b_glg = Buf("glg")
        glb = sb("glb", [128, 256]); b_glb = Buf("glb")
        bsT = sb("bsT_sb", [128, 4]); b_bsT = Buf("bsT")
        bsb = sb("bsb", [128, 256]); b_bsb = Buf("bsb")
        rsw = sb("rsw", [128, 4]); b_rsw = Buf("rsw")
        ones1 = sb("ones1", [128, 2], BF16); b_ones1 = Buf("ones1")
        wsT = sb("wsT_sb", [128, 4, 128], BF16); b_wsT = Buf("wsT")
        cw = sb("cw", [128, 2, 3]); b_cw = Buf("cw")
        ident = sb("ident_sb", [128, 128], BF16); b_id = Buf("ident")
        nh = sb("nh", [128, 2]); b_nh = Buf("nh")
        stt = [sb("stt%d" % i, [128, 2, 6]) for i in range(4)]
        mv = [sb("mv%d" % i, [128, 2]) for i in range(4)]
        ve = [sb("ve%d" % i, [128, 1]) for i in range(4)]
        rs = [sb("rs%d" % i, [128, 1]) for i in range(4)]
        nm = [sb("nm%d" % i, [128, 1]) for i in range(4)]
        b_stt = [Buf("stt%d" % i) for i in range(4)]
        b_mv = [Buf("mv%d" % i) for i in range(4)]
        b_ve = [Buf("ve%d" % i) for i in range(4)]
        b_rs = [Buf("rs%d" % i) for i in range(4)]
        b_nm = [Buf("nm%d" % i) for i in range(4)]
        wkall = sb("wkall", [128, 4, 256])
        wk = {}
        for i_, nme in enumerate(("xa", "acc", "tg", "s2")):
            wk[nme] = (wkall[:, i_, :], Buf("wk_" + nme))
        wk["m"] = wk["xa"]
        tgc = wkall[:, 2:4, :].rearrange("p a t -> p (a t)")
        b_tgc_l = [wk["tg"][1], wk["s2"][1]]
        zb = sb("zb", [128, 258]); b_zb = Buf("zb")
        xh = sb("xh", [128, 4]); b_xh = Buf("xh")
        zh = sb("zh", [128, 4]); b_zh = Buf("zh")
        vn3 = sb("vn3", [128, 256], BF16); b_vn3 = Buf("vn3")
        ybt = sb("ybt", [128, 256], BF16); b_ybt = Buf("ybt")
        rinv = sb("rinv", [128, 8]); b_rinv = Buf("rinv")
        ob = sb("ob", [128, 512]); b_ob = Buf("ob")
        ob2 = sb("ob2", [128, 512]); b_ob2 = Buf("ob2")
        kst = ob2; b_kst = b_ob2
        vst = ob; b_vst = b_ob
        yct = sb("yct", [128, 512], BF16); b_yct = Buf("yct")

        NMM = 6
        mmb = [ps("mm%d" % i, [128, 512]) for i in range(NMM)]; b_mm = [Buf("mm%d" % i, True) for i in range(NMM)]
        pv = [ps("pv%d" % i, [128, 512]) for i in range(2)]; b_pv = [Buf("pv0", True), Buf("pv1", True)]
        mm_i = [0]

        pool_banks = {None: [0, 1, 2, 3, 4, 5], "S": [2, 3, 4, 5], "G": [0, 1]}
        pool_i = {None: 0, "S": 0, "G": 0}

        def mm(pool=None):
            if UNZIP and pool != "Gm":
                pool = None
            bl = pool_banks[pool]
            i = bl[pool_i[pool] % len(bl)]
            pool_i[pool] += 1
            return mmb[i], b_mm[i]

        def mmt(pool=None):
            t, b = mm(pool)
            return t[:].bitcast(BF16), b

        def ln_small(site, eng_after="dve"):
            S.op("dve", lambda e: e.tensor_scalar(out=ve[site][:], in0=mv[site][:, 1:2], scalar1=1e-5, scalar2=None, op0=ALU.add),
                 reads=[b_mv[site]], writes=[b_ve[site]])
            S.op("pool", lambda e: e.tensor_tensor(out=rs[site][:], in0=ve[site][:], in1=nh[:, 0:1], op=ALU.pow),
                 reads=[b_ve[site], b_nh], writes=[b_rs[site]], excl_dve=True)
            S.op("dve", lambda e: e.scalar_tensor_tensor(out=nm[site][:], in0=mv[site][:, 0:1], scalar=-1.0, in1=rs[site][:], op0=ALU.mult, op1=ALU.mult),
                 reads=[b_mv[site], b_rs[site]], writes=[b_nm[site]])

        S.dma("pool", lambda e: e.dma_start(out=ident[:], in_=id_d), writes=[b_id], sem_key="c_id")
        S.op("pool", lambda e: e.memset(nh[:], -0.5), writes=[b_nh])
        S.op("pool", lambda e: e.memset(ones1[:], 1.0), writes=[b_ones1])
        S.op("pool", lambda e: e.memset(Vr[:].rearrange("p a h d -> p (a h d)"), 1.0), writes=b_V)
        S.op("pool", lambda e: e.memset(cV[:].rearrange("p a h d -> p (a h d)"), 1.0), writes=[b_cV])
        for v in range(2):
            S.dma("sp", lambda e, v=v: e.dma_start(out=c2f[:, v, :], in_=c2_d[v, :].rearrange("(k p) -> p k", p=128), allow_slow_non_contiguous=True), writes=[b_c2f], sem_key="c_c2")
        S.op("act", lambda e: e.activation(out=c2t[:], in_=c2f[:], func=AF.Tanh, scale=0.5), reads=[b_c2f], writes=[b_c2t])
        S.op("dve", lambda e: e.scalar_tensor_tensor(out=c2t[:], in0=c2t[:], scalar=1.0, in1=c2f[:], op0=ALU.add, op1=ALU.mult), reads=[b_c2f, b_c2t], writes=[b_c2t])
        S.op("dve", lambda e: e.tensor_scalar(out=scT[:].rearrange("p k v -> p v k"), in0=c2t[:], scalar1=0.5, scalar2=None, op0=ALU.mult), reads=[b_c2t], writes=[b_scT])
        b_sxs = [Buf("sxs%d" % u) for u in range(12)]
        b_sxp = [Buf("sxp%d" % u) for u in range(n_pseq)]

        def layer(l):
            last = (l == depth - 1)
            def wdma(dst, src, bufd, key):
                S.dma("pool", lambda e: e.dma_start(out=dst, in_=src.rearrange("(k p) n -> p k n", p=128)), writes=[bufd], sem_key=key)
            wdma(wT[:, :, 1792:2304], win_d[l][:, 2304:2816], b_wTk, "w_Tk")
            wdma(wT[:, :, 768:1280], win_d[l][:, 2816:3328], b_wTv, "w_Tv")
            for c_ in range(2):
                S.dma("sp", lambda e, c_=c_: e.dma_start(out=cw[:, c_, :], in_=convw_d[l][:, c_ * 128:(c_ + 1) * 128].rearrange("j p -> p j"), allow_slow_non_contiguous=True), writes=[b_cw], sem_key="p_cw")
            S.dma("sp", lambda e: e.dma_start(out=glg[:], in_=glg_d[l, :].partition_broadcast(128)), writes=[b_glg], sem_key="p_glg")
            S.dma("sp", lambda e: e.dma_start(out=glb[:], in_=glb_d[l, :].partition_broadcast(128)), writes=[b_glb], sem_key="p_glb")
            S.dma("sp", lambda e: e.dma_start(out=lng[:], in_=lng_d[l, :].partition_broadcast(128)), writes=[b_lng], sem_key="p_lng")
            S.dma("sp", lambda e: e.dma_start(out=lnb[:], in_=lnb_d[l, :].partition_broadcast(128)), writes=[b_lnb], sem_key="p_lnb")
            S.dma("sp", lambda e: e.dma_start(out=bsT[:], in_=bsT_d[l]), writes=[b_bsT], sem_key="p_bsT")
            S.op("dve", lambda e: e.tensor_copy(out=bsb[:].rearrange("p (g c) -> p g c", g=4), in_=bsT[:, :].unsqueeze(2).to_broadcast([128, 4, 64])), reads=[b_bsT], writes=[b_bsb])
            S.dma("pool", lambda e: e.dma_start(out=wsT[:], in_=wsT_d[l]), writes=[b_wsT], sem_key="p_wsT")
            prs, bprs = mm()
            for g in range(4):
                S.op("pe", lambda e, g=g, prs=prs: e.matmul(prs[:, g:g + 1], lhsT=wsT[:, g, :], rhs=ones1[:, 0:1], start=True, stop=True), reads=[b_wsT, b_ones1], writes=[bprs])
            S.op("dve", lambda e, prs=prs: e.tensor_copy(out=rsw[:], in_=prs[:, 0:4]), reads=[bprs], writes=[b_rsw])
            S.op("dve", lambda e: e.tensor_tensor(out=glb[:].rearrange("p (g c) -> p g c", g=4), in0=glb[:].rearrange("p (g c) -> p g c", g=4), in1=rsw[:, :].unsqueeze(2).to_broadcast([128, 4, 64]), op=ALU.mult),
                 reads=[b_glb, b_rsw], writes=[b_glb])
            S.op("dve", lambda e: e.tensor_tensor(out=bsb[:], in0=bsb[:], in1=glb[:], op=ALU.add), reads=[b_bsb, b_glb], writes=[b_bsb])
            S.dma("sp", lambda e: e.dma_start(out=badc[:], in_=bada_d[l, 0:2048].rearrange("(j p) -> p j", p=128), allow_slow_non_contiguous=True), writes=[b_badc], sem_key="p_badc")
            wad = Eb[:, 0:8, :]
            Ssil = Eb[:, 8:12, :].rearrange("p (v a) (b t) -> p v (a b) t", v=2, b=4)
            b_Ssil_l = b_E[8:12]
            for v in range(2):
                S.op("dve", lambda e, v=v: e.tensor_copy(out=Ssil[:, v], in_=scT[:, :, v:v + 1].to_broadcast([128, 8, 128])), reads=[b_scT], writes=b_Ssil_l)
            for ch in range(6):
                S.dma("pool", lambda e, ch=ch: e.dma_start(out=wad, in_=wada_d[l][:, ch * 512:(ch + 1) * 512].rearrange("(k p) n -> p k n", p=128)),
                      writes=b_E[0:8], sem_key="w_ada")
                if ch < 4:
                    pt, bpt = mm()
                    for jb in range(4):
                        for kc in range(8):
                            S.op("pe", lambda e, jb=jb, kc=kc, pt=pt: e.matmul(pt[:, jb * 2:jb * 2 + 2], lhsT=wad[:, kc, jb * 128:(jb + 1) * 128], rhs=scT[:, kc, :], start=(kc == 0), stop=(kc == 7)),
                                 reads=b_E[0:8] + [b_scT], writes=[bpt])
                    for v in range(2):
                        dst = (shp if ch < 2 else sc1p)
                        bd = (b_shp if ch < 2 else b_sc1p)
                        j0 = (ch % 2) * 4
                        S.op("dve", lambda e, v=v, dst=dst, j0=j0, pt=pt, ch=ch: e.scalar_tensor_tensor(
                            out=dst[:, v, j0:j0 + 4], in0=pt[:, v:8:2], scalar=(0.0 if ch < 2 else 1.0), in1=badc[:, ch * 4:ch * 4 + 4], op0=ALU.add, op1=ALU.add),
                            reads=[bpt, b_badc], writes=[bd])
                else:
                    half = ch - 4
                    S.dma("sp", lambda e, half=half: e.dma_start(out=tU[0][:, 0:512], in_=bada_d[l, 2048 + half * 512:2048 + (half + 1) * 512].partition_broadcast(128)), writes=[b_tU[0]], sem_key="p_bg")
                    for v in range(2):
                        pt, bpt = mm()
                        for kc in range(8):
                            S.op("pe", lambda e, v=v, kc=kc, pt=pt: e.matmul(pt[:, :], lhsT=Ssil[:, v, kc, :], rhs=wad[:, kc, :], start=(kc == 0), stop=(kc == 7)),
                                 reads=b_E[0:8] + b_Ssil_l, writes=[bpt])
                        S.op("dve", lambda e, v=v, pt=pt, half=half: e.tensor_tensor(out=gate[:, v, half * 512:(half + 1) * 512], in0=pt[:, :], in1=tU[0][:, 0:512], op=ALU.add),
                             reads=[bpt, b_tU[0]], writes=[b_gate])
            wdma(wF[:, :, 1024:1536], win_d[l][:, 1792:2304], b_wFq, "w_Fq")
            wdma(wF[:, :, 0:1024], win_d[l][:, 0:1024], b_wF, "w_F")
            wdma(wT[:, :, 0:256], win_d[l][:, 1280:1536], b_wTvu, "w_Tvu")
            wdma(wT[:, :, 256:512], win_d[l][:, 1024:1280], b_wTvu, "w_Tvu2")
            wdma(wT[:, :, 1280:1792], win_d[l][:, 3328:3840], b_wTgc, "w_Tgc")
            wdma(wT[:, :, 512:768], win_d[l][:, 1536:1792], b_wTgb, "w_Tgb")
            wdma(wo[:], wout_d[l], b_wo, "w_o")
            S.dma("pool", lambda e: e.dma_start(out=ckT[:], in_=ckT_d[l].rearrange("(a p) k -> p a k", p=128)), writes=[b_ckT], sem_key="c_k")
            for a in range(2):
                S.dma("pool", lambda e, a=a: e.dma_start(out=cV[:, a, :, 0:64], in_=cv_d[l][a * 128:(a + 1) * 128, :].rearrange("p (h d) -> p h d", d=64)), writes=[b_cV], sem_key="c_v")
            for tid in range(7):
                S.dma("sp", lambda e, tid=tid: e.dma_start(out=tU[tid % 2][:], in_=ebt_d[l, tid]), writes=[b_tU[tid % 2]], sem_key="p_eb%d" % (tid % 2))
                S.op("act", lambda e, tid=tid: e.activation(out=EB[:, tid, :], in_=tU[tid % 2][:], func=AF.Exp), reads=[b_tU[tid % 2]], writes=[b_EB])

            def src_x(grp, tile):
                if l == 0:
                    return (xs_d if grp == "S" else xp_d)[tile * 128:(tile + 1) * 128, :]
                return (sxs_d if grp == "S" else sxp_d)[tile * 128:(tile + 1) * 128, :]

            def dst_x(grp, tile):
                if last:
                    return (ys_d if grp == "S" else yp_d)[tile * 128:(tile + 1) * 128, :]
                return (sxs_d if grp == "S" else sxp_d)[tile * 128:(tile + 1) * 128, :]

            def xbuf(grp, u):
                return (b_sxs if grp == "S" else b_sxp)[u]

            cnt = {"xA": 0, "tU": 0}

            def A_dma(grp, u):
                xis = []
                for j in range(2):
                    tile = 2 * u + j
                    xi = cnt["xA"] % 2
                    cnt["xA"] += 1
                    rd = [xbuf(grp, u)] if l > 0 else []
                    S.dma("sp", lambda e, xi=xi, tile=tile: e.dma_start(out=xA[xi][:], in_=src_x(grp, tile)), reads=rd, writes=[b_xA[xi]], sem_key="xA%d" % xi)
                    xis.append(xi)
                return xis

            def A_ln(grp, u, xis):
                for j in range(2):
                    xi = xis[j]
                    for i in range(2):
                        S.op("dve", lambda e, i=i, xi=xi, j=j: e.bn_stats(out=stt[j][:, i, :], in_=xA[xi][:, i * 512:(i + 1) * 512]), reads=[b_xA[xi]], writes=[b_stt[j]])
                    S.op("dve", lambda e, j=j: e.bn_aggr(out=mv[j][:], in_=stt[j][:].rearrange("p a b -> p (a b)")), reads=[b_stt[j]], writes=[b_mv[j]])
                    ln_small(j)
                    S.op("act", lambda e, xi=xi, j=j: e.activation(out=xn[j][:], in_=xA[xi][:], func=AF.Identity, bias=nm[j][:], scale=rs[j][:]),
                         reads=[b_xA[xi], b_nm[j], b_rs[j]], writes=[b_xn[j]])

            def A_pre(grp, u):
                A_ln(grp, u, A_dma(grp, u))

            def A_pe(grp, u, hs):
                v = 0 if grp == "S" else 1
                for j in range(2):
                    tr, b_tr = mmt()
                    for kc in range(8):
                        S.op("pe", lambda e, kc=kc, j=j, tr=tr: e.transpose(out=tr[:, kc * 128:(kc + 1) * 128], in_=xn[j][:, kc * 128:(kc + 1) * 128], identity=ident[:]),
                             reads=[b_xn[j], b_id], writes=[b_tr])
                    for kc in range(8):
                        if kc % 2 == 0:
                            S.op("dve", lambda e, kc=kc, j=j, tr=tr: e.tensor_scalar(out=hT[hs][:, kc, j * 128:(j + 1) * 128], in0=tr[:, kc * 128:(kc + 1) * 128],
                                                                                      scalar1=sc1p[:, v, kc:kc + 1], scalar2=shp[:, v, kc:kc + 1], op0=ALU.mult, op1=ALU.add),
                                 reads=[b_tr, b_sc1p, b_shp], writes=[b_hT[hs]])
                    for kc in range(8):
                        if kc % 2 == 1:
                            S.op("act", lambda e, kc=kc, j=j, tr=tr: e.activation(out=hT[hs][:, kc, j * 128:(j + 1) * 128], in_=tr[:, kc * 128:(kc + 1) * 128], func=AF.Identity,
                                                                                    scale=sc1p[:, v, kc:kc + 1], bias=shp[:, v, kc:kc + 1]),
                                 reads=[b_tr, b_sc1p, b_shp], writes=[b_hT[hs]])

            def proj_T(hs, j, c0, c1):
                pt, bpt = mm()
                n = c1 - c0
                for kc in range(8):
                    S.op("pe", lambda e, kc=kc, pt=pt: e.matmul(pt[:, 0:n], lhsT=hT[hs][:, kc, j * 128:(j + 1) * 128], rhs=wT[:, kc, c0:c1], start=(kc == 0), stop=(kc == 7)),
                         reads=[b_hT[hs], {1792: b_wTk, 768: b_wTv, 0: b_wTvu, 512: b_wTgb, 1280: b_wTgc}[c0]], writes=[bpt])
                return pt, bpt

            def proj_F(hs, pt, bpt, off, cb):
                for kc in range(8):
                    S.op("pe", lambda e, kc=kc: e.matmul(pt[:, off:off + 256], lhsT=wF[:, kc, cb * 128:(cb + 1) * 128], rhs=hT[hs][:, kc, :], start=(kc == 0), stop=(kc == 7)),
                         reads=[b_hT[hs], (b_wFq if cb >= 8 else b_wF)], writes=[bpt])

            def B_kv(grp, u, hs, ks):
                with_out = (grp == "P")
                for j in range(2):
                    vslot = 2 * ks + j
                    pt, bpt = proj_T(hs, j, 1792, 2304)
                    S.op("act", lambda e, pt=pt: e.activation(out=ktok[:], in_=pt[:, :], func=AF.Copy), reads=[bpt], writes=[b_ktok])
                    if with_out:
                        S.op("dve", lambda e, pt=pt: e.tensor_copy(out=kst[:], in_=pt[:, :]), reads=[bpt], writes=[b_kst])
                        S.dma("sp", lambda e, j=j: e.dma_start(out=nk_d[u, l, j * 128:(j + 1) * 128, :], in_=kst[:]), reads=[b_kst], sem_key="o_k", final=True)
                    tr, b_tr = mmt()
                    for a in range(4):
                        S.op("pe", lambda e, a=a, tr=tr: e.transpose(out=tr[:, a * 128:(a + 1) * 128], in_=ktok[:, a * 128:(a + 1) * 128], identity=ident[:]),
                             reads=[b_ktok, b_id], writes=[b_tr])
                    S.op("dve", lambda e, j=j, tr=tr: e.tensor_copy(out=KT[ks][:, :, j * 128:(j + 1) * 128], in_=tr[:, 0:512].rearrange("p (a t) -> p a t", a=4)),
                         reads=[b_tr], writes=[b_KT[ks]])
                    pt, bpt = proj_T(hs, j, 768, 1280)
                    S.op("act", lambda e, pt=pt, vslot=vslot: e.activation(out=Vr[:, vslot, :, 0:64], in_=pt[:, :].rearrange("p (h d) -> p h d", d=64), func=AF.Copy),
                         reads=[bpt], writes=[b_V[vslot]])
                    if with_out:
                        S.op("dve", lambda e, pt=pt: e.tensor_copy(out=vst[:], in_=pt[:, :]), reads=[bpt], writes=[b_vst])
                        S.dma("sp", lambda e, j=j: e.dma_start(out=nv_d[u, l, j * 128:(j + 1) * 128, :], in_=vst[:]), reads=[b_vst], sem_key="o_v", final=True)

            def proj_F_g(hs, pt, bpt, off, cb):
                proj_F(hs, pt, bpt, off, cb)
                yield

            def B_q_gen(grp, u, hs, qs, pool=None):
                for half in range(2):
                    pq, bpq = mm(pool)
                    yield from proj_F_g(hs, pq, bpq, 0, 8 + 2 * half)
                    yield from proj_F_g(hs, pq, bpq, 256, 9 + 2 * half)
                    S.op("act", lambda e, pq=pq, half=half: e.activation(out=QT[qs][:, 2 * half:2 * half + 2, :], in_=pq[:, :].rearrange("p (a t) -> p a t", a=2), func=AF.Copy),
                         reads=[bpq], writes=[b_QT[qs]])

            def run(gen):
                for _ in gen:
                    pass

            def zipper(ga, gb, na=1, nb=2):
                if UNZIP:
                    run(ga)
                    run(gb)
                    return
                da = db = False
                while not (da and db):
                    for _ in range(na):
                        if not da:
                            try:
                                next(ga)
                            except StopIteration:
                                da = True
                    for _ in range(nb):
                        if not db:
                            try:
                                next(gb)
                            except StopIteration:
                                db = True

            def B_q(grp, u, hs, qs):
                run(B_q_gen(grp, u, hs, qs))

            def B_conv_gen(grp, u, hs, qs, hs_prev, hs_next, pool=None):
                yt, byt = yT[qs], b_yT[qs]
                have_h = [hs_prev is not None, hs_next is not None]
                hal = pv[1][:, 384:392]
                b_halo = b_pv[1]
                for cbi, cb in enumerate((0, 1, 4, 5)):
                    for side in range(2):
                        if not have_h[side]:
                            continue
                        hsrc = hT[hs_prev][:, :, 255:256] if side == 0 else hT[hs_next][:, :, 0:1]
                        bsrc = b_hT[hs_prev] if side == 0 else b_hT[hs_next]
                        for kc in range(8):
                            S.op("pe", lambda e, kc=kc, cb=cb, hsrc=hsrc, col=cbi * 2 + side: e.matmul(hal[:, col:col + 1], lhsT=wF[:, kc, cb * 128:(cb + 1) * 128], rhs=hsrc[:, kc, :], start=(kc == 0), stop=(kc == 7)),
                                 reads=[bsrc, b_wF], writes=[b_halo])
                    yield
                S.op("pool", lambda e: e.memset(zh[:], 0.0), writes=[b_zh])
                for side in range(2):
                    if have_h[side]:
                        S.op("dve", lambda e, side=side: e.tensor_copy(out=xh[:, side:4:2], in_=hal[:, side:4:2]), reads=[b_halo], writes=[b_xh])
                        S.op("dve", lambda e, side=side: e.tensor_tensor(out=zh[:, side:4:2], in0=hal[:, 4 + side:8:2], in1=xh[:, side:4:2], op=ALU.mult), reads=[b_halo, b_xh], writes=[b_zh])
                for c in range(2):
                    p1, bp1 = mm(pool)
                    yield from proj_F_g(hs, p1, bp1, 0, 0 + c)
                    yield from proj_F_g(hs, p1, bp1, 256, 4 + c)
                    p2, bp2 = mm(pool)
                    yield from proj_F_g(hs, p2, bp2, 0, 2 + c)
                    yield from proj_F_g(hs, p2, bp2, 256, 6 + c)
                    xa_t, bxa = wk["xa"]; acc, bacc = wk["acc"]; tg, btg = wk["tg"]; s2, bs2 = wk["s2"]; m_, bm = wk["m"]
                    S.op("act", lambda e, p1=p1: e.activation(out=xa_t[:], in_=p1[:, 0:256], func=AF.Copy), reads=[bp1], writes=[bxa])
                    S.op("dve", lambda e, p1=p1: e.tensor_tensor(out=zb[:, 1:257], in0=p1[:, 256:512], in1=xa_t[:], op=ALU.mult), reads=[bp1, bxa], writes=[b_zb])
                    S.op("dve", lambda e, c=c: e.tensor_copy(out=zb[:, 0:258:257], in_=zh[:, 2 * c:2 * c + 2]), reads=[b_zh], writes=[b_zb])
                    S.op("act", lambda e, c=c: e.activation(out=acc[:], in_=zb[:, 1:257], func=AF.Identity, scale=cw[:, c, 1:2]), reads=[b_zb, b_cw], writes=[bacc])
                    S.op("dve", lambda e, c=c: e.scalar_tensor_tensor(out=acc[:], in0=zb[:, 0:256], scalar=cw[:, c, 0:1], in1=acc[:], op0=ALU.mult, op1=ALU.add), reads=[b_zb, b_cw, bacc], writes=[bacc])
                    S.op("dve", lambda e, c=c: e.scalar_tensor_tensor(out=acc[:], in0=zb[:, 2:258], scalar=cw[:, c, 2:3], in1=acc[:], op0=ALU.mult, op1=ALU.add), reads=[b_zb, b_cw, bacc], writes=[bacc])
                    S.op("act", lambda e, p2=p2: e.activation(out=tg[:], in_=p2[:, 256:512], func=AF.Tanh, scale=0.5), reads=[bp2], writes=[btg])
                    S.op("dve", lambda e, p2=p2: e.scalar_tensor_tensor(out=s2[:], in0=tg[:], scalar=1.0, in1=p2[:, 256:512], op0=ALU.add, op1=ALU.mult), reads=[btg, bp2], writes=[bs2])
                    S.op("dve", lambda e, p2=p2: e.tensor_tensor(out=m_[:], in0=p2[:, 0:256], in1=acc[:], op=ALU.mult), reads=[bp2, bacc], writes=[bm])
                    S.op("dve", lambda e, c=c: e.scalar_tensor_tensor(out=yt[:, c, :], in0=m_[:], scalar=0.5, in1=s2[:], op0=ALU.mult, op1=ALU.mult), reads=[bm, bs2], writes=[byt])
                    yield

            def B_conv(grp, u, hs, qs, hs_prev, hs_next):
                run(B_conv_gen(grp, u, hs, qs, hs_prev, hs_next))

            def B_gm(grp, u, hs, qs):
                yt, byt = yT[qs], b_yT[qs]
                for j in range(2):
                    xa_t, bxa = wk["xa"]; acc, bacc = wk["acc"]; tg, btg = wk["tg"]; s2, bs2 = wk["s2"]; m_, bm = wk["m"]
                    pvu, bpvu = proj_T(hs, j, 0, 512)
                    S.op("dve", lambda e, pvu=pvu: e.bn_stats(out=stt[2][:, 0, :], in_=pvu[:, 0:256]), reads=[bpvu], writes=[b_stt[2]])
                    S.op("dve", lambda e: e.bn_aggr(out=mv[2][:], in_=stt[2][:, 0, :]), reads=[b_stt[2]], writes=[b_mv[2]])
                    ln_small(2)
                    pgc, bpgc = proj_T(hs, j, 1280, 1792)
                    S.op("act", lambda e, pgc=pgc: e.activation(out=tgc[:], in_=pgc[:, :], func=AF.Tanh, scale=0.5), reads=[bpgc], writes=b_tgc_l)
                    S.op("dve", lambda e, pgc=pgc, j=j: e.scalar_tensor_tensor(out=sgc[qs][:, j, :], in0=tgc[:], scalar=1.0, in1=pgc[:, :], op0=ALU.add, op1=ALU.mult), reads=b_tgc_l + [bpgc], writes=[b_sgc[qs]])
                    pgb, bpgb = proj_T(hs, j, 512, 768)
                    S.op("act", lambda e, pvu=pvu: e.activation(out=vn3[:], in_=pvu[:, 0:256], func=AF.Identity, bias=nm[2][:], scale=rs[2][:]), reads=[bpvu, b_nm[2], b_rs[2]], writes=[b_vn3])
                    psv, bpsv = mm()
                    for g in range(4):
                        S.op("pe", lambda e, g=g, psv=psv: e.matmul(psv[:, g * 64:(g + 1) * 64], lhsT=wsT[:, g, :], rhs=vn3[:, g * 64:(g + 1) * 64], start=True, stop=True),
                             reads=[b_wsT, b_vn3], writes=[bpsv])
                    S.op("dve", lambda e, psv=psv: e.tensor_tensor(out=acc[:], in0=psv[:, 0:256], in1=glg[:], op=ALU.mult), reads=[bpsv, b_glg], writes=[bacc])
                    S.op("dve", lambda e: e.tensor_tensor(out=acc[:], in0=acc[:], in1=bsb[:], op=ALU.add), reads=[bacc, b_bsb], writes=[bacc])
                    S.op("dve", lambda e, pvu=pvu: e.tensor_tensor(out=m_[:], in0=pvu[:, 256:512], in1=acc[:], op=ALU.mult), reads=[bpvu, bacc], writes=[bm])
                    S.op("act", lambda e, pgb=pgb: e.activation(out=tg[:], in_=pgb[:, 0:256], func=AF.Tanh, scale=0.5), reads=[bpgb], writes=[btg])
                    S.op("dve", lambda e, pgb=pgb: e.scalar_tensor_tensor(out=s2[:], in0=tg[:], scalar=1.0, in1=pgb[:, 0:256], op0=ALU.add, op1=ALU.mult), reads=[btg, bpgb], writes=[bs2])
                    S.op("dve", lambda e: e.scalar_tensor_tensor(out=ybt[:], in0=m_[:], scalar=0.5, in1=s2[:], op0=ALU.mult, op1=ALU.mult), reads=[bm, bs2], writes=[b_ybt])
                    tr, b_tr = mmt()
                    for a in range(2):
                        S.op("pe", lambda e, a=a, tr=tr: e.transpose(out=tr[:, a * 128:(a + 1) * 128], in_=ybt[:, a * 128:(a + 1) * 128], identity=ident[:]), reads=[b_ybt, b_id], writes=[b_tr])
                    S.op("act", lambda e, j=j, tr=tr: e.activation(out=yt[:, 2:4, j * 128:(j + 1) * 128], in_=tr[:, 0:256].rearrange("p (a t) -> p a t", a=2), func=AF.Copy), reads=[b_tr], writes=[byt])

            def C_s_gen(grp, u, qs, chunks, j, pool=None):
                for ci, (kf, kb, vf, vb, tid) in enumerate(chunks):
                    SE, b_SE = mm(pool)
                    SO, b_SO = mm(pool)
                    for h in range(8):
                        pair, half = h // 2, h % 2
                        bank, bbank = (SE, b_SE) if half == 0 else (SO, b_SO)
                        S.op("pe", lambda e, kf=kf, pair=pair, half=half, bank=bank: e.matmul(
                            bank[:, pair * 128:(pair + 1) * 128], lhsT=kf(pair, half), rhs=QT[qs][64 * half:64 * half + 64, pair, j * 128:(j + 1) * 128], start=True, stop=True),
                            reads=[kb, b_QT[qs]], writes=[bbank])
                    for half in range(2):
                        bank, bbank = (SE, b_SE) if half == 0 else (SO, b_SO)
                        ei = 2 * ci + half
                        S.op("act", lambda e, bank=bank, ei=ei: e.activation(out=Eb[:, ei, :], in_=bank[:, :], func=AF.Exp, scale=SCALE), reads=[bbank], writes=[b_E[ei]])
                        if tid is not None:
                            S.op(_eb_eng(ci, half), lambda e, ei=ei, tid=tid, half=half: e.tensor_tensor(out=Eb[:, ei, :], in0=Eb[:, ei, :], in1=EB[:, tid, half * 512:(half + 1) * 512], op=ALU.mult),
                                 reads=[b_E[ei], b_EB], writes=[b_E[ei]])
                    yield

            def C_s(grp, u, qs, keyf, j):
                chunks = keyf(2 * u + j)
                run(C_s_gen(grp, u, qs, chunks, j))
                return chunks

            def C_pv(grp, u, qs, chunks, j):
                nck = len(chunks)
                for h in range(8):
                    pair, half = h // 2, h % 2
                    pb, bpb = pv[h // 4], b_pv[h // 4]
                    for ci, (kf, kb, vf, vb, tid) in enumerate(chunks):
                        ei = 2 * ci + half
                        S.op("pe", lambda e, ei=ei, pair=pair, vf=vf, h=h, pb=pb, ci=ci: e.matmul(
                            pb[:, (h % 4) * 65:(h % 4) * 65 + 65], lhsT=Eb[:, ei, pair * 128:(pair + 1) * 128], rhs=vf(h), start=(ci == 0), stop=(ci == nck - 1)),
                            reads=[b_E[ei], vb], writes=[bpb])
                for g2 in range(2):
                    pb, bpb = pv[g2], b_pv[g2]
                    pv3 = pb[:, 0:260].rearrange("p (h d) -> p h d", d=65)
                    S.op("dve", lambda e, pv3=pv3, g2=g2: e.reciprocal(out=rinv[:, 4 * g2:4 * g2 + 4], in_=pv3[:, :, 64]), reads=[bpb], writes=[b_rinv])
                    S.op("dve", lambda e, pv3=pv3, g2=g2: e.tensor_tensor(out=ob[:, 256 * g2:256 * g2 + 256].rearrange("p (h d) -> p h d", d=64), in0=pv3[:, :, 0:64],
                                                                            in1=rinv[:, 4 * g2:4 * g2 + 4].unsqueeze(2).to_broadcast([128, 4, 64]), op=ALU.mult),
                         reads=[bpb, b_rinv], writes=[b_ob])
                S.op("dve", lambda e: e.scalar_tensor_tensor(out=yct[:], in0=ob[:], scalar=0.5, in1=sgc[qs][:, j, :], op0=ALU.mult, op1=ALU.mult), reads=[b_ob, b_sgc[qs]], writes=[b_yct])

            def C_o(grp, u, qs, j):
                run(C_o_gen(grp, u, qs, j))

            def C_o_gen(grp, u, qs, j, pool=None):
                v = 0 if grp == "S" else 1
                yt, byt = yT[qs], b_yT[qs]
                tile = 2 * u + j
                tr, b_tr = mmt(pool)
                for a in range(4):
                    S.op("pe", lambda e, a=a: e.transpose(out=tr[:, a * 128:(a + 1) * 128], in_=yct[:, a * 128:(a + 1) * 128], identity=ident[:]), reads=[b_yct, b_id], writes=[b_tr])
                S.op("act", lambda e: e.activation(out=yt[:, 4:8, j * 128:(j + 1) * 128], in_=tr[:, 0:512].rearrange("p (a t) -> p a t", a=4), func=AF.Copy), reads=[b_tr], writes=[byt])
                yield
                ti = cnt["tU"] % 2
                cnt["tU"] += 1
                rd = [xbuf(grp, u)] if l > 0 else []
                S.dma("sp", lambda e: e.dma_start(out=tU[ti][:], in_=src_x(grp, tile)), reads=rd, writes=[b_tU[ti]], sem_key="xC%d" % ti)
                for n in range(2):
                    po, bpo = mm(pool)
                    for kc in range(8):
                        S.op("pe", lambda e, kc=kc, n=n, po=po: e.matmul(po[:, :], lhsT=yt[:, kc, j * 128:(j + 1) * 128], rhs=wo[:, kc, n * 512:(n + 1) * 512], start=(kc == 0), stop=(kc == 7)),
                             reads=[byt, b_wo], writes=[bpo])
                        if kc == 3:
                            yield
                    S.op("dve", lambda e, n=n, po=po: e.tensor_tensor(out=ob2[:], in0=po[:, :], in1=gate[:, v, n * 512:(n + 1) * 512], op=ALU.mult),
                         reads=[bpo, b_gate], writes=[b_ob2])
                    S.op("dve", lambda e, n=n: e.scalar_tensor_tensor(out=tU[ti][:, n * 512:(n + 1) * 512], in0=tU[ti][:, n * 512:(n + 1) * 512], scalar=ALPHA, in1=ob2[:], op0=ALU.mult, op1=ALU.add),
                         reads=[b_ob2, b_tU[ti]], writes=[b_tU[ti]])
                    yield
                for i in range(2):
                    S.op("dve", lambda e, i=i: e.bn_stats(out=stt[3][:, i, :], in_=tU[ti][:, i * 512:(i + 1) * 512]), reads=[b_tU[ti]], writes=[b_stt[3]])
                S.op("dve", lambda e: e.bn_aggr(out=mv[3][:], in_=stt[3][:].rearrange("p a b -> p (a b)")), reads=[b_stt[3]], writes=[b_mv[3]])
                ln_small(3)
                S.op("act", lambda e: e.activation(out=tU[ti][:], in_=tU[ti][:], func=AF.Identity, bias=nm[3][:], scale=rs[3][:]), reads=[b_tU[ti], b_nm[3], b_rs[3]], writes=[b_tU[ti]])
                S.op("pool", lambda e: e.tensor_tensor(out=tU[ti][:], in0=tU[ti][:], in1=lng[:], op=ALU.mult), reads=[b_tU[ti], b_lng], writes=[b_tU[ti]])
                S.op("pool", lambda e: e.tensor_tensor(out=tU[ti][:], in0=tU[ti][:], in1=lnb[:], op=ALU.add), reads=[b_tU[ti], b_lnb], writes=[b_tU[ti]])
                S.dma("sp", lambda e: e.dma_start(out=dst_x(grp, tile), in_=tU[ti][:]), reads=[b_tU[ti]], writes=([] if last else [xbuf(grp, u)]),
                      sem_key="xo%d" % ti, final=True)

            def ctx_chunks():
                out = []
                for a in range(2):
                    out.append((lambda pair, half, a=a: ckT[64 * half:64 * half + 64, pair, a * 128:(a + 1) * 128], b_ckT,
                                lambda h, a=a: cV[:, a, h, :], b_cV, None))
                return out

            nP = ns_units - l
            nR = ns_units - 1 - l

            def keyf_S(tile):
                out = []
                for (kt, tid) in attn_keys(tile):
                    ku, kj = kt // 2, kt % 2
                    ksl = ku % NK
                    out.append((lambda pair, half, ksl=ksl, kj=kj: KT[ksl][64 * half:64 * half + 64, pair, kj * 128:(kj + 1) * 128], b_KT[ksl],
                                lambda h, ksl=ksl, kj=kj: Vr[:, 2 * ksl + kj, h, :], b_V[2 * ksl + kj], tid))
                return ctx_chunks() + out

            def pipeline(grp, nP, nR, keyf_of, base, halo):
                def hs_(u): return (base + u) % NH
                def ks_(u): return (base + u) % NK
                def qs_(u): return (base + u) % 2
                def hprev(u): return hs_(u - 1) if (halo and u > 0) else None
                def hnext(u): return hs_(u + 1) if halo else None
                for u0 in range(min(3, nP)):
                    A_pre(grp, u0); A_pe(grp, u0, hs_(u0))
                B_kv(grp, 0, hs_(0), ks_(0)); B_q(grp, 0, hs_(0), qs_(0)); B_conv(grp, 0, hs_(0), qs_(0), None, hnext(0)); B_gm(grp, 0, hs_(0), qs_(0))
                pend = None
                for i in range(nP):
                    u1 = i + 1
                    full1 = u1 < nR
                    if i + 3 < nP:
                        xis_ = A_dma(grp, i + 3)
                    if u1 < nP:
                        B_kv(grp, u1, hs_(u1), ks_(u1))
                    if pend is not None:
                        C_o(*pend)
                        pend = None
                    if i + 3 < nP:
                        A_ln(grp, i + 3, xis_)
                    g_b = iter(())
                    if full1:
                        def g_b_f(u1=u1):
                            yield from B_q_gen(grp, u1, hs_(u1), qs_(u1), "G")
                            yield from B_conv_gen(grp, u1, hs_(u1), qs_(u1), hprev(u1), hnext(u1), "G")
                        g_b = g_b_f()
                    if i < nR:
                        kf = keyf_of(i)
                        ch0 = kf(2 * i)
                        zipper(C_s_gen(grp, i, qs_(i), ch0, 0, "S"), g_b, 1, 2)
                        C_pv(grp, i, qs_(i), ch0, 0)
                        ch1 = kf(2 * i + 1)
                        zipper(C_s_gen(grp, i, qs_(i), ch1, 1, "S"), C_o_gen(grp, i, qs_(i), 0, "G"), 2, 1)
                    else:
                        run(g_b)
                    if i + 3 < nP:
                        A_pe(grp, i + 3, hs_(i + 3))
                    if full1:
                        B_gm(grp, u1, hs_(u1), qs_(u1))
                    if i < nR:
                        C_pv(grp, i, qs_(i), ch1, 1)
                        pend = (grp, i, qs_(i), 1)
                if pend is not None:
                    C_o(*pend)

            def keyf_P_of(u):
                ks = (nP_s + u) % NK
                def keyf(tile):
                    out = []
                    for kj in range(2):
                        out.append((lambda pair, half, kj=kj: KT[ks][64 * half:64 * half + 64, pair, kj * 128:(kj + 1) * 128], b_KT[ks],
                                    lambda h, kj=kj: Vr[:, 2 * ks + kj, h, :], b_V[2 * ks + kj], None))
                    return out
                return keyf

            nP_s = nP
            if not SKIPS:
                pipeline("S", nP, nR, lambda u: keyf_S, 0, True)
            if not SKIPP:
                pipeline("P", n_pseq, n_pseq, keyf_P_of, nP_s, False)

        for l_ in range(depth):
            layer(l_)

        S.emit()
    return nc


def _tables(rpb, flip):
    reps = [(4, 2), (4, 3), (4, 4), (4, 5), (4, 6), (0, 2), (0, 3)]
    out = np.empty((DEPTH, 7, 128, 2, 4, 128), np.float32)
    p = np.arange(128)
    for tid, (t, u) in enumerate(reps):
        ql = t * 128 + p
        kl = u * 128 + p
        qg = 4095 - ql if flip else ql
        kg = 4095 - kl if flip else kl
        qr, qc = qg // 64, qg % 64
        kr, kc = kg // 64, kg % 64
        rs_ = np.clip(qr - 4, 0, 56)
        cs_ = np.clip(qc - 8, 0, 48)
        valid = ((kr[:, None] >= rs_[None, :]) & (kr[:, None] < rs_[None, :] + 8)
                 & (kc[:, None] >= cs_[None, :]) & (kc[:, None] < cs_[None, :] + 16))
        dr = np.clip(kr[:, None] - qr[None, :] + 7, 0, 14)
        dc = np.clip(kc[:, None] - qc[None, :], -15, 15) + 15
        g = rpb[:, :, dr, dc]
        g = np.where(valid[None, None], g, np.float32(NEG))
        g = g.transpose(0, 2, 1, 3).reshape(DEPTH, 128, 4, 2, 128).transpose(0, 1, 3, 2, 4)
        out[:, tid] = g
    return np.ascontiguousarray(out.reshape(DEPTH, 7, 128, 1024))


_NC_CACHE = {}


def kernel(x_prompt, x_sample, cache_k, cache_v, c, c_ctx, w_ada, b_ada, w_in, conv_w,
           gmlp_ln_g, gmlp_ln_b, w_spatial, b_spatial, rpb, w_out, ln_g, ln_b):
    f = lambda a: np.ascontiguousarray(np.asarray(a, dtype=np.float32))
    x_prompt, x_sample, cache_k, cache_v, c, c_ctx = map(f, (x_prompt, x_sample, cache_k, cache_v, c, c_ctx))
    w_ada, b_ada, w_in, conv_w, gmlp_ln_g, gmlp_ln_b = map(f, (w_ada, b_ada, w_in, conv_w, gmlp_ln_g, gmlp_ln_b))
    w_spatial, b_spatial, rpb, w_out, ln_g, ln_b = map(f, (w_spatial, b_spatial, rpb, w_out, ln_g, ln_b))
    if "nc" not in _NC_CACHE:
        _NC_CACHE["nc"] = build()
    nc = _NC_CACHE["nc"]
    ident = np.eye(128, dtype=np.float32)
    tabs = [_tables(rpb, 0), _tables(rpb, 1)]
    in_maps = []
    for i in range(8):
        b, flip = i // 2, i % 2
        xs = x_sample[b][::-1] if flip else x_sample[b]
        xp = x_prompt[4 * i:4 * i + 4]
        if flip:
            xp = xp[:, ::-1]
        ws = w_spatial[:, :, ::-1, ::-1] if flip else w_spatial
        bs = b_spatial[:, :, ::-1] if flip else b_spatial
        cwv = conv_w[:, ::-1, :] if flip else conv_w
        in_maps.append({
            "xp": f(xp.reshape(1024, D)),
            "xs": f(xs[0:3072]),
            "c2": f(np.stack([c[b], c_ctx])),
            "ckT": f(cache_k[b].reshape(DEPTH, 256, 512).transpose(0, 2, 1)),
            "cv": f(cache_v[b].reshape(DEPTH, 256, 512)),
            "w_ada": w_ada, "b_ada": b_ada, "w_in": w_in, "w_out": w_out,
            "conv_w": f(cwv), "gmlp_ln_g": gmlp_ln_g, "gmlp_ln_b": gmlp_ln_b,
            "wsT": f(ws.transpose(0, 3, 1, 2)), "bsT": f(bs.transpose(0, 2, 1)),
            "ebt": tabs[flip], "ln_g": ln_g, "ln_b": ln_b, "ident": ident,
        })
    res = run_bass_kernel_spmd(nc, in_maps, core_ids=list(range(8))).results
    y_prompt = np.empty((32, 256, D), np.float32)
    y_sample = np.empty((4, 4096, D), np.float32)
    new_k = np.empty((32, DEPTH, 256, 8, 64), np.float32)
    new_v = np.empty((32, DEPTH, 256, 8, 64), np.float32)
    for i in range(8):
        b, flip = i // 2, i % 2
        r = res[i]
        yp = np.asarray(r["yp"]).reshape(4, 256, D)
        nk = np.asarray(r["nk"]).reshape(4, DEPTH, 256, 8, 64)
        nv = np.asarray(r["nv"]).reshape(4, DEPTH, 256, 8, 64)
        ys = np.asarray(r["ys"])
        if flip:
            yp = yp[:, ::-1]
            nk = nk[:, :, ::-1]
            nv = nv[:, :, ::-1]
            y_sample[b, 2048:] = ys[::-1]
        else:
            y_sample[b, :2048] = ys
        y_prompt[4 * i:4 * i + 4] = yp
        new_k[4 * i:4 * i + 4] = nk
        new_v[4 * i:4 * i + 4] = nv
    return (y_prompt, y_sample, new_k, new_v)
```
